# Optimizing a Trainium2 kernel written in Bass

```python
import math
import jax, jax.numpy as jnp
from jax import lax
import numpy as np

D_MODEL = 1024
BATCH = 16
SEQ = 4096
DEPTH = 2
DEC_BATCH = 32
DEC_SEQ = 16
PAST_LEN = 2048

CHUNK = 64
N_EVEN = (DEPTH + 1) // 2
N_ODD = DEPTH // 2
ALPHA = (2.0 * DEPTH) ** 0.25
BETA_INIT = (8.0 * DEPTH) ** -0.25
LN_EPS = 1e-5
RMS_EPS = 1e-5

D_FF = 2816

S5_WIDTH = D_MODEL // 2
S5_GROUP = 16
S5_GROUPS = S5_WIDTH // S5_GROUP
S5_STATE = 64
DT_MIN = 0.001
DT_MAX = 0.1
SB_HEADS = 8
SB_HEAD_DIM = 64
SB_WIDTH = SB_HEADS * SB_HEAD_DIM
Q_BLOCK = 128
MIX0_IN = S5_WIDTH + 3 * SB_WIDTH
MIX0_OUT = S5_WIDTH + SB_WIDTH

SSD_INNER = 2 * D_MODEL
SSD_HEAD_DIM = 64
SSD_HEADS = SSD_INNER // SSD_HEAD_DIM
SSD_GROUPS = 4
SSD_HPG = SSD_HEADS // SSD_GROUPS
SSD_STATE = 128
SSD_CONV = 4
SSD_GN = SSD_GROUPS * SSD_STATE
SSD_CONV_DIM = SSD_INNER + 2 * SSD_GN
SSD_IN = SSD_INNER + SSD_CONV_DIM + SSD_HEADS

kernel_name = "s5_stickbreak_ssd_streaming_encoder_step"


def layer_norm(x, g, b):
    xf = x.astype(jnp.float32)
    mu = jnp.mean(xf, axis=-1, keepdims=True)
    var = jnp.mean(jnp.square(xf - mu), axis=-1, keepdims=True)
    return ((xf - mu) * lax.rsqrt(var + LN_EPS) * g + b).astype(x.dtype)


def swiglu(x, w_gate, w_up, w_down):
    return (jax.nn.silu(x @ w_gate) * (x @ w_up)) @ w_down


def _linear_combine(e1, e2):
    a1, b1 = e1
    a2, b2 = e2
    return a1 * a2, a2 * b1 + b2


def s5_discretize(a_re, a_im, log_dt, b_re, b_im, c_re, c_im):
    a = lax.complex(a_re.astype(jnp.float32), a_im.astype(jnp.float32))
    dt = jnp.exp(log_dt.astype(jnp.float32))[:, None]
    a_bar = jnp.exp(a * dt)
    b = lax.complex(b_re.astype(jnp.float32), b_im.astype(jnp.float32))
    b_bar = ((a_bar - 1.0) / a)[..., None] * b
    c_mat = lax.complex(c_re.astype(jnp.float32), c_im.astype(jnp.float32))
    return a_bar, b_bar, c_mat


def s5_recurrence(u, h0, a_bar, b_bar, c_mat):
    bsz, length = u.shape[0], u.shape[1]
    c = min(CHUNK, length)
    n = length // c
    u_chunks = jnp.moveaxis(u.reshape(bsz, n, c, S5_GROUPS, S5_GROUP), 1, 0)

    def step(h, u_c):
        bu = jnp.einsum('bcgp,gnp->bcgn', u_c.astype(jnp.complex64), b_bar)
        bu = bu.at[:, 0].add(a_bar * h)
        a_c = jnp.broadcast_to(a_bar, bu.shape)
        _, hs = lax.associative_scan(_linear_combine, (a_c, bu), axis=1)
        y = jnp.einsum('bcgn,gpn->bcgp', hs, c_mat).real
        return hs[:, -1], y

    h_last, ys = lax.scan(step, h0, u_chunks)
    return jnp.moveaxis(ys, 0, 1).reshape(bsz, length, S5_GROUPS, S5_GROUP), h_last


def sb_block(q_blk, q_pos, k, v):
    z = jnp.einsum('bqhd,bkhd->bhqk', q_blk, k).astype(jnp.float32) * (SB_HEAD_DIM ** -0.5)
    k_pos = jnp.arange(k.shape[1])
    mask = k_pos[None, :] < q_pos[:, None]
    sp = jnp.where(mask, jax.nn.softplus(z), 0.0)
    suffix = lax.cumsum(sp, axis=3, reverse=True) - sp
    w = jnp.where(mask, jnp.exp(jax.nn.log_sigmoid(z) - suffix), 0.0)
    return jnp.einsum('bhqk,bkhd->bqhd', w, v)


def stick_breaking(q, k, v, q_offset):
    bsz, lq = q.shape[0], q.shape[1]
    qb = min(Q_BLOCK, lq)
    nb = lq // qb
    q_blocks = jnp.moveaxis(q.reshape(bsz, nb, qb, SB_HEADS, SB_HEAD_DIM), 1, 0)
    pos = (q_offset + jnp.arange(lq)).reshape(nb, qb)
    out = lax.map(lambda args: sb_block(args[0], args[1], k, v), (q_blocks, pos))
    return jnp.moveaxis(out, 0, 1).reshape(bsz, lq, SB_WIDTH)


def mixer_ab(x, w_in, a_re, a_im, log_dt, b_re, b_im, c_re, c_im, d_skip, w_glu, b_glu, w_out,
             h0, k_cache, v_cache):
    bsz, length, _ = x.shape
    proj = x @ w_in
    u = proj[..., :S5_WIDTH].astype(jnp.float32)
    q = proj[..., S5_WIDTH:S5_WIDTH + SB_WIDTH].reshape(bsz, length, SB_HEADS, SB_HEAD_DIM)
    k = proj[..., S5_WIDTH + SB_WIDTH:S5_WIDTH + 2 * SB_WIDTH].reshape(bsz, length, SB_HEADS, SB_HEAD_DIM)
    v = proj[..., S5_WIDTH + 2 * SB_WIDTH:].reshape(bsz, length, SB_HEADS, SB_HEAD_DIM)
    a_bar, b_bar, c_mat = s5_discretize(a_re, a_im, log_dt, b_re, b_im, c_re, c_im)
    if h0 is None:
        h0 = jnp.zeros((bsz, S5_GROUPS, S5_STATE), jnp.complex64)
    ys, h_last = s5_recurrence(u.reshape(bsz, length, S5_GROUPS, S5_GROUP), h0, a_bar, b_bar, c_mat)
    y = ys.reshape(bsz, length, S5_WIDTH) + d_skip * u
    g = jax.nn.gelu(y)
    s5_out = g * jax.nn.sigmoid(g @ w_glu + b_glu)
    if k_cache is None:
        k_all, v_all, offset = k, v, 0
    else:
        k_all = jnp.concatenate([k_cache.astype(k.dtype), k], axis=1)
        v_all = jnp.concatenate([v_cache.astype(v.dtype), v], axis=1)
        offset = k_cache.shape[1]
    sb_out = stick_breaking(q, k_all, v_all, offset)
    out = jnp.concatenate([s5_out.astype(x.dtype), sb_out.astype(x.dtype)], axis=-1) @ w_out
    return out, h_last.real, h_last.imag, k, v


def ssd_recurrence(xs, dt, a, bm, cm, h0):
    bsz, length = xs.shape[0], xs.shape[1]
    c = min(CHUNK, length)
    n = length // c
    split = lambda t: jnp.moveaxis(t.reshape((bsz, n, c) + t.shape[2:]), 1, 0)
    causal = jnp.tril(jnp.ones((c, c), dtype=bool))[None, :, :, None, None]

    def step(h, inp):
        x_c, dt_c, b_c, c_c = inp
        cs = jnp.cumsum(dt_c * a, axis=1)
        seg = cs[:, :, None] - cs[:, None]
        decay = jnp.exp(jnp.where(causal, seg, -jnp.inf))
        cb = jnp.einsum('btgn,bsgn->btsg', c_c, b_c)
        w = cb[..., None] * decay * dt_c[:, None]
        y = jnp.einsum('btsgr,bsgrp->btgrp', w, x_c)
        y = y + jnp.einsum('btgn,bgrpn->btgrp', c_c, h) * jnp.exp(cs)[..., None]
        to_end = jnp.exp(cs[:, -1:] - cs) * dt_c
        h_new = h * jnp.exp(cs[:, -1])[..., None, None] + jnp.einsum('bsgr,bsgn,bsgrp->bgrpn', to_end, b_c, x_c)
        return h_new, y

    h_last, ys = lax.scan(step, h0, (split(xs), split(dt), split(bm), split(cm)))
    return jnp.moveaxis(ys, 0, 1).reshape(xs.shape), h_last


def mixer_ssd(x, w_in, conv_w, conv_b, dt_bias, a_log, d_skip, norm_g, w_out, conv_state, ssm_state):
    bsz, length, _ = x.shape
    proj = x @ w_in
    z = proj[..., :SSD_INNER]
    xbc = proj[..., SSD_INNER:SSD_INNER + SSD_CONV_DIM]
    dt_raw = proj[..., SSD_INNER + SSD_CONV_DIM:]
    if conv_state is None:
        conv_state = jnp.zeros((bsz, SSD_CONV - 1, SSD_CONV_DIM), xbc.dtype)
    ext = jnp.concatenate([conv_state.astype(xbc.dtype), xbc], axis=1)
    new_conv = ext[:, length:]
    conv = conv_b + sum(ext[:, w:w + length] * conv_w[w] for w in range(SSD_CONV))
    xbc = jax.nn.silu(conv)
    xs = xbc[..., :SSD_INNER].reshape(bsz, length, SSD_GROUPS, SSD_HPG, SSD_HEAD_DIM)
    bm = xbc[..., SSD_INNER:SSD_INNER + SSD_GN].reshape(bsz, length, SSD_GROUPS, SSD_STATE)
    cm = xbc[..., SSD_INNER + SSD_GN:].reshape(bsz, length, SSD_GROUPS, SSD_STATE)
    dt = jax.nn.softplus(dt_raw.astype(jnp.float32) + dt_bias).reshape(bsz, length, SSD_GROUPS, SSD_HPG)
    a = -jnp.exp(a_log.astype(jnp.float32)).reshape(SSD_GROUPS, SSD_HPG)
    if ssm_state is None:
        h0 = jnp.zeros((bsz, SSD_GROUPS, SSD_HPG, SSD_HEAD_DIM, SSD_STATE), jnp.float32)
    else:
        h0 = ssm_state.astype(jnp.float32).reshape(bsz, SSD_GROUPS, SSD_HPG, SSD_HEAD_DIM, SSD_STATE)
    y, h_last = ssd_recurrence(xs, dt, a, bm, cm, h0)
    y = y + d_skip.reshape(SSD_GROUPS, SSD_HPG)[..., None] * xs
    gw = SSD_HPG * SSD_HEAD_DIM
    yg = y.reshape(bsz, length, SSD_GROUPS, gw) * jax.nn.silu(z.astype(jnp.float32)).reshape(bsz, length, SSD_GROUPS, gw)
    yg = yg * lax.rsqrt(jnp.mean(jnp.square(yg), axis=-1, keepdims=True) + RMS_EPS) * norm_g.reshape(SSD_GROUPS, gw)
    out = yg.reshape(bsz, length, SSD_INNER).astype(x.dtype) @ w_out
    return out, h_last.reshape(bsz, SSD_HEADS, SSD_HEAD_DIM, SSD_STATE), new_conv


def setup_inputs(seed: int = 0) -> dict:
    key = jax.random.key(seed)
    ks = iter(jax.random.split(key, 48))
    nrm = lambda shape, scale=1.0: scale * jax.random.normal(next(ks), shape, jnp.float32)
    uni = lambda shape, lo, hi: jax.random.uniform(next(ks), shape, jnp.float32, minval=lo, maxval=hi)
    x_prompt = nrm((BATCH, SEQ, D_MODEL))
    x_sample = nrm((DEC_BATCH, DEC_SEQ, D_MODEL))
    state_s5_re = nrm((N_EVEN, DEC_BATCH, S5_GROUPS, S5_STATE), 0.1)
    state_s5_im = nrm((N_EVEN, DEC_BATCH, S5_GROUPS, S5_STATE), 0.1)
    cache_sb_k = nrm((N_EVEN, DEC_BATCH, PAST_LEN, SB_HEADS, SB_HEAD_DIM))
    cache_sb_v = nrm((N_EVEN, DEC_BATCH, PAST_LEN, SB_HEADS, SB_HEAD_DIM))
    state_ssd = nrm((N_ODD, DEC_BATCH, SSD_HEADS, SSD_HEAD_DIM, SSD_STATE), 0.1)
    state_conv = nrm((N_ODD, DEC_BATCH, SSD_CONV - 1, SSD_CONV_DIM))
    ln_g = 1.0 + nrm((DEPTH, 3, D_MODEL), 0.01)
    ln_b = nrm((DEPTH, 3, D_MODEL), 0.01)
    ffn_w_gate = nrm((DEPTH, 2, D_MODEL, D_FF), D_MODEL ** -0.5)
    ffn_w_up = nrm((DEPTH, 2, D_MODEL, D_FF), D_MODEL ** -0.5)
    ffn_w_down = nrm((DEPTH, 2, D_FF, D_MODEL), BETA_INIT * D_FF ** -0.5)
    mix0_w_in = nrm((N_EVEN, D_MODEL, MIX0_IN), D_MODEL ** -0.5)
    s5_a_re = -0.5 + nrm((N_EVEN, S5_GROUPS, S5_STATE), 0.01)
    s5_a_im = jnp.pi * jnp.arange(S5_STATE, dtype=jnp.float32) + nrm((N_EVEN, S5_GROUPS, S5_STATE), 0.01)
    s5_log_dt = uni((N_EVEN, S5_GROUPS), math.log(DT_MIN), math.log(DT_MAX))
    s5_b_re = nrm((N_EVEN, S5_GROUPS, S5_STATE, S5_GROUP), (2.0 * S5_GROUP) ** -0.5)
    s5_b_im = nrm((N_EVEN, S5_GROUPS, S5_STATE, S5_GROUP), (2.0 * S5_GROUP) ** -0.5)
    s5_c_re = nrm((N_EVEN, S5_GROUPS, S5_GROUP, S5_STATE), S5_STATE ** -0.5)
    s5_c_im = nrm((N_EVEN, S5_GROUPS, S5_GROUP, S5_STATE), S5_STATE ** -0.5)
    s5_d = nrm((N_EVEN, S5_WIDTH))
    s5_w_glu = nrm((N_EVEN, S5_WIDTH, S5_WIDTH), S5_WIDTH ** -0.5)
    s5_b_glu = nrm((N_EVEN, S5_WIDTH), 0.01)
    mix0_w_out = nrm((N_EVEN, MIX0_OUT, D_MODEL), BETA_INIT * MIX0_OUT ** -0.5)
    ssd_w_in = nrm((N_ODD, D_MODEL, SSD_IN), D_MODEL ** -0.5)
    ssd_conv_w = nrm((N_ODD, SSD_CONV, SSD_CONV_DIM), SSD_CONV ** -0.5)
    ssd_conv_b = nrm((N_ODD, SSD_CONV_DIM), 0.01)
    dt0 = jnp.exp(uni((N_ODD, SSD_HEADS), math.log(DT_MIN), math.log(DT_MAX)))
    ssd_dt_bias = dt0 + jnp.log(-jnp.expm1(-dt0))
    ssd_a_log = jnp.log(uni((N_ODD, SSD_HEADS), 1.0, 16.0))
    ssd_d = 1.0 + nrm((N_ODD, SSD_HEADS), 0.01)
    ssd_norm_g = 1.0 + nrm((N_ODD, SSD_INNER), 0.01)
    ssd_w_out = nrm((N_ODD, SSD_INNER, D_MODEL), BETA_INIT * SSD_INNER ** -0.5)
    return {"x_prompt": x_prompt, "x_sample": x_sample,
            "state_s5_re": state_s5_re, "state_s5_im": state_s5_im,
            "cache_sb_k": cache_sb_k, "cache_sb_v": cache_sb_v,
            "state_ssd": state_ssd, "state_conv": state_conv,
            "ln_g": ln_g, "ln_b": ln_b,
            "ffn_w_gate": ffn_w_gate, "ffn_w_up": ffn_w_up, "ffn_w_down": ffn_w_down,
            "mix0_w_in": mix0_w_in, "s5_a_re": s5_a_re, "s5_a_im": s5_a_im, "s5_log_dt": s5_log_dt,
            "s5_b_re": s5_b_re, "s5_b_im": s5_b_im, "s5_c_re": s5_c_re, "s5_c_im": s5_c_im,
            "s5_d": s5_d, "s5_w_glu": s5_w_glu, "s5_b_glu": s5_b_glu, "mix0_w_out": mix0_w_out,
            "ssd_w_in": ssd_w_in, "ssd_conv_w": ssd_conv_w, "ssd_conv_b": ssd_conv_b,
            "ssd_dt_bias": ssd_dt_bias, "ssd_a_log": ssd_a_log, "ssd_d": ssd_d,
            "ssd_norm_g": ssd_norm_g, "ssd_w_out": ssd_w_out}


def reference(x_prompt, x_sample, state_s5_re, state_s5_im, cache_sb_k, cache_sb_v, state_ssd, state_conv,
              ln_g, ln_b, ffn_w_gate, ffn_w_up, ffn_w_down,
              mix0_w_in, s5_a_re, s5_a_im, s5_log_dt, s5_b_re, s5_b_im, s5_c_re, s5_c_im,
              s5_d, s5_w_glu, s5_b_glu, mix0_w_out,
              ssd_w_in, ssd_conv_w, ssd_conv_b, ssd_dt_bias, ssd_a_log, ssd_d, ssd_norm_g, ssd_w_out):

    def trunk(x, states):
        s5_re_l, s5_im_l, k_l, v_l, ssd_l, conv_l = [], [], [], [], [], []
        for l in range(DEPTH):
            x = layer_norm(ALPHA * x + 0.5 * swiglu(x, ffn_w_gate[l, 0], ffn_w_up[l, 0], ffn_w_down[l, 0]),
                           ln_g[l, 0], ln_b[l, 0])
            i = l // 2
            if l % 2 == 0:
                if states is None:
                    h0, kc, vc = None, None, None
                else:
                    h0 = lax.complex(states[0][i].astype(jnp.float32), states[1][i].astype(jnp.float32))
                    kc, vc = states[2][i], states[3][i]
                m, h_re, h_im, k_new, v_new = mixer_ab(
                    x, mix0_w_in[i], s5_a_re[i], s5_a_im[i], s5_log_dt[i], s5_b_re[i], s5_b_im[i],
                    s5_c_re[i], s5_c_im[i], s5_d[i], s5_w_glu[i], s5_b_glu[i], mix0_w_out[i], h0, kc, vc)
                s5_re_l.append(h_re)
                s5_im_l.append(h_im)
                k_l.append(k_new)
                v_l.append(v_new)
            else:
                if states is None:
                    cs, ss = None, None
                else:
                    cs, ss = states[5][i], states[4][i]
                m, h_ssd, c_new = mixer_ssd(
                    x, ssd_w_in[i], ssd_conv_w[i], ssd_conv_b[i], ssd_dt_bias[i], ssd_a_log[i], ssd_d[i],
                    ssd_norm_g[i], ssd_w_out[i], cs, ss)
                ssd_l.append(h_ssd)
                conv_l.append(c_new)
            x = layer_norm(ALPHA * x + m, ln_g[l, 1], ln_b[l, 1])
            x = layer_norm(ALPHA * x + 0.5 * swiglu(x, ffn_w_gate[l, 1], ffn_w_up[l, 1], ffn_w_down[l, 1]),
                           ln_g[l, 2], ln_b[l, 2])
        return (x, jnp.stack(s5_re_l), jnp.stack(s5_im_l), jnp.stack(k_l), jnp.stack(v_l),
                jnp.stack(ssd_l), jnp.stack(conv_l))

    y_prompt, s5_re_p, s5_im_p, k_p, v_p, ssd_p, conv_p = trunk(x_prompt, None)
    y_sample, s5_re_s, s5_im_s, k_s, v_s, ssd_s, conv_s = trunk(
        x_sample, (state_s5_re, state_s5_im, cache_sb_k, cache_sb_v, state_ssd, state_conv))
    return (y_prompt, y_sample, s5_re_p, s5_im_p, k_p, v_p, ssd_p, conv_p,
            s5_re_s, s5_im_s, k_s, v_s, ssd_s, conv_s)
```

```python
import math
import numpy as np
import concourse.bass as bass
import concourse.mybir as mybir
from concourse.bass_utils import run_bass_kernel_spmd
from contextlib import ExitStack

F32 = mybir.dt.float32
BF16 = mybir.dt.bfloat16
AF = mybir.ActivationFunctionType
ALU = mybir.AluOpType

D = 1024
DFF = 2816
NCH = 8
MCH = 22
ALPHA = (2.0 * 2) ** 0.25
LN_EPS = 1e-5
RMS_EPS = 1e-5
PAST = 2048
DEC_SEQ = 16
LSUB = 64
MAGIC = 12582912.0


class Buf:
    __slots__ = ("name", "w", "r", "dsem", "dtot", "psum", "strict")

    def __init__(self, name="", psum=False, strict=False):
        self.name = name
        self.psum = psum
        self.strict = strict
        self.w = None
        self.r = {}
        self.dsem = None
        self.dtot = 0


class Eng:
    def __init__(self, name, sem):
        self.name = name
        self.sem = sem
        self.n = 0
        self.seen = {}
        self.prog = []


class _Rec:
    def __init__(self):
        self.call = None

    def __getattr__(self, name):
        def f(*args, **kwargs):
            self.call = (name, args, kwargs)
            return None
        return f


class FW:
    def __init__(self, nc, es):
        self.nc = nc
        self.es = es
        self.engs = {}
        for name in ("pe", "dve", "act", "pool", "sp"):
            sem = es.enter_context(nc.semaphore("sem_" + name))
            self.engs[name] = Eng(name, sem)
        self.dma_bufs = []
        self.nsem = 5
        self.uid = 0
        self.nosame = False

    def sbuf(self, shape, dtype, name=None):
        self.uid += 1
        return self.es.enter_context(self.nc.sbuf_tensor(name or ("sb%d" % self.uid), list(shape), dtype))

    def psum(self, shape, dtype=F32, name=None):
        self.uid += 1
        return self.es.enter_context(self.nc.psum_tensor(name or ("ps%d" % self.uid), list(shape), dtype))

    def _wait(self, e, tok, strict=False):
        if tok[0] == "e":
            _, name, seq = tok
            if name == e.name and (name in ("pe", "sp") or (self.nosame and not strict)):
                return
            if e.seen.get(name, 0) >= seq:
                return
            e.prog.append(("w", self.engs[name].sem, seq))
            e.seen[name] = seq
        else:
            _, b, val = tok
            key = ("d", id(b))
            if e.seen.get(key, 0) >= val:
                return
            e.prog.append(("w", b.dsem, val))
            e.seen[key] = val

    def _deps(self, e, reads, writes):
        for b in reads:
            if b.w is not None:
                self._wait(e, b.w, b.strict)
            if b.psum:
                for t in b.r.values():
                    if not (t[0] == "e" and t[1] == e.name):
                        self._wait(e, t)
        for b in writes:
            if b.w is not None:
                self._wait(e, b.w)
            for t in b.r.values():
                if not (t[0] == "e" and t[1] == e.name):
                    self._wait(e, t)

    def op(self, ename, fn, reads=(), writes=()):
        e = self.engs[ename]
        self._deps(e, reads, writes)
        e.n += 1
        rec = _Rec()
        fn(rec)
        e.prog.append(("o", rec.call, e.sem, 1))
        tok = ("e", ename, e.n)
        for b in reads:
            b.r[ename] = tok
        for b in writes:
            b.w = tok
            b.r = {}
        return tok

    def dma(self, qname, out, in_, track, reads=(), writes=(), slow=False):
        e = self.engs[qname]
        self._deps(e, reads, writes)
        if track.dsem is None:
            track.dsem = self.es.enter_context(self.nc.semaphore("dsem_%d" % self.nsem))
            self.nsem += 1
            self.dma_bufs.append(track)
        track.dtot += 16
        if slow:
            e.prog.append(("o", ("dma_start", (), dict(out=out, in_=in_, allow_slow_non_contiguous=True)), track.dsem, 16))
        else:
            e.prog.append(("o", ("dma_start", (), dict(out=out, in_=in_)), track.dsem, 16))
        tok = ("d", track, track.dtot)
        key = ("d", id(track))
        for b in reads:
            b.r[key] = tok
        for b in writes:
            b.w = tok
            b.r = {}
        return tok

    def mark(self, name):
        if not hasattr(self, "marks"):
            self.marks = []
        self.marks.append((name, {k: e.n for k, e in self.engs.items()}))

    def barrier(self, bufs=()):
        names = ("pe", "dve", "act", "pool")
        for a in names:
            ea = self.engs[a]
            for b in bufs:
                if b.dsem is not None:
                    self._wait(ea, ("d", b, b.dtot))
            for bn in names:
                if bn != a and self.engs[bn].n > 0:
                    self._wait(ea, ("e", bn, self.engs[bn].n))

    def finish(self):
        e = self.engs["sp"]
        for b in self.dma_bufs:
            e.prog.append(("w", b.dsem, b.dtot))
        for name, o in self.engs.items():
            if name != "sp" and o.n > 0:
                e.prog.append(("w", o.sem, o.n))
        with self.nc.Block() as block:
            def mk(prog):
                def body(h):
                    for it in prog:
                        if it[0] == "w":
                            h.wait_ge(it[1], it[2])
                        else:
                            name, args, kwargs = it[1]
                            getattr(h, name)(*args, **kwargs).then_inc(it[2], it[3])
                return body
            block.tensor(mk(self.engs["pe"].prog))
            block.vector(mk(self.engs["dve"].prog))
            block.scalar(mk(self.engs["act"].prog))
            block.gpsimd(mk(self.engs["pool"].prog))
            block.sync(mk(self.engs["sp"].prog))


def bc(ap, shape):
    return ap.broadcast_to(list(shape))


class Builder:
    def __init__(self, SEQ=4096, NP=2, NS=4, dbg=False, do_sample=True, layers=2):
        self.SEQ, self.NP, self.NS = SEQ, NP, NS
        self.dbg = dbg
        self.do_sample = do_sample
        self.layers = layers
        self.nc = bass.Bass("TRN2", target_bir_lowering=False)
        self.dbg_outs = {}

    def din(self, name, shape):
        return self.nc.dram_tensor(name, list(shape), F32, kind="ExternalInput").ap()

    def dout(self, name, shape):
        return self.nc.dram_tensor(name, list(shape), F32, kind="ExternalOutput").ap()

    def declare(self):
        NP, NS, SEQ = self.NP, self.NS, self.SEQ
        I = {}
        I["x_prompt"] = self.din("x_prompt", [NP, SEQ, D])
        I["x_sample"] = self.din("x_sample", [NS, DEC_SEQ, D])
        I["state_s5_re"] = self.din("state_s5_re", [1, NS, 32, 64])
        I["state_s5_im"] = self.din("state_s5_im", [1, NS, 32, 64])
        I["cache_sb_k"] = self.din("cache_sb_k", [1, NS, PAST, 8, 64])
        I["cache_sb_v"] = self.din("cache_sb_v", [1, NS, PAST, 8, 64])
        I["state_ssd"] = self.din("state_ssd", [1, NS, 32, 64, 128])
        I["state_conv"] = self.din("state_conv", [1, NS, 3, 3072])
        for nm, shp in (("ln_g", [2, 3, D]), ("ln_b", [2, 3, D]), ("ffn_w_gate", [2, 2, D, DFF]),
                        ("ffn_w_up", [2, 2, D, DFF]), ("ffn_w_down", [2, 2, DFF, D]),
                        ("mix0_w_in", [1, D, 2048]), ("s5_a_re", [1, 32, 64]), ("s5_a_im", [1, 32, 64]),
                        ("s5_log_dt", [1, 32]), ("s5_b_re", [1, 32, 64, 16]), ("s5_b_im", [1, 32, 64, 16]),
                        ("s5_c_re", [1, 32, 16, 64]), ("s5_c_im", [1, 32, 16, 64]), ("s5_d", [1, 512]),
                        ("s5_w_glu", [1, 512, 512]), ("s5_b_glu", [1, 512]), ("mix0_w_out", [1, D, D]),
                        ("ssd_w_in", [1, D, 5152]), ("ssd_conv_w", [1, 4, 3072]), ("ssd_conv_b", [1, 3072]),
                        ("ssd_dt_bias", [1, 32]), ("ssd_a_log", [1, 32]), ("ssd_d", [1, 32]),
                        ("ssd_norm_g", [1, 2048]), ("ssd_w_out", [1, 2048, D])):
            I[nm] = self.din(nm, shp)
        O = {}
        O["y_prompt"] = self.dout("y_prompt", [NP, SEQ, D])
        O["y_sample"] = self.dout("y_sample", [NS, DEC_SEQ, D])
        for sfx, nb, sl in (("prompt", NP, SEQ), ("sample", NS, DEC_SEQ)):
            O["s5_re_" + sfx] = self.dout("s5_re_" + sfx, [1, nb, 32, 64])
            O["s5_im_" + sfx] = self.dout("s5_im_" + sfx, [1, nb, 32, 64])
            O["sb_k_" + sfx] = self.dout("sb_k_" + sfx, [1, nb, sl, 8, 64])
            O["sb_v_" + sfx] = self.dout("sb_v_" + sfx, [1, nb, sl, 8, 64])
            O["ssd_" + sfx] = self.dout("ssd_" + sfx, [1, nb, 32, 64, 128])
            O["conv_" + sfx] = self.dout("conv_" + sfx, [1, nb, 3, 3072])
        self.I, self.O = I, O
        S = {}
        for nm in ("ffn_w_gate", "ffn_w_up", "ffn_w_down", "mix0_w_in", "s5_w_glu", "mix0_w_out",
                   "ssd_w_in", "ssd_w_out"):
            shp = list(I[nm].shape)
            S[nm] = self.nc.dram_tensor("scr_" + nm, shp, BF16, kind="Internal").ap()
        self.S = S
        self.SB = {}
        for nm in S:
            if nm.startswith("ffn"):
                for l in range(2):
                    for j in range(2):
                        self.SB[(nm, l, j)] = Buf("scr")
            else:
                self.SB[nm] = Buf("scr")

    def dbg_out(self, name, shape):
        if name not in self.dbg_outs:
            self.dbg_outs[name] = self.dout("dbg_" + name, shape)
        return self.dbg_outs[name]

    def build(self):
        self.declare()
        with ExitStack() as es:
            self.fw = FW(self.nc, es)
            self.alloc()
            self.setup()
            tiles = []
            for s in range(self.NP):
                for i in range(self.SEQ // 512):
                    tiles.append(("p", s, i))
            for t in tiles:
                self.run_tile(t)
            if self.do_sample:
                self.run_tile(("s", 0, 0))
            self.fw.finish()
        return self.nc

    def alloc(self):
        fw = self.fw
        self.xf = fw.sbuf([128, NCH, 512], F32, "xf")
        self.Bxf = [Buf("xf%d" % c) for c in range(NCH)]
        self.xb = fw.sbuf([128, NCH, 512], BF16, "xb")
        self.Bxb = [Buf("xb%d" % c) for c in range(NCH)]
        self.hb = fw.sbuf([128, MCH, 512], BF16, "hb")
        self.Bh = [Buf("h%d" % m) for m in range(MCH)]
        self.hbf = self.hb[:].rearrange("p m n -> p (m n)").bitcast(F32)
        self.ma = fw.sbuf([128, 6144], F32, "ma")
        self.mab = self.ma[:].bitcast(BF16)
        self.NSLAB = 3
        self.slab = [fw.sbuf([128, 4096], BF16, "slab%d" % i) for i in range(self.NSLAB)]
        self.Bslab = [Buf("slab%d" % i) for i in range(self.NSLAB)]
        self.slab_i = 0
        self.KT = fw.sbuf([128, 4, 4096], BF16, "KT")
        self.VT = fw.sbuf([128, 32, 512], BF16, "VT")
        self.BKT = [Buf("KT%d" % i) for i in range(8)]
        self.BVT = [Buf("VT%d" % i) for i in range(8)]
        self.PS = [fw.psum([128, 512], F32, "psb%d" % i) for i in range(8)]
        self.BP = [Buf("ps%d" % i, psum=True) for i in range(8)]
        self.sg = [fw.sbuf([128, 512], F32, "sg%d" % i) for i in range(2)]
        self.Bsg = [Buf("sg%d" % i) for i in range(2)]
        self.knew = self.KT[:, :, 3072:3136]
        self.vnew = self.VT[0:16, 20:24, :]
        self.lnt = fw.sbuf([128, 4, 512], F32, "lnt")
        self.Blnt = [Buf("lnt%d" % i) for i in range(4)]

    def mm(self, out, lhsT, rhs, start, stop, reads, writes, **kw):
        self.fw.op("pe", lambda h: h.matmul(out, lhsT=lhsT, rhs=rhs, start=start, stop=stop, **kw),
                   reads=reads, writes=writes)

    def load_T(self, rows_ap, R, dest, Bdest):
        fw = self.fw
        fw.dma("sp", self.stg[0:R, :], rows_ap, self.Bstg, writes=[self.Bstg])
        fw.op("pe", lambda h: h.transpose(self.PS[7][:, 0:R], self.stg[0:R, :], self.ident[0:R, 0:R]),
              reads=[self.Bstg, self.Bconst], writes=[self.BP[7]])
        fw.op("dve", lambda h: h.tensor_copy(out=dest, in_=self.PS[7][:, 0:R]), reads=[self.BP[7]], writes=[Bdest])

    def setup(self):
        fw, nc, I, S = self.fw, self.nc, self.I, self.S
        def cast(nm, idx_list):
            for idx in idx_list:
                src = I[nm]
                dst = S[nm]
                for i in idx:
                    src = src[i]
                    dst = dst[i]
                key = (nm,) + tuple(idx) if nm.startswith("ffn") else nm
                fw.dma("pool", dst, src, self.SB[key], writes=[self.SB[key]])
        lj = [(l, j) for l in range(2) for j in range(2)]
        cast("ffn_w_gate", lj[:1]); cast("ffn_w_up", lj[:1]); cast("ffn_w_down", lj[:1])
        cast("mix0_w_in", [(0,)]); cast("s5_w_glu", [(0,)]); cast("mix0_w_out", [(0,)])
        cast("ffn_w_gate", lj[1:]); cast("ffn_w_up", lj[1:]); cast("ffn_w_down", lj[1:])
        cast("ssd_w_in", [(0,)]); cast("ssd_w_out", [(0,)])

        self.Bconst = Buf("const")
        self.Bstg = Buf("stg")
        self.stg = fw.sbuf([128, 128], F32, "stg")
        self.ident = fw.sbuf([128, 128], F32, "ident")
        self.identb = fw.sbuf([128, 128], BF16, "identb")
        self.onesb = fw.sbuf([128, 128], BF16, "onesb")
        self.ones1 = fw.sbuf([128, 128], BF16, "ones1")
        self.uincl = fw.sbuf([128, 128], BF16, "uincl")
        self.onesf = fw.sbuf([128, 128], F32, "onesf")
        C = self.Bconst
        fw.op("pool", lambda h: h.memset(self.ident[:], 1.0), writes=[C])
        fw.op("pool", lambda h: h.affine_select(out=self.ident[:], in_=self.ident[:], pattern=[[-1, 128]],
                                                compare_op=ALU.is_equal, fill=0.0, base=0, channel_multiplier=1),
              reads=[C], writes=[C])
        fw.op("pool", lambda h: h.tensor_copy(out=self.identb[:], in_=self.ident[:]), reads=[C], writes=[C])
        fw.op("pool", lambda h: h.memset(self.onesb[:], 1.0 / 1024.0), writes=[C])
        fw.op("pool", lambda h: h.memset(self.ones1[:], 1.0), writes=[C])
        fw.op("pool", lambda h: h.memset(self.onesf[:], 1.0), writes=[C])
        fw.op("pool", lambda h: h.memset(self.uincl[:], 1.0), writes=[C])
        fw.op("pool", lambda h: h.affine_select(out=self.uincl[:], in_=self.uincl[:], pattern=[[-1, 128]],
                                                compare_op=ALU.is_ge, fill=0.0, base=0, channel_multiplier=1),
              reads=[C], writes=[C])
        self.lng = fw.sbuf([128, 48], F32, "lng")
        self.lnb = fw.sbuf([128, 48], F32, "lnb")
        self.load_T(I["ln_g"].rearrange("l j (c p) -> (l j c) p", p=128), 48, self.lng[:], C)
        self.load_T(I["ln_b"].rearrange("l j (c p) -> (l j c) p", p=128), 48, self.lnb[:], C)
        self.setup_mix0()
        if self.layers > 1:
            self.setup_ssd()
        fw.barrier([self.Bconst, self.Bstg])

    def get_slab(self, pieces):
        i = self.slab_i
        self.slab_i = (i + 1) % self.NSLAB
        t, B = self.slab[i], self.Bslab[i]
        for dst_fn, src, key in pieces:
            self.fw.dma("sp", dst_fn(t), src, B, reads=[self.SB[key]], writes=[B])
        return t, B

    def layer_norm(self, idx, N):
        fw = self.fw
        xf, xb, hb = self.xf, self.xb, self.hb
        Bxf, Bxb, Bh, BP = self.Bxf, self.Bxb, self.Bh, self.BP
        ybf = hb[:, 0:8, 0:N]
        sq = hb[:, 8:16, 0:N]
        fw.op("dve", lambda h: h.tensor_copy(out=ybf, in_=xf[:, :, 0:N]), reads=Bxf, writes=Bh[0:8])
        fw.op("act", lambda h: h.activation(out=sq, in_=xf[:, :, 0:N], func=AF.Square), reads=Bxf, writes=Bh[8:16])
        pm, pq = self.PS[0], self.PS[1]
        for c in range(NCH):
            self.mm(pm[:, 0:N], self.onesb[:], hb[:, c, 0:N], c == 0, c == NCH - 1, [Bh[c], self.Bconst], [BP[0]])
        for c in range(NCH):
            self.mm(pq[:, 0:N], self.onesb[:], hb[:, 8 + c, 0:N], c == 0, c == NCH - 1, [Bh[8 + c], self.Bconst], [BP[1]])
        mean, rstd, nmr = self.lnt[:, 0, 0:N], self.lnt[:, 1, 0:N], self.lnt[:, 2, 0:N]
        Bl = self.Blnt
        fw.op("act", lambda h: h.activation(out=mean, in_=pm[:, 0:N], func=AF.Copy), reads=[BP[0]], writes=[Bl[0]])
        fw.op("dve", lambda h: h.tensor_tensor(out=rstd, in0=pm[:, 0:N], in1=mean, op=ALU.mult), reads=[BP[0], Bl[0]], writes=[Bl[1]])
        fw.op("dve", lambda h: h.tensor_tensor(out=rstd, in0=pq[:, 0:N], in1=rstd, op=ALU.subtract), reads=[BP[1], Bl[1]], writes=[Bl[1]])
        fw.op("dve", lambda h: h.tensor_scalar(out=rstd, in0=rstd, scalar1=0.0, scalar2=LN_EPS, op0=ALU.max, op1=ALU.add), reads=[Bl[1]], writes=[Bl[1]])
        fw.op("act", lambda h: h.activation(out=rstd, in_=rstd, func=AF.Sqrt), reads=[Bl[1]], writes=[Bl[1]])
        fw.op("dve", lambda h: h.reciprocal(out=rstd, in_=rstd), reads=[Bl[1]], writes=[Bl[1]])
        fw.op("dve", lambda h: h.scalar_tensor_tensor(out=nmr, in0=mean, scalar=-1.0, in1=rstd, op0=ALU.mult, op1=ALU.mult),
              reads=[Bl[0], Bl[1]], writes=[Bl[2]])
        xv = xf[:, :, 0:N]
        fw.op("dve", lambda h: h.tensor_tensor(out=xv, in0=xv, in1=bc(rstd.unsqueeze(1), [128, NCH, N]), op=ALU.mult),
              reads=Bxf + [Bl[1]], writes=Bxf)
        fw.op("dve", lambda h: h.tensor_tensor(out=xv, in0=xv, in1=bc(nmr.unsqueeze(1), [128, NCH, N]), op=ALU.add),
              reads=Bxf + [Bl[2]], writes=Bxf)
        for c in range(NCH):
            col = idx * 8 + c
            fw.op("act", lambda h, c=c, col=col: h.activation(out=xf[:, c, 0:N], in_=xf[:, c, 0:N], func=AF.Identity,
                                                               scale=self.lng[:, col:col + 1], bias=self.lnb[:, col:col + 1]),
                  reads=[Bxf[c], self.Bconst], writes=[Bxf[c]])
            eng = "dve" if c % 2 == 0 else "pool"
            fw.op(eng, lambda h, c=c: h.tensor_copy(out=xb[:, c, 0:N], in_=xf[:, c, 0:N]), reads=[Bxf[c]], writes=[Bxb[c]])

    def ffn(self, l, j, N):
        fw, S = self.fw, self.S
        xf, xb, hb = self.xf, self.xb, self.hb
        Bxf, Bxb, Bh, BP, PS = self.Bxf, self.Bxb, self.Bh, self.BP, self.PS
        wg = S["ffn_w_gate"][l, j].rearrange("(k p) f -> p k f", p=128)
        wu = S["ffn_w_up"][l, j].rearrange("(k p) f -> p k f", p=128)
        wd = S["ffn_w_down"][l, j].rearrange("(m p) f -> p m f", p=128)
        for s in range(11):
            cols = slice(256 * s, 256 * s + 256)
            t, B = self.get_slab([
                (lambda t: t[:, 0:2048].rearrange("p (k f) -> p k f", k=8), wg[:, :, cols], ("ffn_w_gate", l, j)),
                (lambda t: t[:, 2048:4096].rearrange("p (k f) -> p k f", k=8), wu[:, :, cols], ("ffn_w_up", l, j))])
            tg = t[:, 0:2048].rearrange("p (k f) -> p k f", k=8)
            tu = t[:, 2048:4096].rearrange("p (k f) -> p k f", k=8)
            for mi in range(2):
                m = 2 * s + mi
                pg, pu = PS[m % 2], PS[2 + m % 2]
                Bg, Bu = BP[m % 2], BP[2 + m % 2]
                for k in range(NCH):
                    self.mm(pg[:, 0:N], tg[:, k, mi * 128:(mi + 1) * 128], xb[:, k, 0:N], k == 0, k == NCH - 1, [B, Bxb[k]], [Bg])
                for k in range(NCH):
                    self.mm(pu[:, 0:N], tu[:, k, mi * 128:(mi + 1) * 128], xb[:, k, 0:N], k == 0, k == NCH - 1, [B, Bxb[k]], [Bu])
                sg, Bs = self.sg[m % 2], self.Bsg[m % 2]
                fw.op("act", lambda h, sg=sg, pg=pg: h.activation(out=sg[:, 0:N], in_=pg[:, 0:N], func=AF.Silu), reads=[Bg], writes=[Bs])
                fw.op("dve", lambda h, sg=sg, pu=pu, m=m: h.scalar_tensor_tensor(out=hb[:, m, 0:N], in0=sg[:, 0:N], scalar=0.5, in1=pu[:, 0:N],
                                                                                op0=ALU.mult, op1=ALU.mult),
                      reads=[Bs, Bu], writes=[Bh[m]])
        for o2 in range(4):
            cols = slice(256 * o2, 256 * o2 + 256)
            for half in range(2):
                t, B = self.get_slab([(lambda t: t[:, 0:2816].rearrange("p (m f) -> p m f", m=11),
                                       wd[:, 11 * half:11 * half + 11, cols], ("ffn_w_down", l, j))])
                tv = t[:, 0:2816].rearrange("p (m f) -> p m f", m=11)
                for oi in range(2):
                    o = 2 * o2 + oi
                    pd, Bd = PS[4 + o % 4], BP[4 + o % 4]
                    for mm_ in range(11):
                        m = 11 * half + mm_
                        self.mm(pd[:, 0:N], tv[:, mm_, oi * 128:(oi + 1) * 128], hb[:, m, 0:N], m == 0, m == MCH - 1, [B, Bh[m]], [Bd])
            for oi in range(2):
                o = 2 * o2 + oi
                pd, Bd = PS[4 + o % 4], BP[4 + o % 4]
                fw.op("dve", lambda h, o=o, pd=pd: h.scalar_tensor_tensor(out=xf[:, o, 0:N], in0=xf[:, o, 0:N], scalar=ALPHA, in1=pd[:, 0:N],
                                                                         op0=ALU.mult, op1=ALU.add),
                      reads=[Bxf[o], Bd], writes=[Bxf[o]])
        self.layer_norm(l * 3 + (0 if j == 0 else 2), N)

    def load_x(self, tile, N):
        fw = self.fw
        kind, s, i = tile
        nb = (N + 127) // 128
        stg = self.hbf[:, 0:4096].rearrange("p (b f) -> p b f", b=4)
        Bst = self.Bh[0:16]
        for b in range(nb):
            rows = min(128, N - b * 128)
            if kind == "p":
                src = self.I["x_prompt"][s, i * 512 + b * 128:i * 512 + b * 128 + rows, :]
            else:
                src = self.I["x_sample"].rearrange("s t d -> (s t) d")[b * 128:b * 128 + rows, :]
            fw.dma("sp", stg[0:rows, b, :], src, Bst[4 * b], writes=Bst[4 * b:4 * b + 4])
        for c in range(NCH):
            ps, Bp = self.PS[c % 4], self.BP[c % 4]
            for b in range(nb):
                rows = min(128, N - b * 128)
                fw.op("pe", lambda h, ps=ps, b=b, c=c, rows=rows: h.transpose(ps[:, b * 128:b * 128 + rows], stg[0:rows, b, c * 128:(c + 1) * 128],
                                                                           self.ident[0:rows, 0:rows]),
                      reads=Bst[4 * b:4 * b + 4] + [self.Bconst], writes=[Bp])
            fw.op("act", lambda h, ps=ps, c=c: h.activation(out=self.xf[:, c, 0:N], in_=ps[:, 0:N], func=AF.Copy), reads=[Bp], writes=[self.Bxf[c]])
            fw.op("dve", lambda h, ps=ps, c=c: h.tensor_copy(out=self.xb[:, c, 0:N], in_=ps[:, 0:N]), reads=[Bp], writes=[self.Bxb[c]])

    def store_y(self, tile, N):
        fw = self.fw
        kind, s, i = tile
        nb = (N + 127) // 128
        stg = self.hbf[:, 0:4096].rearrange("p (b f) -> p b f", b=4)
        Bst = self.Bh[0:16]
        for b in range(nb):
            rows = min(128, N - b * 128)
            for half in range(2):
                ps, Bp = self.PS[(2 * b + half) % 4], self.BP[(2 * b + half) % 4]
                for cc in range(4):
                    c = 4 * half + cc
                    fw.op("pe", lambda h, ps=ps, b=b, c=c, cc=cc, rows=rows: h.transpose(ps[0:rows, cc * 128:(cc + 1) * 128],
                                                                                     self.xf[:, c, b * 128:b * 128 + rows], self.ident[:, :]),
                          reads=[self.Bxf[c], self.Bconst], writes=[Bp])
                eng = "act" if half == 0 else "dve"
                if eng == "act":
                    fw.op("act", lambda h, ps=ps, b=b, half=half, rows=rows: h.activation(out=stg[0:rows, b, half * 512:(half + 1) * 512], in_=ps[0:rows, :], func=AF.Copy),
                          reads=[Bp], writes=[Bst[4 * b + 2 * half], Bst[4 * b + 2 * half + 1]])
                else:
                    fw.op("dve", lambda h, ps=ps, b=b, half=half, rows=rows: h.tensor_copy(out=stg[0:rows, b, half * 512:(half + 1) * 512], in_=ps[0:rows, :]),
                          reads=[Bp], writes=[Bst[4 * b + 2 * half], Bst[4 * b + 2 * half + 1]])
            if kind == "p":
                dst = self.O["y_prompt"][s, i * 512 + b * 128:i * 512 + b * 128 + rows, :]
            else:
                dst = self.O["y_sample"].rearrange("s t d -> (s t) d")[b * 128:b * 128 + rows, :]
            fw.dma("sp", dst, stg[0:rows, b, :], Bst[4 * b], reads=Bst[4 * b:4 * b + 4])

    def dbg_sb(self, name, ap, B, shape):
        if not self.dbg or name in self.dbg_outs:
            return
        o = self.dbg_out(name, shape)
        self.fw.dma("sp", o, ap, B, reads=[B])

    def dump_x(self, name, tile, N):
        if not self.dbg:
            return
        kind, s, i = tile
        ntile = self.NP * (self.SEQ // 512) + 1
        o = self.dbg_out(name, [ntile, NCH, 128, 512])
        ti = (s * (self.SEQ // 512) + i) if kind == "p" else ntile - 1
        for c in range(NCH):
            self.fw.dma("sp", o[ti, c, :, 0:N], self.xf[:, c, 0:N], self.Bxf[c], reads=[self.Bxf[c]])

    def run_tile(self, tile):
        import os
        self.stop = int(os.environ.get("KSTOP", "9"))
        kind = tile[0]
        N = 512 if kind == "p" else self.NS * DEC_SEQ
        if self.stop < 1:
            return
        self.fw.mark("tile %s %d %d" % tile)
        self.load_x(tile, N)
        self.fw.mark("ffn00")
        ksub = os.environ.get("KSUB", "")
        if ksub == "load":
            self.dump_x("ffn00", tile, N)
            return
        if ksub == "ln":
            self.layer_norm(0, N)
            self.dump_x("ffn00", tile, N)
            return
        self.ffn(0, 0, N)
        self.dump_x("ffn00", tile, N)
        if self.stop < 2:
            return
        self.fw.mark("mixer0")
        self.mixer0(tile, N)
        self.dump_x("mix0", tile, N)
        self.fw.mark("ffn01")
        self.ffn(0, 1, N)
        self.dump_x("l0", tile, N)
        if self.layers > 1:
            self.fw.mark("ffn10")
            self.ffn(1, 0, N)
            self.fw.mark("mixer1")
            self.mixer1(tile, N)
            self.dump_x("mix1", tile, N)
            self.fw.mark("ffn11")
            self.ffn(1, 1, N)
        self.fw.mark("store")
        self.store_y(tile, N)
        self.fw.mark("end")


def _ts(h, out, in0, s1, s2, op0, op1=None):
    if op1 is None:
        return h.tensor_scalar(out=out, in0=in0, scalar1=s1, scalar2=None, op0=op0)
    return h.tensor_scalar(out=out, in0=in0, scalar1=s1, scalar2=s2, op0=op0, op1=op1)


def setup_mix0(self):
    fw, I = self.fw, self.I
    C = self.Bconst
    dve = lambda fn, r=(C,), w=(C,): fw.op("dve", fn, reads=list(r), writes=list(w))
    self.s5d = fw.sbuf([128, 4], F32, "s5d")
    self.bglu = fw.sbuf([128, 4], F32, "bglu")
    self.load_T(I["s5_d"].rearrange("o (c p) -> (o c) p", p=128), 4, self.s5d[:], C)
    self.load_T(I["s5_b_glu"].rearrange("o (c p) -> (o c) p", p=128), 4, self.bglu[:], C)
    self.negones = fw.sbuf([128, 128], BF16, "negones")
    self.nuincl = fw.sbuf([128, 128], BF16, "nuincl")
    fw.op("pool", lambda h: h.memset(self.negones[:], -1.0), writes=[C])
    fw.op("pool", lambda h: h.tensor_scalar(out=self.nuincl[:], in0=self.uincl[:], scalar1=-1.0, scalar2=None, op0=ALU.mult),
          reads=[C], writes=[C])
    L = LSUB
    pr = self.ma[:, 3712:3904].rearrange("p (i j) -> p i j", i=12)
    P = lambda i: pr[:, i, :]
    self.load_T(I["s5_a_re"].rearrange("o (j g) n -> (o j) (g n)", g=2), 16, P(0), C)
    self.load_T(I["s5_a_im"].rearrange("o (j g) n -> (o j) (g n)", g=2), 16, P(1), C)
    ld = I["s5_log_dt"].rearrange("o (j t) -> o j t", t=2)
    fw.dma("sp", pr[0:64, 2, :], ld[0:1, :, 0].broadcast_to([64, 16]), C, writes=[C], slow=True)
    fw.dma("sp", pr[64:128, 2, :], ld[0:1, :, 1].broadcast_to([64, 16]), C, writes=[C], slow=True)
    fw.op("act", lambda h: h.activation(out=P(2), in_=P(2), func=AF.Exp), reads=[C], writes=[C])
    dve(lambda h: h.tensor_tensor(out=P(3), in0=P(0), in1=P(2), op=ALU.mult))
    dve(lambda h: h.tensor_tensor(out=P(4), in0=P(1), in1=P(2), op=ALU.mult))
    self.s5r = fw.sbuf([128, 16], F32, "s5r")
    fw.op("act", lambda h: h.activation(out=self.s5r[:], in_=P(3), func=AF.Exp), reads=[C], writes=[C])
    dve(lambda h: _ts(h, P(5), P(4), 1.0 / (2 * math.pi), MAGIC, ALU.mult, ALU.add))
    dve(lambda h: _ts(h, P(5), P(5), MAGIC, None, ALU.subtract))
    dve(lambda h: h.scalar_tensor_tensor(out=P(5), in0=P(5), scalar=-2 * math.pi, in1=P(4), op0=ALU.mult, op1=ALU.add))
    dve(lambda h: _ts(h, P(5), P(5), 0.125, None, ALU.mult))
    dve(lambda h: h.tensor_tensor(out=P(6), in0=P(5), in1=P(5), op=ALU.mult))
    sc = [-1.0 / 6, 1.0 / 120, -1.0 / 5040, 1.0 / 362880]
    cc = [-0.5, 1.0 / 24, -1.0 / 720, 1.0 / 40320, -1.0 / 3628800]
    dve(lambda h: _ts(h, P(7), P(6), sc[3], sc[2], ALU.mult, ALU.add))
    for co in (sc[1], sc[0], 1.0):
        dve(lambda h: h.tensor_tensor(out=P(7), in0=P(7), in1=P(6), op=ALU.mult))
        dve(lambda h, co=co: _ts(h, P(7), P(7), co, None, ALU.add))
    dve(lambda h: h.tensor_tensor(out=P(7), in0=P(7), in1=P(5), op=ALU.mult))
    dve(lambda h: _ts(h, P(8), P(6), cc[4], cc[3], ALU.mult, ALU.add))
    for co in (cc[2], cc[1], cc[0], 1.0):
        dve(lambda h: h.tensor_tensor(out=P(8), in0=P(8), in1=P(6), op=ALU.mult))
        dve(lambda h, co=co: _ts(h, P(8), P(8), co, None, ALU.add))
    for _ in range(3):
        dve(lambda h: h.tensor_tensor(out=P(9), in0=P(8), in1=P(7), op=ALU.mult))
        dve(lambda h: h.tensor_tensor(out=P(8), in0=P(8), in1=P(8), op=ALU.mult))
        dve(lambda h: h.tensor_tensor(out=P(7), in0=P(7), in1=P(7), op=ALU.mult))
        dve(lambda h: h.tensor_tensor(out=P(8), in0=P(8), in1=P(7), op=ALU.subtract))
        dve(lambda h: _ts(h, P(7), P(9), 2.0, None, ALU.mult))
    self.Ct = fw.sbuf([128, 16, L + 1], F32, "Ct")
    self.St = fw.sbuf([128, 16, L + 1], F32, "St")
    Ct, St = self.Ct, self.St
    dve(lambda h: h.memset(Ct[:, :, 0:1], 1.0))
    dve(lambda h: h.memset(St[:, :, 0:1], 0.0))
    dve(lambda h: h.tensor_copy(out=Ct[:, :, 1:2], in_=P(8).unsqueeze(2)))
    dve(lambda h: h.tensor_copy(out=St[:, :, 1:2], in_=P(7).unsqueeze(2)))
    tmpA = self.ma[:, 0:2048].rearrange("p (j l) -> p j l", j=16)
    n = 2
    while n <= L:
        hn = n // 2
        ch, sh = Ct[:, :, hn:hn + 1], St[:, :, hn:hn + 1]
        dve(lambda h, ch=ch, sh=sh: h.tensor_tensor(out=P(9).unsqueeze(2), in0=ch, in1=sh, op=ALU.mult))
        dve(lambda h, ch=ch: h.tensor_tensor(out=P(10).unsqueeze(2), in0=ch, in1=ch, op=ALU.mult))
        dve(lambda h, sh=sh: h.tensor_tensor(out=P(11).unsqueeze(2), in0=sh, in1=sh, op=ALU.mult))
        dve(lambda h, n=n: h.tensor_tensor(out=Ct[:, :, n:n + 1], in0=P(10).unsqueeze(2), in1=P(11).unsqueeze(2), op=ALU.subtract))
        dve(lambda h, n=n: _ts(h, St[:, :, n:n + 1], P(9).unsqueeze(2), 2.0, None, ALU.mult))
        m = min(n, L + 1 - n)
        if m > 1:
            cn = bc(Ct[:, :, n:n + 1], [128, 16, m - 1])
            sn = bc(St[:, :, n:n + 1], [128, 16, m - 1])
            c0, s0 = Ct[:, :, 1:m], St[:, :, 1:m]
            tA = tmpA[:, :, 0:m - 1]
            dve(lambda h, c0=c0, cn=cn, tA=tA: h.tensor_tensor(out=tA, in0=c0, in1=cn, op=ALU.mult))
            dve(lambda h, s0=s0, sn=sn, n=n, m=m: h.tensor_tensor(out=Ct[:, :, n + 1:n + m], in0=s0, in1=sn, op=ALU.mult))
            dve(lambda h, tA=tA, n=n, m=m: h.tensor_tensor(out=Ct[:, :, n + 1:n + m], in0=tA, in1=Ct[:, :, n + 1:n + m], op=ALU.subtract))
            dve(lambda h, s0=s0, cn=cn, tA=tA: h.tensor_tensor(out=tA, in0=s0, in1=cn, op=ALU.mult))
            dve(lambda h, c0=c0, sn=sn, n=n, m=m: h.tensor_tensor(out=St[:, :, n + 1:n + m], in0=c0, in1=sn, op=ALU.mult))
            dve(lambda h, tA=tA, n=n, m=m: h.tensor_tensor(out=St[:, :, n + 1:n + m], in0=tA, in1=St[:, :, n + 1:n + m], op=ALU.add))
        n *= 2
    dve(lambda h: h.tensor_tensor(out=P(9), in0=self.s5r[:], in1=P(8), op=ALU.mult))
    dve(lambda h: h.tensor_tensor(out=P(10), in0=self.s5r[:], in1=P(7), op=ALU.mult))
    dve(lambda h: _ts(h, P(9), P(9), -1.0, None, ALU.add))
    dve(lambda h: h.tensor_tensor(out=P(2), in0=P(0), in1=P(0), op=ALU.mult))
    dve(lambda h: h.tensor_tensor(out=P(3), in0=P(1), in1=P(1), op=ALU.mult))
    dve(lambda h: h.tensor_tensor(out=P(2), in0=P(2), in1=P(3), op=ALU.add))
    dve(lambda h: h.reciprocal(out=P(2), in_=P(2)))
    dve(lambda h: h.tensor_tensor(out=P(3), in0=P(9), in1=P(0), op=ALU.mult))
    dve(lambda h: h.tensor_tensor(out=P(4), in0=P(10), in1=P(1), op=ALU.mult))
    dve(lambda h: h.tensor_tensor(out=P(3), in0=P(3), in1=P(4), op=ALU.add))
    dve(lambda h: h.tensor_tensor(out=P(5), in0=P(3), in1=P(2), op=ALU.mult))
    dve(lambda h: h.tensor_tensor(out=P(3), in0=P(10), in1=P(0), op=ALU.mult))
    dve(lambda h: h.tensor_tensor(out=P(4), in0=P(9), in1=P(1), op=ALU.mult))
    dve(lambda h: h.tensor_tensor(out=P(3), in0=P(3), in1=P(4), op=ALU.subtract))
    dve(lambda h: h.tensor_tensor(out=P(6), in0=P(3), in1=P(2), op=ALU.mult))
    braw = self.ma[:, 2048:3072].rearrange("p (t j q) -> p t j q", t=4, j=16)
    for t, nm in ((0, "s5_b_re"), (1, "s5_b_im")):
        fw.dma("sp", braw[:, t, :, :], I[nm][0].rearrange("g n q -> (g n) q").rearrange("(j p) q -> p j q", p=128), C, writes=[C])
    fre = bc(P(5).unsqueeze(2), [128, 16, 16])
    fim = bc(P(6).unsqueeze(2), [128, 16, 16])
    t3 = tmpA[:, :, 0:16]
    dve(lambda h: h.tensor_tensor(out=braw[:, 2], in0=braw[:, 0], in1=fre, op=ALU.mult))
    dve(lambda h: h.tensor_tensor(out=t3, in0=braw[:, 1], in1=fim, op=ALU.mult))
    dve(lambda h: h.tensor_tensor(out=braw[:, 2], in0=braw[:, 2], in1=t3, op=ALU.subtract))
    dve(lambda h: h.tensor_tensor(out=braw[:, 3], in0=braw[:, 1], in1=fre, op=ALU.mult))
    dve(lambda h: h.tensor_tensor(out=t3, in0=braw[:, 0], in1=fim, op=ALU.mult))
    dve(lambda h: h.tensor_tensor(out=braw[:, 3], in0=braw[:, 3], in1=t3, op=ALU.add))
    if self.dbg:
        o = self.dbg_out("s5par", [128, 12, 16])
        fw.dma("sp", o, pr, C, reads=[C])
        o = self.dbg_out("s5ct", [128, 16, L + 1])
        fw.dma("sp", o, Ct[:], C, reads=[C])
        o = self.dbg_out("s5st", [128, 16, L + 1])
        fw.dma("sp", o, St[:], C, reads=[C])
        o = self.dbg_out("s5r", [128, 16])
        fw.dma("sp", o, self.s5r[:], C, reads=[C])
        o = self.dbg_out("s5braw", [128, 4, 16, 16])
        fw.dma("sp", o, braw, C, reads=[C])
    self.LBc = fw.sbuf([128, 8, 128], BF16, "LBc")
    self.LCc = fw.sbuf([128, 16, 2, 32], BF16, "LCc")
    mj = self.ma[:, 3584:3712]
    Bmj = Buf("mj")
    for j in range(16):
        c, q = j // 4, j % 4
        for reim in range(2):
            fw.op("dve", lambda h: h.memset(mj[:], 0.0), writes=[Bmj])
            g0 = (2 * j) % 8
            fw.op("dve", lambda h, j=j, reim=reim, g0=g0: h.tensor_copy(out=mj[0:64, g0 * 16:g0 * 16 + 16], in_=braw[0:64, 2 + reim, j, :]),
                  reads=[C], writes=[Bmj])
            fw.op("dve", lambda h, j=j, reim=reim, g0=g0: h.tensor_copy(out=mj[64:128, (g0 + 1) * 16:(g0 + 1) * 16 + 16], in_=braw[64:128, 2 + reim, j, :]),
                  reads=[C], writes=[Bmj])
            fw.op("pe", lambda h: h.transpose(self.PS[7][:, 0:128], mj[:], self.ident[:]), reads=[Bmj, C], writes=[self.BP[7]])
            fw.op("dve", lambda h, c=c, q=q, reim=reim: h.tensor_copy(out=self.LBc[32 * q:32 * q + 32, c * 2 + reim, :], in_=self.PS[7][32 * q:32 * q + 32, 0:128]),
                  reads=[self.BP[7]], writes=[C])
    craw = self.ma[:, 3072:3584].rearrange("p (t c n) -> p t c n", t=2, c=4)
    for t, nm in ((0, "s5_c_re"), (1, "s5_c_im")):
        fw.dma("sp", craw[:, t, :, :], I[nm][0].rearrange("g p n -> (g p) n").rearrange("(c q) n -> q c n", q=128), C, writes=[C])
    e8 = fw.sbuf([128, 2, 8], F32, "e8")
    fw.op("pool", lambda h: h.memset(e8[:, 0, :], 1.0), reads=[C], writes=[C])
    fw.op("pool", lambda h: h.affine_select(out=e8[:, 0, :], in_=e8[:, 0, :], pattern=[[-16, 8]], compare_op=ALU.is_ge, fill=0.0,
                                            base=0, channel_multiplier=1), reads=[C], writes=[C])
    fw.op("pool", lambda h: h.affine_select(out=e8[:, 0, :], in_=e8[:, 0, :], pattern=[[16, 8]], compare_op=ALU.is_ge, fill=0.0,
                                            base=15, channel_multiplier=-1), reads=[C], writes=[C])
    fw.op("pool", lambda h: h.tensor_scalar(out=e8[:, 1, :], in0=e8[:, 0, :], scalar1=-1.0, scalar2=None, op0=ALU.mult), reads=[C], writes=[C])
    for j in range(16):
        c, q = j // 4, j % 4
        g0 = (2 * j) % 8
        for reim in range(2):
            for gl in range(2):
                fw.op("dve", lambda h, c=c, reim=reim, gl=gl, g0=g0: h.tensor_scalar(
                    out=mj[:, gl * 64:(gl + 1) * 64], in0=craw[:, reim, c, :], scalar1=e8[:, reim, g0 + gl:g0 + gl + 1], scalar2=None, op0=ALU.mult),
                    reads=[C], writes=[Bmj])
            fw.op("pe", lambda h: h.transpose(self.PS[7][:, 0:128], mj[:], self.ident[:]), reads=[Bmj, C], writes=[self.BP[7]])
            fw.op("dve", lambda h, j=j, q=q, reim=reim: h.tensor_copy(out=self.LCc[:, j, reim, :], in_=self.PS[7][:, 32 * q:32 * q + 32]),
                  reads=[self.BP[7]], writes=[C])
    self.G = fw.sbuf([128, 16, 2], F32, "s5G")
    self.BG = Buf("s5G", strict=True)
    self.HL = fw.sbuf([128, 2, 16], F32, "s5HL")
    self.BHL = Buf("s5HL")
    self.s5tmp = fw.sbuf([128, 4], F32, "s5tmp")
    self.Bs5tmp = Buf("s5tmp")
    self.crowb2 = [self.lnt[0:1, 2 + i, :].bitcast(BF16).rearrange("p (a n) -> p a n", a=2) for i in range(2)]
    self.Bcrowb2 = [Buf("crowb0"), Buf("crowb1")]
    self.Bcrow = Buf("crow")
    self.Bcrowb = Buf("crowb")


def s5_phase(self, tile, N):
    fw = self.fw
    kind, s, ti = tile
    C = self.Bconst
    PS, BP = self.PS, self.BP
    ma, mab, hbf = self.ma, self.mab, self.hbf
    uf = hbf[:, 0:2048].rearrange("p (c n) -> p c n", c=4)
    yf = hbf[:, 2048:4096].rearrange("p (c n) -> p c n", c=4)
    ub = self.hb[:, 16:20, :]
    s5out = mab[:, 2048:4096].rearrange("p (c n) -> p c n", c=4)
    T = [ma[:, 3072 + 512 * i:3072 + 512 * (i + 1)] for i in range(4)]
    BT = self.BT
    hre = [mab[:, 10240 + 1024 * i:10240 + 1024 * i + 512] for i in range(2)]
    him = [mab[:, 10240 + 1024 * i + 512:10240 + 1024 * (i + 1)] for i in range(2)]
    Bhh = self.Bhh
    Ct, St = self.Ct, self.St
    if kind == "p":
        nsub, L = 512 // LSUB, LSUB
    else:
        nsub, L = self.NS, DEC_SEQ
    dve = lambda fn, r, w: fw.op("dve", fn, reads=list(r), writes=list(w))
    last_tile = (kind == "s") or (ti == self.SEQ // 512 - 1)
    if kind == "p" and ti == 0:
        dve(lambda h: h.memset(self.G[:], 0.0), [], [self.BG])
    for j in range(16):
        c, q = j // 4, j % 4
        pa, pb = PS[0], PS[1]
        rhs = ub[32 * q:32 * q + 32, c, 0:N]
        self.mm(pa[:, 0:N], self.LBc[32 * q:32 * q + 32, c * 2 + 0, :], rhs, True, True, [C, self.Bub], [BP[0]], tile_position=(32 * q, 0))
        self.mm(pb[:, 0:N], self.LBc[32 * q:32 * q + 32, c * 2 + 1, :], rhs, True, True, [C, self.Bub], [BP[1]], tile_position=(32 * q, 0))
        v3 = lambda ap: ap[:, 0:N].rearrange("p (s l) -> p s l", s=nsub)
        ct = bc(Ct[:, j:j + 1, 0:L], [128, nsub, L])
        st = bc(St[:, j:j + 1, 0:L], [128, nsub, L])
        A, Bb, Dd, E = T
        dve(lambda h: h.tensor_tensor(out=v3(A), in0=v3(pa), in1=ct, op=ALU.mult), [BP[0], C], [BT[0]])
        dve(lambda h: h.tensor_tensor(out=v3(Bb), in0=v3(pb), in1=st, op=ALU.mult), [BP[1], C], [BT[1]])
        dve(lambda h: h.tensor_tensor(out=A[:, 0:N], in0=A[:, 0:N], in1=Bb[:, 0:N], op=ALU.add), [BT[0], BT[1]], [BT[0]])
        dve(lambda h: h.tensor_tensor(out=v3(Dd), in0=v3(pb), in1=ct, op=ALU.mult), [BP[1], C], [BT[2]])
        dve(lambda h: h.tensor_tensor(out=v3(Bb), in0=v3(pa), in1=st, op=ALU.mult), [BP[0], C], [BT[1]])
        dve(lambda h: h.tensor_tensor(out=Dd[:, 0:N], in0=Dd[:, 0:N], in1=Bb[:, 0:N], op=ALU.subtract), [BT[2], BT[1]], [BT[2]])
        if j == 5 and kind == "p" and self.dbg:
            dve(lambda h: h.tensor_copy(out=E[:, 0:N], in_=pa[:, 0:N]), [BP[0]], [BT[3]])
            self.dbg_sb("bu_re", E[:, 0:N], BT[3], [128, N])
            dve(lambda h: h.tensor_copy(out=self.sg[0][:, 0:256].rearrange("p (a b) -> p a b", a=2), in_=self.LBc[:, 2:4, :]), [C], [self.Bsg[0]])
            self.dbg_sb("lbc", self.sg[0][:, 0:256], self.Bsg[0], [128, 256])
            self.dbg_sb("ub", ub[:, :, 0:N].bitcast(mybir.dt.uint16), self.Bub, [128, 4, N]) if False else None
        if j == 5 and kind == "p":
            self.dbg_sb("ut_re", A[:, 0:N], BT[0], [128, N])
            self.dbg_sb("ut_im", Dd[:, 0:N], BT[2], [128, N])
            self.dbg_sb("uf", uf[:, :, 0:N], self.Buf_uf, [128, 4, N])
        rr = bc(self.s5r[:, j:j + 1], [128, L])
        for sub in range(nsub):
            sl = slice(sub * L, (sub + 1) * L)
            if kind == "s":
                gi = self.Gs[:, sub, j, :]
                BGi = self.BGs
            else:
                gi = self.G[:, j, :]
                BGi = self.BG
            dve(lambda h, sl=sl, gi=gi: h.tensor_tensor_scan(out=Bb[:, sl], data0=rr, data1=A[:, sl], initial=gi[:, 0:1], op0=ALU.mult, op1=ALU.add),
                [BT[0], BGi, C], [BT[1]])
            dve(lambda h, sl=sl, gi=gi: h.tensor_tensor_scan(out=E[:, sl], data0=rr, data1=Dd[:, sl], initial=gi[:, 1:2], op0=ALU.mult, op1=ALU.add),
                [BT[2], BGi, C], [BT[3]])
            e0 = (sub + 1) * L - 1
            gre_l, gim_l = Bb[:, e0:e0 + 1], E[:, e0:e0 + 1]
            tmp = self.s5tmp
            if kind == "p":
                cL, sL = Ct[:, j, L:L + 1], St[:, j, L:L + 1]
                dve(lambda h, gim_l=gim_l, sL=sL: h.tensor_tensor(out=tmp[:, 0:1], in0=gim_l, in1=sL, op=ALU.mult), [BT[3], C], [self.Bs5tmp])
                dve(lambda h, gre_l=gre_l, sL=sL: h.tensor_tensor(out=tmp[:, 1:2], in0=gre_l, in1=sL, op=ALU.mult), [BT[1], C], [self.Bs5tmp])
                dve(lambda h, gre_l=gre_l, cL=cL, j=j: h.scalar_tensor_tensor(out=self.G[:, j, 0:1], in0=gre_l, scalar=cL, in1=tmp[:, 0:1], op0=ALU.mult, op1=ALU.subtract),
                    [BT[1], C, self.Bs5tmp], [self.BG])
                dve(lambda h, gim_l=gim_l, cL=cL, j=j: h.scalar_tensor_tensor(out=self.G[:, j, 1:2], in0=gim_l, scalar=cL, in1=tmp[:, 1:2], op0=ALU.mult, op1=ALU.add),
                    [BT[3], C, self.Bs5tmp], [self.BG])
            if last_tile and (kind == "s" or sub == nsub - 1):
                c1, s1 = Ct[:, j, L - 1:L], St[:, j, L - 1:L]
                if kind == "s":
                    ore, oim = self.HLs[:, sub, 0, j:j + 1], self.HLs[:, sub, 1, j:j + 1]
                else:
                    ore, oim = self.HL[:, 0, j:j + 1], self.HL[:, 1, j:j + 1]
                dve(lambda h, gim_l=gim_l, s1=s1: h.tensor_tensor(out=tmp[:, 2:3], in0=gim_l, in1=s1, op=ALU.mult), [BT[3], C], [self.Bs5tmp])
                dve(lambda h, gre_l=gre_l, s1=s1: h.tensor_tensor(out=tmp[:, 3:4], in0=gre_l, in1=s1, op=ALU.mult), [BT[1], C], [self.Bs5tmp])
                dve(lambda h, gre_l=gre_l, c1=c1, ore=ore: h.scalar_tensor_tensor(out=ore, in0=gre_l, scalar=c1, in1=tmp[:, 2:3], op0=ALU.mult, op1=ALU.subtract),
                    [BT[1], C, self.Bs5tmp], [self.BHL])
                dve(lambda h, gim_l=gim_l, c1=c1, oim=oim: h.scalar_tensor_tensor(out=oim, in0=gim_l, scalar=c1, in1=tmp[:, 3:4], op0=ALU.mult, op1=ALU.add),
                    [BT[3], C, self.Bs5tmp], [self.BHL])
        if j == 5 and kind == "p":
            self.dbg_sb("g_re", Bb[:, 0:N], BT[1], [128, N])
            self.dbg_sb("g_im", E[:, 0:N], BT[3], [128, N])
        hr, hi = hre[j % 2], him[j % 2]
        Bh2 = Bhh[j % 2]
        dve(lambda h: h.tensor_tensor(out=v3(A), in0=v3(Bb), in1=ct, op=ALU.mult), [BT[1], C], [BT[0]])
        dve(lambda h: h.tensor_tensor(out=v3(Dd), in0=v3(E), in1=st, op=ALU.mult), [BT[3], C], [BT[2]])
        dve(lambda h, hr=hr: h.tensor_tensor(out=hr[:, 0:N], in0=A[:, 0:N], in1=Dd[:, 0:N], op=ALU.subtract), [BT[0], BT[2]], [Bh2])
        dve(lambda h: h.tensor_tensor(out=v3(A), in0=v3(E), in1=ct, op=ALU.mult), [BT[3], C], [BT[0]])
        dve(lambda h: h.tensor_tensor(out=v3(Dd), in0=v3(Bb), in1=st, op=ALU.mult), [BT[1], C], [BT[2]])
        dve(lambda h, hi=hi: h.tensor_tensor(out=hi[:, 0:N], in0=A[:, 0:N], in1=Dd[:, 0:N], op=ALU.add), [BT[0], BT[2]], [Bh2])
        py, Bpy = PS[2 + c % 2], BP[2 + c % 2]
        self.mm(py[32 * q:32 * q + 32, 0:N], self.LCc[:, j, 0, :], hr[:, 0:N], True, False, [C, Bh2], [Bpy], tile_position=(0, 32 * q))
        self.mm(py[32 * q:32 * q + 32, 0:N], self.LCc[:, j, 1, :], hi[:, 0:N], False, True, [C, Bh2], [Bpy], tile_position=(0, 32 * q))
        if q == 3:
            dve(lambda h, c=c, py=py: h.scalar_tensor_tensor(out=yf[:, c, 0:N], in0=uf[:, c, 0:N], scalar=self.s5d[:, c:c + 1], in1=py[:, 0:N],
                                                            op0=ALU.mult, op1=ALU.add), [self.Buf_uf, Bpy, C], [self.Buf_yf])
    if last_tile:
        nseq = self.NS if kind == "s" else 1
        for sq in range(nseq):
            for reim in range(2):
                src = self.HLs[:, sq, reim, :] if kind == "s" else self.HL[:, reim, :]
                fw.op("pe", lambda h, src=src: h.transpose(PS[7][0:16, 0:128], src, self.ident[:]), reads=[self.BHL, C], writes=[BP[7]])
                fw.op("dve", lambda h: h.tensor_copy(out=self.stg[0:16, :], in_=PS[7][0:16, 0:128]), reads=[BP[7]], writes=[self.Bstg])
                nm = ("s5_re_" if reim == 0 else "s5_im_") + ("sample" if kind == "s" else "prompt")
                seq = sq if kind == "s" else s
                dst = self.O[nm][0, seq].rearrange("(j g) n -> j (g n)", g=2)
                fw.dma("sp", dst, self.stg[0:16, :], self.Bstg, reads=[self.Bstg])
    tmpf = ma[:, 3072:5120].rearrange("p (c n) -> p c n", c=4)
    yv, tv = yf[:, :, 0:N], tmpf[:, :, 0:N]
    gb = ub
    dve(lambda h: h.tensor_tensor(out=tv, in0=yv, in1=yv, op=ALU.mult), [self.Buf_yf], BT)
    dve(lambda h: _ts(h, tv, tv, 0.044715, 1.0, ALU.mult, ALU.add), BT, BT)
    dve(lambda h: h.tensor_tensor(out=tv, in0=tv, in1=yv, op=ALU.mult), BT + [self.Buf_yf], BT)
    fw.op("act", lambda h: h.activation(out=tv, in_=tv, func=AF.Sigmoid, scale=2.0 * math.sqrt(2.0 / math.pi)), reads=BT, writes=BT)
    dve(lambda h: h.tensor_tensor(out=yv, in0=yv, in1=tv, op=ALU.mult), BT + [self.Buf_yf], [self.Buf_yf])
    fw.op("pool", lambda h: h.tensor_copy(out=gb[:, :, 0:N], in_=yv), reads=[self.Buf_yf], writes=[self.Bub])
    wglu = self.S["s5_w_glu"][0].rearrange("(k p) f -> p k f", p=128)
    t, B = self.get_slab([(lambda t: t[:, 0:2048].rearrange("p (k f) -> p k f", k=4), wglu, "s5_w_glu")])
    tg = t[:, 0:2048].rearrange("p (k f) -> p k f", k=4)
    for oc in range(4):
        pg, Bg = PS[oc % 2], BP[oc % 2]
        for k in range(4):
            self.mm(pg[:, 0:N], tg[:, k, oc * 128:(oc + 1) * 128], gb[:, k, 0:N], k == 0, k == 3, [B, self.Bub], [Bg])
        sg, Bs = self.sg[oc % 2], self.Bsg[oc % 2]
        fw.op("act", lambda h, sg=sg, pg=pg, oc=oc: h.activation(out=sg[:, 0:N], in_=pg[:, 0:N], func=AF.Sigmoid, bias=self.bglu[:, oc:oc + 1]),
              reads=[Bg, C], writes=[Bs])
        dve(lambda h, sg=sg, oc=oc: h.tensor_tensor(out=s5out[:, oc, 0:N], in0=yf[:, oc, 0:N], in1=sg[:, 0:N], op=ALU.mult),
            [Bs, self.Buf_yf], [self.Bs5out])


def attn_head_loop(self, N, qcols, blocks, diag_base):
    fw = self.fw
    C = self.Bconst
    PS, BP = self.PS, self.BP
    ma, mab = self.ma, self.mab
    QT = mab[:, 0:2048].rearrange("p (c n) -> p c n", c=4)
    attno = mab[:, 4096:6144].rearrange("p (c n) -> p c n", c=4)
    e_ = [ma[:, 3072 + 512 * i:3072 + 512 * (i + 1)] for i in range(2)]
    spb = [mab[:, 8192 + 512 * i:8192 + 512 * (i + 1)] for i in range(2)]
    wb = [mab[:, 9216 + 512 * i:9216 + 512 * (i + 1)] for i in range(2)]
    Be, Bsp, Bw = self.Be, self.Bsp, self.Bw
    crb = [self.lnt[0:33, 2 + i, :].bitcast(BF16)[:, 0:512] for i in range(2)]
    Bcr = self.Bcrowb2
    q0, q1 = qcols
    nblk = len(blocks)
    for hp in range(4):
        c = hp
        po, Bpo = PS[7], BP[7]
        pst, Bpt = PS[6], BP[6]
        pss = [PS[0], PS[1]]
        Bps = [BP[0], BP[1]]

        def qk(bi):
            k0, kr, vblk, jj = blocks[bi]
            Bkt = self.BKT[min(k0 // 512, 7)]
            for x in range(2):
                hb_ = 64 * x
                self.mm(pss[x][0:kr, 0:N], self.KT[hb_:hb_ + 64, c, k0:k0 + kr], QT[hb_:hb_ + 64, c, q0:q1], True, True,
                        [Bkt, self.BQT], [Bps[x]], tile_position=(hb_, 0))

        qk(0)
        for bi, (k0, kr, vblk, jj) in enumerate(blocks):
            Bvt = self.BVT[min(vblk // 4, 7)]
            psc = [PS[2 + 2 * (bi % 2)], PS[3 + 2 * (bi % 2)]]
            Bpc = [BP[2 + 2 * (bi % 2)], BP[3 + 2 * (bi % 2)]]
            for x in range(2):
                fw.op("act", lambda h, x=x: h.activation(out=e_[x][0:kr, 0:N], in_=pss[x][0:kr, 0:N], func=AF.Exp), reads=[Bps[x]], writes=[Be[x]])
                if jj is not None:
                    fw.op("pool", lambda h, x=x: h.affine_select(out=e_[x][0:kr, 0:N], in_=e_[x][0:kr, 0:N], pattern=[[1, N]], compare_op=ALU.is_gt,
                                                                 fill=0.0, base=-128 * jj, channel_multiplier=-1), reads=[Be[x]], writes=[Be[x]])
                fw.op("act", lambda h, x=x: h.activation(out=spb[x][0:kr, 0:N], in_=e_[x][0:kr, 0:N], func=AF.Ln, bias=1.0), reads=[Be[x]], writes=[Bsp[x]])
                self.mm(psc[x][0:kr, 0:N], self.nuincl[0:kr, 0:kr], spb[x][0:kr, 0:N], True, bi == 0, [C, Bsp[x]], [Bpc[x]])
                if bi > 0:
                    cbp, Bcbp = crb[(bi - 1) % 2], Bcr[(bi - 1) % 2]
                    self.mm(psc[x][0:kr, 0:N], self.negones[32 * x:32 * x + 1, 0:kr], cbp[32 * x:32 * x + 1, 0:N], False, True,
                            [C, Bcbp], [Bpc[x]], tile_position=(32 * x, 0))
            if bi < nblk - 1:
                for x in range(2):
                    self.mm(pst[32 * x:32 * x + 1, 0:N], self.ones1[0:kr, 0:1], spb[x][0:kr, 0:N], bi == 0, bi == nblk - 2,
                            [C, Bsp[x]], [Bpt], tile_position=(0, 32 * x))
                cb_, Bcb = crb[bi % 2], Bcr[bi % 2]
                fw.op("dve", lambda h, cb_=cb_: h.tensor_copy(out=cb_[0:33, 0:N], in_=pst[0:33, 0:N]), reads=[Bpt], writes=[Bcb])
                qk(bi + 1)
            for x in range(2):
                fw.op("act", lambda h, x=x: h.activation(out=psc[x][0:kr, 0:N], in_=psc[x][0:kr, 0:N], func=AF.Exp), reads=[Bpc[x]], writes=[Bpc[x]])
                fw.op("dve", lambda h, x=x: h.tensor_tensor(out=wb[x][0:kr, 0:N], in0=psc[x][0:kr, 0:N], in1=e_[x][0:kr, 0:N], op=ALU.mult),
                      reads=[Bpc[x], Be[x]], writes=[Bw[x]])
            for x in range(2):
                hd = 2 * hp + x
                self.mm(po[64 * x:64 * x + 64, 0:N], self.VT[0:kr, vblk, hd * 64:(hd + 1) * 64], wb[x][0:kr, 0:N], bi == 0, bi == nblk - 1,
                        [Bvt, Bw[x]], [Bpo], tile_position=(0, 64 * x))
        fw.op("dve", lambda h, c=c: h.tensor_copy(out=attno[:, c, q0:q1], in_=po[:, 0:N]), reads=[Bpo], writes=[self.Battno])


def attn_sample_loop(self, qcols, blocks):
    fw = self.fw
    C = self.Bconst
    PS, BP = self.PS, self.BP
    ma, mab = self.ma, self.mab
    QT = mab[:, 0:2048].rearrange("p (c n) -> p c n", c=4)
    attno = mab[:, 4096:6144].rearrange("p (c n) -> p c n", c=4)
    e_ = [ma[:, 3072 + 512 * i:3072 + 512 * (i + 1)] for i in range(2)]
    spb = [mab[:, 8192 + 512 * i:8192 + 512 * (i + 1)] for i in range(2)]
    wb = [mab[:, 9216 + 512 * i:9216 + 512 * (i + 1)] for i in range(2)]
    Be, Bsp, Bw = self.Be, self.Bsp, self.Bw
    crb = [self.lnt[0:1, 2 + i, :].bitcast(BF16)[:, 0:512] for i in range(2)]
    Bcr = self.Bcrowb2
    q0, q1 = qcols
    NQ = q1 - q0
    W = 8 * NQ
    nblk = len(blocks)
    po, Bpo = PS[7], BP[7]
    pst, Bpt = PS[6], BP[6]

    def qk(bi):
        k0, kr, vblk, jj = blocks[bi]
        Bkt = self.BKT[min(k0 // 512, 7)]
        for hd in range(8):
            c, xx = hd // 2, hd % 2
            hb_ = 64 * xx
            pss, Bps = PS[2 * (bi % 2) + xx], BP[2 * (bi % 2) + xx]
            self.mm(pss[0:kr, c * NQ:(c + 1) * NQ], self.KT[hb_:hb_ + 64, c, k0:k0 + kr], QT[hb_:hb_ + 64, c, q0:q1], True, True,
                    [Bkt, self.BQT], [Bps], tile_position=(hb_, 0))

    qk(0)
    for bi, (k0, kr, vblk, jj) in enumerate(blocks):
        x = bi % 2
        Bvt = self.BVT[min(vblk // 4, 7)]
        psc, Bpc = PS[4 + x], BP[4 + x]
        H4 = 4 * NQ
        for xx in range(2):
            pss, Bps = PS[2 * x + xx], BP[2 * x + xx]
            fw.op("act", lambda h, xx=xx, pss=pss: h.activation(out=e_[x][0:kr, xx * H4:(xx + 1) * H4], in_=pss[0:kr, 0:H4], func=AF.Exp),
                  reads=[Bps], writes=[Be[x]])
        if jj is not None:
            fw.op("pool", lambda h: h.affine_select(out=e_[x][0:kr, 0:W], in_=e_[x][0:kr, 0:W], pattern=[[0, 8], [1, NQ]], compare_op=ALU.is_gt,
                                                    fill=0.0, base=-128 * jj, channel_multiplier=-1), reads=[Be[x]], writes=[Be[x]])
        fw.op("act", lambda h: h.activation(out=spb[x][0:kr, 0:W], in_=e_[x][0:kr, 0:W], func=AF.Ln, bias=1.0), reads=[Be[x]], writes=[Bsp[x]])
        self.mm(psc[0:kr, 0:W], self.nuincl[0:kr, 0:kr], spb[x][0:kr, 0:W], True, bi == 0, [C, Bsp[x]], [Bpc])
        if bi > 0:
            self.mm(psc[0:kr, 0:W], self.negones[0:1, 0:kr], crb[(bi - 1) % 2][0:1, 0:W], False, True, [C, Bcr[(bi - 1) % 2]], [Bpc])
        if bi < nblk - 1:
            self.mm(pst[0:1, 0:W], self.ones1[0:kr, 0:1], spb[x][0:kr, 0:W], bi == 0, bi == nblk - 2, [C, Bsp[x]], [Bpt])
            fw.op("dve", lambda h: h.tensor_copy(out=crb[x][0:1, 0:W], in_=pst[0:1, 0:W]), reads=[Bpt], writes=[Bcr[x]])
            qk(bi + 1)
        fw.op("act", lambda h: h.activation(out=psc[0:kr, 0:W], in_=psc[0:kr, 0:W], func=AF.Exp), reads=[Bpc], writes=[Bpc])
        fw.op("dve", lambda h: h.tensor_tensor(out=wb[x][0:kr, 0:W], in0=psc[0:kr, 0:W], in1=e_[x][0:kr, 0:W], op=ALU.mult),
              reads=[Bpc, Be[x]], writes=[Bw[x]])
        for hd in range(8):
            hp, xx = hd // 2, hd % 2
            slot = xx * 4 + hp
            self.mm(po[64 * xx:64 * xx + 64, hp * NQ:(hp + 1) * NQ], self.VT[0:kr, vblk, hd * 64:(hd + 1) * 64], wb[x][0:kr, slot * NQ:(slot + 1) * NQ],
                    bi == 0 and hp == 0, bi == nblk - 1, [Bvt, Bw[x]], [Bpo], tile_position=(0, 64 * xx), skip_group_check=True)
    fw.op("dve", lambda h: h.tensor_copy(out=attno[:, :, q0:q1], in_=po[:, 0:4 * NQ].rearrange("p (c q) -> p c q", c=4)), reads=[Bpo], writes=[self.Battno])


def mixer0(self, tile, N):
    fw = self.fw
    kind, s, ti = tile
    C = self.Bconst
    PS, BP = self.PS, self.BP
    ma, mab, hbf = self.ma, self.mab, self.hbf
    xf, xb, Bxf, Bxb = self.xf, self.xb, self.Bxf, self.Bxb
    if not hasattr(self, "Buf_uf"):
        self.Buf_uf, self.Buf_yf, self.Bub, self.Bkvst = Buf("uf"), Buf("yf"), Buf("ub"), Buf("kvst")
        self.BQT, self.Bs5out, self.Battno = Buf("QT"), Buf("s5out"), Buf("attno")
        self.BT = [Buf("T%d" % i) for i in range(4)]
        self.Bhh = [Buf("hh%d" % i) for i in range(2)]
        self.Be = [Buf("e%d" % i) for i in range(2)]
        self.Bsp = [Buf("sp%d" % i) for i in range(2)]
        self.Bw = [Buf("w%d" % i) for i in range(2)]
        self.Bknew = Buf("knew")
    fw.barrier()
    uf = hbf[:, 0:2048].rearrange("p (c n) -> p c n", c=4)
    ub = self.hb[:, 16:20, :]
    kvst = hbf[:, 5120:5632]
    QT = mab[:, 0:2048].rearrange("p (c n) -> p c n", c=4)
    s5out = mab[:, 2048:4096].rearrange("p (c n) -> p c n", c=4)
    attno = mab[:, 4096:6144].rearrange("p (c n) -> p c n", c=4)
    win = self.S["mix0_w_in"][0].rearrange("(k p) f -> p k f", p=128)
    pos = ti * 512

    def slab_in(ci):
        t, B = self.get_slab([(lambda t: t[:, :].rearrange("p (k f) -> p k f", k=8), win[:, :, 512 * ci:512 * ci + 512], "mix0_w_in")])
        return t[:, :].rearrange("p (k f) -> p k f", k=8), B

    tw, B = slab_in(0)
    for oc in range(4):
        pp, Bp = PS[oc % 2], BP[oc % 2]
        for k in range(NCH):
            self.mm(pp[:, 0:N], tw[:, k, oc * 128:(oc + 1) * 128], xb[:, k, 0:N], k == 0, k == NCH - 1, [B, Bxb[k]], [Bp])
        fw.op("act", lambda h, pp=pp, oc=oc: h.activation(out=uf[:, oc, 0:N], in_=pp[:, 0:N], func=AF.Copy), reads=[Bp], writes=[self.Buf_uf])
        fw.op("dve", lambda h, pp=pp, oc=oc: h.tensor_copy(out=ub[:, oc, 0:N], in_=pp[:, 0:N]), reads=[Bp], writes=[self.Bub])
    tw, B = slab_in(1)
    for oc in range(4):
        pp, Bp = PS[2 + oc % 2], BP[2 + oc % 2]
        for k in range(NCH):
            self.mm(pp[:, 0:N], tw[:, k, oc * 128:(oc + 1) * 128], xb[:, k, 0:N], k == 0, k == NCH - 1, [B, Bxb[k]], [Bp])
        fw.op("act", lambda h, pp=pp, oc=oc: h.activation(out=QT[:, oc, 0:N], in_=pp[:, 0:N], func=AF.Copy, scale=0.125), reads=[Bp], writes=[self.BQT])
    tw, B = slab_in(2)
    for oc in range(4):
        pp, Bp = PS[oc % 2], BP[oc % 2]
        for k in range(NCH):
            self.mm(pp[:, 0:N], tw[:, k, oc * 128:(oc + 1) * 128], xb[:, k, 0:N], k == 0, k == NCH - 1, [B, Bxb[k]], [Bp])
        if kind == "p":
            fw.op("act", lambda h, pp=pp, oc=oc: h.activation(out=self.KT[:, oc, pos:pos + N], in_=pp[:, 0:N], func=AF.Copy), reads=[Bp], writes=[self.BKT[ti]])
        else:
            fw.op("act", lambda h, pp=pp, oc=oc: h.activation(out=self.knew[:, oc, 0:N], in_=pp[:, 0:N], func=AF.Copy), reads=[Bp], writes=[self.Bknew])
    twv, Bv = slab_in(3)
    if kind == "p":
        segs = [(b * 128, 128, s, pos + b * 128) for b in range(4)]
    else:
        segs = [(sq * DEC_SEQ, DEC_SEQ, sq, 0) for sq in range(self.NS)]
    sfx = "prompt" if kind == "p" else "sample"
    for (c0, rows, seq, sp_) in segs:
        for which, (tws, Bs_) in enumerate(((tw, B), (twv, Bv))):
            pp, Bp = PS[2 + which], BP[2 + which]
            for k in range(NCH):
                self.mm(pp[0:rows, :], xb[:, k, c0:c0 + rows], tws[:, k, :], k == 0, k == NCH - 1, [Bs_, Bxb[k]], [Bp])
            fw.op("act", lambda h, pp=pp, rows=rows: h.activation(out=kvst[0:rows, :], in_=pp[0:rows, :], func=AF.Copy), reads=[Bp], writes=[self.Bkvst])
            if which == 1:
                if kind == "p":
                    blk = sp_ // 128
                    fw.op("dve", lambda h, pp=pp, blk=blk: h.tensor_copy(out=self.VT[:, blk, :], in_=pp[:, :]), reads=[Bp], writes=[self.BVT[blk // 4]])
                else:
                    fw.op("dve", lambda h, pp=pp, seq=seq, rows=rows: h.tensor_copy(out=self.vnew[0:rows, seq, :], in_=pp[0:rows, :]), reads=[Bp], writes=[self.Bknew])
            nm = ("sb_k_" if which == 0 else "sb_v_") + sfx
            dst = self.O[nm][0, seq, sp_:sp_ + rows].rearrange("t h d -> t (h d)")
            fw.dma("sp", dst, kvst[0:rows, :], self.Bkvst, reads=[self.Bkvst])
    if self.stop < 3:
        return
    if kind == "s":
        self.s5_sample_init()
    fw.mark("s5")
    s5_phase(self, tile, N)
    fw.barrier()
    fw.mark("attn")
    if self.stop < 4:
        return
    if kind == "p":
        nkb = 4 * (ti + 1)
        blocks = []
        for kb in reversed(range(nkb)):
            jj = kb - 4 * ti if kb >= 4 * ti else None
            blocks.append((kb * 128, 128, kb, jj))
        attn_head_loop(self, N, (0, N), blocks, 0)
    else:
        for sq in range(self.NS):
            self.load_cache(sq)
            blocks = [(PAST, DEC_SEQ, 16, 0)] + [(kb * 128, 128, kb, None) for kb in reversed(range(PAST // 128))]
            attn_sample_loop(self, (sq * DEC_SEQ, (sq + 1) * DEC_SEQ), blocks)
    fw.mark("out0")
    wout = self.S["mix0_w_out"][0].rearrange("(k p) f -> p k f", p=128)
    for half in range(2):
        t, B = self.get_slab([(lambda t: t[:, :].rearrange("p (k f) -> p k f", k=8), wout[:, :, 512 * half:512 * half + 512], "mix0_w_out")])
        tw = t[:, :].rearrange("p (k f) -> p k f", k=8)
        for oc in range(4):
            o = 4 * half + oc
            pp, Bp = PS[o % 2], BP[o % 2]
            for k in range(NCH):
                rhs = s5out[:, k, 0:N] if k < 4 else attno[:, k - 4, 0:N]
                Br = self.Bs5out if k < 4 else self.Battno
                self.mm(pp[:, 0:N], tw[:, k, oc * 128:(oc + 1) * 128], rhs, k == 0, k == NCH - 1, [B, Br], [Bp])
            fw.op("dve", lambda h, o=o, pp=pp: h.scalar_tensor_tensor(out=xf[:, o, 0:N], in0=xf[:, o, 0:N], scalar=ALPHA, in1=pp[:, 0:N],
                                                                     op0=ALU.mult, op1=ALU.add), reads=[Bxf[o], Bp], writes=[Bxf[o]])
    fw.barrier([self.Bkvst])
    self.layer_norm(1, N)


def s5_sample_init(self):
    fw, I = self.fw, self.I
    C = self.Bconst
    NS = self.NS
    if not hasattr(self, "Gs"):
        self.Gs = fw.sbuf([128, NS, 16, 2], F32, "Gs")
        self.BGs = Buf("Gs", strict=True)
        self.HLs = fw.sbuf([128, NS, 2, 16], F32, "HLs")
        self.h0 = fw.sbuf([128, 2, NS, 16], F32, "h0")
        self.h0t = fw.sbuf([128, NS, 16], F32, "h0t")
    for reim, nm in ((0, "state_s5_re"), (1, "state_s5_im")):
        self.load_T(I[nm][0].rearrange("s (j g) n -> (s j) (g n)", g=2), NS * 16, self.h0[:, reim].rearrange("p s j -> p (s j)"), self.BGs)
    c1 = bc(self.Ct[:, :, 1:2].rearrange("p j o -> p o j"), [128, NS, 16])
    s1 = bc(self.St[:, :, 1:2].rearrange("p j o -> p o j"), [128, NS, 16])
    hre, him = self.h0[:, 0], self.h0[:, 1]
    B = self.BGs
    dve = lambda fn: fw.op("dve", fn, reads=[B, C], writes=[B])
    dve(lambda h: h.tensor_tensor(out=self.h0t[:], in0=him, in1=s1, op=ALU.mult))
    dve(lambda h: h.tensor_tensor(out=self.Gs[:, :, :, 0], in0=hre, in1=c1, op=ALU.mult))
    dve(lambda h: h.tensor_tensor(out=self.Gs[:, :, :, 0], in0=self.Gs[:, :, :, 0], in1=self.h0t[:], op=ALU.subtract))
    dve(lambda h: h.tensor_tensor(out=self.h0t[:], in0=hre, in1=s1, op=ALU.mult))
    dve(lambda h: h.tensor_tensor(out=self.Gs[:, :, :, 1], in0=him, in1=c1, op=ALU.mult))
    dve(lambda h: h.tensor_tensor(out=self.Gs[:, :, :, 1], in0=self.Gs[:, :, :, 1], in1=self.h0t[:], op=ALU.add))


def load_cache(self, sq):
    fw, I = self.fw, self.I
    C = self.Bconst
    PS, BP = self.PS, self.BP
    stg = self.hbf[:, 0:4096].rearrange("p (b f) -> p b f", b=8)
    Bst = [self.Buf_uf, self.Buf_yf]
    kc = I["cache_sb_k"][0, sq].rearrange("(b p) h d -> p b (h d)", p=128)
    vc = I["cache_sb_v"][0, sq].rearrange("(b p) h d -> p b (h d)", p=128)
    fw.dma("pool", self.VT[:, 0:16, :], vc, self.BVT[0], writes=self.BVT[0:4])
    for half in range(2):
        for q4 in range(2):
            b0 = half * 8 + q4 * 4
            fw.dma("sp", stg[:, q4 * 4:q4 * 4 + 4, :], kc[:, b0:b0 + 4, :], Bst[q4], writes=[Bst[q4]])
        for c in range(4):
            for g in range(2):
                pp, Bp = PS[(2 * c + g) % 4], BP[(2 * c + g) % 4]
                for bb in range(4):
                    b = g * 4 + bb
                    fw.op("pe", lambda h, pp=pp, b=b, bb=bb, c=c: h.transpose(pp[:, bb * 128:(bb + 1) * 128], stg[:, b, c * 128:(c + 1) * 128], self.ident[:]),
                          reads=[Bst[g], C], writes=[Bp])
                col0 = (half * 8 + g * 4) * 128
                eng = "act" if (c + g) % 2 == 0 else "dve"
                if eng == "act":
                    fw.op("act", lambda h, pp=pp, c=c, col0=col0: h.activation(out=self.KT[:, c, col0:col0 + 512], in_=pp[:, :], func=AF.Copy),
                          reads=[Bp], writes=[self.BKT[col0 // 512]])
                else:
                    fw.op("dve", lambda h, pp=pp, c=c, col0=col0: h.tensor_copy(out=self.KT[:, c, col0:col0 + 512], in_=pp[:, :]),
                          reads=[Bp], writes=[self.BKT[col0 // 512]])
    fw.op("dve", lambda h: h.tensor_copy(out=self.KT[:, :, PAST:PAST + DEC_SEQ], in_=self.knew[:, :, sq * DEC_SEQ:(sq + 1) * DEC_SEQ]),
          reads=[self.Bknew], writes=[self.BKT[4]])
    fw.op("dve", lambda h: h.tensor_copy(out=self.VT[0:DEC_SEQ, 16, :], in_=self.vnew[0:DEC_SEQ, sq, :]), reads=[self.Bknew], writes=[self.BVT[4]])

def setup_ssd(self):
    fw, I = self.fw, self.I
    C = self.Bconst
    self.cw = fw.sbuf([128, 96], F32, "cw")
    self.cb = fw.sbuf([128, 24], F32, "cb")
    self.ng = fw.sbuf([128, 16], F32, "ng")
    self.load_T(I["ssd_conv_w"].rearrange("o w (c p) -> (o w c) p", p=128), 96, self.cw[:], C)
    self.load_T(I["ssd_conv_b"].rearrange("o (c p) -> (o c) p", p=128), 24, self.cb[:], C)
    self.load_T(I["ssd_norm_g"].rearrange("o (c p) -> (o c) p", p=128), 16, self.ng[:], C)
    self.dtb = fw.sbuf([128, 32], F32, "dtb")
    self.Arow = fw.sbuf([128, 32], F32, "Arow")
    fw.dma("sp", self.dtb[:], I["ssd_dt_bias"][0:1, :].broadcast_to([128, 32]), C, writes=[C])
    fw.dma("sp", self.Arow[:], I["ssd_a_log"][0:1, :].broadcast_to([128, 32]), C, writes=[C])
    fw.op("act", lambda h: h.activation(out=self.Arow[:], in_=self.Arow[:], func=AF.Exp), reads=[C], writes=[C])
    fw.op("dve", lambda h: h.tensor_scalar(out=self.Arow[:], in0=self.Arow[:], scalar1=-1.0, scalar2=None, op0=ALU.mult), reads=[C], writes=[C])
    self.dvec = fw.sbuf([128, 16], F32, "dvec")
    dd = I["ssd_d"].rearrange("o (c t) -> o c t", t=2)
    fw.dma("sp", self.dvec[0:64, :], dd[0:1, :, 0].broadcast_to([64, 16]), C, writes=[C], slow=True)
    fw.dma("sp", self.dvec[64:128, :], dd[0:1, :, 1].broadcast_to([64, 16]), C, writes=[C], slow=True)
    self.tri = fw.sbuf([64, 64], F32, "tri")
    fw.op("pool", lambda h: h.memset(self.tri[:], 1.0), writes=[C])
    fw.op("pool", lambda h: h.affine_select(out=self.tri[:], in_=self.tri[:], pattern=[[1, 64]], compare_op=ALU.is_ge, fill=0.0,
                                            base=0, channel_multiplier=-1), reads=[C], writes=[C])
    self.ones512 = fw.sbuf([128, 128], BF16, "ones512")
    fw.op("pool", lambda h: h.memset(self.ones512[:], 1.0 / 512.0), writes=[C])
    self.hst = fw.sbuf([128, 2048], F32, "hst")
    self.hstb = fw.sbuf([128, 2048], BF16, "hstb")
    self.Bhst = [Buf("hst%d" % g) for g in range(4)]
    self.Bhstb = [Buf("hstb%d" % g) for g in range(4)]
    self.ctail = fw.sbuf([128, 24, 3], F32, "ctail")
    self.Bctail = Buf("ctail")
    self.dtt = fw.sbuf([64, 8, 4, 32], F32, "dtt")
    self.dend = fw.sbuf([128, 8, 32], F32, "dend")
    self.Bdtt = Buf("dtt")
    self.mcb = fw.sbuf([64, 64], F32, "mcb")
    self.Bmcb = Buf("mcb")


def ssd_core(self, c0, n, nseg, Lc):
    fw = self.fw
    C = self.Bconst
    PS, BP = self.PS, self.BP
    ma, mab, hbf = self.ma, self.mab, self.hbf
    xb, Bxb = self.xb, self.Bxb
    B1 = self.B1
    BT_ = mab[:, 0:2048].rearrange("p (c n) -> p c n", c=4)
    CT_ = mab[:, 2048:4096].rearrange("p (c n) -> p c n", c=4)
    zs = mab[:, 4096:6144].rearrange("p (c n) -> p c n", c=4)
    xsT = mab[:, 6144:8192].rearrange("p (c n) -> p c n", c=4)
    stg = ma[:, 4096:4611]
    DE = ma[:, 4612:5124]
    E2 = ma[:, 5124:5636]
    yg = ma[:, 4096:6144].rearrange("p (c n) -> p c n", c=4)
    yn = self.hb[:, 0:16, :]
    hbb = self.hb[:].rearrange("p m n -> p (m n)")
    sgb0 = self.sg[0][:].bitcast(BF16)
    sgb1 = self.sg[1][:].bitcast(BF16)
    Wt_ = [hbb[:, 8192:8704], sgb0[:, 0:512]]
    Ctl_ = [hbb[:, 8704:9216], sgb0[:, 512:1024]]
    xt_ = [hbb[:, 9216:9728], sgb1[:, 0:512]]
    xh_ = [hbb[:, 9728:10240], sgb1[:, 512:1024]]
    btm_ = [hbb[:, 10240:10368], hbb[:, 10368:10496]]
    self.ssd_it = 0
    win = self.S["ssd_w_in"][0].rearrange("(k p) f -> p k f", p=128)
    dve = lambda fn, r, w: fw.op("dve", fn, reads=list(r), writes=list(w))
    act = lambda fn, r, w: fw.op("act", fn, reads=list(r), writes=list(w))

    def slab_in(ci):
        w = 512 if ci < 10 else 32
        t, B = self.get_slab([(lambda t: t[:, 0:8 * w].rearrange("p (k f) -> p k f", k=8), win[:, :, 512 * ci:512 * ci + w], "ssd_w_in")])
        return t[:, 0:8 * w].rearrange("p (k f) -> p k f", k=8), B

    self.pp_i = 0

    def proj(tw, B, cc):
        i = 5 + self.pp_i % 2
        self.pp_i += 1
        pp, Bp = PS[i], BP[i]
        for k in range(NCH):
            self.mm(pp[:, 0:n], tw[:, k, cc * 128:(cc + 1) * 128], xb[:, k, c0:c0 + n], k == 0, k == NCH - 1, [B, Bxb[k]], [Bp])
        return pp, Bp

    def proj_conv(tw, B, cc, ci, out, Bout):
        pp, Bp = proj(tw, B, cc)
        dve(lambda h: h.tensor_copy(out=stg[:, 0:3], in_=self.ctail[:, ci, :]), [self.Bctail], [B1["stg"]])
        act(lambda h: h.activation(out=stg[:, 3:3 + n], in_=pp[:, 0:n], func=AF.Copy), [Bp], [B1["stg"]])
        dve(lambda h: h.tensor_copy(out=self.ctail[:, ci, :], in_=stg[:, n:n + 3]), [B1["stg"]], [self.Bctail])
        self.acc_i = getattr(self, "acc_i", 0) + 1
        acc, Bacc = (E2[:, 0:n], B1["E2"]) if self.acc_i % 2 == 0 else (DE[:, 0:n], B1["DE"])
        dve(lambda h: h.tensor_scalar(out=acc, in0=stg[:, 0:n], scalar1=self.cw[:, ci:ci + 1], scalar2=self.cb[:, ci:ci + 1], op0=ALU.mult, op1=ALU.add),
            [B1["stg"], C], [Bacc])
        for w in range(1, 4):
            dve(lambda h, w=w: h.scalar_tensor_tensor(out=acc, in0=stg[:, w:w + n], scalar=self.cw[:, w * 24 + ci:w * 24 + ci + 1], in1=acc,
                                                     op0=ALU.mult, op1=ALU.add), [B1["stg"], Bacc, C], [Bacc])
        act(lambda h: h.activation(out=out, in_=acc, func=AF.Silu), [Bacc], [Bout])

    fw.mark("m1_dt")
    tw, B = slab_in(10)
    dtt, Bd = self.dtt, self.Bdtt
    for sg in range(nseg):
        a = c0 + sg * Lc
        pp, Bp = PS[5], BP[5]
        for k in range(NCH):
            self.mm(pp[0:Lc, 0:32], xb[:, k, a:a + Lc], tw[:, k, 0:32], k == 0, k == NCH - 1, [B, Bxb[k]], [Bp])
        dt_, dtA, cs_, te = (dtt[0:Lc, sg, i, :] for i in range(4))
        dve(lambda h: h.tensor_tensor(out=dt_, in0=pp[0:Lc, 0:32], in1=self.dtb[0:Lc, :], op=ALU.add), [Bp, C], [Bd])
        act(lambda h: h.activation(out=dt_, in_=dt_, func=AF.Exp), [Bd], [Bd])
        act(lambda h: h.activation(out=dt_, in_=dt_, func=AF.Ln, bias=1.0), [Bd], [Bd])
        dve(lambda h: h.tensor_tensor(out=dtA, in0=dt_, in1=self.Arow[0:Lc, :], op=ALU.mult), [Bd, C], [Bd])
        pc, Bpc = PS[6], BP[6]
        self.mm(pc[0:Lc, 0:32], self.tri[0:Lc, 0:Lc], dtA, True, True, [C, Bd], [Bpc])
        pe_, Bpe = PS[7], BP[7]
        self.mm(pe_[:, 0:32], self.onesf[0:Lc, :], dtA, True, True, [C, Bd], [Bpe])
        act(lambda h: h.activation(out=cs_, in_=pc[0:Lc, 0:32], func=AF.Copy), [Bpc], [Bd])
        dve(lambda h: h.tensor_tensor(out=te, in0=pe_[0:Lc, 0:32], in1=cs_, op=ALU.subtract), [Bpe, Bd], [Bd])
        act(lambda h: h.activation(out=te, in_=te, func=AF.Exp), [Bd], [Bd])
        dve(lambda h: h.tensor_tensor(out=te, in0=te, in1=dt_, op=ALU.mult), [Bd], [Bd])
        act(lambda h, sg=sg: h.activation(out=self.dend[:, sg, :], in_=pe_[:, 0:32], func=AF.Exp), [Bpe], [Bd])
    fw.mark("m1_bc")
    tw, B = slab_in(8)
    for g in range(4):
        proj_conv(tw, B, g, 16 + g, BT_[:, g, 0:n], B1["BT"])
    tw, B = slab_in(9)
    for g in range(4):
        proj_conv(tw, B, g, 20 + g, CT_[:, g, 0:n], B1["CT"])
    pxt_b = PS[6][:].bitcast(BF16)
    for g in range(4):
        fw.mark("m1_inproj")
        tw, B = slab_in(g)
        for hc in range(4):
            pp, Bp = proj(tw, B, hc)
            act(lambda h, hc=hc, pp=pp: h.activation(out=zs[:, hc, 0:n], in_=pp[:, 0:n], func=AF.Silu), [Bp], [B1["zs"]])
        tw, B = slab_in(4 + g)
        for hc in range(4):
            proj_conv(tw, B, hc, 4 * g + hc, xsT[:, hc, 0:n], B1["xs"])
        fw.mark("m1_seg")
        for sg in range(nseg):
            a = sg * Lc
            par = self.ssd_it % 2
            self.ssd_it += 1
            Wt, Ctl, xt, xh, btm = Wt_[par], Ctl_[par], xt_[par], xh_[par], btm_[par]
            BW, BCt, Bxt, Bxh, Bbt = (B1[k + str(par)] for k in ("W", "Ct", "xt", "xh", "btm"))
            dt_, dtA, cs_, te = (dtt[0:Lc, sg, i, :] for i in range(4))
            pcb, Bpcb = PS[5], BP[5]
            self.mm(pcb[0:Lc, 0:Lc], BT_[:, g, a:a + Lc], CT_[:, g, a:a + Lc], True, True, [B1["BT"], B1["CT"]], [Bpcb])
            dve(lambda h: h.tensor_tensor(out=self.mcb[0:Lc, 0:Lc], in0=pcb[0:Lc, 0:Lc], in1=self.tri[0:Lc, 0:Lc], op=ALU.mult), [Bpcb, C], [self.Bmcb])
            pr, Bpr = PS[4], BP[4]
            for hh in range(8):
                hd = 8 * g + hh
                self.mm(pr[:, hh * Lc:(hh + 1) * Lc], dtA[:, hd:hd + 1].broadcast_to([Lc, 128]), self.tri[0:Lc, 0:Lc], True, True, [Bd, C], [Bpr])
            v3 = lambda ap, P: ap[0:P, 0:8 * Lc].rearrange("p (h t) -> p h t", h=8)
            dve(lambda h: h.tensor_tensor(out=v3(DE, Lc), in0=v3(pr, Lc), in1=bc(cs_[:, 8 * g:8 * g + 8].unsqueeze(2), [Lc, 8, Lc]), op=ALU.subtract),
                [Bpr, Bd], [B1["DE"]])
            act(lambda h: h.activation(out=DE[0:Lc, 0:8 * Lc], in_=DE[0:Lc, 0:8 * Lc], func=AF.Exp), [B1["DE"]], [B1["DE"]])
            dve(lambda h: h.scalar_tensor_tensor(out=v3(Wt, Lc), in0=v3(DE, Lc), scalar=1.0, in1=bc(self.mcb[0:Lc, 0:Lc].unsqueeze(1), [Lc, 8, Lc]),
                                                 op0=ALU.min, op1=ALU.mult), [B1["DE"], self.Bmcb], [BW])
            act(lambda h: h.activation(out=E2[:, 0:8 * Lc], in_=pr[:, 0:8 * Lc], func=AF.Exp), [Bpr], [B1["E2"]])
            dve(lambda h: h.tensor_tensor(out=v3(Ctl, 128), in0=v3(E2, 128), in1=bc(CT_[:, g, a:a + Lc].unsqueeze(1), [128, 8, Lc]), op=ALU.mult),
                [B1["E2"], B1["CT"]], [BCt])
            pxt, Bpxt = pxt_b, BP[6]
            for hc in range(4):
                fw.op("pe", lambda h, hc=hc: h.transpose(pxt[0:Lc, hc * 128:(hc + 1) * 128], xsT[:, hc, a:a + Lc], self.identb[:, :]),
                      reads=[B1["xs"], C], writes=[Bpxt])
            fw.op("pe", lambda h: h.transpose(pxt[0:Lc, 512:640], BT_[:, g, a:a + Lc], self.identb[:, :]), reads=[B1["BT"], C], writes=[Bpxt])
            x3 = lambda ap: ap[0:Lc, 0:512].rearrange("p (h q) -> p h q", h=8)
            dve(lambda h: h.tensor_tensor(out=x3(xt), in0=x3(pxt), in1=bc(dt_[:, 8 * g:8 * g + 8].unsqueeze(2), [Lc, 8, 64]), op=ALU.mult),
                [Bpxt, Bd], [Bxt])
            dve(lambda h: h.tensor_tensor(out=x3(xh), in0=x3(pxt), in1=bc(te[:, 8 * g:8 * g + 8].unsqueeze(2), [Lc, 8, 64]), op=ALU.mult),
                [Bpxt, Bd], [Bxh])
            act(lambda h: h.activation(out=btm[0:Lc, 0:128], in_=pxt[0:Lc, 512:640], func=AF.Copy), [Bpxt], [Bbt])
            for hh in range(8):
                hc, pb = hh // 2, 64 * (hh % 2)
                hd = 8 * g + hh
                py, Bpy = PS[hc], BP[hc]
                self.mm(py[pb:pb + 64, a:a + Lc], xt[0:Lc, hh * 64:(hh + 1) * 64], Wt[0:Lc, hh * Lc:(hh + 1) * Lc], True, False,
                        [Bxt, BW], [Bpy], tile_position=(0, pb))
                self.mm(py[pb:pb + 64, a:a + Lc], self.hstb[:, hd * 64:(hd + 1) * 64], Ctl[:, hh * Lc:(hh + 1) * Lc], False, True,
                        [self.Bhstb[g], BCt], [Bpy], tile_position=(0, pb))
            pS, BpS = PS[7], BP[7]
            self.mm(pS[:, 0:512], btm[0:Lc, 0:128], xh[0:Lc, 0:512], True, True, [Bbt, Bxh], [BpS])
            hg = self.hst[:, 512 * g:512 * (g + 1)]
            h3 = hg.rearrange("p (h q) -> p h q", h=8)
            dve(lambda h, sg=sg: h.tensor_tensor(out=h3, in0=h3, in1=bc(self.dend[:, sg, 8 * g:8 * g + 8].unsqueeze(2), [128, 8, 64]), op=ALU.mult),
                [self.Bhst[g], Bd], [self.Bhst[g]])
            dve(lambda h: h.tensor_tensor(out=hg, in0=hg, in1=pS[:, 0:512], op=ALU.add), [self.Bhst[g], BpS], [self.Bhst[g]])
            act(lambda h: h.activation(out=self.hstb[:, 512 * g:512 * (g + 1)], in_=hg, func=AF.Copy), [self.Bhst[g]], [self.Bhstb[g]])
        fw.mark("m1_epi")
        fw.barrier()
        for hc in range(4):
            py, Bpy = PS[hc], BP[hc]
            dve(lambda h, hc=hc, py=py: h.scalar_tensor_tensor(out=yg[:, hc, 0:n], in0=xsT[:, hc, 0:n], scalar=self.dvec[:, 4 * g + hc:4 * g + hc + 1],
                                                               in1=py[:, 0:n], op0=ALU.mult, op1=ALU.add), [B1["xs"], Bpy, C], [B1["yg"]])
            dve(lambda h, hc=hc: h.tensor_tensor(out=yg[:, hc, 0:n], in0=yg[:, hc, 0:n], in1=zs[:, hc, 0:n], op=ALU.mult), [B1["yg"], B1["zs"]], [B1["yg"]])
        act(lambda h: h.activation(out=xsT[:, :, 0:n], in_=yg[:, :, 0:n], func=AF.Square), [B1["yg"]], [B1["xs"]])
        pms, Bpms = PS[5], BP[5]
        for hc in range(4):
            self.mm(pms[:, 0:n], self.ones512[:], xsT[:, hc, 0:n], hc == 0, hc == 3, [C, B1["xs"]], [Bpms])
        rstd = self.lnt[:, 1, 0:n]
        Bl = self.Blnt[1]
        dve(lambda h: h.tensor_scalar(out=rstd, in0=pms[:, 0:n], scalar1=RMS_EPS, scalar2=None, op0=ALU.add), [Bpms], [Bl])
        act(lambda h: h.activation(out=rstd, in_=rstd, func=AF.Sqrt), [Bl], [Bl])
        dve(lambda h: h.reciprocal(out=rstd, in_=rstd), [Bl], [Bl])
        for hc in range(4):
            ch = 4 * g + hc
            dve(lambda h, hc=hc, ch=ch: h.scalar_tensor_tensor(out=yn[:, ch, c0:c0 + n], in0=yg[:, hc, 0:n], scalar=self.ng[:, ch:ch + 1], in1=rstd,
                                                               op0=ALU.mult, op1=ALU.mult), [B1["yg"], Bl, C], [B1["yn"]])
        fw.barrier()


def _st_stage(self, r):
    slot = 0 if r % 2 == 0 else 3
    return self.lnt[:, slot, :].rearrange("p (t n) -> p t n", t=4), self.Blnt[slot]


def ssd_state_in(self, sq):
    fw, I = self.fw, self.I
    C = self.Bconst
    src = I["state_ssd"][0, sq].rearrange("h p n -> (h p) n").rearrange("(r t p) n -> r p t n", t=4, p=128)
    for r in range(4):
        st, Bst = _st_stage(self, r)
        ps, Bp = self.PS[6 + r % 2], self.BP[6 + r % 2]
        fw.dma("sp", st, src[r], Bst, writes=[Bst])
        for t in range(4):
            fw.op("pe", lambda h, t=t: h.transpose(ps[:, t * 128:(t + 1) * 128], st[:, t, :], self.ident[:, :]), reads=[Bst, C], writes=[Bp])
        fw.op("dve", lambda h: h.tensor_copy(out=self.hst[:, 512 * r:512 * (r + 1)], in_=ps[:, 0:512]), reads=[Bp], writes=[self.Bhst[r]])
        fw.op("act", lambda h: h.activation(out=self.hstb[:, 512 * r:512 * (r + 1)], in_=ps[:, 0:512], func=AF.Copy), reads=[Bp], writes=[self.Bhstb[r]])
    srcc = I["state_conv"][0, sq].rearrange("w (c p) -> (w c) p", p=128)
    fw.dma("sp", self.stg[0:72, :], srcc, self.Bstg, writes=[self.Bstg])
    fw.op("pe", lambda h: h.transpose(self.PS[7][:, 0:72], self.stg[0:72, :], self.ident[0:72, 0:72]), reads=[self.Bstg, C], writes=[self.BP[7]])
    fw.op("dve", lambda h: h.tensor_copy(out=self.ctail[:].rearrange("p c w -> p w c"), in_=self.PS[7][:, 0:72].rearrange("p (w c) -> p w c", w=3)),
          reads=[self.BP[7]], writes=[self.Bctail])


def ssd_state_out(self, sfx, seq):
    fw = self.fw
    C = self.Bconst
    dst = self.O["ssd_" + sfx][0, seq].rearrange("h p n -> (h p) n").rearrange("(r t p) n -> r p t n", t=4, p=128)
    for r in range(4):
        st, Bst = _st_stage(self, r)
        ps, Bp = self.PS[6 + r % 2], self.BP[6 + r % 2]
        for t in range(4):
            i = 4 * r + t
            fw.op("pe", lambda h, i=i, t=t: h.transpose(ps[:, t * 128:(t + 1) * 128], self.hst[:, i * 128:(i + 1) * 128], self.ident[:, :]),
                  reads=[self.Bhst[r], C], writes=[Bp])
        fw.op("dve", lambda h: h.tensor_copy(out=st, in_=ps[:, 0:512].rearrange("p (t n) -> p t n", t=4)), reads=[Bp], writes=[Bst])
        fw.dma("sp", dst[r], st, Bst, reads=[Bst])
    tmp = self.lnt[:, 2, 0:72]
    fw.op("dve", lambda h: h.tensor_copy(out=tmp.rearrange("p (w c) -> p w c", w=3), in_=self.ctail[:].rearrange("p c w -> p w c")),
          reads=[self.Bctail], writes=[self.Blnt[2]])
    fw.op("pe", lambda h: h.transpose(self.PS[7][0:72, 0:128], tmp, self.ident[:, :]), reads=[self.Blnt[2], C], writes=[self.BP[7]])
    fw.op("dve", lambda h: h.tensor_copy(out=self.stg[0:72, :], in_=self.PS[7][0:72, 0:128]), reads=[self.BP[7]], writes=[self.Bstg])
    dstc = self.O["conv_" + sfx][0, seq].rearrange("w (c p) -> (w c) p", p=128)
    fw.dma("sp", dstc, self.stg[0:72, :], self.Bstg, reads=[self.Bstg])


def mixer1(self, tile, N):
    fw = self.fw
    kind, s, ti = tile
    C = self.Bconst
    PS, BP = self.PS, self.BP
    xf, Bxf = self.xf, self.Bxf
    if not hasattr(self, "B1"):
        self.B1 = {k: Buf(k) for k in ("stg", "E2", "DE", "BT", "CT", "zs", "xs", "yg", "yn")}
        for k in ("W", "Ct", "xt", "xh", "btm"):
            for par in range(2):
                self.B1[k + str(par)] = Buf(k + str(par))
    fw.barrier()
    if kind == "p":
        if ti == 0:
            fw.op("dve", lambda h: h.memset(self.hst[:], 0.0), writes=self.Bhst)
            fw.op("pool", lambda h: h.memset(self.hstb[:], 0.0), writes=self.Bhstb)
            fw.op("pool", lambda h: h.memset(self.ctail[:], 0.0), writes=[self.Bctail])
        ssd_core(self, 0, 512, 8, 64)
        if ti == self.SEQ // 512 - 1:
            ssd_state_out(self, "prompt", s)
    else:
        for sq in range(self.NS):
            ssd_state_in(self, sq)
            ssd_core(self, sq * DEC_SEQ, DEC_SEQ, 1, DEC_SEQ)
            ssd_state_out(self, "sample", sq)
    fw.mark("m1_out")
    yn = self.hb[:, 0:16, :]
    wout = self.S["ssd_w_out"][0].rearrange("(k p) f -> p k f", p=128)
    for o2 in range(4):
        t, B = self.get_slab([(lambda t: t[:, :].rearrange("p (k f) -> p k f", k=16), wout[:, :, 256 * o2:256 * o2 + 256], "ssd_w_out")])
        tw = t[:, :].rearrange("p (k f) -> p k f", k=16)
        for oi in range(2):
            o = 2 * o2 + oi
            pp, Bp = PS[o % 2], BP[o % 2]
            for k in range(16):
                self.mm(pp[:, 0:N], tw[:, k, oi * 128:(oi + 1) * 128], yn[:, k, 0:N], k == 0, k == 15, [B, self.B1["yn"]], [Bp])
            fw.op("dve", lambda h, o=o, pp=pp: h.scalar_tensor_tensor(out=xf[:, o, 0:N], in0=xf[:, o, 0:N], scalar=ALPHA, in1=pp[:, 0:N],
                                                                     op0=ALU.mult, op1=ALU.add), reads=[Bxf[o], Bp], writes=[Bxf[o]])
    fw.barrier([self.Bstg, self.Blnt[0], self.Blnt[3]])
    self.layer_norm(4, N)

Builder.setup_mix0 = setup_mix0
Builder.mixer0 = mixer0
Builder.s5_sample_init = s5_sample_init
Builder.load_cache = load_cache
Builder.setup_ssd = setup_ssd
Builder.mixer1 = mixer1

_OUT_ORDER = ["y_prompt", "y_sample", "s5_re_prompt", "s5_im_prompt", "sb_k_prompt", "sb_v_prompt", "ssd_prompt",
              "conv_prompt", "s5_re_sample", "s5_im_sample", "sb_k_sample", "sb_v_sample", "ssd_sample", "conv_sample"]
_BATCH_AXIS = {"x_prompt": 0, "x_sample": 0, "state_s5_re": 1, "state_s5_im": 1, "cache_sb_k": 1, "cache_sb_v": 1,
               "state_ssd": 1, "state_conv": 1}


def make_in_maps(inputs, n_cores):
    maps = []
    for c in range(n_cores):
        m = {}
        for k, v in inputs.items():
            v = np.asarray(v)
            if k in _BATCH_AXIS:
                ax = _BATCH_AXIS[k]
                n = v.shape[ax] // n_cores
                sl = [slice(None)] * v.ndim
                sl[ax] = slice(c * n, (c + 1) * n)
                m[k] = np.ascontiguousarray(v[tuple(sl)])
            else:
                m[k] = v
        maps.append(m)
    return maps


def gather(results):
    outs = []
    for nm in _OUT_ORDER:
        ax = 0 if nm in ("y_prompt", "y_sample") else 1
        outs.append(np.concatenate([r[nm] for r in results], axis=ax).astype(np.float32))
    return tuple(outs)


def kernel(**inputs):
    n = 8
    b = Builder(SEQ=4096, NP=2, NS=4)
    nc = b.build()
    res = run_bass_kernel_spmd(nc, make_in_maps(inputs, n), core_ids=list(range(n)))
    return gather(res.results)
```

```python
import math
import numpy as np
import concourse.bass as bass
import concourse.mybir as mybir
from concourse.bass_utils import run_bass_kernel_spmd
from contextlib import ExitStack

F32 = mybir.dt.float32
BF16 = mybir.dt.bfloat16
AF = mybir.ActivationFunctionType
ALU = mybir.AluOpType

D = 1024
DFF = 2816
NCH = 8
MCH = 22
ALPHA = (2.0 * 2) ** 0.25
LN_EPS = 1e-5
RMS_EPS = 1e-5
PAST = 2048
DEC_SEQ = 16
LSUB = 64
MAGIC = 12582912.0


class Buf:
    __slots__ = ("name", "w", "r", "dsem", "dtot", "psum", "strict")

    def __init__(self, name="", psum=False, strict=False):
        self.name = name
        self.psum = psum
        self.strict = strict
        self.w = None
        self.r = {}
        self.dsem = None
        self.dtot = 0


class Eng:
    def __init__(self, name, sem):
        self.name = name
        self.sem = sem
        self.n = 0
        self.seen = {}
        self.prog = []


class _Rec:
    def __init__(self):
        self.call = None

    def __getattr__(self, name):
        def f(*args, **kwargs):
            self.call = (name, args, kwargs)
            return None
        return f


class FW:
    def __init__(self, nc, es):
        self.nc = nc
        self.es = es
        self.engs = {}
        for name in ("pe", "dve", "act", "pool", "sp"):
            sem = es.enter_context(nc.semaphore("sem_" + name))
            self.engs[name] = Eng(name, sem)
        self.dma_bufs = []
        self.nsem = 5
        self.uid = 0
        self.nosame = False

    def sbuf(self, shape, dtype, name=None):
        self.uid += 1
        return self.es.enter_context(self.nc.sbuf_tensor(name or ("sb%d" % self.uid), list(shape), dtype))

    def psum(self, shape, dtype=F32, name=None):
        self.uid += 1
        return self.es.enter_context(self.nc.psum_tensor(name or ("ps%d" % self.uid), list(shape), dtype))

    def _wait(self, e, tok, strict=False):
        if tok[0] == "e":
            _, name, seq = tok
            if name == e.name and (name in ("pe", "sp") or (self.nosame and not strict)):
                return
            if e.seen.get(name, 0) >= seq:
                return
            e.prog.append(("w", self.engs[name].sem, seq))
            e.seen[name] = seq
        else:
            _, b, val = tok
            key = ("d", id(b))
            if e.seen.get(key, 0) >= val:
                return
            e.prog.append(("w", b.dsem, val))
            e.seen[key] = val

    def _deps(self, e, reads, writes):
        for b in reads:
            if b.w is not None:
                self._wait(e, b.w, b.strict)
            if b.psum:
                for t in b.r.values():
                    if not (t[0] == "e" and t[1] == e.name):
                        self._wait(e, t)
        for b in writes:
            if b.w is not None:
                self._wait(e, b.w)
            for t in b.r.values():
                if not (t[0] == "e" and t[1] == e.name):
                    self._wait(e, t)

    def op(self, ename, fn, reads=(), writes=()):
        e = self.engs[ename]
        self._deps(e, reads, writes)
        e.n += 1
        rec = _Rec()
        fn(rec)
        e.prog.append(("o", rec.call, e.sem, 1))
        tok = ("e", ename, e.n)
        for b in reads:
            b.r[ename] = tok
        for b in writes:
            b.w = tok
            b.r = {}
        return tok

    def dma(self, qname, out, in_, track, reads=(), writes=(), slow=False):
        e = self.engs[qname]
        self._deps(e, reads, writes)
        if track.dsem is None:
            track.dsem = self.es.enter_context(self.nc.semaphore("dsem_%d" % self.nsem))
            self.nsem += 1
            self.dma_bufs.append(track)
        track.dtot += 16
        if slow:
            e.prog.append(("o", ("dma_start", (), dict(out=out, in_=in_, allow_slow_non_contiguous=True)), track.dsem, 16))
        else:
            e.prog.append(("o", ("dma_start", (), dict(out=out, in_=in_)), track.dsem, 16))
        tok = ("d", track, track.dtot)
        key = ("d", id(track))
        for b in reads:
            b.r[key] = tok
        for b in writes:
            b.w = tok
            b.r = {}
        return tok

    def mark(self, name):
        if not hasattr(self, "marks"):
            self.marks = []
        self.marks.append((name, {k: e.n for k, e in self.engs.items()}))

    def barrier(self, bufs=()):
        names = ("pe", "dve", "act", "pool")
        for a in names:
            ea = self.engs[a]
            for b in bufs:
                if b.dsem is not None:
                    self._wait(ea, ("d", b, b.dtot))
            for bn in names:
                if bn != a and self.engs[bn].n > 0:
                    self._wait(ea, ("e", bn, self.engs[bn].n))

    def finish(self):
        e = self.engs["sp"]
        for b in self.dma_bufs:
            e.prog.append(("w", b.dsem, b.dtot))
        for name, o in self.engs.items():
            if name != "sp" and o.n > 0:
                e.prog.append(("w", o.sem, o.n))
        with self.nc.Block() as block:
            def mk(prog):
                def body(h):
                    for it in prog:
                        if it[0] == "w":
                            h.wait_ge(it[1], it[2])
                        else:
                            name, args, kwargs = it[1]
                            getattr(h, name)(*args, **kwargs).then_inc(it[2], it[3])
                return body
            block.tensor(mk(self.engs["pe"].prog))
            block.vector(mk(self.engs["dve"].prog))
            block.scalar(mk(self.engs["act"].prog))
            block.gpsimd(mk(self.engs["pool"].prog))
            block.sync(mk(self.engs["sp"].prog))


def bc(ap, shape):
    return ap.broadcast_to(list(shape))


class Builder:
    def __init__(self, SEQ=4096, NP=2, NS=4, dbg=False, do_sample=True, layers=2):
        self.SEQ, self.NP, self.NS = SEQ, NP, NS
        self.dbg = dbg
        self.do_sample = do_sample
        self.layers = layers
        self.nc = bass.Bass("TRN2", target_bir_lowering=False)
        self.dbg_outs = {}

    def din(self, name, shape):
        return self.nc.dram_tensor(name, list(shape), F32, kind="ExternalInput").ap()

    def dout(self, name, shape):
        return self.nc.dram_tensor(name, list(shape), F32, kind="ExternalOutput").ap()

    def declare(self):
        NP, NS, SEQ = self.NP, self.NS, self.SEQ
        I = {}
        I["x_prompt"] = self.din("x_prompt", [NP, SEQ, D])
        I["x_sample"] = self.din("x_sample", [NS, DEC_SEQ, D])
        I["state_s5_re"] = self.din("state_s5_re", [1, NS, 32, 64])
        I["state_s5_im"] = self.din("state_s5_im", [1, NS, 32, 64])
        I["cache_sb_k"] = self.din("cache_sb_k", [1, NS, PAST, 8, 64])
        I["cache_sb_v"] = self.din("cache_sb_v", [1, NS, PAST, 8, 64])
        I["state_ssd"] = self.din("state_ssd", [1, NS, 32, 64, 128])
        I["state_conv"] = self.din("state_conv", [1, NS, 3, 3072])
        for nm, shp in (("ln_g", [2, 3, D]), ("ln_b", [2, 3, D]), ("ffn_w_gate", [2, 2, D, DFF]),
                        ("ffn_w_up", [2, 2, D, DFF]), ("ffn_w_down", [2, 2, DFF, D]),
                        ("mix0_w_in", [1, D, 2048]), ("s5_a_re", [1, 32, 64]), ("s5_a_im", [1, 32, 64]),
                        ("s5_log_dt", [1, 32]), ("s5_b_re", [1, 32, 64, 16]), ("s5_b_im", [1, 32, 64, 16]),
                        ("s5_c_re", [1, 32, 16, 64]), ("s5_c_im", [1, 32, 16, 64]), ("s5_d", [1, 512]),
                        ("s5_w_glu", [1, 512, 512]), ("s5_b_glu", [1, 512]), ("mix0_w_out", [1, D, D]),
                        ("ssd_w_in", [1, D, 5152]), ("ssd_conv_w", [1, 4, 3072]), ("ssd_conv_b", [1, 3072]),
                        ("ssd_dt_bias", [1, 32]), ("ssd_a_log", [1, 32]), ("ssd_d", [1, 32]),
                        ("ssd_norm_g", [1, 2048]), ("ssd_w_out", [1, 2048, D])):
            I[nm] = self.din(nm, shp)
        O = {}
        O["y_prompt"] = self.dout("y_prompt", [NP, SEQ, D])
        O["y_sample"] = self.dout("y_sample", [NS, DEC_SEQ, D])
        for sfx, nb, sl in (("prompt", NP, SEQ), ("sample", NS, DEC_SEQ)):
            O["s5_re_" + sfx] = self.dout("s5_re_" + sfx, [1, nb, 32, 64])
            O["s5_im_" + sfx] = self.dout("s5_im_" + sfx, [1, nb, 32, 64])
            O["sb_k_" + sfx] = self.dout("sb_k_" + sfx, [1, nb, sl, 8, 64])
            O["sb_v_" + sfx] = self.dout("sb_v_" + sfx, [1, nb, sl, 8, 64])
            O["ssd_" + sfx] = self.dout("ssd_" + sfx, [1, nb, 32, 64, 128])
            O["conv_" + sfx] = self.dout("conv_" + sfx, [1, nb, 3, 3072])
        self.I, self.O = I, O
        S = {}
        for nm in ("ffn_w_gate", "ffn_w_up", "ffn_w_down", "mix0_w_in", "s5_w_glu", "mix0_w_out",
                   "ssd_w_in", "ssd_w_out"):
            shp = list(I[nm].shape)
            S[nm] = self.nc.dram_tensor("scr_" + nm, shp, BF16, kind="Internal").ap()
        self.S = S
        self.SB = {}
        for nm in S:
            if nm.startswith("ffn"):
                for l in range(2):
                    for j in range(2):
                        self.SB[(nm, l, j)] = Buf("scr")
            else:
                self.SB[nm] = Buf("scr")

    def dbg_out(self, name, shape):
        if name not in self.dbg_outs:
            self.dbg_outs[name] = self.dout("dbg_" + name, shape)
        return self.dbg_outs[name]

    def build(self):
        self.declare()
        with ExitStack() as es:
            self.fw = FW(self.nc, es)
            self.alloc()
            self.setup()
            tiles = []
            for s in range(self.NP):
                for i in range(self.SEQ // 512):
                    tiles.append(("p", s, i))
            for t in tiles:
                self.run_tile(t)
            if self.do_sample:
                self.run_tile(("s", 0, 0))
            self.fw.finish()
        return self.nc

    def alloc(self):
        fw = self.fw
        self.xf = fw.sbuf([128, NCH, 512], F32, "xf")
        self.Bxf = [Buf("xf%d" % c) for c in range(NCH)]
        self.xb = fw.sbuf([128, NCH, 512], BF16, "xb")
        self.Bxb = [Buf("xb%d" % c) for c in range(NCH)]
        self.hb = fw.sbuf([128, MCH, 512], BF16, "hb")
        self.Bh = [Buf("h%d" % m) for m in range(MCH)]
        self.hbf = self.hb[:].rearrange("p m n -> p (m n)").bitcast(F32)
        self.ma = fw.sbuf([128, 6144], F32, "ma")
        self.mab = self.ma[:].bitcast(BF16)
        self.NSLAB = 3
        self.slab = [fw.sbuf([128, 4096], BF16, "slab%d" % i) for i in range(self.NSLAB)]
        self.Bslab = [Buf("slab%d" % i) for i in range(self.NSLAB)]
        self.slab_i = 0
        self.KT = fw.sbuf([128, 4, 4096], BF16, "KT")
        self.VT = fw.sbuf([128, 32, 512], BF16, "VT")
        self.BKT = [Buf("KT%d" % i) for i in range(8)]
        self.BVT = [Buf("VT%d" % i) for i in range(8)]
        self.PS = [fw.psum([128, 512], F32, "psb%d" % i) for i in range(8)]
        self.BP = [Buf("ps%d" % i, psum=True) for i in range(8)]
        self.sg = [fw.sbuf([128, 512], F32, "sg%d" % i) for i in range(2)]
        self.Bsg = [Buf("sg%d" % i) for i in range(2)]
        self.knew = self.KT[:, :, 3072:3136]
        self.vnew = self.VT[0:16, 20:24, :]
        self.lnt = fw.sbuf([128, 4, 512], F32, "lnt")
        self.Blnt = [Buf("lnt%d" % i) for i in range(4)]

    def mm(self, out, lhsT, rhs, start, stop, reads, writes, **kw):
        self.fw.op("pe", lambda h: h.matmul(out, lhsT=lhsT, rhs=rhs, start=start, stop=stop, **kw),
                   reads=reads, writes=writes)

    def load_T(self, rows_ap, R, dest, Bdest):
        fw = self.fw
        fw.dma("sp", self.stg[0:R, :], rows_ap, self.Bstg, writes=[self.Bstg])
        fw.op("pe", lambda h: h.transpose(self.PS[7][:, 0:R], self.stg[0:R, :], self.ident[0:R, 0:R]),
              reads=[self.Bstg, self.Bconst], writes=[self.BP[7]])
        fw.op("dve", lambda h: h.tensor_copy(out=dest, in_=self.PS[7][:, 0:R]), reads=[self.BP[7]], writes=[Bdest])

    def setup(self):
        fw, nc, I, S = self.fw, self.nc, self.I, self.S
        def cast(nm, idx_list):
            for idx in idx_list:
                src = I[nm]
                dst = S[nm]
                for i in idx:
                    src = src[i]
                    dst = dst[i]
                key = (nm,) + tuple(idx) if nm.startswith("ffn") else nm
                fw.dma("pool", dst, src, self.SB[key], writes=[self.SB[key]])
        lj = [(l, j) for l in range(2) for j in range(2)]
        cast("ffn_w_gate", lj[:1]); cast("ffn_w_up", lj[:1]); cast("ffn_w_down", lj[:1])
        cast("mix0_w_in", [(0,)]); cast("s5_w_glu", [(0,)]); cast("mix0_w_out", [(0,)])
        cast("ffn_w_gate", lj[1:]); cast("ffn_w_up", lj[1:]); cast("ffn_w_down", lj[1:])
        cast("ssd_w_in", [(0,)]); cast("ssd_w_out", [(0,)])

        self.Bconst = Buf("const")
        self.Bstg = Buf("stg")
        self.stg = fw.sbuf([128, 128], F32, "stg")
        self.ident = fw.sbuf([128, 128], F32, "ident")
        self.identb = fw.sbuf([128, 128], BF16, "identb")
        self.onesb = fw.sbuf([128, 128], BF16, "onesb")
        self.ones1 = fw.sbuf([128, 128], BF16, "ones1")
        self.onesf = fw.sbuf([128, 128], F32, "onesf")
        C = self.Bconst
        fw.op("pool", lambda h: h.memset(self.ident[:], 1.0), writes=[C])
        fw.op("pool", lambda h: h.affine_select(out=self.ident[:], in_=self.ident[:], pattern=[[-1, 128]],
                                                compare_op=ALU.is_equal, fill=0.0, base=0, channel_multiplier=1),
              reads=[C], writes=[C])
        fw.op("pool", lambda h: h.tensor_copy(out=self.identb[:], in_=self.ident[:]), reads=[C], writes=[C])
        fw.op("pool", lambda h: h.memset(self.onesb[:], 1.0 / 1024.0), writes=[C])
        fw.op("pool", lambda h: h.memset(self.ones1[:], 1.0), writes=[C])
        fw.op("pool", lambda h: h.memset(self.onesf[:], 1.0), writes=[C])
        self.lng = fw.sbuf([128, 48], F32, "lng")
        self.lnb = fw.sbuf([128, 48], F32, "lnb")
        self.load_T(I["ln_g"].rearrange("l j (c p) -> (l j c) p", p=128), 48, self.lng[:], C)
        self.load_T(I["ln_b"].rearrange("l j (c p) -> (l j c) p", p=128), 48, self.lnb[:], C)
        self.setup_mix0()
        if self.layers > 1:
            self.setup_ssd()
        fw.barrier([self.Bconst, self.Bstg])

    def get_slab(self, pieces):
        i = self.slab_i
        self.slab_i = (i + 1) % self.NSLAB
        t, B = self.slab[i], self.Bslab[i]
        for dst_fn, src, key in pieces:
            self.fw.dma("sp", dst_fn(t), src, B, reads=[self.SB[key]], writes=[B])
        return t, B

    def layer_norm(self, idx, N):
        fw = self.fw
        xf, xb, hb = self.xf, self.xb, self.hb
        Bxf, Bxb, Bh, BP = self.Bxf, self.Bxb, self.Bh, self.BP
        ybf = hb[:, 0:8, 0:N]
        sq = hb[:, 8:16, 0:N]
        fw.op("dve", lambda h: h.tensor_copy(out=ybf, in_=xf[:, :, 0:N]), reads=Bxf, writes=Bh[0:8])
        fw.op("act", lambda h: h.activation(out=sq, in_=xf[:, :, 0:N], func=AF.Square), reads=Bxf, writes=Bh[8:16])
        pm, pq = self.PS[0], self.PS[1]
        for c in range(NCH):
            self.mm(pm[:, 0:N], self.onesb[:], hb[:, c, 0:N], c == 0, c == NCH - 1, [Bh[c], self.Bconst], [BP[0]])
        for c in range(NCH):
            self.mm(pq[:, 0:N], self.onesb[:], hb[:, 8 + c, 0:N], c == 0, c == NCH - 1, [Bh[8 + c], self.Bconst], [BP[1]])
        mean, rstd, nmr = self.lnt[:, 0, 0:N], self.lnt[:, 1, 0:N], self.lnt[:, 2, 0:N]
        Bl = self.Blnt
        fw.op("act", lambda h: h.activation(out=mean, in_=pm[:, 0:N], func=AF.Copy), reads=[BP[0]], writes=[Bl[0]])
        fw.op("dve", lambda h: h.tensor_tensor(out=rstd, in0=pm[:, 0:N], in1=mean, op=ALU.mult), reads=[BP[0], Bl[0]], writes=[Bl[1]])
        fw.op("dve", lambda h: h.tensor_tensor(out=rstd, in0=pq[:, 0:N], in1=rstd, op=ALU.subtract), reads=[BP[1], Bl[1]], writes=[Bl[1]])
        fw.op("dve", lambda h: h.tensor_scalar(out=rstd, in0=rstd, scalar1=0.0, scalar2=LN_EPS, op0=ALU.max, op1=ALU.add), reads=[Bl[1]], writes=[Bl[1]])
        fw.op("act", lambda h: h.activation(out=rstd, in_=rstd, func=AF.Sqrt), reads=[Bl[1]], writes=[Bl[1]])
        fw.op("dve", lambda h: h.reciprocal(out=rstd, in_=rstd), reads=[Bl[1]], writes=[Bl[1]])
        fw.op("dve", lambda h: h.scalar_tensor_tensor(out=nmr, in0=mean, scalar=-1.0, in1=rstd, op0=ALU.mult, op1=ALU.mult),
              reads=[Bl[0], Bl[1]], writes=[Bl[2]])
        xv = xf[:, :, 0:N]
        fw.op("dve", lambda h: h.tensor_tensor(out=xv, in0=xv, in1=bc(rstd.unsqueeze(1), [128, NCH, N]), op=ALU.mult),
              reads=Bxf + [Bl[1]], writes=Bxf)
        fw.op("dve", lambda h: h.tensor_tensor(out=xv, in0=xv, in1=bc(nmr.unsqueeze(1), [128, NCH, N]), op=ALU.add),
              reads=Bxf + [Bl[2]], writes=Bxf)
        for c in range(NCH):
            col = idx * 8 + c
            fw.op("act", lambda h, c=c, col=col: h.activation(out=xf[:, c, 0:N], in_=xf[:, c, 0:N], func=AF.Identity,
                                                               scale=self.lng[:, col:col + 1], bias=self.lnb[:, col:col + 1]),
                  reads=[Bxf[c], self.Bconst], writes=[Bxf[c]])
            eng = "dve" if c % 2 == 0 else "pool"
            fw.op(eng, lambda h, c=c: h.tensor_copy(out=xb[:, c, 0:N], in_=xf[:, c, 0:N]), reads=[Bxf[c]], writes=[Bxb[c]])

    def ffn(self, l, j, N):
        fw, S = self.fw, self.S
        xf, xb, hb = self.xf, self.xb, self.hb
        Bxf, Bxb, Bh, BP, PS = self.Bxf, self.Bxb, self.Bh, self.BP, self.PS
        wg = S["ffn_w_gate"][l, j].rearrange("(k p) f -> p k f", p=128)
        wu = S["ffn_w_up"][l, j].rearrange("(k p) f -> p k f", p=128)
        wd = S["ffn_w_down"][l, j].rearrange("(m p) f -> p m f", p=128)
        for s in range(11):
            cols = slice(256 * s, 256 * s + 256)
            t, B = self.get_slab([
                (lambda t: t[:, 0:2048].rearrange("p (k f) -> p k f", k=8), wg[:, :, cols], ("ffn_w_gate", l, j)),
                (lambda t: t[:, 2048:4096].rearrange("p (k f) -> p k f", k=8), wu[:, :, cols], ("ffn_w_up", l, j))])
            tg = t[:, 0:2048].rearrange("p (k f) -> p k f", k=8)
            tu = t[:, 2048:4096].rearrange("p (k f) -> p k f", k=8)
            for mi in range(2):
                m = 2 * s + mi
                pg, pu = PS[m % 2], PS[2 + m % 2]
                Bg, Bu = BP[m % 2], BP[2 + m % 2]
                for k in range(NCH):
                    self.mm(pg[:, 0:N], tg[:, k, mi * 128:(mi + 1) * 128], xb[:, k, 0:N], k == 0, k == NCH - 1, [B, Bxb[k]], [Bg])
                for k in range(NCH):
                    self.mm(pu[:, 0:N], tu[:, k, mi * 128:(mi + 1) * 128], xb[:, k, 0:N], k == 0, k == NCH - 1, [B, Bxb[k]], [Bu])
                sg, Bs = self.sg[m % 2], self.Bsg[m % 2]
                fw.op("act", lambda h, sg=sg, pg=pg: h.activation(out=sg[:, 0:N], in_=pg[:, 0:N], func=AF.Silu), reads=[Bg], writes=[Bs])
                fw.op("dve", lambda h, sg=sg, pu=pu, m=m: h.scalar_tensor_tensor(out=hb[:, m, 0:N], in0=sg[:, 0:N], scalar=0.5, in1=pu[:, 0:N],
                                                                                op0=ALU.mult, op1=ALU.mult),
                      reads=[Bs, Bu], writes=[Bh[m]])
        for o2 in range(4):
            cols = slice(256 * o2, 256 * o2 + 256)
            for half in range(2):
                t, B = self.get_slab([(lambda t: t[:, 0:2816].rearrange("p (m f) -> p m f", m=11),
                                       wd[:, 11 * half:11 * half + 11, cols], ("ffn_w_down", l, j))])
                tv = t[:, 0:2816].rearrange("p (m f) -> p m f", m=11)
                for oi in range(2):
                    o = 2 * o2 + oi
                    pd, Bd = PS[4 + o % 4], BP[4 + o % 4]
                    for mm_ in range(11):
                        m = 11 * half + mm_
                        self.mm(pd[:, 0:N], tv[:, mm_, oi * 128:(oi + 1) * 128], hb[:, m, 0:N], m == 0, m == MCH - 1, [B, Bh[m]], [Bd])
            for oi in range(2):
                o = 2 * o2 + oi
                pd, Bd = PS[4 + o % 4], BP[4 + o % 4]
                fw.op("dve", lambda h, o=o, pd=pd: h.scalar_tensor_tensor(out=xf[:, o, 0:N], in0=xf[:, o, 0:N], scalar=ALPHA, in1=pd[:, 0:N],
                                                                         op0=ALU.mult, op1=ALU.add),
                      reads=[Bxf[o], Bd], writes=[Bxf[o]])
        self.layer_norm(l * 3 + (0 if j == 0 else 2), N)

    def load_x(self, tile, N):
        fw = self.fw
        kind, s, i = tile
        nb = (N + 127) // 128
        stg = self.hbf[:, 0:4096].rearrange("p (b f) -> p b f", b=4)
        Bst = self.Bh[0:16]
        for b in range(nb):
            rows = min(128, N - b * 128)
            if kind == "p":
                src = self.I["x_prompt"][s, i * 512 + b * 128:i * 512 + b * 128 + rows, :]
            else:
                src = self.I["x_sample"].rearrange("s t d -> (s t) d")[b * 128:b * 128 + rows, :]
            fw.dma("sp", stg[0:rows, b, :], src, Bst[4 * b], writes=Bst[4 * b:4 * b + 4])
        for c in range(NCH):
            ps, Bp = self.PS[c % 4], self.BP[c % 4]
            for b in range(nb):
                rows = min(128, N - b * 128)
                fw.op("pe", lambda h, ps=ps, b=b, c=c, rows=rows: h.transpose(ps[:, b * 128:b * 128 + rows], stg[0:rows, b, c * 128:(c + 1) * 128],
                                                                           self.ident[0:rows, 0:rows]),
                      reads=Bst[4 * b:4 * b + 4] + [self.Bconst], writes=[Bp])
            fw.op("act", lambda h, ps=ps, c=c: h.activation(out=self.xf[:, c, 0:N], in_=ps[:, 0:N], func=AF.Copy), reads=[Bp], writes=[self.Bxf[c]])
            fw.op("dve", lambda h, ps=ps, c=c: h.tensor_copy(out=self.xb[:, c, 0:N], in_=ps[:, 0:N]), reads=[Bp], writes=[self.Bxb[c]])

    def store_y(self, tile, N):
        fw = self.fw
        kind, s, i = tile
        nb = (N + 127) // 128
        stg = self.hbf[:, 0:4096].rearrange("p (b f) -> p b f", b=4)
        Bst = self.Bh[0:16]
        for b in range(nb):
            rows = min(128, N - b * 128)
            for half in range(2):
                ps, Bp = self.PS[(2 * b + half) % 4], self.BP[(2 * b + half) % 4]
                for cc in range(4):
                    c = 4 * half + cc
                    fw.op("pe", lambda h, ps=ps, b=b, c=c, cc=cc, rows=rows: h.transpose(ps[0:rows, cc * 128:(cc + 1) * 128],
                                                                                     self.xf[:, c, b * 128:b * 128 + rows], self.ident[:, :]),
                          reads=[self.Bxf[c], self.Bconst], writes=[Bp])
                eng = "act" if half == 0 else "dve"
                if eng == "act":
                    fw.op("act", lambda h, ps=ps, b=b, half=half, rows=rows: h.activation(out=stg[0:rows, b, half * 512:(half + 1) * 512], in_=ps[0:rows, :], func=AF.Copy),
                          reads=[Bp], writes=[Bst[4 * b + 2 * half], Bst[4 * b + 2 * half + 1]])
                else:
                    fw.op("dve", lambda h, ps=ps, b=b, half=half, rows=rows: h.tensor_copy(out=stg[0:rows, b, half * 512:(half + 1) * 512], in_=ps[0:rows, :]),
                          reads=[Bp], writes=[Bst[4 * b + 2 * half], Bst[4 * b + 2 * half + 1]])
            if kind == "p":
                dst = self.O["y_prompt"][s, i * 512 + b * 128:i * 512 + b * 128 + rows, :]
            else:
                dst = self.O["y_sample"].rearrange("s t d -> (s t) d")[b * 128:b * 128 + rows, :]
            fw.dma("sp", dst, stg[0:rows, b, :], Bst[4 * b], reads=Bst[4 * b:4 * b + 4])

    def dbg_sb(self, name, ap, B, shape):
        if not self.dbg or name in self.dbg_outs:
            return
        o = self.dbg_out(name, shape)
        self.fw.dma("sp", o, ap, B, reads=[B])

    def dump_x(self, name, tile, N):
        if not self.dbg:
            return
        kind, s, i = tile
        ntile = self.NP * (self.SEQ // 512) + 1
        o = self.dbg_out(name, [ntile, NCH, 128, 512])
        ti = (s * (self.SEQ // 512) + i) if kind == "p" else ntile - 1
        for c in range(NCH):
            self.fw.dma("sp", o[ti, c, :, 0:N], self.xf[:, c, 0:N], self.Bxf[c], reads=[self.Bxf[c]])

    def run_tile(self, tile):
        import os
        self.stop = int(os.environ.get("KSTOP", "9"))
        kind = tile[0]
        N = 512 if kind == "p" else self.NS * DEC_SEQ
        if self.stop < 1:
            return
        self.fw.mark("tile %s %d %d" % tile)
        self.load_x(tile, N)
        self.fw.mark("ffn00")
        ksub = os.environ.get("KSUB", "")
        if ksub == "load":
            self.dump_x("ffn00", tile, N)
            return
        if ksub == "ln":
            self.layer_norm(0, N)
            self.dump_x("ffn00", tile, N)
            return
        self.ffn(0, 0, N)
        self.dump_x("ffn00", tile, N)
        if self.stop < 2:
            return
        self.fw.mark("mixer0")
        self.mixer0(tile, N)
        self.dump_x("mix0", tile, N)
        self.fw.mark("ffn01")
        self.ffn(0, 1, N)
        self.dump_x("l0", tile, N)
        if self.layers > 1:
            self.fw.mark("ffn10")
            self.ffn(1, 0, N)
            self.fw.mark("mixer1")
            self.mixer1(tile, N)
            self.dump_x("mix1", tile, N)
            self.fw.mark("ffn11")
            self.ffn(1, 1, N)
        self.fw.mark("store")
        self.store_y(tile, N)
        self.fw.mark("end")


def _ts(h, out, in0, s1, s2, op0, op1=None):
    if op1 is None:
        return h.tensor_scalar(out=out, in0=in0, scalar1=s1, scalar2=None, op0=op0)
    return h.tensor_scalar(out=out, in0=in0, scalar1=s1, scalar2=s2, op0=op0, op1=op1)


def setup_mix0(self):
    fw, I = self.fw, self.I
    C = self.Bconst
    dve = lambda fn, r=(C,), w=(C,): fw.op("dve", fn, reads=list(r), writes=list(w))
    self.s5d = fw.sbuf([128, 4], F32, "s5d")
    self.bglu = fw.sbuf([128, 4], F32, "bglu")
    self.load_T(I["s5_d"].rearrange("o (c p) -> (o c) p", p=128), 4, self.s5d[:], C)
    self.load_T(I["s5_b_glu"].rearrange("o (c p) -> (o c) p", p=128), 4, self.bglu[:], C)
    self.negones = fw.sbuf([128, 128], BF16, "negones")
    self.nuincl = fw.sbuf([128, 128], BF16, "nuincl")
    fw.op("pool", lambda h: h.memset(self.negones[:], -1.0), writes=[C])
    fw.op("pool", lambda h: h.memset(self.nuincl[:], -1.0), writes=[C])
    fw.op("pool", lambda h: h.affine_select(out=self.nuincl[:], in_=self.nuincl[:], pattern=[[-1, 128]], compare_op=ALU.is_ge, fill=0.0,
                                            base=0, channel_multiplier=1), reads=[C], writes=[C])
    L = LSUB
    pr = self.ma[:, 3712:3904].rearrange("p (i j) -> p i j", i=12)
    P = lambda i: pr[:, i, :]
    self.load_T(I["s5_a_re"].rearrange("o (j g) n -> (o j) (g n)", g=2), 16, P(0), C)
    self.load_T(I["s5_a_im"].rearrange("o (j g) n -> (o j) (g n)", g=2), 16, P(1), C)
    ld = I["s5_log_dt"].rearrange("o (j t) -> o j t", t=2)
    fw.dma("sp", pr[0:64, 2, :], ld[0:1, :, 0].broadcast_to([64, 16]), C, writes=[C], slow=True)
    fw.dma("sp", pr[64:128, 2, :], ld[0:1, :, 1].broadcast_to([64, 16]), C, writes=[C], slow=True)
    fw.op("act", lambda h: h.activation(out=P(2), in_=P(2), func=AF.Exp), reads=[C], writes=[C])
    dve(lambda h: h.tensor_tensor(out=P(3), in0=P(0), in1=P(2), op=ALU.mult))
    dve(lambda h: h.tensor_tensor(out=P(4), in0=P(1), in1=P(2), op=ALU.mult))
    self.s5r = fw.sbuf([128, 16], F32, "s5r")
    fw.op("act", lambda h: h.activation(out=self.s5r[:], in_=P(3), func=AF.Exp), reads=[C], writes=[C])
    dve(lambda h: _ts(h, P(5), P(4), 1.0 / (2 * math.pi), MAGIC, ALU.mult, ALU.add))
    dve(lambda h: _ts(h, P(5), P(5), MAGIC, None, ALU.subtract))
    dve(lambda h: h.scalar_tensor_tensor(out=P(5), in0=P(5), scalar=-2 * math.pi, in1=P(4), op0=ALU.mult, op1=ALU.add))
    dve(lambda h: _ts(h, P(5), P(5), 0.125, None, ALU.mult))
    dve(lambda h: h.tensor_tensor(out=P(6), in0=P(5), in1=P(5), op=ALU.mult))
    sc = [-1.0 / 6, 1.0 / 120, -1.0 / 5040, 1.0 / 362880]
    cc = [-0.5, 1.0 / 24, -1.0 / 720, 1.0 / 40320, -1.0 / 3628800]
    dve(lambda h: _ts(h, P(7), P(6), sc[3], sc[2], ALU.mult, ALU.add))
    for co in (sc[1], sc[0], 1.0):
        dve(lambda h: h.tensor_tensor(out=P(7), in0=P(7), in1=P(6), op=ALU.mult))
        dve(lambda h, co=co: _ts(h, P(7), P(7), co, None, ALU.add))
    dve(lambda h: h.tensor_tensor(out=P(7), in0=P(7), in1=P(5), op=ALU.mult))
    dve(lambda h: _ts(h, P(8), P(6), cc[4], cc[3], ALU.mult, ALU.add))
    for co in (cc[2], cc[1], cc[0], 1.0):
        dve(lambda h: h.tensor_tensor(out=P(8), in0=P(8), in1=P(6), op=ALU.mult))
        dve(lambda h, co=co: _ts(h, P(8), P(8), co, None, ALU.add))
    for _ in range(3):
        dve(lambda h: h.tensor_tensor(out=P(9), in0=P(8), in1=P(7), op=ALU.mult))
        dve(lambda h: h.tensor_tensor(out=P(8), in0=P(8), in1=P(8), op=ALU.mult))
        dve(lambda h: h.tensor_tensor(out=P(7), in0=P(7), in1=P(7), op=ALU.mult))
        dve(lambda h: h.tensor_tensor(out=P(8), in0=P(8), in1=P(7), op=ALU.subtract))
        dve(lambda h: _ts(h, P(7), P(9), 2.0, None, ALU.mult))
    self.Ct = fw.sbuf([128, 16, L + 1], F32, "Ct")
    self.St = fw.sbuf([128, 16, L + 1], F32, "St")
    Ct, St = self.Ct, self.St
    dve(lambda h: h.memset(Ct[:, :, 0:1], 1.0))
    dve(lambda h: h.memset(St[:, :, 0:1], 0.0))
    dve(lambda h: h.tensor_copy(out=Ct[:, :, 1:2], in_=P(8).unsqueeze(2)))
    dve(lambda h: h.tensor_copy(out=St[:, :, 1:2], in_=P(7).unsqueeze(2)))
    tmpA = self.ma[:, 0:2048].rearrange("p (j l) -> p j l", j=16)
    n = 2
    while n <= L:
        hn = n // 2
        ch, sh = Ct[:, :, hn:hn + 1], St[:, :, hn:hn + 1]
        dve(lambda h, ch=ch, sh=sh: h.tensor_tensor(out=P(9).unsqueeze(2), in0=ch, in1=sh, op=ALU.mult))
        dve(lambda h, ch=ch: h.tensor_tensor(out=P(10).unsqueeze(2), in0=ch, in1=ch, op=ALU.mult))
        dve(lambda h, sh=sh: h.tensor_tensor(out=P(11).unsqueeze(2), in0=sh, in1=sh, op=ALU.mult))
        dve(lambda h, n=n: h.tensor_tensor(out=Ct[:, :, n:n + 1], in0=P(10).unsqueeze(2), in1=P(11).unsqueeze(2), op=ALU.subtract))
        dve(lambda h, n=n: _ts(h, St[:, :, n:n + 1], P(9).unsqueeze(2), 2.0, None, ALU.mult))
        m = min(n, L + 1 - n)
        if m > 1:
            cn = bc(Ct[:, :, n:n + 1], [128, 16, m - 1])
            sn = bc(St[:, :, n:n + 1], [128, 16, m - 1])
            c0, s0 = Ct[:, :, 1:m], St[:, :, 1:m]
            tA = tmpA[:, :, 0:m - 1]
            dve(lambda h, c0=c0, cn=cn, tA=tA: h.tensor_tensor(out=tA, in0=c0, in1=cn, op=ALU.mult))
            dve(lambda h, s0=s0, sn=sn, n=n, m=m: h.tensor_tensor(out=Ct[:, :, n + 1:n + m], in0=s0, in1=sn, op=ALU.mult))
            dve(lambda h, tA=tA, n=n, m=m: h.tensor_tensor(out=Ct[:, :, n + 1:n + m], in0=tA, in1=Ct[:, :, n + 1:n + m], op=ALU.subtract))
            dve(lambda h, s0=s0, cn=cn, tA=tA: h.tensor_tensor(out=tA, in0=s0, in1=cn, op=ALU.mult))
            dve(lambda h, c0=c0, sn=sn, n=n, m=m: h.tensor_tensor(out=St[:, :, n + 1:n + m], in0=c0, in1=sn, op=ALU.mult))
            dve(lambda h, tA=tA, n=n, m=m: h.tensor_tensor(out=St[:, :, n + 1:n + m], in0=tA, in1=St[:, :, n + 1:n + m], op=ALU.add))
        n *= 2
    dve(lambda h: h.tensor_tensor(out=P(9), in0=self.s5r[:], in1=P(8), op=ALU.mult))
    dve(lambda h: h.tensor_tensor(out=P(10), in0=self.s5r[:], in1=P(7), op=ALU.mult))
    dve(lambda h: _ts(h, P(9), P(9), -1.0, None, ALU.add))
    dve(lambda h: h.tensor_tensor(out=P(2), in0=P(0), in1=P(0), op=ALU.mult))
    dve(lambda h: h.tensor_tensor(out=P(3), in0=P(1), in1=P(1), op=ALU.mult))
    dve(lambda h: h.tensor_tensor(out=P(2), in0=P(2), in1=P(3), op=ALU.add))
    dve(lambda h: h.reciprocal(out=P(2), in_=P(2)))
    dve(lambda h: h.tensor_tensor(out=P(3), in0=P(9), in1=P(0), op=ALU.mult))
    dve(lambda h: h.tensor_tensor(out=P(4), in0=P(10), in1=P(1), op=ALU.mult))
    dve(lambda h: h.tensor_tensor(out=P(3), in0=P(3), in1=P(4), op=ALU.add))
    dve(lambda h: h.tensor_tensor(out=P(5), in0=P(3), in1=P(2), op=ALU.mult))
    dve(lambda h: h.tensor_tensor(out=P(3), in0=P(10), in1=P(0), op=ALU.mult))
    dve(lambda h: h.tensor_tensor(out=P(4), in0=P(9), in1=P(1), op=ALU.mult))
    dve(lambda h: h.tensor_tensor(out=P(3), in0=P(3), in1=P(4), op=ALU.subtract))
    dve(lambda h: h.tensor_tensor(out=P(6), in0=P(3), in1=P(2), op=ALU.mult))
    braw = self.ma[:, 2048:3072].rearrange("p (t j q) -> p t j q", t=4, j=16)
    for t, nm in ((0, "s5_b_re"), (1, "s5_b_im")):
        fw.dma("sp", braw[:, t, :, :], I[nm][0].rearrange("g n q -> (g n) q").rearrange("(j p) q -> p j q", p=128), C, writes=[C])
    fre = bc(P(5).unsqueeze(2), [128, 16, 16])
    fim = bc(P(6).unsqueeze(2), [128, 16, 16])
    t3 = tmpA[:, :, 0:16]
    dve(lambda h: h.tensor_tensor(out=braw[:, 2], in0=braw[:, 0], in1=fre, op=ALU.mult))
    dve(lambda h: h.tensor_tensor(out=t3, in0=braw[:, 1], in1=fim, op=ALU.mult))
    dve(lambda h: h.tensor_tensor(out=braw[:, 2], in0=braw[:, 2], in1=t3, op=ALU.subtract))
    dve(lambda h: h.tensor_tensor(out=braw[:, 3], in0=braw[:, 1], in1=fre, op=ALU.mult))
    dve(lambda h: h.tensor_tensor(out=t3, in0=braw[:, 0], in1=fim, op=ALU.mult))
    dve(lambda h: h.tensor_tensor(out=braw[:, 3], in0=braw[:, 3], in1=t3, op=ALU.add))
    if self.dbg:
        o = self.dbg_out("s5par", [128, 12, 16])
        fw.dma("sp", o, pr, C, reads=[C])
        o = self.dbg_out("s5ct", [128, 16, L + 1])
        fw.dma("sp", o, Ct[:], C, reads=[C])
        o = self.dbg_out("s5st", [128, 16, L + 1])
        fw.dma("sp", o, St[:], C, reads=[C])
        o = self.dbg_out("s5r", [128, 16])
        fw.dma("sp", o, self.s5r[:], C, reads=[C])
        o = self.dbg_out("s5braw", [128, 4, 16, 16])
        fw.dma("sp", o, braw, C, reads=[C])
    self.LBc = fw.sbuf([128, 8, 128], BF16, "LBc")
    self.LCc = fw.sbuf([128, 16, 2, 32], BF16, "LCc")
    mj = self.ma[:, 3584:3712]
    Bmj = Buf("mj")
    for j in range(16):
        c, q = j // 4, j % 4
        for reim in range(2):
            fw.op("dve", lambda h: h.memset(mj[:], 0.0), writes=[Bmj])
            g0 = (2 * j) % 8
            fw.op("dve", lambda h, j=j, reim=reim, g0=g0: h.tensor_copy(out=mj[0:64, g0 * 16:g0 * 16 + 16], in_=braw[0:64, 2 + reim, j, :]),
                  reads=[C], writes=[Bmj])
            fw.op("dve", lambda h, j=j, reim=reim, g0=g0: h.tensor_copy(out=mj[64:128, (g0 + 1) * 16:(g0 + 1) * 16 + 16], in_=braw[64:128, 2 + reim, j, :]),
                  reads=[C], writes=[Bmj])
            fw.op("pe", lambda h: h.transpose(self.PS[7][:, 0:128], mj[:], self.ident[:]), reads=[Bmj, C], writes=[self.BP[7]])
            fw.op("dve", lambda h, c=c, q=q, reim=reim: h.tensor_copy(out=self.LBc[32 * q:32 * q + 32, c * 2 + reim, :], in_=self.PS[7][32 * q:32 * q + 32, 0:128]),
                  reads=[self.BP[7]], writes=[C])
    craw = self.ma[:, 3072:3584].rearrange("p (t c n) -> p t c n", t=2, c=4)
    for t, nm in ((0, "s5_c_re"), (1, "s5_c_im")):
        fw.dma("sp", craw[:, t, :, :], I[nm][0].rearrange("g p n -> (g p) n").rearrange("(c q) n -> q c n", q=128), C, writes=[C])
    e8 = fw.sbuf([128, 2, 8], F32, "e8")
    fw.op("pool", lambda h: h.memset(e8[:, 0, :], 1.0), reads=[C], writes=[C])
    fw.op("pool", lambda h: h.affine_select(out=e8[:, 0, :], in_=e8[:, 0, :], pattern=[[-16, 8]], compare_op=ALU.is_ge, fill=0.0,
                                            base=0, channel_multiplier=1), reads=[C], writes=[C])
    fw.op("pool", lambda h: h.affine_select(out=e8[:, 0, :], in_=e8[:, 0, :], pattern=[[16, 8]], compare_op=ALU.is_ge, fill=0.0,
                                            base=15, channel_multiplier=-1), reads=[C], writes=[C])
    fw.op("pool", lambda h: h.tensor_scalar(out=e8[:, 1, :], in0=e8[:, 0, :], scalar1=-1.0, scalar2=None, op0=ALU.mult), reads=[C], writes=[C])
    for j in range(16):
        c, q = j // 4, j % 4
        g0 = (2 * j) % 8
        for reim in range(2):
            for gl in range(2):
                fw.op("dve", lambda h, c=c, reim=reim, gl=gl, g0=g0: h.tensor_scalar(
                    out=mj[:, gl * 64:(gl + 1) * 64], in0=craw[:, reim, c, :], scalar1=e8[:, reim, g0 + gl:g0 + gl + 1], scalar2=None, op0=ALU.mult),
                    reads=[C], writes=[Bmj])
            fw.op("pe", lambda h: h.transpose(self.PS[7][:, 0:128], mj[:], self.ident[:]), reads=[Bmj, C], writes=[self.BP[7]])
            fw.op("dve", lambda h, j=j, q=q, reim=reim: h.tensor_copy(out=self.LCc[:, j, reim, :], in_=self.PS[7][:, 32 * q:32 * q + 32]),
                  reads=[self.BP[7]], writes=[C])
    self.G = fw.sbuf([128, 16, 2], F32, "s5G")
    self.BG = Buf("s5G", strict=True)
    self.HL = fw.sbuf([128, 2, 16], F32, "s5HL")
    self.BHL = Buf("s5HL")
    self.s5tmp = fw.sbuf([128, 4], F32, "s5tmp")
    self.Bs5tmp = Buf("s5tmp")
    self.cmask = fw.sbuf([128, 256], BF16, "cmask")
    fw.op("pool", lambda h: h.memset(self.cmask[:], -30000.0), writes=[C])
    fw.op("pool", lambda h: h.affine_select(out=self.cmask[:, 128:256], in_=self.cmask[:, 128:256], pattern=[[-1, 128]], compare_op=ALU.is_ge, fill=0.0,
                                            base=0, channel_multiplier=1), reads=[C], writes=[C])
    self.crowb2 = [self.lnt[0:1, 2 + i, :].bitcast(BF16).rearrange("p (a n) -> p a n", a=2) for i in range(2)]
    self.Bcrowb2 = [Buf("crowb0"), Buf("crowb1")]
    self.Bcrow = Buf("crow")
    self.Bcrowb = Buf("crowb")


def s5_phase(self, tile, N):
    fw = self.fw
    kind, s, ti = tile
    C = self.Bconst
    PS, BP = self.PS, self.BP
    ma, mab, hbf = self.ma, self.mab, self.hbf
    uf = hbf[:, 0:2048].rearrange("p (c n) -> p c n", c=4)
    yf = hbf[:, 2048:4096].rearrange("p (c n) -> p c n", c=4)
    ub = self.hb[:, 16:20, :]
    s5out = mab[:, 2048:4096].rearrange("p (c n) -> p c n", c=4)
    T = [ma[:, 3072 + 512 * i:3072 + 512 * (i + 1)] for i in range(4)]
    BT = self.BT
    hre = [mab[:, 10240 + 1024 * i:10240 + 1024 * i + 512] for i in range(2)]
    him = [mab[:, 10240 + 1024 * i + 512:10240 + 1024 * (i + 1)] for i in range(2)]
    Bhh = self.Bhh
    Ct, St = self.Ct, self.St
    if kind == "p":
        nsub, L = 512 // LSUB, LSUB
    else:
        nsub, L = self.NS, DEC_SEQ
    dve = lambda fn, r, w: fw.op("dve", fn, reads=list(r), writes=list(w))
    last_tile = (kind == "s") or (ti == self.SEQ // 512 - 1)
    if kind == "p" and ti == 0:
        dve(lambda h: h.memset(self.G[:], 0.0), [], [self.BG])
    for j in range(16):
        c, q = j // 4, j % 4
        pa, pb = PS[0], PS[1]
        rhs = ub[32 * q:32 * q + 32, c, 0:N]
        self.mm(pa[:, 0:N], self.LBc[32 * q:32 * q + 32, c * 2 + 0, :], rhs, True, True, [C, self.Bub], [BP[0]], tile_position=(32 * q, 0))
        self.mm(pb[:, 0:N], self.LBc[32 * q:32 * q + 32, c * 2 + 1, :], rhs, True, True, [C, self.Bub], [BP[1]], tile_position=(32 * q, 0))
        v3 = lambda ap: ap[:, 0:N].rearrange("p (s l) -> p s l", s=nsub)
        ct = bc(Ct[:, j:j + 1, 0:L], [128, nsub, L])
        st = bc(St[:, j:j + 1, 0:L], [128, nsub, L])
        A, Bb, Dd, E = T
        dve(lambda h: h.tensor_tensor(out=v3(A), in0=v3(pa), in1=ct, op=ALU.mult), [BP[0], C], [BT[0]])
        dve(lambda h: h.tensor_tensor(out=v3(Bb), in0=v3(pb), in1=st, op=ALU.mult), [BP[1], C], [BT[1]])
        dve(lambda h: h.tensor_tensor(out=A[:, 0:N], in0=A[:, 0:N], in1=Bb[:, 0:N], op=ALU.add), [BT[0], BT[1]], [BT[0]])
        dve(lambda h: h.tensor_tensor(out=v3(Dd), in0=v3(pb), in1=ct, op=ALU.mult), [BP[1], C], [BT[2]])
        dve(lambda h: h.tensor_tensor(out=v3(Bb), in0=v3(pa), in1=st, op=ALU.mult), [BP[0], C], [BT[1]])
        dve(lambda h: h.tensor_tensor(out=Dd[:, 0:N], in0=Dd[:, 0:N], in1=Bb[:, 0:N], op=ALU.subtract), [BT[2], BT[1]], [BT[2]])
        if j == 5 and kind == "p" and self.dbg:
            dve(lambda h: h.tensor_copy(out=E[:, 0:N], in_=pa[:, 0:N]), [BP[0]], [BT[3]])
            self.dbg_sb("bu_re", E[:, 0:N], BT[3], [128, N])
            dve(lambda h: h.tensor_copy(out=self.sg[0][:, 0:256].rearrange("p (a b) -> p a b", a=2), in_=self.LBc[:, 2:4, :]), [C], [self.Bsg[0]])
            self.dbg_sb("lbc", self.sg[0][:, 0:256], self.Bsg[0], [128, 256])
            self.dbg_sb("ub", ub[:, :, 0:N].bitcast(mybir.dt.uint16), self.Bub, [128, 4, N]) if False else None
        if j == 5 and kind == "p":
            self.dbg_sb("ut_re", A[:, 0:N], BT[0], [128, N])
            self.dbg_sb("ut_im", Dd[:, 0:N], BT[2], [128, N])
            self.dbg_sb("uf", uf[:, :, 0:N], self.Buf_uf, [128, 4, N])
        rr = bc(self.s5r[:, j:j + 1], [128, L])
        for sub in range(nsub):
            sl = slice(sub * L, (sub + 1) * L)
            if kind == "s":
                gi = self.Gs[:, sub, j, :]
                BGi = self.BGs
            else:
                gi = self.G[:, j, :]
                BGi = self.BG
            dve(lambda h, sl=sl, gi=gi: h.tensor_tensor_scan(out=Bb[:, sl], data0=rr, data1=A[:, sl], initial=gi[:, 0:1], op0=ALU.mult, op1=ALU.add),
                [BT[0], BGi, C], [BT[1]])
            dve(lambda h, sl=sl, gi=gi: h.tensor_tensor_scan(out=E[:, sl], data0=rr, data1=Dd[:, sl], initial=gi[:, 1:2], op0=ALU.mult, op1=ALU.add),
                [BT[2], BGi, C], [BT[3]])
            e0 = (sub + 1) * L - 1
            gre_l, gim_l = Bb[:, e0:e0 + 1], E[:, e0:e0 + 1]
            tmp = self.s5tmp
            if kind == "p":
                cL, sL = Ct[:, j, L:L + 1], St[:, j, L:L + 1]
                dve(lambda h, gim_l=gim_l, sL=sL: h.tensor_tensor(out=tmp[:, 0:1], in0=gim_l, in1=sL, op=ALU.mult), [BT[3], C], [self.Bs5tmp])
                dve(lambda h, gre_l=gre_l, sL=sL: h.tensor_tensor(out=tmp[:, 1:2], in0=gre_l, in1=sL, op=ALU.mult), [BT[1], C], [self.Bs5tmp])
                dve(lambda h, gre_l=gre_l, cL=cL, j=j: h.scalar_tensor_tensor(out=self.G[:, j, 0:1], in0=gre_l, scalar=cL, in1=tmp[:, 0:1], op0=ALU.mult, op1=ALU.subtract),
                    [BT[1], C, self.Bs5tmp], [self.BG])
                dve(lambda h, gim_l=gim_l, cL=cL, j=j: h.scalar_tensor_tensor(out=self.G[:, j, 1:2], in0=gim_l, scalar=cL, in1=tmp[:, 1:2], op0=ALU.mult, op1=ALU.add),
                    [BT[3], C, self.Bs5tmp], [self.BG])
            if last_tile and (kind == "s" or sub == nsub - 1):
                c1, s1 = Ct[:, j, L - 1:L], St[:, j, L - 1:L]
                if kind == "s":
                    ore, oim = self.HLs[:, sub, 0, j:j + 1], self.HLs[:, sub, 1, j:j + 1]
                else:
                    ore, oim = self.HL[:, 0, j:j + 1], self.HL[:, 1, j:j + 1]
                dve(lambda h, gim_l=gim_l, s1=s1: h.tensor_tensor(out=tmp[:, 2:3], in0=gim_l, in1=s1, op=ALU.mult), [BT[3], C], [self.Bs5tmp])
                dve(lambda h, gre_l=gre_l, s1=s1: h.tensor_tensor(out=tmp[:, 3:4], in0=gre_l, in1=s1, op=ALU.mult), [BT[1], C], [self.Bs5tmp])
                dve(lambda h, gre_l=gre_l, c1=c1, ore=ore: h.scalar_tensor_tensor(out=ore, in0=gre_l, scalar=c1, in1=tmp[:, 2:3], op0=ALU.mult, op1=ALU.subtract),
                    [BT[1], C, self.Bs5tmp], [self.BHL])
                dve(lambda h, gim_l=gim_l, c1=c1, oim=oim: h.scalar_tensor_tensor(out=oim, in0=gim_l, scalar=c1, in1=tmp[:, 3:4], op0=ALU.mult, op1=ALU.add),
                    [BT[3], C, self.Bs5tmp], [self.BHL])
        if j == 5 and kind == "p":
            self.dbg_sb("g_re", Bb[:, 0:N], BT[1], [128, N])
            self.dbg_sb("g_im", E[:, 0:N], BT[3], [128, N])
        hr, hi = hre[j % 2], him[j % 2]
        Bh2 = Bhh[j % 2]
        dve(lambda h: h.tensor_tensor(out=v3(A), in0=v3(Bb), in1=ct, op=ALU.mult), [BT[1], C], [BT[0]])
        dve(lambda h: h.tensor_tensor(out=v3(Dd), in0=v3(E), in1=st, op=ALU.mult), [BT[3], C], [BT[2]])
        dve(lambda h, hr=hr: h.tensor_tensor(out=hr[:, 0:N], in0=A[:, 0:N], in1=Dd[:, 0:N], op=ALU.subtract), [BT[0], BT[2]], [Bh2])
        dve(lambda h: h.tensor_tensor(out=v3(A), in0=v3(E), in1=ct, op=ALU.mult), [BT[3], C], [BT[0]])
        dve(lambda h: h.tensor_tensor(out=v3(Dd), in0=v3(Bb), in1=st, op=ALU.mult), [BT[1], C], [BT[2]])
        dve(lambda h, hi=hi: h.tensor_tensor(out=hi[:, 0:N], in0=A[:, 0:N], in1=Dd[:, 0:N], op=ALU.add), [BT[0], BT[2]], [Bh2])
        py, Bpy = PS[2 + c % 2], BP[2 + c % 2]
        self.mm(py[32 * q:32 * q + 32, 0:N], self.LCc[:, j, 0, :], hr[:, 0:N], True, False, [C, Bh2], [Bpy], tile_position=(0, 32 * q))
        self.mm(py[32 * q:32 * q + 32, 0:N], self.LCc[:, j, 1, :], hi[:, 0:N], False, True, [C, Bh2], [Bpy], tile_position=(0, 32 * q))
        if q == 3:
            dve(lambda h, c=c, py=py: h.scalar_tensor_tensor(out=yf[:, c, 0:N], in0=uf[:, c, 0:N], scalar=self.s5d[:, c:c + 1], in1=py[:, 0:N],
                                                            op0=ALU.mult, op1=ALU.add), [self.Buf_uf, Bpy, C], [self.Buf_yf])
    if last_tile:
        nseq = self.NS if kind == "s" else 1
        for sq in range(nseq):
            for reim in range(2):
                src = self.HLs[:, sq, reim, :] if kind == "s" else self.HL[:, reim, :]
                fw.op("pe", lambda h, src=src: h.transpose(PS[7][0:16, 0:128], src, self.ident[:]), reads=[self.BHL, C], writes=[BP[7]])
                fw.op("dve", lambda h: h.tensor_copy(out=self.stg[0:16, :], in_=PS[7][0:16, 0:128]), reads=[BP[7]], writes=[self.Bstg])
                nm = ("s5_re_" if reim == 0 else "s5_im_") + ("sample" if kind == "s" else "prompt")
                seq = sq if kind == "s" else s
                dst = self.O[nm][0, seq].rearrange("(j g) n -> j (g n)", g=2)
                fw.dma("sp", dst, self.stg[0:16, :], self.Bstg, reads=[self.Bstg])
    tmpf = ma[:, 3072:5120].rearrange("p (c n) -> p c n", c=4)
    yv, tv = yf[:, :, 0:N], tmpf[:, :, 0:N]
    gb = ub
    dve(lambda h: h.tensor_tensor(out=tv, in0=yv, in1=yv, op=ALU.mult), [self.Buf_yf], BT)
    dve(lambda h: _ts(h, tv, tv, 0.044715, 1.0, ALU.mult, ALU.add), BT, BT)
    dve(lambda h: h.tensor_tensor(out=tv, in0=tv, in1=yv, op=ALU.mult), BT + [self.Buf_yf], BT)
    fw.op("act", lambda h: h.activation(out=tv, in_=tv, func=AF.Sigmoid, scale=2.0 * math.sqrt(2.0 / math.pi)), reads=BT, writes=BT)
    dve(lambda h: h.tensor_tensor(out=yv, in0=yv, in1=tv, op=ALU.mult), BT + [self.Buf_yf], [self.Buf_yf])
    fw.op("pool", lambda h: h.tensor_copy(out=gb[:, :, 0:N], in_=yv), reads=[self.Buf_yf], writes=[self.Bub])
    wglu = self.S["s5_w_glu"][0].rearrange("(k p) f -> p k f", p=128)
    t, B = self.get_slab([(lambda t: t[:, 0:2048].rearrange("p (k f) -> p k f", k=4), wglu, "s5_w_glu")])
    tg = t[:, 0:2048].rearrange("p (k f) -> p k f", k=4)
    for oc in range(4):
        pg, Bg = PS[oc % 2], BP[oc % 2]
        for k in range(4):
            self.mm(pg[:, 0:N], tg[:, k, oc * 128:(oc + 1) * 128], gb[:, k, 0:N], k == 0, k == 3, [B, self.Bub], [Bg])
        sg, Bs = self.sg[oc % 2], self.Bsg[oc % 2]
        fw.op("act", lambda h, sg=sg, pg=pg, oc=oc: h.activation(out=sg[:, 0:N], in_=pg[:, 0:N], func=AF.Sigmoid, bias=self.bglu[:, oc:oc + 1]),
              reads=[Bg, C], writes=[Bs])
        dve(lambda h, sg=sg, oc=oc: h.tensor_tensor(out=s5out[:, oc, 0:N], in0=yf[:, oc, 0:N], in1=sg[:, 0:N], op=ALU.mult),
            [Bs, self.Buf_yf], [self.Bs5out])


def attn_head_loop(self, N, qcols, blocks, diag_base):
    fw = self.fw
    C = self.Bconst
    PS, BP = self.PS, self.BP
    ma, mab = self.ma, self.mab
    QT = mab[:, 0:2048].rearrange("p (c n) -> p c n", c=4)
    attno = mab[:, 4096:6144].rearrange("p (c n) -> p c n", c=4)
    e_ = [ma[:, 3072 + 512 * i:3072 + 512 * (i + 1)] for i in range(2)]
    spb = [mab[:, 8192 + 512 * i:8192 + 512 * (i + 1)] for i in range(2)]
    wb = [mab[:, 9216 + 512 * i:9216 + 512 * (i + 1)] for i in range(2)]
    Be, Bsp, Bw = self.Be, self.Bsp, self.Bw
    crb = [self.lnt[0:33, 2 + i, :].bitcast(BF16)[:, 0:512] for i in range(2)]
    Bcr = self.Bcrowb2
    q0, q1 = qcols
    nblk = len(blocks)
    for hp in range(4):
        c = hp
        po, Bpo = PS[7], BP[7]
        pst, Bpt = PS[6], BP[6]
        pss = [PS[0], PS[1]]
        Bps = [BP[0], BP[1]]

        def qk(bi):
            k0, kr, vblk, jj = blocks[bi]
            Bkt = self.BKT[min(k0 // 512, 7)]
            for x in range(2):
                hb_ = 64 * x
                self.mm(pss[x][0:kr, 0:N], self.KT[hb_:hb_ + 64, c, k0:k0 + kr], QT[hb_:hb_ + 64, c, q0:q1], True, jj is None,
                        [Bkt, self.BQT], [Bps[x]], tile_position=(hb_, 0))
            if jj is not None:
                for x in range(2):
                    for a in range(jj + 1):
                        mo = 0 if a < jj else 128
                        self.mm(pss[x][0:kr, 128 * a:128 * a + 128], self.identb[0:kr, 0:kr], self.cmask[0:kr, mo:mo + 128], False, a == jj,
                                [C], [Bps[x]])

        qk(0)
        for bi, (k0, kr, vblk, jj) in enumerate(blocks):
            Bvt = self.BVT[min(vblk // 4, 7)]
            psc = [PS[2 + 2 * (bi % 2)], PS[3 + 2 * (bi % 2)]]
            Bpc = [BP[2 + 2 * (bi % 2)], BP[3 + 2 * (bi % 2)]]
            for x in range(2):
                fw.op("act", lambda h, x=x: h.activation(out=e_[x][0:kr, 0:N], in_=pss[x][0:kr, 0:N], func=AF.Exp), reads=[Bps[x]], writes=[Be[x]])
                fw.op("act", lambda h, x=x: h.activation(out=spb[x][0:kr, 0:N], in_=e_[x][0:kr, 0:N], func=AF.Ln, bias=1.0), reads=[Be[x]], writes=[Bsp[x]])
                self.mm(psc[x][0:kr, 0:N], self.nuincl[0:kr, 0:kr], spb[x][0:kr, 0:N], True, bi == 0, [C, Bsp[x]], [Bpc[x]])
                if bi > 0:
                    cbp, Bcbp = crb[(bi - 1) % 2], Bcr[(bi - 1) % 2]
                    self.mm(psc[x][0:kr, 0:N], self.negones[32 * x:32 * x + 1, 0:kr], cbp[32 * x:32 * x + 1, 0:N], False, True,
                            [C, Bcbp], [Bpc[x]], tile_position=(32 * x, 0))
            if bi < nblk - 1:
                for x in range(2):
                    self.mm(pst[32 * x:32 * x + 1, 0:N], self.ones1[0:kr, 0:1], spb[x][0:kr, 0:N], bi == 0, bi == nblk - 2,
                            [C, Bsp[x]], [Bpt], tile_position=(0, 32 * x))
                cb_, Bcb = crb[bi % 2], Bcr[bi % 2]
                fw.op("dve", lambda h, cb_=cb_: h.tensor_copy(out=cb_[0:33, 0:N], in_=pst[0:33, 0:N]), reads=[Bpt], writes=[Bcb])
                qk(bi + 1)
            for x in range(2):
                fw.op("act", lambda h, x=x: h.activation(out=psc[x][0:kr, 0:N], in_=psc[x][0:kr, 0:N], func=AF.Exp), reads=[Bpc[x]], writes=[Bpc[x]])
                fw.op("dve", lambda h, x=x: h.tensor_tensor(out=wb[x][0:kr, 0:N], in0=psc[x][0:kr, 0:N], in1=e_[x][0:kr, 0:N], op=ALU.mult),
                      reads=[Bpc[x], Be[x]], writes=[Bw[x]])
            for x in range(2):
                hd = 2 * hp + x
                self.mm(po[64 * x:64 * x + 64, 0:N], self.VT[0:kr, vblk, hd * 64:(hd + 1) * 64], wb[x][0:kr, 0:N], bi == 0, bi == nblk - 1,
                        [Bvt, Bw[x]], [Bpo], tile_position=(0, 64 * x))
        fw.op("dve", lambda h, c=c: h.tensor_copy(out=attno[:, c, q0:q1], in_=po[:, 0:N]), reads=[Bpo], writes=[self.Battno])


def attn_sample_loop(self, qcols, blocks):
    fw = self.fw
    C = self.Bconst
    PS, BP = self.PS, self.BP
    ma, mab = self.ma, self.mab
    QT = mab[:, 0:2048].rearrange("p (c n) -> p c n", c=4)
    attno = mab[:, 4096:6144].rearrange("p (c n) -> p c n", c=4)
    e_ = [ma[:, 3072 + 512 * i:3072 + 512 * (i + 1)] for i in range(2)]
    spb = [mab[:, 8192 + 512 * i:8192 + 512 * (i + 1)] for i in range(2)]
    wb = [mab[:, 9216 + 512 * i:9216 + 512 * (i + 1)] for i in range(2)]
    Be, Bsp, Bw = self.Be, self.Bsp, self.Bw
    crb = [self.lnt[0:1, 2 + i, :].bitcast(BF16)[:, 0:512] for i in range(2)]
    Bcr = self.Bcrowb2
    q0, q1 = qcols
    NQ = q1 - q0
    W = 8 * NQ
    nblk = len(blocks)
    po, Bpo = PS[7], BP[7]
    pst, Bpt = PS[6], BP[6]

    def qk(bi):
        k0, kr, vblk, jj = blocks[bi]
        Bkt = self.BKT[min(k0 // 512, 7)]
        for hd in range(8):
            c, xx = hd // 2, hd % 2
            hb_ = 64 * xx
            pss, Bps = PS[2 * (bi % 2) + xx], BP[2 * (bi % 2) + xx]
            self.mm(pss[0:kr, c * NQ:(c + 1) * NQ], self.KT[hb_:hb_ + 64, c, k0:k0 + kr], QT[hb_:hb_ + 64, c, q0:q1], True, True,
                    [Bkt, self.BQT], [Bps], tile_position=(hb_, 0))

    qk(0)
    for bi, (k0, kr, vblk, jj) in enumerate(blocks):
        x = bi % 2
        Bvt = self.BVT[min(vblk // 4, 7)]
        psc, Bpc = PS[4 + x], BP[4 + x]
        H4 = 4 * NQ
        for xx in range(2):
            pss, Bps = PS[2 * x + xx], BP[2 * x + xx]
            fw.op("act", lambda h, xx=xx, pss=pss: h.activation(out=e_[x][0:kr, xx * H4:(xx + 1) * H4], in_=pss[0:kr, 0:H4], func=AF.Exp),
                  reads=[Bps], writes=[Be[x]])
        if jj is not None:
            fw.op("pool", lambda h: h.affine_select(out=e_[x][0:kr, 0:W], in_=e_[x][0:kr, 0:W], pattern=[[0, 8], [1, NQ]], compare_op=ALU.is_gt,
                                                    fill=0.0, base=-128 * jj, channel_multiplier=-1), reads=[Be[x]], writes=[Be[x]])
        fw.op("act", lambda h: h.activation(out=spb[x][0:kr, 0:W], in_=e_[x][0:kr, 0:W], func=AF.Ln, bias=1.0), reads=[Be[x]], writes=[Bsp[x]])
        self.mm(psc[0:kr, 0:W], self.nuincl[0:kr, 0:kr], spb[x][0:kr, 0:W], True, bi == 0, [C, Bsp[x]], [Bpc])
        if bi > 0:
            self.mm(psc[0:kr, 0:W], self.negones[0:1, 0:kr], crb[(bi - 1) % 2][0:1, 0:W], False, True, [C, Bcr[(bi - 1) % 2]], [Bpc])
        if bi < nblk - 1:
            self.mm(pst[0:1, 0:W], self.ones1[0:kr, 0:1], spb[x][0:kr, 0:W], bi == 0, bi == nblk - 2, [C, Bsp[x]], [Bpt])
            fw.op("dve", lambda h: h.tensor_copy(out=crb[x][0:1, 0:W], in_=pst[0:1, 0:W]), reads=[Bpt], writes=[Bcr[x]])
            qk(bi + 1)
        fw.op("act", lambda h: h.activation(out=psc[0:kr, 0:W], in_=psc[0:kr, 0:W], func=AF.Exp), reads=[Bpc], writes=[Bpc])
        fw.op("dve", lambda h: h.tensor_tensor(out=wb[x][0:kr, 0:W], in0=psc[0:kr, 0:W], in1=e_[x][0:kr, 0:W], op=ALU.mult),
              reads=[Bpc, Be[x]], writes=[Bw[x]])
        for hd in range(8):
            hp, xx = hd // 2, hd % 2
            slot = xx * 4 + hp
            self.mm(po[64 * xx:64 * xx + 64, hp * NQ:(hp + 1) * NQ], self.VT[0:kr, vblk, hd * 64:(hd + 1) * 64], wb[x][0:kr, slot * NQ:(slot + 1) * NQ],
                    bi == 0 and hp == 0, bi == nblk - 1, [Bvt, Bw[x]], [Bpo], tile_position=(0, 64 * xx), skip_group_check=True)
    fw.op("dve", lambda h: h.tensor_copy(out=attno[:, :, q0:q1], in_=po[:, 0:4 * NQ].rearrange("p (c q) -> p c q", c=4)), reads=[Bpo], writes=[self.Battno])


def mixer0(self, tile, N):
    fw = self.fw
    kind, s, ti = tile
    C = self.Bconst
    PS, BP = self.PS, self.BP
    ma, mab, hbf = self.ma, self.mab, self.hbf
    xf, xb, Bxf, Bxb = self.xf, self.xb, self.Bxf, self.Bxb
    if not hasattr(self, "Buf_uf"):
        self.Buf_uf, self.Buf_yf, self.Bub, self.Bkvst = Buf("uf"), Buf("yf"), Buf("ub"), Buf("kvst")
        self.BQT, self.Bs5out, self.Battno = Buf("QT"), Buf("s5out"), Buf("attno")
        self.BT = [Buf("T%d" % i) for i in range(4)]
        self.Bhh = [Buf("hh%d" % i) for i in range(2)]
        self.Be = [Buf("e%d" % i) for i in range(2)]
        self.Bsp = [Buf("sp%d" % i) for i in range(2)]
        self.Bw = [Buf("w%d" % i) for i in range(2)]
        self.Bknew = Buf("knew")
    fw.barrier()
    uf = hbf[:, 0:2048].rearrange("p (c n) -> p c n", c=4)
    ub = self.hb[:, 16:20, :]
    kvst = hbf[:, 5120:5632]
    QT = mab[:, 0:2048].rearrange("p (c n) -> p c n", c=4)
    s5out = mab[:, 2048:4096].rearrange("p (c n) -> p c n", c=4)
    attno = mab[:, 4096:6144].rearrange("p (c n) -> p c n", c=4)
    win = self.S["mix0_w_in"][0].rearrange("(k p) f -> p k f", p=128)
    pos = ti * 512

    def slab_in(ci):
        t, B = self.get_slab([(lambda t: t[:, :].rearrange("p (k f) -> p k f", k=8), win[:, :, 512 * ci:512 * ci + 512], "mix0_w_in")])
        return t[:, :].rearrange("p (k f) -> p k f", k=8), B

    tw, B = slab_in(0)
    for oc in range(4):
        pp, Bp = PS[oc % 2], BP[oc % 2]
        for k in range(NCH):
            self.mm(pp[:, 0:N], tw[:, k, oc * 128:(oc + 1) * 128], xb[:, k, 0:N], k == 0, k == NCH - 1, [B, Bxb[k]], [Bp])
        fw.op("act", lambda h, pp=pp, oc=oc: h.activation(out=uf[:, oc, 0:N], in_=pp[:, 0:N], func=AF.Copy), reads=[Bp], writes=[self.Buf_uf])
        fw.op("dve", lambda h, pp=pp, oc=oc: h.tensor_copy(out=ub[:, oc, 0:N], in_=pp[:, 0:N]), reads=[Bp], writes=[self.Bub])
    tw, B = slab_in(1)
    for oc in range(4):
        pp, Bp = PS[2 + oc % 2], BP[2 + oc % 2]
        for k in range(NCH):
            self.mm(pp[:, 0:N], tw[:, k, oc * 128:(oc + 1) * 128], xb[:, k, 0:N], k == 0, k == NCH - 1, [B, Bxb[k]], [Bp])
        fw.op("act", lambda h, pp=pp, oc=oc: h.activation(out=QT[:, oc, 0:N], in_=pp[:, 0:N], func=AF.Copy, scale=0.125), reads=[Bp], writes=[self.BQT])
    tw, B = slab_in(2)
    for oc in range(4):
        pp, Bp = PS[oc % 2], BP[oc % 2]
        for k in range(NCH):
            self.mm(pp[:, 0:N], tw[:, k, oc * 128:(oc + 1) * 128], xb[:, k, 0:N], k == 0, k == NCH - 1, [B, Bxb[k]], [Bp])
        if kind == "p":
            fw.op("act", lambda h, pp=pp, oc=oc: h.activation(out=self.KT[:, oc, pos:pos + N], in_=pp[:, 0:N], func=AF.Copy), reads=[Bp], writes=[self.BKT[ti]])
        else:
            fw.op("act", lambda h, pp=pp, oc=oc: h.activation(out=self.knew[:, oc, 0:N], in_=pp[:, 0:N], func=AF.Copy), reads=[Bp], writes=[self.Bknew])
    twv, Bv = slab_in(3)
    if kind == "p":
        segs = [(b * 128, 128, s, pos + b * 128) for b in range(4)]
    else:
        segs = [(sq * DEC_SEQ, DEC_SEQ, sq, 0) for sq in range(self.NS)]
    sfx = "prompt" if kind == "p" else "sample"
    for (c0, rows, seq, sp_) in segs:
        for which, (tws, Bs_) in enumerate(((tw, B), (twv, Bv))):
            pp, Bp = PS[2 + which], BP[2 + which]
            for k in range(NCH):
                self.mm(pp[0:rows, :], xb[:, k, c0:c0 + rows], tws[:, k, :], k == 0, k == NCH - 1, [Bs_, Bxb[k]], [Bp])
            fw.op("act", lambda h, pp=pp, rows=rows: h.activation(out=kvst[0:rows, :], in_=pp[0:rows, :], func=AF.Copy), reads=[Bp], writes=[self.Bkvst])
            if which == 1:
                if kind == "p":
                    blk = sp_ // 128
                    fw.op("dve", lambda h, pp=pp, blk=blk: h.tensor_copy(out=self.VT[:, blk, :], in_=pp[:, :]), reads=[Bp], writes=[self.BVT[blk // 4]])
                else:
                    fw.op("dve", lambda h, pp=pp, seq=seq, rows=rows: h.tensor_copy(out=self.vnew[0:rows, seq, :], in_=pp[0:rows, :]), reads=[Bp], writes=[self.Bknew])
            nm = ("sb_k_" if which == 0 else "sb_v_") + sfx
            dst = self.O[nm][0, seq, sp_:sp_ + rows].rearrange("t h d -> t (h d)")
            fw.dma("sp", dst, kvst[0:rows, :], self.Bkvst, reads=[self.Bkvst])
    if self.stop < 3:
        return
    if kind == "s":
        self.s5_sample_init()
    fw.mark("s5")
    s5_phase(self, tile, N)
    fw.barrier()
    fw.mark("attn")
    if self.stop < 4:
        return
    if kind == "p":
        nkb = 4 * (ti + 1)
        blocks = []
        for kb in reversed(range(nkb)):
            jj = kb - 4 * ti if kb >= 4 * ti else None
            blocks.append((kb * 128, 128, kb, jj))
        attn_head_loop(self, N, (0, N), blocks, 0)
    else:
        for sq in range(self.NS):
            self.load_cache(sq)
            blocks = [(PAST, DEC_SEQ, 16, 0)] + [(kb * 128, 128, kb, None) for kb in reversed(range(PAST // 128))]
            attn_sample_loop(self, (sq * DEC_SEQ, (sq + 1) * DEC_SEQ), blocks)
    fw.mark("out0")
    wout = self.S["mix0_w_out"][0].rearrange("(k p) f -> p k f", p=128)
    for half in range(2):
        t, B = self.get_slab([(lambda t: t[:, :].rearrange("p (k f) -> p k f", k=8), wout[:, :, 512 * half:512 * half + 512], "mix0_w_out")])
        tw = t[:, :].rearrange("p (k f) -> p k f", k=8)
        for oc in range(4):
            o = 4 * half + oc
            pp, Bp = PS[o % 2], BP[o % 2]
            for k in range(NCH):
                rhs = s5out[:, k, 0:N] if k < 4 else attno[:, k - 4, 0:N]
                Br = self.Bs5out if k < 4 else self.Battno
                self.mm(pp[:, 0:N], tw[:, k, oc * 128:(oc + 1) * 128], rhs, k == 0, k == NCH - 1, [B, Br], [Bp])
            fw.op("dve", lambda h, o=o, pp=pp: h.scalar_tensor_tensor(out=xf[:, o, 0:N], in0=xf[:, o, 0:N], scalar=ALPHA, in1=pp[:, 0:N],
                                                                     op0=ALU.mult, op1=ALU.add), reads=[Bxf[o], Bp], writes=[Bxf[o]])
    fw.barrier([self.Bkvst])
    self.layer_norm(1, N)


def s5_sample_init(self):
    fw, I = self.fw, self.I
    C = self.Bconst
    NS = self.NS
    if not hasattr(self, "Gs"):
        self.Gs = fw.sbuf([128, NS, 16, 2], F32, "Gs")
        self.BGs = Buf("Gs", strict=True)
        self.HLs = fw.sbuf([128, NS, 2, 16], F32, "HLs")
        self.h0 = fw.sbuf([128, 2, NS, 16], F32, "h0")
        self.h0t = fw.sbuf([128, NS, 16], F32, "h0t")
    for reim, nm in ((0, "state_s5_re"), (1, "state_s5_im")):
        self.load_T(I[nm][0].rearrange("s (j g) n -> (s j) (g n)", g=2), NS * 16, self.h0[:, reim].rearrange("p s j -> p (s j)"), self.BGs)
    c1 = bc(self.Ct[:, :, 1:2].rearrange("p j o -> p o j"), [128, NS, 16])
    s1 = bc(self.St[:, :, 1:2].rearrange("p j o -> p o j"), [128, NS, 16])
    hre, him = self.h0[:, 0], self.h0[:, 1]
    B = self.BGs
    dve = lambda fn: fw.op("dve", fn, reads=[B, C], writes=[B])
    dve(lambda h: h.tensor_tensor(out=self.h0t[:], in0=him, in1=s1, op=ALU.mult))
    dve(lambda h: h.tensor_tensor(out=self.Gs[:, :, :, 0], in0=hre, in1=c1, op=ALU.mult))
    dve(lambda h: h.tensor_tensor(out=self.Gs[:, :, :, 0], in0=self.Gs[:, :, :, 0], in1=self.h0t[:], op=ALU.subtract))
    dve(lambda h: h.tensor_tensor(out=self.h0t[:], in0=hre, in1=s1, op=ALU.mult))
    dve(lambda h: h.tensor_tensor(out=self.Gs[:, :, :, 1], in0=him, in1=c1, op=ALU.mult))
    dve(lambda h: h.tensor_tensor(out=self.Gs[:, :, :, 1], in0=self.Gs[:, :, :, 1], in1=self.h0t[:], op=ALU.add))


def load_cache(self, sq):
    fw, I = self.fw, self.I
    C = self.Bconst
    PS, BP = self.PS, self.BP
    stg = self.hbf[:, 0:4096].rearrange("p (b f) -> p b f", b=8)
    Bst = [self.Buf_uf, self.Buf_yf]
    kc = I["cache_sb_k"][0, sq].rearrange("(b p) h d -> p b (h d)", p=128)
    vc = I["cache_sb_v"][0, sq].rearrange("(b p) h d -> p b (h d)", p=128)
    fw.dma("pool", self.VT[:, 0:16, :], vc, self.BVT[0], writes=self.BVT[0:4])
    for half in range(2):
        for q4 in range(2):
            b0 = half * 8 + q4 * 4
            fw.dma("sp", stg[:, q4 * 4:q4 * 4 + 4, :], kc[:, b0:b0 + 4, :], Bst[q4], writes=[Bst[q4]])
        for c in range(4):
            for g in range(2):
                pp, Bp = PS[(2 * c + g) % 4], BP[(2 * c + g) % 4]
                for bb in range(4):
                    b = g * 4 + bb
                    fw.op("pe", lambda h, pp=pp, b=b, bb=bb, c=c: h.transpose(pp[:, bb * 128:(bb + 1) * 128], stg[:, b, c * 128:(c + 1) * 128], self.ident[:]),
                          reads=[Bst[g], C], writes=[Bp])
                col0 = (half * 8 + g * 4) * 128
                eng = "act" if (c + g) % 2 == 0 else "dve"
                if eng == "act":
                    fw.op("act", lambda h, pp=pp, c=c, col0=col0: h.activation(out=self.KT[:, c, col0:col0 + 512], in_=pp[:, :], func=AF.Copy),
                          reads=[Bp], writes=[self.BKT[col0 // 512]])
                else:
                    fw.op("dve", lambda h, pp=pp, c=c, col0=col0: h.tensor_copy(out=self.KT[:, c, col0:col0 + 512], in_=pp[:, :]),
                          reads=[Bp], writes=[self.BKT[col0 // 512]])
    fw.op("dve", lambda h: h.tensor_copy(out=self.KT[:, :, PAST:PAST + DEC_SEQ], in_=self.knew[:, :, sq * DEC_SEQ:(sq + 1) * DEC_SEQ]),
          reads=[self.Bknew], writes=[self.BKT[4]])
    fw.op("dve", lambda h: h.tensor_copy(out=self.VT[0:DEC_SEQ, 16, :], in_=self.vnew[0:DEC_SEQ, sq, :]), reads=[self.Bknew], writes=[self.BVT[4]])

def setup_ssd(self):
    fw, I = self.fw, self.I
    C = self.Bconst
    self.cw = fw.sbuf([128, 96], F32, "cw")
    self.cb = fw.sbuf([128, 24], F32, "cb")
    self.ng = fw.sbuf([128, 16], F32, "ng")
    self.load_T(I["ssd_conv_w"].rearrange("o w (c p) -> (o w c) p", p=128), 96, self.cw[:], C)
    self.load_T(I["ssd_conv_b"].rearrange("o (c p) -> (o c) p", p=128), 24, self.cb[:], C)
    self.load_T(I["ssd_norm_g"].rearrange("o (c p) -> (o c) p", p=128), 16, self.ng[:], C)
    self.dtb = fw.sbuf([128, 32], F32, "dtb")
    self.Arow = fw.sbuf([128, 32], F32, "Arow")
    fw.dma("sp", self.dtb[:], I["ssd_dt_bias"][0:1, :].broadcast_to([128, 32]), C, writes=[C])
    fw.dma("sp", self.Arow[:], I["ssd_a_log"][0:1, :].broadcast_to([128, 32]), C, writes=[C])
    fw.op("act", lambda h: h.activation(out=self.Arow[:], in_=self.Arow[:], func=AF.Exp), reads=[C], writes=[C])
    fw.op("dve", lambda h: h.tensor_scalar(out=self.Arow[:], in0=self.Arow[:], scalar1=-1.0, scalar2=None, op0=ALU.mult), reads=[C], writes=[C])
    self.dvec = fw.sbuf([128, 16], F32, "dvec")
    dd = I["ssd_d"].rearrange("o (c t) -> o c t", t=2)
    fw.dma("sp", self.dvec[0:64, :], dd[0:1, :, 0].broadcast_to([64, 16]), C, writes=[C], slow=True)
    fw.dma("sp", self.dvec[64:128, :], dd[0:1, :, 1].broadcast_to([64, 16]), C, writes=[C], slow=True)
    self.tri = fw.sbuf([64, 64], F32, "tri")
    fw.op("pool", lambda h: h.memset(self.tri[:], 1.0), writes=[C])
    fw.op("pool", lambda h: h.affine_select(out=self.tri[:], in_=self.tri[:], pattern=[[1, 64]], compare_op=ALU.is_ge, fill=0.0,
                                            base=0, channel_multiplier=-1), reads=[C], writes=[C])
    self.ones512 = fw.sbuf([128, 128], BF16, "ones512")
    fw.op("pool", lambda h: h.memset(self.ones512[:], 1.0 / 512.0), writes=[C])
    self.hst = fw.sbuf([128, 2048], F32, "hst")
    self.hstb = fw.sbuf([128, 2048], BF16, "hstb")
    self.Bhst = [Buf("hst%d" % g) for g in range(4)]
    self.Bhstb = [Buf("hstb%d" % g) for g in range(4)]
    self.ctail = fw.sbuf([128, 24, 3], F32, "ctail")
    self.Bctail = Buf("ctail")
    self.dtt = fw.sbuf([64, 8, 4, 32], F32, "dtt")
    self.dend = fw.sbuf([128, 8, 32], F32, "dend")
    self.Bdtt = Buf("dtt")
    self.mcb = fw.sbuf([64, 64], F32, "mcb")
    self.Bmcb = Buf("mcb")


def ssd_core(self, c0, n, nseg, Lc):
    fw = self.fw
    C = self.Bconst
    PS, BP = self.PS, self.BP
    ma, mab, hbf = self.ma, self.mab, self.hbf
    xb, Bxb = self.xb, self.Bxb
    B1 = self.B1
    BT_ = mab[:, 0:2048].rearrange("p (c n) -> p c n", c=4)
    CT_ = mab[:, 2048:4096].rearrange("p (c n) -> p c n", c=4)
    zs = mab[:, 4096:6144].rearrange("p (c n) -> p c n", c=4)
    xsT = mab[:, 6144:8192].rearrange("p (c n) -> p c n", c=4)
    stg = ma[:, 4096:4611]
    DE = ma[:, 4612:5124]
    E2 = ma[:, 5124:5636]
    yg = ma[:, 4096:6144].rearrange("p (c n) -> p c n", c=4)
    yn = self.hb[:, 0:16, :]
    hbb = self.hb[:].rearrange("p m n -> p (m n)")
    sgb0 = self.sg[0][:].bitcast(BF16)
    sgb1 = self.sg[1][:].bitcast(BF16)
    Wt_ = [hbb[:, 8192:8704], sgb0[:, 0:512]]
    Ctl_ = [hbb[:, 8704:9216], sgb0[:, 512:1024]]
    xt_ = [hbb[:, 9216:9728], sgb1[:, 0:512]]
    xh_ = [hbb[:, 9728:10240], sgb1[:, 512:1024]]
    btm_ = [hbb[:, 10240:10368], hbb[:, 10368:10496]]
    self.ssd_it = 0
    win = self.S["ssd_w_in"][0].rearrange("(k p) f -> p k f", p=128)
    dve = lambda fn, r, w: fw.op("dve", fn, reads=list(r), writes=list(w))
    act = lambda fn, r, w: fw.op("act", fn, reads=list(r), writes=list(w))

    def slab_in(ci):
        w = 512 if ci < 10 else 32
        t, B = self.get_slab([(lambda t: t[:, 0:8 * w].rearrange("p (k f) -> p k f", k=8), win[:, :, 512 * ci:512 * ci + w], "ssd_w_in")])
        return t[:, 0:8 * w].rearrange("p (k f) -> p k f", k=8), B

    self.pp_i = 0

    def proj(tw, B, cc):
        i = 5 + self.pp_i % 2
        self.pp_i += 1
        pp, Bp = PS[i], BP[i]
        for k in range(NCH):
            self.mm(pp[:, 0:n], tw[:, k, cc * 128:(cc + 1) * 128], xb[:, k, c0:c0 + n], k == 0, k == NCH - 1, [B, Bxb[k]], [Bp])
        return pp, Bp

    def proj_conv(tw, B, cc, ci, out, Bout):
        pp, Bp = proj(tw, B, cc)
        dve(lambda h: h.tensor_copy(out=stg[:, 0:3], in_=self.ctail[:, ci, :]), [self.Bctail], [B1["stg"]])
        act(lambda h: h.activation(out=stg[:, 3:3 + n], in_=pp[:, 0:n], func=AF.Copy), [Bp], [B1["stg"]])
        dve(lambda h: h.tensor_copy(out=self.ctail[:, ci, :], in_=stg[:, n:n + 3]), [B1["stg"]], [self.Bctail])
        self.acc_i = getattr(self, "acc_i", 0) + 1
        acc, Bacc = (E2[:, 0:n], B1["E2"]) if self.acc_i % 2 == 0 else (DE[:, 0:n], B1["DE"])
        dve(lambda h: h.tensor_scalar(out=acc, in0=stg[:, 0:n], scalar1=self.cw[:, ci:ci + 1], scalar2=self.cb[:, ci:ci + 1], op0=ALU.mult, op1=ALU.add),
            [B1["stg"], C], [Bacc])
        for w in range(1, 4):
            dve(lambda h, w=w: h.scalar_tensor_tensor(out=acc, in0=stg[:, w:w + n], scalar=self.cw[:, w * 24 + ci:w * 24 + ci + 1], in1=acc,
                                                     op0=ALU.mult, op1=ALU.add), [B1["stg"], Bacc, C], [Bacc])
        act(lambda h: h.activation(out=out, in_=acc, func=AF.Silu), [Bacc], [Bout])

    fw.mark("m1_dt")
    tw, B = slab_in(10)
    dtt, Bd = self.dtt, self.Bdtt
    for sg in range(nseg):
        a = c0 + sg * Lc
        pp, Bp = PS[5], BP[5]
        for k in range(NCH):
            self.mm(pp[0:Lc, 0:32], xb[:, k, a:a + Lc], tw[:, k, 0:32], k == 0, k == NCH - 1, [B, Bxb[k]], [Bp])
        dt_, dtA, cs_, te = (dtt[0:Lc, sg, i, :] for i in range(4))
        dve(lambda h: h.tensor_tensor(out=dt_, in0=pp[0:Lc, 0:32], in1=self.dtb[0:Lc, :], op=ALU.add), [Bp, C], [Bd])
        act(lambda h: h.activation(out=dt_, in_=dt_, func=AF.Exp), [Bd], [Bd])
        act(lambda h: h.activation(out=dt_, in_=dt_, func=AF.Ln, bias=1.0), [Bd], [Bd])
        dve(lambda h: h.tensor_tensor(out=dtA, in0=dt_, in1=self.Arow[0:Lc, :], op=ALU.mult), [Bd, C], [Bd])
        pc, Bpc = PS[6], BP[6]
        self.mm(pc[0:Lc, 0:32], self.tri[0:Lc, 0:Lc], dtA, True, True, [C, Bd], [Bpc])
        pe_, Bpe = PS[7], BP[7]
        self.mm(pe_[:, 0:32], self.onesf[0:Lc, :], dtA, True, True, [C, Bd], [Bpe])
        act(lambda h: h.activation(out=cs_, in_=pc[0:Lc, 0:32], func=AF.Copy), [Bpc], [Bd])
        dve(lambda h: h.tensor_tensor(out=te, in0=pe_[0:Lc, 0:32], in1=cs_, op=ALU.subtract), [Bpe, Bd], [Bd])
        act(lambda h: h.activation(out=te, in_=te, func=AF.Exp), [Bd], [Bd])
        dve(lambda h: h.tensor_tensor(out=te, in0=te, in1=dt_, op=ALU.mult), [Bd], [Bd])
        act(lambda h, sg=sg: h.activation(out=self.dend[:, sg, :], in_=pe_[:, 0:32], func=AF.Exp), [Bpe], [Bd])
    fw.mark("m1_bc")
    tw, B = slab_in(8)
    for g in range(4):
        proj_conv(tw, B, g, 16 + g, BT_[:, g, 0:n], B1["BT"])
    tw, B = slab_in(9)
    for g in range(4):
        proj_conv(tw, B, g, 20 + g, CT_[:, g, 0:n], B1["CT"])
    pxt_b = PS[6][:].bitcast(BF16)
    for g in range(4):
        fw.mark("m1_inproj")
        tw, B = slab_in(g)
        for hc in range(4):
            pp, Bp = proj(tw, B, hc)
            act(lambda h, hc=hc, pp=pp: h.activation(out=zs[:, hc, 0:n], in_=pp[:, 0:n], func=AF.Silu), [Bp], [B1["zs"]])
        tw, B = slab_in(4 + g)
        for hc in range(4):
            proj_conv(tw, B, hc, 4 * g + hc, xsT[:, hc, 0:n], B1["xs"])
        fw.mark("m1_seg")
        for sg in range(nseg):
            a = sg * Lc
            par = self.ssd_it % 2
            self.ssd_it += 1
            Wt, Ctl, xt, xh, btm = Wt_[par], Ctl_[par], xt_[par], xh_[par], btm_[par]
            BW, BCt, Bxt, Bxh, Bbt = (B1[k + str(par)] for k in ("W", "Ct", "xt", "xh", "btm"))
            dt_, dtA, cs_, te = (dtt[0:Lc, sg, i, :] for i in range(4))
            pcb, Bpcb = PS[5], BP[5]
            self.mm(pcb[0:Lc, 0:Lc], BT_[:, g, a:a + Lc], CT_[:, g, a:a + Lc], True, True, [B1["BT"], B1["CT"]], [Bpcb])
            dve(lambda h: h.tensor_tensor(out=self.mcb[0:Lc, 0:Lc], in0=pcb[0:Lc, 0:Lc], in1=self.tri[0:Lc, 0:Lc], op=ALU.mult), [Bpcb, C], [self.Bmcb])
            pr, Bpr = PS[4], BP[4]
            for hh in range(8):
                hd = 8 * g + hh
                self.mm(pr[:, hh * Lc:(hh + 1) * Lc], dtA[:, hd:hd + 1].broadcast_to([Lc, 128]), self.tri[0:Lc, 0:Lc], True, True, [Bd, C], [Bpr])
            v3 = lambda ap, P: ap[0:P, 0:8 * Lc].rearrange("p (h t) -> p h t", h=8)
            dve(lambda h: h.tensor_tensor(out=v3(DE, Lc), in0=v3(pr, Lc), in1=bc(cs_[:, 8 * g:8 * g + 8].unsqueeze(2), [Lc, 8, Lc]), op=ALU.subtract),
                [Bpr, Bd], [B1["DE"]])
            act(lambda h: h.activation(out=DE[0:Lc, 0:8 * Lc], in_=DE[0:Lc, 0:8 * Lc], func=AF.Exp), [B1["DE"]], [B1["DE"]])
            dve(lambda h: h.scalar_tensor_tensor(out=v3(Wt, Lc), in0=v3(DE, Lc), scalar=1.0, in1=bc(self.mcb[0:Lc, 0:Lc].unsqueeze(1), [Lc, 8, Lc]),
                                                 op0=ALU.min, op1=ALU.mult), [B1["DE"], self.Bmcb], [BW])
            act(lambda h: h.activation(out=E2[:, 0:8 * Lc], in_=pr[:, 0:8 * Lc], func=AF.Exp), [Bpr], [B1["E2"]])
            dve(lambda h: h.tensor_tensor(out=v3(Ctl, 128), in0=v3(E2, 128), in1=bc(CT_[:, g, a:a + Lc].unsqueeze(1), [128, 8, Lc]), op=ALU.mult),
                [B1["E2"], B1["CT"]], [BCt])
            pxt, Bpxt = pxt_b, BP[6]
            for hc in range(4):
                fw.op("pe", lambda h, hc=hc: h.transpose(pxt[0:Lc, hc * 128:(hc + 1) * 128], xsT[:, hc, a:a + Lc], self.identb[:, :]),
                      reads=[B1["xs"], C], writes=[Bpxt])
            fw.op("pe", lambda h: h.transpose(pxt[0:Lc, 512:640], BT_[:, g, a:a + Lc], self.identb[:, :]), reads=[B1["BT"], C], writes=[Bpxt])
            x3 = lambda ap: ap[0:Lc, 0:512].rearrange("p (h q) -> p h q", h=8)
            dve(lambda h: h.tensor_tensor(out=x3(xt), in0=x3(pxt), in1=bc(dt_[:, 8 * g:8 * g + 8].unsqueeze(2), [Lc, 8, 64]), op=ALU.mult),
                [Bpxt, Bd], [Bxt])
            dve(lambda h: h.tensor_tensor(out=x3(xh), in0=x3(pxt), in1=bc(te[:, 8 * g:8 * g + 8].unsqueeze(2), [Lc, 8, 64]), op=ALU.mult),
                [Bpxt, Bd], [Bxh])
            act(lambda h: h.activation(out=btm[0:Lc, 0:128], in_=pxt[0:Lc, 512:640], func=AF.Copy), [Bpxt], [Bbt])
            for hh in range(8):
                hc, pb = hh // 2, 64 * (hh % 2)
                hd = 8 * g + hh
                py, Bpy = PS[hc], BP[hc]
                self.mm(py[pb:pb + 64, a:a + Lc], xt[0:Lc, hh * 64:(hh + 1) * 64], Wt[0:Lc, hh * Lc:(hh + 1) * Lc], True, False,
                        [Bxt, BW], [Bpy], tile_position=(0, pb))
                self.mm(py[pb:pb + 64, a:a + Lc], self.hstb[:, hd * 64:(hd + 1) * 64], Ctl[:, hh * Lc:(hh + 1) * Lc], False, True,
                        [self.Bhstb[g], BCt], [Bpy], tile_position=(0, pb))
            pS, BpS = PS[7], BP[7]
            self.mm(pS[:, 0:512], btm[0:Lc, 0:128], xh[0:Lc, 0:512], True, True, [Bbt, Bxh], [BpS])
            hg = self.hst[:, 512 * g:512 * (g + 1)]
            h3 = hg.rearrange("p (h q) -> p h q", h=8)
            dve(lambda h, sg=sg: h.tensor_tensor(out=h3, in0=h3, in1=bc(self.dend[:, sg, 8 * g:8 * g + 8].unsqueeze(2), [128, 8, 64]), op=ALU.mult),
                [self.Bhst[g], Bd], [self.Bhst[g]])
            dve(lambda h: h.tensor_tensor(out=hg, in0=hg, in1=pS[:, 0:512], op=ALU.add), [self.Bhst[g], BpS], [self.Bhst[g]])
            act(lambda h: h.activation(out=self.hstb[:, 512 * g:512 * (g + 1)], in_=hg, func=AF.Copy), [self.Bhst[g]], [self.Bhstb[g]])
        fw.mark("m1_epi")
        fw.barrier()
        for hc in range(4):
            py, Bpy = PS[hc], BP[hc]
            dve(lambda h, hc=hc, py=py: h.scalar_tensor_tensor(out=yg[:, hc, 0:n], in0=xsT[:, hc, 0:n], scalar=self.dvec[:, 4 * g + hc:4 * g + hc + 1],
                                                               in1=py[:, 0:n], op0=ALU.mult, op1=ALU.add), [B1["xs"], Bpy, C], [B1["yg"]])
            dve(lambda h, hc=hc: h.tensor_tensor(out=yg[:, hc, 0:n], in0=yg[:, hc, 0:n], in1=zs[:, hc, 0:n], op=ALU.mult), [B1["yg"], B1["zs"]], [B1["yg"]])
        act(lambda h: h.activation(out=xsT[:, :, 0:n], in_=yg[:, :, 0:n], func=AF.Square), [B1["yg"]], [B1["xs"]])
        pms, Bpms = PS[5], BP[5]
        for hc in range(4):
            self.mm(pms[:, 0:n], self.ones512[:], xsT[:, hc, 0:n], hc == 0, hc == 3, [C, B1["xs"]], [Bpms])
        rstd = self.lnt[:, 1, 0:n]
        Bl = self.Blnt[1]
        dve(lambda h: h.tensor_scalar(out=rstd, in0=pms[:, 0:n], scalar1=RMS_EPS, scalar2=None, op0=ALU.add), [Bpms], [Bl])
        act(lambda h: h.activation(out=rstd, in_=rstd, func=AF.Sqrt), [Bl], [Bl])
        dve(lambda h: h.reciprocal(out=rstd, in_=rstd), [Bl], [Bl])
        for hc in range(4):
            ch = 4 * g + hc
            dve(lambda h, hc=hc, ch=ch: h.scalar_tensor_tensor(out=yn[:, ch, c0:c0 + n], in0=yg[:, hc, 0:n], scalar=self.ng[:, ch:ch + 1], in1=rstd,
                                                               op0=ALU.mult, op1=ALU.mult), [B1["yg"], Bl, C], [B1["yn"]])
        fw.barrier()


def _st_stage(self, r):
    slot = 0 if r % 2 == 0 else 3
    return self.lnt[:, slot, :].rearrange("p (t n) -> p t n", t=4), self.Blnt[slot]


def ssd_state_in(self, sq):
    fw, I = self.fw, self.I
    C = self.Bconst
    src = I["state_ssd"][0, sq].rearrange("h p n -> (h p) n").rearrange("(r t p) n -> r p t n", t=4, p=128)
    for r in range(4):
        st, Bst = _st_stage(self, r)
        ps, Bp = self.PS[6 + r % 2], self.BP[6 + r % 2]
        fw.dma("sp", st, src[r], Bst, writes=[Bst])
        for t in range(4):
            fw.op("pe", lambda h, t=t: h.transpose(ps[:, t * 128:(t + 1) * 128], st[:, t, :], self.ident[:, :]), reads=[Bst, C], writes=[Bp])
        fw.op("dve", lambda h: h.tensor_copy(out=self.hst[:, 512 * r:512 * (r + 1)], in_=ps[:, 0:512]), reads=[Bp], writes=[self.Bhst[r]])
        fw.op("act", lambda h: h.activation(out=self.hstb[:, 512 * r:512 * (r + 1)], in_=ps[:, 0:512], func=AF.Copy), reads=[Bp], writes=[self.Bhstb[r]])
    srcc = I["state_conv"][0, sq].rearrange("w (c p) -> (w c) p", p=128)
    fw.dma("sp", self.stg[0:72, :], srcc, self.Bstg, writes=[self.Bstg])
    fw.op("pe", lambda h: h.transpose(self.PS[7][:, 0:72], self.stg[0:72, :], self.ident[0:72, 0:72]), reads=[self.Bstg, C], writes=[self.BP[7]])
    fw.op("dve", lambda h: h.tensor_copy(out=self.ctail[:].rearrange("p c w -> p w c"), in_=self.PS[7][:, 0:72].rearrange("p (w c) -> p w c", w=3)),
          reads=[self.BP[7]], writes=[self.Bctail])


def ssd_state_out(self, sfx, seq):
    fw = self.fw
    C = self.Bconst
    dst = self.O["ssd_" + sfx][0, seq].rearrange("h p n -> (h p) n").rearrange("(r t p) n -> r p t n", t=4, p=128)
    for r in range(4):
        st, Bst = _st_stage(self, r)
        ps, Bp = self.PS[6 + r % 2], self.BP[6 + r % 2]
        for t in range(4):
            i = 4 * r + t
            fw.op("pe", lambda h, i=i, t=t: h.transpose(ps[:, t * 128:(t + 1) * 128], self.hst[:, i * 128:(i + 1) * 128], self.ident[:, :]),
                  reads=[self.Bhst[r], C], writes=[Bp])
        fw.op("dve", lambda h: h.tensor_copy(out=st, in_=ps[:, 0:512].rearrange("p (t n) -> p t n", t=4)), reads=[Bp], writes=[Bst])
        fw.dma("sp", dst[r], st, Bst, reads=[Bst])
    tmp = self.lnt[:, 2, 0:72]
    fw.op("dve", lambda h: h.tensor_copy(out=tmp.rearrange("p (w c) -> p w c", w=3), in_=self.ctail[:].rearrange("p c w -> p w c")),
          reads=[self.Bctail], writes=[self.Blnt[2]])
    fw.op("pe", lambda h: h.transpose(self.PS[7][0:72, 0:128], tmp, self.ident[:, :]), reads=[self.Blnt[2], C], writes=[self.BP[7]])
    fw.op("dve", lambda h: h.tensor_copy(out=self.stg[0:72, :], in_=self.PS[7][0:72, 0:128]), reads=[self.BP[7]], writes=[self.Bstg])
    dstc = self.O["conv_" + sfx][0, seq].rearrange("w (c p) -> (w c) p", p=128)
    fw.dma("sp", dstc, self.stg[0:72, :], self.Bstg, reads=[self.Bstg])


def mixer1(self, tile, N):
    fw = self.fw
    kind, s, ti = tile
    C = self.Bconst
    PS, BP = self.PS, self.BP
    xf, Bxf = self.xf, self.Bxf
    if not hasattr(self, "B1"):
        self.B1 = {k: Buf(k) for k in ("stg", "E2", "DE", "BT", "CT", "zs", "xs", "yg", "yn")}
        for k in ("W", "Ct", "xt", "xh", "btm"):
            for par in range(2):
                self.B1[k + str(par)] = Buf(k + str(par))
    fw.barrier()
    if kind == "p":
        if ti == 0:
            fw.op("dve", lambda h: h.memset(self.hst[:], 0.0), writes=self.Bhst)
            fw.op("pool", lambda h: h.memset(self.hstb[:], 0.0), writes=self.Bhstb)
            fw.op("pool", lambda h: h.memset(self.ctail[:], 0.0), writes=[self.Bctail])
        ssd_core(self, 0, 512, 8, 64)
        if ti == self.SEQ // 512 - 1:
            ssd_state_out(self, "prompt", s)
    else:
        for sq in range(self.NS):
            ssd_state_in(self, sq)
            ssd_core(self, sq * DEC_SEQ, DEC_SEQ, 1, DEC_SEQ)
            ssd_state_out(self, "sample", sq)
    fw.mark("m1_out")
    yn = self.hb[:, 0:16, :]
    wout = self.S["ssd_w_out"][0].rearrange("(k p) f -> p k f", p=128)
    for o2 in range(4):
        t, B = self.get_slab([(lambda t: t[:, :].rearrange("p (k f) -> p k f", k=16), wout[:, :, 256 * o2:256 * o2 + 256], "ssd_w_out")])
        tw = t[:, :].rearrange("p (k f) -> p k f", k=16)
        for oi in range(2):
            o = 2 * o2 + oi
            pp, Bp = PS[o % 2], BP[o % 2]
            for k in range(16):
                self.mm(pp[:, 0:N], tw[:, k, oi * 128:(oi + 1) * 128], yn[:, k, 0:N], k == 0, k == 15, [B, self.B1["yn"]], [Bp])
            fw.op("dve", lambda h, o=o, pp=pp: h.scalar_tensor_tensor(out=xf[:, o, 0:N], in0=xf[:, o, 0:N], scalar=ALPHA, in1=pp[:, 0:N],
                                                                     op0=ALU.mult, op1=ALU.add), reads=[Bxf[o], Bp], writes=[Bxf[o]])
    fw.barrier([self.Bstg, self.Blnt[0], self.Blnt[3]])
    self.layer_norm(4, N)

Builder.setup_mix0 = setup_mix0
Builder.mixer0 = mixer0
Builder.s5_sample_init = s5_sample_init
Builder.load_cache = load_cache
Builder.setup_ssd = setup_ssd
Builder.mixer1 = mixer1

_OUT_ORDER = ["y_prompt", "y_sample", "s5_re_prompt", "s5_im_prompt", "sb_k_prompt", "sb_v_prompt", "ssd_prompt",
              "conv_prompt", "s5_re_sample", "s5_im_sample", "sb_k_sample", "sb_v_sample", "ssd_sample", "conv_sample"]
_BATCH_AXIS = {"x_prompt": 0, "x_sample": 0, "state_s5_re": 1, "state_s5_im": 1, "cache_sb_k": 1, "cache_sb_v": 1,
               "state_ssd": 1, "state_conv": 1}


def make_in_maps(inputs, n_cores):
    maps = []
    for c in range(n_cores):
        m = {}
        for k, v in inputs.items():
            v = np.asarray(v)
            if k in _BATCH_AXIS:
                ax = _BATCH_AXIS[k]
                n = v.shape[ax] // n_cores
                sl = [slice(None)] * v.ndim
                sl[ax] = slice(c * n, (c + 1) * n)
                m[k] = np.ascontiguousarray(v[tuple(sl)])
            else:
                m[k] = v
        maps.append(m)
    return maps


def gather(results):
    outs = []
    for nm in _OUT_ORDER:
        ax = 0 if nm in ("y_prompt", "y_sample") else 1
        outs.append(np.concatenate([r[nm] for r in results], axis=ax).astype(np.float32))
    return tuple(outs)


def kernel(**inputs):
    n = 8
    b = Builder(SEQ=4096, NP=2, NS=4)
    nc = b.build()
    res = run_bass_kernel_spmd(nc, make_in_maps(inputs, n), core_ids=list(range(n)))
    return gather(res.results)
```

```python
import math
import numpy as np
import concourse.bass as bass
import concourse.mybir as mybir
from concourse.bass_utils import run_bass_kernel_spmd
from contextlib import ExitStack

F32 = mybir.dt.float32
BF16 = mybir.dt.bfloat16
AF = mybir.ActivationFunctionType
ALU = mybir.AluOpType

D = 1024
DFF = 2816
NCH = 8
MCH = 22
ALPHA = (2.0 * 2) ** 0.25
LN_EPS = 1e-5
RMS_EPS = 1e-5
PAST = 2048
DEC_SEQ = 16
LSUB = 64
MAGIC = 12582912.0


class Buf:
    __slots__ = ("name", "w", "r", "dsem", "dtot", "psum", "strict")

    def __init__(self, name="", psum=False, strict=False):
        self.name = name
        self.psum = psum
        self.strict = strict
        self.w = None
        self.r = {}
        self.dsem = None
        self.dtot = 0


class Eng:
    def __init__(self, name, sem):
        self.name = name
        self.sem = sem
        self.n = 0
        self.seen = {}
        self.prog = []


class _Rec:
    def __init__(self):
        self.call = None

    def __getattr__(self, name):
        def f(*args, **kwargs):
            self.call = (name, args, kwargs)
            return None
        return f


class FW:
    def __init__(self, nc, es):
        self.nc = nc
        self.es = es
        self.engs = {}
        for name in ("pe", "dve", "act", "pool", "sp"):
            sem = es.enter_context(nc.semaphore("sem_" + name))
            self.engs[name] = Eng(name, sem)
        self.dma_bufs = []
        self.nsem = 5
        self.uid = 0
        self.nosame = False

    def sbuf(self, shape, dtype, name=None):
        self.uid += 1
        return self.es.enter_context(self.nc.sbuf_tensor(name or ("sb%d" % self.uid), list(shape), dtype))

    def psum(self, shape, dtype=F32, name=None):
        self.uid += 1
        return self.es.enter_context(self.nc.psum_tensor(name or ("ps%d" % self.uid), list(shape), dtype))

    def _wait(self, e, tok, strict=False):
        if tok[0] == "e":
            _, name, seq = tok
            if name == e.name and (name in ("pe", "sp") or (self.nosame and not strict)):
                return
            if e.seen.get(name, 0) >= seq:
                return
            e.prog.append(("w", self.engs[name].sem, seq))
            e.seen[name] = seq
        else:
            _, b, val = tok
            key = ("d", id(b))
            if e.seen.get(key, 0) >= val:
                return
            e.prog.append(("w", b.dsem, val))
            e.seen[key] = val

    def _deps(self, e, reads, writes):
        for b in reads:
            if b.w is not None:
                self._wait(e, b.w, b.strict)
            if b.psum:
                for t in b.r.values():
                    if not (t[0] == "e" and t[1] == e.name):
                        self._wait(e, t)
        for b in writes:
            if b.w is not None:
                self._wait(e, b.w)
            for t in b.r.values():
                if not (t[0] == "e" and t[1] == e.name):
                    self._wait(e, t)

    def op(self, ename, fn, reads=(), writes=()):
        e = self.engs[ename]
        self._deps(e, reads, writes)
        e.n += 1
        rec = _Rec()
        fn(rec)
        e.prog.append(("o", rec.call, e.sem, 1))
        tok = ("e", ename, e.n)
        for b in reads:
            b.r[ename] = tok
        for b in writes:
            b.w = tok
            b.r = {}
        return tok

    def dma(self, qname, out, in_, track, reads=(), writes=(), slow=False):
        e = self.engs[qname]
        self._deps(e, reads, writes)
        if track.dsem is None:
            track.dsem = self.es.enter_context(self.nc.semaphore("dsem_%d" % self.nsem))
            self.nsem += 1
            self.dma_bufs.append(track)
        track.dtot += 16
        if slow:
            e.prog.append(("o", ("dma_start", (), dict(out=out, in_=in_, allow_slow_non_contiguous=True)), track.dsem, 16))
        else:
            e.prog.append(("o", ("dma_start", (), dict(out=out, in_=in_)), track.dsem, 16))
        tok = ("d", track, track.dtot)
        key = ("d", id(track))
        for b in reads:
            b.r[key] = tok
        for b in writes:
            b.w = tok
            b.r = {}
        return tok

    def mark(self, name):
        if not hasattr(self, "marks"):
            self.marks = []
        self.marks.append((name, {k: e.n for k, e in self.engs.items()}))

    def barrier(self, bufs=()):
        names = ("pe", "dve", "act", "pool")
        for a in names:
            ea = self.engs[a]
            for b in bufs:
                if b.dsem is not None:
                    self._wait(ea, ("d", b, b.dtot))
            for bn in names:
                if bn != a and self.engs[bn].n > 0:
                    self._wait(ea, ("e", bn, self.engs[bn].n))

    def finish(self):
        e = self.engs["sp"]
        for b in self.dma_bufs:
            e.prog.append(("w", b.dsem, b.dtot))
        for name, o in self.engs.items():
            if name != "sp" and o.n > 0:
                e.prog.append(("w", o.sem, o.n))
        with self.nc.Block() as block:
            def mk(prog):
                def body(h):
                    for it in prog:
                        if it[0] == "w":
                            h.wait_ge(it[1], it[2])
                        else:
                            name, args, kwargs = it[1]
                            getattr(h, name)(*args, **kwargs).then_inc(it[2], it[3])
                return body
            block.tensor(mk(self.engs["pe"].prog))
            block.vector(mk(self.engs["dve"].prog))
            block.scalar(mk(self.engs["act"].prog))
            block.gpsimd(mk(self.engs["pool"].prog))
            block.sync(mk(self.engs["sp"].prog))


def bc(ap, shape):
    return ap.broadcast_to(list(shape))


class Builder:
    def __init__(self, SEQ=4096, NP=2, NS=4, dbg=False, do_sample=True, layers=2):
        self.SEQ, self.NP, self.NS = SEQ, NP, NS
        self.dbg = dbg
        self.do_sample = do_sample
        self.layers = layers
        self.nc = bass.Bass("TRN2", target_bir_lowering=False)
        self.dbg_outs = {}

    def din(self, name, shape):
        return self.nc.dram_tensor(name, list(shape), F32, kind="ExternalInput").ap()

    def dout(self, name, shape):
        return self.nc.dram_tensor(name, list(shape), F32, kind="ExternalOutput").ap()

    def declare(self):
        NP, NS, SEQ = self.NP, self.NS, self.SEQ
        I = {}
        I["x_prompt"] = self.din("x_prompt", [NP, SEQ, D])
        I["x_sample"] = self.din("x_sample", [NS, DEC_SEQ, D])
        I["state_s5_re"] = self.din("state_s5_re", [1, NS, 32, 64])
        I["state_s5_im"] = self.din("state_s5_im", [1, NS, 32, 64])
        I["cache_sb_k"] = self.din("cache_sb_k", [1, NS, PAST, 8, 64])
        I["cache_sb_v"] = self.din("cache_sb_v", [1, NS, PAST, 8, 64])
        I["state_ssd"] = self.din("state_ssd", [1, NS, 32, 64, 128])
        I["state_conv"] = self.din("state_conv", [1, NS, 3, 3072])
        for nm, shp in (("ln_g", [2, 3, D]), ("ln_b", [2, 3, D]), ("ffn_w_gate", [2, 2, D, DFF]),
                        ("ffn_w_up", [2, 2, D, DFF]), ("ffn_w_down", [2, 2, DFF, D]),
                        ("mix0_w_in", [1, D, 2048]), ("s5_a_re", [1, 32, 64]), ("s5_a_im", [1, 32, 64]),
                        ("s5_log_dt", [1, 32]), ("s5_b_re", [1, 32, 64, 16]), ("s5_b_im", [1, 32, 64, 16]),
                        ("s5_c_re", [1, 32, 16, 64]), ("s5_c_im", [1, 32, 16, 64]), ("s5_d", [1, 512]),
                        ("s5_w_glu", [1, 512, 512]), ("s5_b_glu", [1, 512]), ("mix0_w_out", [1, D, D]),
                        ("ssd_w_in", [1, D, 5152]), ("ssd_conv_w", [1, 4, 3072]), ("ssd_conv_b", [1, 3072]),
                        ("ssd_dt_bias", [1, 32]), ("ssd_a_log", [1, 32]), ("ssd_d", [1, 32]),
                        ("ssd_norm_g", [1, 2048]), ("ssd_w_out", [1, 2048, D])):
            I[nm] = self.din(nm, shp)
        O = {}
        O["y_prompt"] = self.dout("y_prompt", [NP, SEQ, D])
        O["y_sample"] = self.dout("y_sample", [NS, DEC_SEQ, D])
        for sfx, nb, sl in (("prompt", NP, SEQ), ("sample", NS, DEC_SEQ)):
            O["s5_re_" + sfx] = self.dout("s5_re_" + sfx, [1, nb, 32, 64])
            O["s5_im_" + sfx] = self.dout("s5_im_" + sfx, [1, nb, 32, 64])
            O["sb_k_" + sfx] = self.dout("sb_k_" + sfx, [1, nb, sl, 8, 64])
            O["sb_v_" + sfx] = self.dout("sb_v_" + sfx, [1, nb, sl, 8, 64])
            O["ssd_" + sfx] = self.dout("ssd_" + sfx, [1, nb, 32, 64, 128])
            O["conv_" + sfx] = self.dout("conv_" + sfx, [1, nb, 3, 3072])
        self.I, self.O = I, O
        S = {}
        for nm in ("ffn_w_gate", "ffn_w_up", "ffn_w_down", "mix0_w_in", "s5_w_glu", "mix0_w_out",
                   "ssd_w_in", "ssd_w_out"):
            shp = list(I[nm].shape)
            S[nm] = self.nc.dram_tensor("scr_" + nm, shp, BF16, kind="Internal").ap()
        self.S = S
        self.SB = {}
        for nm in S:
            if nm.startswith("ffn"):
                for l in range(2):
                    for j in range(2):
                        self.SB[(nm, l, j)] = Buf("scr")
            else:
                self.SB[nm] = Buf("scr")

    def dbg_out(self, name, shape):
        if name not in self.dbg_outs:
            self.dbg_outs[name] = self.dout("dbg_" + name, shape)
        return self.dbg_outs[name]

    def build(self):
        self.declare()
        with ExitStack() as es:
            self.fw = FW(self.nc, es)
            self.alloc()
            self.setup()
            tiles = []
            for s in range(self.NP):
                for i in range(self.SEQ // 512):
                    tiles.append(("p", s, i))
            for t in tiles:
                self.run_tile(t)
            if self.do_sample:
                self.run_tile(("s", 0, 0))
            self.fw.finish()
        return self.nc

    def alloc(self):
        fw = self.fw
        self.xf = fw.sbuf([128, NCH, 512], F32, "xf")
        self.Bxf = [Buf("xf%d" % c) for c in range(NCH)]
        self.xb = fw.sbuf([128, NCH, 512], BF16, "xb")
        self.Bxb = [Buf("xb%d" % c) for c in range(NCH)]
        self.hb = fw.sbuf([128, MCH, 512], BF16, "hb")
        self.Bh = [Buf("h%d" % m) for m in range(MCH)]
        self.hbf = self.hb[:].rearrange("p m n -> p (m n)").bitcast(F32)
        self.ma = fw.sbuf([128, 6144], F32, "ma")
        self.mab = self.ma[:].bitcast(BF16)
        self.NSLAB = 3
        self.slab = [fw.sbuf([128, 4096], BF16, "slab%d" % i) for i in range(self.NSLAB)]
        self.Bslab = [Buf("slab%d" % i) for i in range(self.NSLAB)]
        self.slab_i = 0
        self.KT = fw.sbuf([128, 4, 4096], BF16, "KT")
        self.VT = fw.sbuf([128, 32, 512], BF16, "VT")
        self.BKT = [Buf("KT%d" % i) for i in range(8)]
        self.BVT = [Buf("VT%d" % i) for i in range(8)]
        self.PS = [fw.psum([128, 512], F32, "psb%d" % i) for i in range(8)]
        self.BP = [Buf("ps%d" % i, psum=True) for i in range(8)]
        self.sg = [fw.sbuf([128, 512], F32, "sg%d" % i) for i in range(2)]
        self.Bsg = [Buf("sg%d" % i) for i in range(2)]
        self.knew = self.KT[:, :, 3072:3136]
        self.vnew = self.VT[0:16, 20:24, :]
        self.lnt = fw.sbuf([128, 4, 512], F32, "lnt")
        self.Blnt = [Buf("lnt%d" % i) for i in range(4)]

    def mm(self, out, lhsT, rhs, start, stop, reads, writes, **kw):
        self.fw.op("pe", lambda h: h.matmul(out, lhsT=lhsT, rhs=rhs, start=start, stop=stop, **kw),
                   reads=reads, writes=writes)

    def load_T(self, rows_ap, R, dest, Bdest):
        fw = self.fw
        fw.dma("sp", self.stg[0:R, :], rows_ap, self.Bstg, writes=[self.Bstg])
        fw.op("pe", lambda h: h.transpose(self.PS[7][:, 0:R], self.stg[0:R, :], self.ident[0:R, 0:R]),
              reads=[self.Bstg, self.Bconst], writes=[self.BP[7]])
        fw.op("dve", lambda h: h.tensor_copy(out=dest, in_=self.PS[7][:, 0:R]), reads=[self.BP[7]], writes=[Bdest])

    def setup(self):
        fw, nc, I, S = self.fw, self.nc, self.I, self.S
        def cast(nm, idx_list):
            for idx in idx_list:
                src = I[nm]
                dst = S[nm]
                for i in idx:
                    src = src[i]
                    dst = dst[i]
                key = (nm,) + tuple(idx) if nm.startswith("ffn") else nm
                fw.dma("pool", dst, src, self.SB[key], writes=[self.SB[key]])
        lj = [(l, j) for l in range(2) for j in range(2)]
        cast("ffn_w_gate", lj[:1]); cast("ffn_w_up", lj[:1]); cast("ffn_w_down", lj[:1])
        cast("mix0_w_in", [(0,)]); cast("s5_w_glu", [(0,)]); cast("mix0_w_out", [(0,)])
        cast("ffn_w_gate", lj[1:]); cast("ffn_w_up", lj[1:]); cast("ffn_w_down", lj[1:])
        cast("ssd_w_in", [(0,)]); cast("ssd_w_out", [(0,)])

        self.Bconst = Buf("const")
        self.Bstg = Buf("stg")
        self.stg = fw.sbuf([128, 128], F32, "stg")
        self.ident = fw.sbuf([128, 128], F32, "ident")
        self.identb = fw.sbuf([128, 128], BF16, "identb")
        self.onesb = fw.sbuf([128, 128], BF16, "onesb")
        self.ones1 = fw.sbuf([128, 128], BF16, "ones1")
        self.uincl = fw.sbuf([128, 128], BF16, "uincl")
        self.onesf = fw.sbuf([128, 128], F32, "onesf")
        C = self.Bconst
        fw.op("pool", lambda h: h.memset(self.ident[:], 1.0), writes=[C])
        fw.op("pool", lambda h: h.affine_select(out=self.ident[:], in_=self.ident[:], pattern=[[-1, 128]],
                                                compare_op=ALU.is_equal, fill=0.0, base=0, channel_multiplier=1),
              reads=[C], writes=[C])
        fw.op("pool", lambda h: h.tensor_copy(out=self.identb[:], in_=self.ident[:]), reads=[C], writes=[C])
        fw.op("pool", lambda h: h.memset(self.onesb[:], 1.0 / 1024.0), writes=[C])
        fw.op("pool", lambda h: h.memset(self.ones1[:], 1.0), writes=[C])
        fw.op("pool", lambda h: h.memset(self.onesf[:], 1.0), writes=[C])
        fw.op("pool", lambda h: h.memset(self.uincl[:], 1.0), writes=[C])
        fw.op("pool", lambda h: h.affine_select(out=self.uincl[:], in_=self.uincl[:], pattern=[[-1, 128]],
                                                compare_op=ALU.is_ge, fill=0.0, base=0, channel_multiplier=1),
              reads=[C], writes=[C])
        self.lng = fw.sbuf([128, 48], F32, "lng")
        self.lnb = fw.sbuf([128, 48], F32, "lnb")
        self.load_T(I["ln_g"].rearrange("l j (c p) -> (l j c) p", p=128), 48, self.lng[:], C)
        self.load_T(I["ln_b"].rearrange("l j (c p) -> (l j c) p", p=128), 48, self.lnb[:], C)
        self.setup_mix0()
        if self.layers > 1:
            self.setup_ssd()
        fw.barrier([self.Bconst, self.Bstg])

    def get_slab(self, pieces):
        i = self.slab_i
        self.slab_i = (i + 1) % self.NSLAB
        t, B = self.slab[i], self.Bslab[i]
        for dst_fn, src, key in pieces:
            self.fw.dma("sp", dst_fn(t), src, B, reads=[self.SB[key]], writes=[B])
        return t, B

    def layer_norm(self, idx, N):
        fw = self.fw
        xf, xb, hb = self.xf, self.xb, self.hb
        Bxf, Bxb, Bh, BP = self.Bxf, self.Bxb, self.Bh, self.BP
        ybf = hb[:, 0:8, 0:N]
        sq = hb[:, 8:16, 0:N]
        fw.op("dve", lambda h: h.tensor_copy(out=ybf, in_=xf[:, :, 0:N]), reads=Bxf, writes=Bh[0:8])
        fw.op("act", lambda h: h.activation(out=sq, in_=xf[:, :, 0:N], func=AF.Square), reads=Bxf, writes=Bh[8:16])
        pm, pq = self.PS[0], self.PS[1]
        for c in range(NCH):
            self.mm(pm[:, 0:N], self.onesb[:], hb[:, c, 0:N], c == 0, c == NCH - 1, [Bh[c], self.Bconst], [BP[0]])
        for c in range(NCH):
            self.mm(pq[:, 0:N], self.onesb[:], hb[:, 8 + c, 0:N], c == 0, c == NCH - 1, [Bh[8 + c], self.Bconst], [BP[1]])
        mean, rstd, nmr = self.lnt[:, 0, 0:N], self.lnt[:, 1, 0:N], self.lnt[:, 2, 0:N]
        Bl = self.Blnt
        fw.op("act", lambda h: h.activation(out=mean, in_=pm[:, 0:N], func=AF.Copy), reads=[BP[0]], writes=[Bl[0]])
        fw.op("dve", lambda h: h.tensor_tensor(out=rstd, in0=pm[:, 0:N], in1=mean, op=ALU.mult), reads=[BP[0], Bl[0]], writes=[Bl[1]])
        fw.op("dve", lambda h: h.tensor_tensor(out=rstd, in0=pq[:, 0:N], in1=rstd, op=ALU.subtract), reads=[BP[1], Bl[1]], writes=[Bl[1]])
        fw.op("dve", lambda h: h.tensor_scalar(out=rstd, in0=rstd, scalar1=0.0, scalar2=LN_EPS, op0=ALU.max, op1=ALU.add), reads=[Bl[1]], writes=[Bl[1]])
        fw.op("act", lambda h: h.activation(out=rstd, in_=rstd, func=AF.Sqrt), reads=[Bl[1]], writes=[Bl[1]])
        fw.op("dve", lambda h: h.reciprocal(out=rstd, in_=rstd), reads=[Bl[1]], writes=[Bl[1]])
        fw.op("dve", lambda h: h.scalar_tensor_tensor(out=nmr, in0=mean, scalar=-1.0, in1=rstd, op0=ALU.mult, op1=ALU.mult),
              reads=[Bl[0], Bl[1]], writes=[Bl[2]])
        xv = xf[:, :, 0:N]
        fw.op("dve", lambda h: h.tensor_tensor(out=xv, in0=xv, in1=bc(rstd.unsqueeze(1), [128, NCH, N]), op=ALU.mult),
              reads=Bxf + [Bl[1]], writes=Bxf)
        fw.op("dve", lambda h: h.tensor_tensor(out=xv, in0=xv, in1=bc(nmr.unsqueeze(1), [128, NCH, N]), op=ALU.add),
              reads=Bxf + [Bl[2]], writes=Bxf)
        for c in range(NCH):
            col = idx * 8 + c
            fw.op("act", lambda h, c=c, col=col: h.activation(out=xf[:, c, 0:N], in_=xf[:, c, 0:N], func=AF.Identity,
                                                               scale=self.lng[:, col:col + 1], bias=self.lnb[:, col:col + 1]),
                  reads=[Bxf[c], self.Bconst], writes=[Bxf[c]])
            eng = "dve" if c % 2 == 0 else "pool"
            fw.op(eng, lambda h, c=c: h.tensor_copy(out=xb[:, c, 0:N], in_=xf[:, c, 0:N]), reads=[Bxf[c]], writes=[Bxb[c]])

    def ffn(self, l, j, N):
        fw, S = self.fw, self.S
        xf, xb, hb = self.xf, self.xb, self.hb
        Bxf, Bxb, Bh, BP, PS = self.Bxf, self.Bxb, self.Bh, self.BP, self.PS
        wg = S["ffn_w_gate"][l, j].rearrange("(k p) f -> p k f", p=128)
        wu = S["ffn_w_up"][l, j].rearrange("(k p) f -> p k f", p=128)
        wd = S["ffn_w_down"][l, j].rearrange("(m p) f -> p m f", p=128)
        for s in range(11):
            cols = slice(256 * s, 256 * s + 256)
            t, B = self.get_slab([
                (lambda t: t[:, 0:2048].rearrange("p (k f) -> p k f", k=8), wg[:, :, cols], ("ffn_w_gate", l, j)),
                (lambda t: t[:, 2048:4096].rearrange("p (k f) -> p k f", k=8), wu[:, :, cols], ("ffn_w_up", l, j))])
            tg = t[:, 0:2048].rearrange("p (k f) -> p k f", k=8)
            tu = t[:, 2048:4096].rearrange("p (k f) -> p k f", k=8)
            for mi in range(2):
                m = 2 * s + mi
                pg, pu = PS[m % 2], PS[2 + m % 2]
                Bg, Bu = BP[m % 2], BP[2 + m % 2]
                for k in range(NCH):
                    self.mm(pg[:, 0:N], tg[:, k, mi * 128:(mi + 1) * 128], xb[:, k, 0:N], k == 0, k == NCH - 1, [B, Bxb[k]], [Bg])
                for k in range(NCH):
                    self.mm(pu[:, 0:N], tu[:, k, mi * 128:(mi + 1) * 128], xb[:, k, 0:N], k == 0, k == NCH - 1, [B, Bxb[k]], [Bu])
                sg, Bs = self.sg[m % 2], self.Bsg[m % 2]
                fw.op("act", lambda h, sg=sg, pg=pg: h.activation(out=sg[:, 0:N], in_=pg[:, 0:N], func=AF.Silu), reads=[Bg], writes=[Bs])
                fw.op("dve", lambda h, sg=sg, pu=pu, m=m: h.scalar_tensor_tensor(out=hb[:, m, 0:N], in0=sg[:, 0:N], scalar=0.5, in1=pu[:, 0:N],
                                                                                op0=ALU.mult, op1=ALU.mult),
                      reads=[Bs, Bu], writes=[Bh[m]])
        for o2 in range(4):
            cols = slice(256 * o2, 256 * o2 + 256)
            for half in range(2):
                t, B = self.get_slab([(lambda t: t[:, 0:2816].rearrange("p (m f) -> p m f", m=11),
                                       wd[:, 11 * half:11 * half + 11, cols], ("ffn_w_down", l, j))])
                tv = t[:, 0:2816].rearrange("p (m f) -> p m f", m=11)
                for oi in range(2):
                    o = 2 * o2 + oi
                    pd, Bd = PS[4 + o % 4], BP[4 + o % 4]
                    for mm_ in range(11):
                        m = 11 * half + mm_
                        self.mm(pd[:, 0:N], tv[:, mm_, oi * 128:(oi + 1) * 128], hb[:, m, 0:N], m == 0, m == MCH - 1, [B, Bh[m]], [Bd])
            for oi in range(2):
                o = 2 * o2 + oi
                pd, Bd = PS[4 + o % 4], BP[4 + o % 4]
                fw.op("dve", lambda h, o=o, pd=pd: h.scalar_tensor_tensor(out=xf[:, o, 0:N], in0=xf[:, o, 0:N], scalar=ALPHA, in1=pd[:, 0:N],
                                                                         op0=ALU.mult, op1=ALU.add),
                      reads=[Bxf[o], Bd], writes=[Bxf[o]])
        self.layer_norm(l * 3 + (0 if j == 0 else 2), N)

    def load_x(self, tile, N):
        fw = self.fw
        kind, s, i = tile
        nb = (N + 127) // 128
        stg = self.hbf[:, 0:4096].rearrange("p (b f) -> p b f", b=4)
        Bst = self.Bh[0:16]
        for b in range(nb):
            rows = min(128, N - b * 128)
            if kind == "p":
                src = self.I["x_prompt"][s, i * 512 + b * 128:i * 512 + b * 128 + rows, :]
            else:
                src = self.I["x_sample"].rearrange("s t d -> (s t) d")[b * 128:b * 128 + rows, :]
            fw.dma("sp", stg[0:rows, b, :], src, Bst[4 * b], writes=Bst[4 * b:4 * b + 4])
        for c in range(NCH):
            ps, Bp = self.PS[c % 4], self.BP[c % 4]
            for b in range(nb):
                rows = min(128, N - b * 128)
                fw.op("pe", lambda h, ps=ps, b=b, c=c, rows=rows: h.transpose(ps[:, b * 128:b * 128 + rows], stg[0:rows, b, c * 128:(c + 1) * 128],
                                                                           self.ident[0:rows, 0:rows]),
                      reads=Bst[4 * b:4 * b + 4] + [self.Bconst], writes=[Bp])
            fw.op("act", lambda h, ps=ps, c=c: h.activation(out=self.xf[:, c, 0:N], in_=ps[:, 0:N], func=AF.Copy), reads=[Bp], writes=[self.Bxf[c]])
            fw.op("dve", lambda h, ps=ps, c=c: h.tensor_copy(out=self.xb[:, c, 0:N], in_=ps[:, 0:N]), reads=[Bp], writes=[self.Bxb[c]])

    def store_y(self, tile, N):
        fw = self.fw
        kind, s, i = tile
        nb = (N + 127) // 128
        stg = self.hbf[:, 0:4096].rearrange("p (b f) -> p b f", b=4)
        Bst = self.Bh[0:16]
        for b in range(nb):
            rows = min(128, N - b * 128)
            for half in range(2):
                ps, Bp = self.PS[(2 * b + half) % 4], self.BP[(2 * b + half) % 4]
                for cc in range(4):
                    c = 4 * half + cc
                    fw.op("pe", lambda h, ps=ps, b=b, c=c, cc=cc, rows=rows: h.transpose(ps[0:rows, cc * 128:(cc + 1) * 128],
                                                                                     self.xf[:, c, b * 128:b * 128 + rows], self.ident[:, :]),
                          reads=[self.Bxf[c], self.Bconst], writes=[Bp])
                eng = "act" if half == 0 else "dve"
                if eng == "act":
                    fw.op("act", lambda h, ps=ps, b=b, half=half, rows=rows: h.activation(out=stg[0:rows, b, half * 512:(half + 1) * 512], in_=ps[0:rows, :], func=AF.Copy),
                          reads=[Bp], writes=[Bst[4 * b + 2 * half], Bst[4 * b + 2 * half + 1]])
                else:
                    fw.op("dve", lambda h, ps=ps, b=b, half=half, rows=rows: h.tensor_copy(out=stg[0:rows, b, half * 512:(half + 1) * 512], in_=ps[0:rows, :]),
                          reads=[Bp], writes=[Bst[4 * b + 2 * half], Bst[4 * b + 2 * half + 1]])
            if kind == "p":
                dst = self.O["y_prompt"][s, i * 512 + b * 128:i * 512 + b * 128 + rows, :]
            else:
                dst = self.O["y_sample"].rearrange("s t d -> (s t) d")[b * 128:b * 128 + rows, :]
            fw.dma("sp", dst, stg[0:rows, b, :], Bst[4 * b], reads=Bst[4 * b:4 * b + 4])

    def dbg_sb(self, name, ap, B, shape):
        if not self.dbg or name in self.dbg_outs:
            return
        o = self.dbg_out(name, shape)
        self.fw.dma("sp", o, ap, B, reads=[B])

    def dump_x(self, name, tile, N):
        if not self.dbg:
            return
        kind, s, i = tile
        ntile = self.NP * (self.SEQ // 512) + 1
        o = self.dbg_out(name, [ntile, NCH, 128, 512])
        ti = (s * (self.SEQ // 512) + i) if kind == "p" else ntile - 1
        for c in range(NCH):
            self.fw.dma("sp", o[ti, c, :, 0:N], self.xf[:, c, 0:N], self.Bxf[c], reads=[self.Bxf[c]])

    def run_tile(self, tile):
        import os
        self.stop = int(os.environ.get("KSTOP", "9"))
        kind = tile[0]
        N = 512 if kind == "p" else self.NS * DEC_SEQ
        if self.stop < 1:
            return
        self.fw.mark("tile %s %d %d" % tile)
        self.load_x(tile, N)
        self.fw.mark("ffn00")
        ksub = os.environ.get("KSUB", "")
        if ksub == "load":
            self.dump_x("ffn00", tile, N)
            return
        if ksub == "ln":
            self.layer_norm(0, N)
            self.dump_x("ffn00", tile, N)
            return
        self.ffn(0, 0, N)
        self.dump_x("ffn00", tile, N)
        if self.stop < 2:
            return
        self.fw.mark("mixer0")
        self.mixer0(tile, N)
        self.dump_x("mix0", tile, N)
        self.fw.mark("ffn01")
        self.ffn(0, 1, N)
        self.dump_x("l0", tile, N)
        if self.layers > 1:
            self.fw.mark("ffn10")
            self.ffn(1, 0, N)
            self.fw.mark("mixer1")
            self.mixer1(tile, N)
            self.dump_x("mix1", tile, N)
            self.fw.mark("ffn11")
            self.ffn(1, 1, N)
        self.fw.mark("store")
        self.store_y(tile, N)
        self.fw.mark("end")


def _ts(h, out, in0, s1, s2, op0, op1=None):
    if op1 is None:
        return h.tensor_scalar(out=out, in0=in0, scalar1=s1, scalar2=None, op0=op0)
    return h.tensor_scalar(out=out, in0=in0, scalar1=s1, scalar2=s2, op0=op0, op1=op1)


def setup_mix0(self):
    fw, I = self.fw, self.I
    C = self.Bconst
    dve = lambda fn, r=(C,), w=(C,): fw.op("dve", fn, reads=list(r), writes=list(w))
    self.s5d = fw.sbuf([128, 4], F32, "s5d")
    self.bglu = fw.sbuf([128, 4], F32, "bglu")
    self.load_T(I["s5_d"].rearrange("o (c p) -> (o c) p", p=128), 4, self.s5d[:], C)
    self.load_T(I["s5_b_glu"].rearrange("o (c p) -> (o c) p", p=128), 4, self.bglu[:], C)
    self.negones = fw.sbuf([128, 128], BF16, "negones")
    self.nuincl = fw.sbuf([128, 128], BF16, "nuincl")
    fw.op("pool", lambda h: h.memset(self.negones[:], -1.0), writes=[C])
    fw.op("pool", lambda h: h.tensor_scalar(out=self.nuincl[:], in0=self.uincl[:], scalar1=-1.0, scalar2=None, op0=ALU.mult),
          reads=[C], writes=[C])
    L = LSUB
    pr = self.ma[:, 3712:3904].rearrange("p (i j) -> p i j", i=12)
    P = lambda i: pr[:, i, :]
    self.load_T(I["s5_a_re"].rearrange("o (j g) n -> (o j) (g n)", g=2), 16, P(0), C)
    self.load_T(I["s5_a_im"].rearrange("o (j g) n -> (o j) (g n)", g=2), 16, P(1), C)
    ld = I["s5_log_dt"].rearrange("o (j t) -> o j t", t=2)
    fw.dma("sp", pr[0:64, 2, :], ld[0:1, :, 0].broadcast_to([64, 16]), C, writes=[C], slow=True)
    fw.dma("sp", pr[64:128, 2, :], ld[0:1, :, 1].broadcast_to([64, 16]), C, writes=[C], slow=True)
    fw.op("act", lambda h: h.activation(out=P(2), in_=P(2), func=AF.Exp), reads=[C], writes=[C])
    dve(lambda h: h.tensor_tensor(out=P(3), in0=P(0), in1=P(2), op=ALU.mult))
    dve(lambda h: h.tensor_tensor(out=P(4), in0=P(1), in1=P(2), op=ALU.mult))
    self.s5r = fw.sbuf([128, 16], F32, "s5r")
    fw.op("act", lambda h: h.activation(out=self.s5r[:], in_=P(3), func=AF.Exp), reads=[C], writes=[C])
    dve(lambda h: _ts(h, P(5), P(4), 1.0 / (2 * math.pi), MAGIC, ALU.mult, ALU.add))
    dve(lambda h: _ts(h, P(5), P(5), MAGIC, None, ALU.subtract))
    dve(lambda h: h.scalar_tensor_tensor(out=P(5), in0=P(5), scalar=-2 * math.pi, in1=P(4), op0=ALU.mult, op1=ALU.add))
    dve(lambda h: _ts(h, P(5), P(5), 0.125, None, ALU.mult))
    dve(lambda h: h.tensor_tensor(out=P(6), in0=P(5), in1=P(5), op=ALU.mult))
    sc = [-1.0 / 6, 1.0 / 120, -1.0 / 5040, 1.0 / 362880]
    cc = [-0.5, 1.0 / 24, -1.0 / 720, 1.0 / 40320, -1.0 / 3628800]
    dve(lambda h: _ts(h, P(7), P(6), sc[3], sc[2], ALU.mult, ALU.add))
    for co in (sc[1], sc[0], 1.0):
        dve(lambda h: h.tensor_tensor(out=P(7), in0=P(7), in1=P(6), op=ALU.mult))
        dve(lambda h, co=co: _ts(h, P(7), P(7), co, None, ALU.add))
    dve(lambda h: h.tensor_tensor(out=P(7), in0=P(7), in1=P(5), op=ALU.mult))
    dve(lambda h: _ts(h, P(8), P(6), cc[4], cc[3], ALU.mult, ALU.add))
    for co in (cc[2], cc[1], cc[0], 1.0):
        dve(lambda h: h.tensor_tensor(out=P(8), in0=P(8), in1=P(6), op=ALU.mult))
        dve(lambda h, co=co: _ts(h, P(8), P(8), co, None, ALU.add))
    for _ in range(3):
        dve(lambda h: h.tensor_tensor(out=P(9), in0=P(8), in1=P(7), op=ALU.mult))
        dve(lambda h: h.tensor_tensor(out=P(8), in0=P(8), in1=P(8), op=ALU.mult))
        dve(lambda h: h.tensor_tensor(out=P(7), in0=P(7), in1=P(7), op=ALU.mult))
        dve(lambda h: h.tensor_tensor(out=P(8), in0=P(8), in1=P(7), op=ALU.subtract))
        dve(lambda h: _ts(h, P(7), P(9), 2.0, None, ALU.mult))
    self.Ct = fw.sbuf([128, 16, L + 1], F32, "Ct")
    self.St = fw.sbuf([128, 16, L + 1], F32, "St")
    Ct, St = self.Ct, self.St
    dve(lambda h: h.memset(Ct[:, :, 0:1], 1.0))
    dve(lambda h: h.memset(St[:, :, 0:1], 0.0))
    dve(lambda h: h.tensor_copy(out=Ct[:, :, 1:2], in_=P(8).unsqueeze(2)))
    dve(lambda h: h.tensor_copy(out=St[:, :, 1:2], in_=P(7).unsqueeze(2)))
    tmpA = self.ma[:, 0:2048].rearrange("p (j l) -> p j l", j=16)
    n = 2
    while n <= L:
        hn = n // 2
        ch, sh = Ct[:, :, hn:hn + 1], St[:, :, hn:hn + 1]
        dve(lambda h, ch=ch, sh=sh: h.tensor_tensor(out=P(9).unsqueeze(2), in0=ch, in1=sh, op=ALU.mult))
        dve(lambda h, ch=ch: h.tensor_tensor(out=P(10).unsqueeze(2), in0=ch, in1=ch, op=ALU.mult))
        dve(lambda h, sh=sh: h.tensor_tensor(out=P(11).unsqueeze(2), in0=sh, in1=sh, op=ALU.mult))
        dve(lambda h, n=n: h.tensor_tensor(out=Ct[:, :, n:n + 1], in0=P(10).unsqueeze(2), in1=P(11).unsqueeze(2), op=ALU.subtract))
        dve(lambda h, n=n: _ts(h, St[:, :, n:n + 1], P(9).unsqueeze(2), 2.0, None, ALU.mult))
        m = min(n, L + 1 - n)
        if m > 1:
            cn = bc(Ct[:, :, n:n + 1], [128, 16, m - 1])
            sn = bc(St[:, :, n:n + 1], [128, 16, m - 1])
            c0, s0 = Ct[:, :, 1:m], St[:, :, 1:m]
            tA = tmpA[:, :, 0:m - 1]
            dve(lambda h, c0=c0, cn=cn, tA=tA: h.tensor_tensor(out=tA, in0=c0, in1=cn, op=ALU.mult))
            dve(lambda h, s0=s0, sn=sn, n=n, m=m: h.tensor_tensor(out=Ct[:, :, n + 1:n + m], in0=s0, in1=sn, op=ALU.mult))
            dve(lambda h, tA=tA, n=n, m=m: h.tensor_tensor(out=Ct[:, :, n + 1:n + m], in0=tA, in1=Ct[:, :, n + 1:n + m], op=ALU.subtract))
            dve(lambda h, s0=s0, cn=cn, tA=tA: h.tensor_tensor(out=tA, in0=s0, in1=cn, op=ALU.mult))
            dve(lambda h, c0=c0, sn=sn, n=n, m=m: h.tensor_tensor(out=St[:, :, n + 1:n + m], in0=c0, in1=sn, op=ALU.mult))
            dve(lambda h, tA=tA, n=n, m=m: h.tensor_tensor(out=St[:, :, n + 1:n + m], in0=tA, in1=St[:, :, n + 1:n + m], op=ALU.add))
        n *= 2
    dve(lambda h: h.tensor_tensor(out=P(9), in0=self.s5r[:], in1=P(8), op=ALU.mult))
    dve(lambda h: h.tensor_tensor(out=P(10), in0=self.s5r[:], in1=P(7), op=ALU.mult))
    dve(lambda h: _ts(h, P(9), P(9), -1.0, None, ALU.add))
    dve(lambda h: h.tensor_tensor(out=P(2), in0=P(0), in1=P(0), op=ALU.mult))
    dve(lambda h: h.tensor_tensor(out=P(3), in0=P(1), in1=P(1), op=ALU.mult))
    dve(lambda h: h.tensor_tensor(out=P(2), in0=P(2), in1=P(3), op=ALU.add))
    dve(lambda h: h.reciprocal(out=P(2), in_=P(2)))
    dve(lambda h: h.tensor_tensor(out=P(3), in0=P(9), in1=P(0), op=ALU.mult))
    dve(lambda h: h.tensor_tensor(out=P(4), in0=P(10), in1=P(1), op=ALU.mult))
    dve(lambda h: h.tensor_tensor(out=P(3), in0=P(3), in1=P(4), op=ALU.add))
    dve(lambda h: h.tensor_tensor(out=P(5), in0=P(3), in1=P(2), op=ALU.mult))
    dve(lambda h: h.tensor_tensor(out=P(3), in0=P(10), in1=P(0), op=ALU.mult))
    dve(lambda h: h.tensor_tensor(out=P(4), in0=P(9), in1=P(1), op=ALU.mult))
    dve(lambda h: h.tensor_tensor(out=P(3), in0=P(3), in1=P(4), op=ALU.subtract))
    dve(lambda h: h.tensor_tensor(out=P(6), in0=P(3), in1=P(2), op=ALU.mult))
    braw = self.ma[:, 2048:3072].rearrange("p (t j q) -> p t j q", t=4, j=16)
    for t, nm in ((0, "s5_b_re"), (1, "s5_b_im")):
        fw.dma("sp", braw[:, t, :, :], I[nm][0].rearrange("g n q -> (g n) q").rearrange("(j p) q -> p j q", p=128), C, writes=[C])
    fre = bc(P(5).unsqueeze(2), [128, 16, 16])
    fim = bc(P(6).unsqueeze(2), [128, 16, 16])
    t3 = tmpA[:, :, 0:16]
    dve(lambda h: h.tensor_tensor(out=braw[:, 2], in0=braw[:, 0], in1=fre, op=ALU.mult))
    dve(lambda h: h.tensor_tensor(out=t3, in0=braw[:, 1], in1=fim, op=ALU.mult))
    dve(lambda h: h.tensor_tensor(out=braw[:, 2], in0=braw[:, 2], in1=t3, op=ALU.subtract))
    dve(lambda h: h.tensor_tensor(out=braw[:, 3], in0=braw[:, 1], in1=fre, op=ALU.mult))
    dve(lambda h: h.tensor_tensor(out=t3, in0=braw[:, 0], in1=fim, op=ALU.mult))
    dve(lambda h: h.tensor_tensor(out=braw[:, 3], in0=braw[:, 3], in1=t3, op=ALU.add))
    if self.dbg:
        o = self.dbg_out("s5par", [128, 12, 16])
        fw.dma("sp", o, pr, C, reads=[C])
        o = self.dbg_out("s5ct", [128, 16, L + 1])
        fw.dma("sp", o, Ct[:], C, reads=[C])
        o = self.dbg_out("s5st", [128, 16, L + 1])
        fw.dma("sp", o, St[:], C, reads=[C])
        o = self.dbg_out("s5r", [128, 16])
        fw.dma("sp", o, self.s5r[:], C, reads=[C])
        o = self.dbg_out("s5braw", [128, 4, 16, 16])
        fw.dma("sp", o, braw, C, reads=[C])
    self.LBc = fw.sbuf([128, 8, 128], BF16, "LBc")
    self.LCc = fw.sbuf([128, 16, 2, 32], BF16, "LCc")
    mj = self.ma[:, 3584:3712]
    Bmj = Buf("mj")
    for j in range(16):
        c, q = j // 4, j % 4
        for reim in range(2):
            fw.op("dve", lambda h: h.memset(mj[:], 0.0), writes=[Bmj])
            g0 = (2 * j) % 8
            fw.op("dve", lambda h, j=j, reim=reim, g0=g0: h.tensor_copy(out=mj[0:64, g0 * 16:g0 * 16 + 16], in_=braw[0:64, 2 + reim, j, :]),
                  reads=[C], writes=[Bmj])
            fw.op("dve", lambda h, j=j, reim=reim, g0=g0: h.tensor_copy(out=mj[64:128, (g0 + 1) * 16:(g0 + 1) * 16 + 16], in_=braw[64:128, 2 + reim, j, :]),
                  reads=[C], writes=[Bmj])
            fw.op("pe", lambda h: h.transpose(self.PS[7][:, 0:128], mj[:], self.ident[:]), reads=[Bmj, C], writes=[self.BP[7]])
            fw.op("dve", lambda h, c=c, q=q, reim=reim: h.tensor_copy(out=self.LBc[32 * q:32 * q + 32, c * 2 + reim, :], in_=self.PS[7][32 * q:32 * q + 32, 0:128]),
                  reads=[self.BP[7]], writes=[C])
    craw = self.ma[:, 3072:3584].rearrange("p (t c n) -> p t c n", t=2, c=4)
    for t, nm in ((0, "s5_c_re"), (1, "s5_c_im")):
        fw.dma("sp", craw[:, t, :, :], I[nm][0].rearrange("g p n -> (g p) n").rearrange("(c q) n -> q c n", q=128), C, writes=[C])
    e8 = fw.sbuf([128, 2, 8], F32, "e8")
    fw.op("pool", lambda h: h.memset(e8[:, 0, :], 1.0), reads=[C], writes=[C])
    fw.op("pool", lambda h: h.affine_select(out=e8[:, 0, :], in_=e8[:, 0, :], pattern=[[-16, 8]], compare_op=ALU.is_ge, fill=0.0,
                                            base=0, channel_multiplier=1), reads=[C], writes=[C])
    fw.op("pool", lambda h: h.affine_select(out=e8[:, 0, :], in_=e8[:, 0, :], pattern=[[16, 8]], compare_op=ALU.is_ge, fill=0.0,
                                            base=15, channel_multiplier=-1), reads=[C], writes=[C])
    fw.op("pool", lambda h: h.tensor_scalar(out=e8[:, 1, :], in0=e8[:, 0, :], scalar1=-1.0, scalar2=None, op0=ALU.mult), reads=[C], writes=[C])
    for j in range(16):
        c, q = j // 4, j % 4
        g0 = (2 * j) % 8
        for reim in range(2):
            for gl in range(2):
                fw.op("dve", lambda h, c=c, reim=reim, gl=gl, g0=g0: h.tensor_scalar(
                    out=mj[:, gl * 64:(gl + 1) * 64], in0=craw[:, reim, c, :], scalar1=e8[:, reim, g0 + gl:g0 + gl + 1], scalar2=None, op0=ALU.mult),
                    reads=[C], writes=[Bmj])
            fw.op("pe", lambda h: h.transpose(self.PS[7][:, 0:128], mj[:], self.ident[:]), reads=[Bmj, C], writes=[self.BP[7]])
            fw.op("dve", lambda h, j=j, q=q, reim=reim: h.tensor_copy(out=self.LCc[:, j, reim, :], in_=self.PS[7][:, 32 * q:32 * q + 32]),
                  reads=[self.BP[7]], writes=[C])
    self.G = fw.sbuf([128, 16, 2], F32, "s5G")
    self.BG = Buf("s5G", strict=True)
    self.HL = fw.sbuf([128, 2, 16], F32, "s5HL")
    self.BHL = Buf("s5HL")
    self.s5tmp = fw.sbuf([128, 4], F32, "s5tmp")
    self.Bs5tmp = Buf("s5tmp")
    self.crowb2 = [self.lnt[0:1, 2 + i, :].bitcast(BF16).rearrange("p (a n) -> p a n", a=2) for i in range(2)]
    self.Bcrowb2 = [Buf("crowb0"), Buf("crowb1")]
    self.Bcrow = Buf("crow")
    self.Bcrowb = Buf("crowb")


def s5_phase(self, tile, N):
    fw = self.fw
    kind, s, ti = tile
    C = self.Bconst
    PS, BP = self.PS, self.BP
    ma, mab, hbf = self.ma, self.mab, self.hbf
    uf = hbf[:, 0:2048].rearrange("p (c n) -> p c n", c=4)
    yf = hbf[:, 2048:4096].rearrange("p (c n) -> p c n", c=4)
    ub = self.hb[:, 16:20, :]
    s5out = mab[:, 2048:4096].rearrange("p (c n) -> p c n", c=4)
    T = [ma[:, 3072 + 512 * i:3072 + 512 * (i + 1)] for i in range(4)]
    BT = self.BT
    hre = [mab[:, 10240 + 1024 * i:10240 + 1024 * i + 512] for i in range(2)]
    him = [mab[:, 10240 + 1024 * i + 512:10240 + 1024 * (i + 1)] for i in range(2)]
    Bhh = self.Bhh
    Ct, St = self.Ct, self.St
    if kind == "p":
        nsub, L = 512 // LSUB, LSUB
    else:
        nsub, L = self.NS, DEC_SEQ
    dve = lambda fn, r, w: fw.op("dve", fn, reads=list(r), writes=list(w))
    last_tile = (kind == "s") or (ti == self.SEQ // 512 - 1)
    if kind == "p" and ti == 0:
        dve(lambda h: h.memset(self.G[:], 0.0), [], [self.BG])
    for j in range(16):
        c, q = j // 4, j % 4
        pa, pb = PS[0], PS[1]
        rhs = ub[32 * q:32 * q + 32, c, 0:N]
        self.mm(pa[:, 0:N], self.LBc[32 * q:32 * q + 32, c * 2 + 0, :], rhs, True, True, [C, self.Bub], [BP[0]], tile_position=(32 * q, 0))
        self.mm(pb[:, 0:N], self.LBc[32 * q:32 * q + 32, c * 2 + 1, :], rhs, True, True, [C, self.Bub], [BP[1]], tile_position=(32 * q, 0))
        v3 = lambda ap: ap[:, 0:N].rearrange("p (s l) -> p s l", s=nsub)
        ct = bc(Ct[:, j:j + 1, 0:L], [128, nsub, L])
        st = bc(St[:, j:j + 1, 0:L], [128, nsub, L])
        A, Bb, Dd, E = T
        dve(lambda h: h.tensor_tensor(out=v3(A), in0=v3(pa), in1=ct, op=ALU.mult), [BP[0], C], [BT[0]])
        dve(lambda h: h.tensor_tensor(out=v3(Bb), in0=v3(pb), in1=st, op=ALU.mult), [BP[1], C], [BT[1]])
        dve(lambda h: h.tensor_tensor(out=A[:, 0:N], in0=A[:, 0:N], in1=Bb[:, 0:N], op=ALU.add), [BT[0], BT[1]], [BT[0]])
        dve(lambda h: h.tensor_tensor(out=v3(Dd), in0=v3(pb), in1=ct, op=ALU.mult), [BP[1], C], [BT[2]])
        dve(lambda h: h.tensor_tensor(out=v3(Bb), in0=v3(pa), in1=st, op=ALU.mult), [BP[0], C], [BT[1]])
        dve(lambda h: h.tensor_tensor(out=Dd[:, 0:N], in0=Dd[:, 0:N], in1=Bb[:, 0:N], op=ALU.subtract), [BT[2], BT[1]], [BT[2]])
        if j == 5 and kind == "p" and self.dbg:
            dve(lambda h: h.tensor_copy(out=E[:, 0:N], in_=pa[:, 0:N]), [BP[0]], [BT[3]])
            self.dbg_sb("bu_re", E[:, 0:N], BT[3], [128, N])
            dve(lambda h: h.tensor_copy(out=self.sg[0][:, 0:256].rearrange("p (a b) -> p a b", a=2), in_=self.LBc[:, 2:4, :]), [C], [self.Bsg[0]])
            self.dbg_sb("lbc", self.sg[0][:, 0:256], self.Bsg[0], [128, 256])
            self.dbg_sb("ub", ub[:, :, 0:N].bitcast(mybir.dt.uint16), self.Bub, [128, 4, N]) if False else None
        if j == 5 and kind == "p":
            self.dbg_sb("ut_re", A[:, 0:N], BT[0], [128, N])
            self.dbg_sb("ut_im", Dd[:, 0:N], BT[2], [128, N])
            self.dbg_sb("uf", uf[:, :, 0:N], self.Buf_uf, [128, 4, N])
        rr = bc(self.s5r[:, j:j + 1], [128, L])
        for sub in range(nsub):
            sl = slice(sub * L, (sub + 1) * L)
            if kind == "s":
                gi = self.Gs[:, sub, j, :]
                BGi = self.BGs
            else:
                gi = self.G[:, j, :]
                BGi = self.BG
            dve(lambda h, sl=sl, gi=gi: h.tensor_tensor_scan(out=Bb[:, sl], data0=rr, data1=A[:, sl], initial=gi[:, 0:1], op0=ALU.mult, op1=ALU.add),
                [BT[0], BGi, C], [BT[1]])
            dve(lambda h, sl=sl, gi=gi: h.tensor_tensor_scan(out=E[:, sl], data0=rr, data1=Dd[:, sl], initial=gi[:, 1:2], op0=ALU.mult, op1=ALU.add),
                [BT[2], BGi, C], [BT[3]])
            e0 = (sub + 1) * L - 1
            gre_l, gim_l = Bb[:, e0:e0 + 1], E[:, e0:e0 + 1]
            tmp = self.s5tmp
            if kind == "p":
                cL, sL = Ct[:, j, L:L + 1], St[:, j, L:L + 1]
                dve(lambda h, gim_l=gim_l, sL=sL: h.tensor_tensor(out=tmp[:, 0:1], in0=gim_l, in1=sL, op=ALU.mult), [BT[3], C], [self.Bs5tmp])
                dve(lambda h, gre_l=gre_l, sL=sL: h.tensor_tensor(out=tmp[:, 1:2], in0=gre_l, in1=sL, op=ALU.mult), [BT[1], C], [self.Bs5tmp])
                dve(lambda h, gre_l=gre_l, cL=cL, j=j: h.scalar_tensor_tensor(out=self.G[:, j, 0:1], in0=gre_l, scalar=cL, in1=tmp[:, 0:1], op0=ALU.mult, op1=ALU.subtract),
                    [BT[1], C, self.Bs5tmp], [self.BG])
                dve(lambda h, gim_l=gim_l, cL=cL, j=j: h.scalar_tensor_tensor(out=self.G[:, j, 1:2], in0=gim_l, scalar=cL, in1=tmp[:, 1:2], op0=ALU.mult, op1=ALU.add),
                    [BT[3], C, self.Bs5tmp], [self.BG])
            if last_tile and (kind == "s" or sub == nsub - 1):
                c1, s1 = Ct[:, j, L - 1:L], St[:, j, L - 1:L]
                if kind == "s":
                    ore, oim = self.HLs[:, sub, 0, j:j + 1], self.HLs[:, sub, 1, j:j + 1]
                else:
                    ore, oim = self.HL[:, 0, j:j + 1], self.HL[:, 1, j:j + 1]
                dve(lambda h, gim_l=gim_l, s1=s1: h.tensor_tensor(out=tmp[:, 2:3], in0=gim_l, in1=s1, op=ALU.mult), [BT[3], C], [self.Bs5tmp])
                dve(lambda h, gre_l=gre_l, s1=s1: h.tensor_tensor(out=tmp[:, 3:4], in0=gre_l, in1=s1, op=ALU.mult), [BT[1], C], [self.Bs5tmp])
                dve(lambda h, gre_l=gre_l, c1=c1, ore=ore: h.scalar_tensor_tensor(out=ore, in0=gre_l, scalar=c1, in1=tmp[:, 2:3], op0=ALU.mult, op1=ALU.subtract),
                    [BT[1], C, self.Bs5tmp], [self.BHL])
                dve(lambda h, gim_l=gim_l, c1=c1, oim=oim: h.scalar_tensor_tensor(out=oim, in0=gim_l, scalar=c1, in1=tmp[:, 3:4], op0=ALU.mult, op1=ALU.add),
                    [BT[3], C, self.Bs5tmp], [self.BHL])
        if j == 5 and kind == "p":
            self.dbg_sb("g_re", Bb[:, 0:N], BT[1], [128, N])
            self.dbg_sb("g_im", E[:, 0:N], BT[3], [128, N])
        hr, hi = hre[j % 2], him[j % 2]
        Bh2 = Bhh[j % 2]
        dve(lambda h: h.tensor_tensor(out=v3(A), in0=v3(Bb), in1=ct, op=ALU.mult), [BT[1], C], [BT[0]])
        dve(lambda h: h.tensor_tensor(out=v3(Dd), in0=v3(E), in1=st, op=ALU.mult), [BT[3], C], [BT[2]])
        dve(lambda h, hr=hr: h.tensor_tensor(out=hr[:, 0:N], in0=A[:, 0:N], in1=Dd[:, 0:N], op=ALU.subtract), [BT[0], BT[2]], [Bh2])
        dve(lambda h: h.tensor_tensor(out=v3(A), in0=v3(E), in1=ct, op=ALU.mult), [BT[3], C], [BT[0]])
        dve(lambda h: h.tensor_tensor(out=v3(Dd), in0=v3(Bb), in1=st, op=ALU.mult), [BT[1], C], [BT[2]])
        dve(lambda h, hi=hi: h.tensor_tensor(out=hi[:, 0:N], in0=A[:, 0:N], in1=Dd[:, 0:N], op=ALU.add), [BT[0], BT[2]], [Bh2])
        py, Bpy = PS[2 + c % 2], BP[2 + c % 2]
        self.mm(py[32 * q:32 * q + 32, 0:N], self.LCc[:, j, 0, :], hr[:, 0:N], True, False, [C, Bh2], [Bpy], tile_position=(0, 32 * q))
        self.mm(py[32 * q:32 * q + 32, 0:N], self.LCc[:, j, 1, :], hi[:, 0:N], False, True, [C, Bh2], [Bpy], tile_position=(0, 32 * q))
        if q == 3:
            dve(lambda h, c=c, py=py: h.scalar_tensor_tensor(out=yf[:, c, 0:N], in0=uf[:, c, 0:N], scalar=self.s5d[:, c:c + 1], in1=py[:, 0:N],
                                                            op0=ALU.mult, op1=ALU.add), [self.Buf_uf, Bpy, C], [self.Buf_yf])
    if last_tile:
        nseq = self.NS if kind == "s" else 1
        for sq in range(nseq):
            for reim in range(2):
                src = self.HLs[:, sq, reim, :] if kind == "s" else self.HL[:, reim, :]
                fw.op("pe", lambda h, src=src: h.transpose(PS[7][0:16, 0:128], src, self.ident[:]), reads=[self.BHL, C], writes=[BP[7]])
                fw.op("dve", lambda h: h.tensor_copy(out=self.stg[0:16, :], in_=PS[7][0:16, 0:128]), reads=[BP[7]], writes=[self.Bstg])
                nm = ("s5_re_" if reim == 0 else "s5_im_") + ("sample" if kind == "s" else "prompt")
                seq = sq if kind == "s" else s
                dst = self.O[nm][0, seq].rearrange("(j g) n -> j (g n)", g=2)
                fw.dma("sp", dst, self.stg[0:16, :], self.Bstg, reads=[self.Bstg])
    tmpf = ma[:, 3072:5120].rearrange("p (c n) -> p c n", c=4)
    yv, tv = yf[:, :, 0:N], tmpf[:, :, 0:N]
    gb = ub
    dve(lambda h: h.tensor_tensor(out=tv, in0=yv, in1=yv, op=ALU.mult), [self.Buf_yf], BT)
    dve(lambda h: _ts(h, tv, tv, 0.044715, 1.0, ALU.mult, ALU.add), BT, BT)
    dve(lambda h: h.tensor_tensor(out=tv, in0=tv, in1=yv, op=ALU.mult), BT + [self.Buf_yf], BT)
    fw.op("act", lambda h: h.activation(out=tv, in_=tv, func=AF.Sigmoid, scale=2.0 * math.sqrt(2.0 / math.pi)), reads=BT, writes=BT)
    dve(lambda h: h.tensor_tensor(out=yv, in0=yv, in1=tv, op=ALU.mult), BT + [self.Buf_yf], [self.Buf_yf])
    fw.op("pool", lambda h: h.tensor_copy(out=gb[:, :, 0:N], in_=yv), reads=[self.Buf_yf], writes=[self.Bub])
    wglu = self.S["s5_w_glu"][0].rearrange("(k p) f -> p k f", p=128)
    t, B = self.get_slab([(lambda t: t[:, 0:2048].rearrange("p (k f) -> p k f", k=4), wglu, "s5_w_glu")])
    tg = t[:, 0:2048].rearrange("p (k f) -> p k f", k=4)
    for oc in range(4):
        pg, Bg = PS[oc % 2], BP[oc % 2]
        for k in range(4):
            self.mm(pg[:, 0:N], tg[:, k, oc * 128:(oc + 1) * 128], gb[:, k, 0:N], k == 0, k == 3, [B, self.Bub], [Bg])
        sg, Bs = self.sg[oc % 2], self.Bsg[oc % 2]
        fw.op("act", lambda h, sg=sg, pg=pg, oc=oc: h.activation(out=sg[:, 0:N], in_=pg[:, 0:N], func=AF.Sigmoid, bias=self.bglu[:, oc:oc + 1]),
              reads=[Bg, C], writes=[Bs])
        dve(lambda h, sg=sg, oc=oc: h.tensor_tensor(out=s5out[:, oc, 0:N], in0=yf[:, oc, 0:N], in1=sg[:, 0:N], op=ALU.mult),
            [Bs, self.Buf_yf], [self.Bs5out])


def attn_head_loop(self, N, qcols, blocks, diag_base):
    fw = self.fw
    C = self.Bconst
    PS, BP = self.PS, self.BP
    ma, mab = self.ma, self.mab
    QT = mab[:, 0:2048].rearrange("p (c n) -> p c n", c=4)
    attno = mab[:, 4096:6144].rearrange("p (c n) -> p c n", c=4)
    e_ = [ma[:, 3072 + 512 * i:3072 + 512 * (i + 1)] for i in range(2)]
    spb = [mab[:, 8192 + 512 * i:8192 + 512 * (i + 1)] for i in range(2)]
    wb = [mab[:, 9216 + 512 * i:9216 + 512 * (i + 1)] for i in range(2)]
    Be, Bsp, Bw = self.Be, self.Bsp, self.Bw
    crb = [self.lnt[0:33, 2 + i, :].bitcast(BF16)[:, 0:512] for i in range(2)]
    Bcr = self.Bcrowb2
    q0, q1 = qcols
    nblk = len(blocks)
    for hp in range(4):
        c = hp
        po, Bpo = PS[7], BP[7]
        pst, Bpt = PS[6], BP[6]
        pss = [PS[0], PS[1]]
        Bps = [BP[0], BP[1]]

        def qk(bi):
            k0, kr, vblk, jj = blocks[bi]
            Bkt = self.BKT[min(k0 // 512, 7)]
            for x in range(2):
                hb_ = 64 * x
                self.mm(pss[x][0:kr, 0:N], self.KT[hb_:hb_ + 64, c, k0:k0 + kr], QT[hb_:hb_ + 64, c, q0:q1], True, True,
                        [Bkt, self.BQT], [Bps[x]], tile_position=(hb_, 0))

        qk(0)
        for bi, (k0, kr, vblk, jj) in enumerate(blocks):
            Bvt = self.BVT[min(vblk // 4, 7)]
            psc = [PS[2 + 2 * (bi % 2)], PS[3 + 2 * (bi % 2)]]
            Bpc = [BP[2 + 2 * (bi % 2)], BP[3 + 2 * (bi % 2)]]
            for x in range(2):
                fw.op("act", lambda h, x=x: h.activation(out=e_[x][0:kr, 0:N], in_=pss[x][0:kr, 0:N], func=AF.Exp), reads=[Bps[x]], writes=[Be[x]])
                if jj is not None:
                    fw.op("pool", lambda h, x=x: h.affine_select(out=e_[x][0:kr, 0:N], in_=e_[x][0:kr, 0:N], pattern=[[1, N]], compare_op=ALU.is_gt,
                                                                 fill=0.0, base=-128 * jj, channel_multiplier=-1), reads=[Be[x]], writes=[Be[x]])
                fw.op("act", lambda h, x=x: h.activation(out=spb[x][0:kr, 0:N], in_=e_[x][0:kr, 0:N], func=AF.Ln, bias=1.0), reads=[Be[x]], writes=[Bsp[x]])
                self.mm(psc[x][0:kr, 0:N], self.nuincl[0:kr, 0:kr], spb[x][0:kr, 0:N], True, bi == 0, [C, Bsp[x]], [Bpc[x]])
                if bi > 0:
                    cbp, Bcbp = crb[(bi - 1) % 2], Bcr[(bi - 1) % 2]
                    self.mm(psc[x][0:kr, 0:N], self.negones[32 * x:32 * x + 1, 0:kr], cbp[32 * x:32 * x + 1, 0:N], False, True,
                            [C, Bcbp], [Bpc[x]], tile_position=(32 * x, 0))
            if bi < nblk - 1:
                for x in range(2):
                    self.mm(pst[32 * x:32 * x + 1, 0:N], self.ones1[0:kr, 0:1], spb[x][0:kr, 0:N], bi == 0, bi == nblk - 2,
                            [C, Bsp[x]], [Bpt], tile_position=(0, 32 * x))
                cb_, Bcb = crb[bi % 2], Bcr[bi % 2]
                fw.op("dve", lambda h, cb_=cb_: h.tensor_copy(out=cb_[0:33, 0:N], in_=pst[0:33, 0:N]), reads=[Bpt], writes=[Bcb])
                qk(bi + 1)
            for x in range(2):
                fw.op("act", lambda h, x=x: h.activation(out=psc[x][0:kr, 0:N], in_=psc[x][0:kr, 0:N], func=AF.Exp), reads=[Bpc[x]], writes=[Bpc[x]])
                fw.op("dve", lambda h, x=x: h.tensor_tensor(out=wb[x][0:kr, 0:N], in0=psc[x][0:kr, 0:N], in1=e_[x][0:kr, 0:N], op=ALU.mult),
                      reads=[Bpc[x], Be[x]], writes=[Bw[x]])
            for x in range(2):
                hd = 2 * hp + x
                self.mm(po[64 * x:64 * x + 64, 0:N], self.VT[0:kr, vblk, hd * 64:(hd + 1) * 64], wb[x][0:kr, 0:N], bi == 0, bi == nblk - 1,
                        [Bvt, Bw[x]], [Bpo], tile_position=(0, 64 * x))
        fw.op("dve", lambda h, c=c: h.tensor_copy(out=attno[:, c, q0:q1], in_=po[:, 0:N]), reads=[Bpo], writes=[self.Battno])


def attn_sample_loop(self, qcols, blocks):
    fw = self.fw
    C = self.Bconst
    PS, BP = self.PS, self.BP
    ma, mab = self.ma, self.mab
    QT = mab[:, 0:2048].rearrange("p (c n) -> p c n", c=4)
    attno = mab[:, 4096:6144].rearrange("p (c n) -> p c n", c=4)
    e_ = [ma[:, 3072 + 512 * i:3072 + 512 * (i + 1)] for i in range(2)]
    spb = [mab[:, 8192 + 512 * i:8192 + 512 * (i + 1)] for i in range(2)]
    wb = [mab[:, 9216 + 512 * i:9216 + 512 * (i + 1)] for i in range(2)]
    Be, Bsp, Bw = self.Be, self.Bsp, self.Bw
    crb = [self.lnt[0:1, 2 + i, :].bitcast(BF16)[:, 0:512] for i in range(2)]
    Bcr = self.Bcrowb2
    q0, q1 = qcols
    NQ = q1 - q0
    W = 8 * NQ
    nblk = len(blocks)
    po, Bpo = PS[7], BP[7]
    pst, Bpt = PS[6], BP[6]

    def qk(bi):
        k0, kr, vblk, jj = blocks[bi]
        Bkt = self.BKT[min(k0 // 512, 7)]
        for hd in range(8):
            c, xx = hd // 2, hd % 2
            hb_ = 64 * xx
            pss, Bps = PS[2 * (bi % 2) + xx], BP[2 * (bi % 2) + xx]
            self.mm(pss[0:kr, c * NQ:(c + 1) * NQ], self.KT[hb_:hb_ + 64, c, k0:k0 + kr], QT[hb_:hb_ + 64, c, q0:q1], True, True,
                    [Bkt, self.BQT], [Bps], tile_position=(hb_, 0))

    qk(0)
    for bi, (k0, kr, vblk, jj) in enumerate(blocks):
        x = bi % 2
        Bvt = self.BVT[min(vblk // 4, 7)]
        psc, Bpc = PS[4 + x], BP[4 + x]
        H4 = 4 * NQ
        for xx in range(2):
            pss, Bps = PS[2 * x + xx], BP[2 * x + xx]
            fw.op("act", lambda h, xx=xx, pss=pss: h.activation(out=e_[x][0:kr, xx * H4:(xx + 1) * H4], in_=pss[0:kr, 0:H4], func=AF.Exp),
                  reads=[Bps], writes=[Be[x]])
        if jj is not None:
            fw.op("pool", lambda h: h.affine_select(out=e_[x][0:kr, 0:W], in_=e_[x][0:kr, 0:W], pattern=[[0, 8], [1, NQ]], compare_op=ALU.is_gt,
                                                    fill=0.0, base=-128 * jj, channel_multiplier=-1), reads=[Be[x]], writes=[Be[x]])
        fw.op("act", lambda h: h.activation(out=spb[x][0:kr, 0:W], in_=e_[x][0:kr, 0:W], func=AF.Ln, bias=1.0), reads=[Be[x]], writes=[Bsp[x]])
        self.mm(psc[0:kr, 0:W], self.nuincl[0:kr, 0:kr], spb[x][0:kr, 0:W], True, bi == 0, [C, Bsp[x]], [Bpc])
        if bi > 0:
            self.mm(psc[0:kr, 0:W], self.negones[0:1, 0:kr], crb[(bi - 1) % 2][0:1, 0:W], False, True, [C, Bcr[(bi - 1) % 2]], [Bpc])
        if bi < nblk - 1:
            self.mm(pst[0:1, 0:W], self.ones1[0:kr, 0:1], spb[x][0:kr, 0:W], bi == 0, bi == nblk - 2, [C, Bsp[x]], [Bpt])
            fw.op("dve", lambda h: h.tensor_copy(out=crb[x][0:1, 0:W], in_=pst[0:1, 0:W]), reads=[Bpt], writes=[Bcr[x]])
            qk(bi + 1)
        fw.op("act", lambda h: h.activation(out=psc[0:kr, 0:W], in_=psc[0:kr, 0:W], func=AF.Exp), reads=[Bpc], writes=[Bpc])
        fw.op("dve", lambda h: h.tensor_tensor(out=wb[x][0:kr, 0:W], in0=psc[0:kr, 0:W], in1=e_[x][0:kr, 0:W], op=ALU.mult),
              reads=[Bpc, Be[x]], writes=[Bw[x]])
        for hd in range(8):
            hp, xx = hd // 2, hd % 2
            slot = xx * 4 + hp
            self.mm(po[64 * xx:64 * xx + 64, hp * NQ:(hp + 1) * NQ], self.VT[0:kr, vblk, hd * 64:(hd + 1) * 64], wb[x][0:kr, slot * NQ:(slot + 1) * NQ],
                    bi == 0 and hp == 0, bi == nblk - 1, [Bvt, Bw[x]], [Bpo], tile_position=(0, 64 * xx), skip_group_check=True)
    fw.op("dve", lambda h: h.tensor_copy(out=attno[:, :, q0:q1], in_=po[:, 0:4 * NQ].rearrange("p (c q) -> p c q", c=4)), reads=[Bpo], writes=[self.Battno])


def mixer0(self, tile, N):
    fw = self.fw
    kind, s, ti = tile
    C = self.Bconst
    PS, BP = self.PS, self.BP
    ma, mab, hbf = self.ma, self.mab, self.hbf
    xf, xb, Bxf, Bxb = self.xf, self.xb, self.Bxf, self.Bxb
    if not hasattr(self, "Buf_uf"):
        self.Buf_uf, self.Buf_yf, self.Bub, self.Bkvst = Buf("uf"), Buf("yf"), Buf("ub"), Buf("kvst")
        self.BQT, self.Bs5out, self.Battno = Buf("QT"), Buf("s5out"), Buf("attno")
        self.BT = [Buf("T%d" % i) for i in range(4)]
        self.Bhh = [Buf("hh%d" % i) for i in range(2)]
        self.Be = [Buf("e%d" % i) for i in range(2)]
        self.Bsp = [Buf("sp%d" % i) for i in range(2)]
        self.Bw = [Buf("w%d" % i) for i in range(2)]
        self.Bknew = Buf("knew")
    fw.barrier()
    uf = hbf[:, 0:2048].rearrange("p (c n) -> p c n", c=4)
    ub = self.hb[:, 16:20, :]
    kvst = hbf[:, 5120:5632]
    QT = mab[:, 0:2048].rearrange("p (c n) -> p c n", c=4)
    s5out = mab[:, 2048:4096].rearrange("p (c n) -> p c n", c=4)
    attno = mab[:, 4096:6144].rearrange("p (c n) -> p c n", c=4)
    win = self.S["mix0_w_in"][0].rearrange("(k p) f -> p k f", p=128)
    pos = ti * 512

    def slab_in(ci):
        t, B = self.get_slab([(lambda t: t[:, :].rearrange("p (k f) -> p k f", k=8), win[:, :, 512 * ci:512 * ci + 512], "mix0_w_in")])
        return t[:, :].rearrange("p (k f) -> p k f", k=8), B

    tw, B = slab_in(0)
    for oc in range(4):
        pp, Bp = PS[oc % 2], BP[oc % 2]
        for k in range(NCH):
            self.mm(pp[:, 0:N], tw[:, k, oc * 128:(oc + 1) * 128], xb[:, k, 0:N], k == 0, k == NCH - 1, [B, Bxb[k]], [Bp])
        fw.op("act", lambda h, pp=pp, oc=oc: h.activation(out=uf[:, oc, 0:N], in_=pp[:, 0:N], func=AF.Copy), reads=[Bp], writes=[self.Buf_uf])
        fw.op("dve", lambda h, pp=pp, oc=oc: h.tensor_copy(out=ub[:, oc, 0:N], in_=pp[:, 0:N]), reads=[Bp], writes=[self.Bub])
    tw, B = slab_in(1)
    for oc in range(4):
        pp, Bp = PS[2 + oc % 2], BP[2 + oc % 2]
        for k in range(NCH):
            self.mm(pp[:, 0:N], tw[:, k, oc * 128:(oc + 1) * 128], xb[:, k, 0:N], k == 0, k == NCH - 1, [B, Bxb[k]], [Bp])
        fw.op("act", lambda h, pp=pp, oc=oc: h.activation(out=QT[:, oc, 0:N], in_=pp[:, 0:N], func=AF.Copy, scale=0.125), reads=[Bp], writes=[self.BQT])
    tw, B = slab_in(2)
    for oc in range(4):
        pp, Bp = PS[oc % 2], BP[oc % 2]
        for k in range(NCH):
            self.mm(pp[:, 0:N], tw[:, k, oc * 128:(oc + 1) * 128], xb[:, k, 0:N], k == 0, k == NCH - 1, [B, Bxb[k]], [Bp])
        if kind == "p":
            fw.op("act", lambda h, pp=pp, oc=oc: h.activation(out=self.KT[:, oc, pos:pos + N], in_=pp[:, 0:N], func=AF.Copy), reads=[Bp], writes=[self.BKT[ti]])
        else:
            fw.op("act", lambda h, pp=pp, oc=oc: h.activation(out=self.knew[:, oc, 0:N], in_=pp[:, 0:N], func=AF.Copy), reads=[Bp], writes=[self.Bknew])
    twv, Bv = slab_in(3)
    if kind == "p":
        segs = [(b * 128, 128, s, pos + b * 128) for b in range(4)]
    else:
        segs = [(sq * DEC_SEQ, DEC_SEQ, sq, 0) for sq in range(self.NS)]
    sfx = "prompt" if kind == "p" else "sample"
    for (c0, rows, seq, sp_) in segs:
        for which, (tws, Bs_) in enumerate(((tw, B), (twv, Bv))):
            pp, Bp = PS[2 + which], BP[2 + which]
            for k in range(NCH):
                self.mm(pp[0:rows, :], xb[:, k, c0:c0 + rows], tws[:, k, :], k == 0, k == NCH - 1, [Bs_, Bxb[k]], [Bp])
            fw.op("act", lambda h, pp=pp, rows=rows: h.activation(out=kvst[0:rows, :], in_=pp[0:rows, :], func=AF.Copy), reads=[Bp], writes=[self.Bkvst])
            if which == 1:
                if kind == "p":
                    blk = sp_ // 128
                    fw.op("dve", lambda h, pp=pp, blk=blk: h.tensor_copy(out=self.VT[:, blk, :], in_=pp[:, :]), reads=[Bp], writes=[self.BVT[blk // 4]])
                else:
                    fw.op("dve", lambda h, pp=pp, seq=seq, rows=rows: h.tensor_copy(out=self.vnew[0:rows, seq, :], in_=pp[0:rows, :]), reads=[Bp], writes=[self.Bknew])
            nm = ("sb_k_" if which == 0 else "sb_v_") + sfx
            dst = self.O[nm][0, seq, sp_:sp_ + rows].rearrange("t h d -> t (h d)")
            fw.dma("sp", dst, kvst[0:rows, :], self.Bkvst, reads=[self.Bkvst])
    if self.stop < 3:
        return
    if kind == "s":
        self.s5_sample_init()
    fw.mark("s5")
    s5_phase(self, tile, N)
    fw.barrier()
    fw.mark("attn")
    if self.stop < 4:
        return
    if kind == "p":
        nkb = 4 * (ti + 1)
        blocks = []
        for kb in reversed(range(nkb)):
            jj = kb - 4 * ti if kb >= 4 * ti else None
            blocks.append((kb * 128, 128, kb, jj))
        attn_head_loop(self, N, (0, N), blocks, 0)
    else:
        for sq in range(self.NS):
            self.load_cache(sq)
            blocks = [(PAST, DEC_SEQ, 16, 0)] + [(kb * 128, 128, kb, None) for kb in reversed(range(PAST // 128))]
            attn_sample_loop(self, (sq * DEC_SEQ, (sq + 1) * DEC_SEQ), blocks)
    fw.mark("out0")
    wout = self.S["mix0_w_out"][0].rearrange("(k p) f -> p k f", p=128)
    for half in range(2):
        t, B = self.get_slab([(lambda t: t[:, :].rearrange("p (k f) -> p k f", k=8), wout[:, :, 512 * half:512 * half + 512], "mix0_w_out")])
        tw = t[:, :].rearrange("p (k f) -> p k f", k=8)
        for oc in range(4):
            o = 4 * half + oc
            pp, Bp = PS[o % 2], BP[o % 2]
            for k in range(NCH):
                rhs = s5out[:, k, 0:N] if k < 4 else attno[:, k - 4, 0:N]
                Br = self.Bs5out if k < 4 else self.Battno
                self.mm(pp[:, 0:N], tw[:, k, oc * 128:(oc + 1) * 128], rhs, k == 0, k == NCH - 1, [B, Br], [Bp])
            fw.op("dve", lambda h, o=o, pp=pp: h.scalar_tensor_tensor(out=xf[:, o, 0:N], in0=xf[:, o, 0:N], scalar=ALPHA, in1=pp[:, 0:N],
                                                                     op0=ALU.mult, op1=ALU.add), reads=[Bxf[o], Bp], writes=[Bxf[o]])
    fw.barrier([self.Bkvst])
    self.layer_norm(1, N)


def s5_sample_init(self):
    fw, I = self.fw, self.I
    C = self.Bconst
    NS = self.NS
    if not hasattr(self, "Gs"):
        self.Gs = fw.sbuf([128, NS, 16, 2], F32, "Gs")
        self.BGs = Buf("Gs", strict=True)
        self.HLs = fw.sbuf([128, NS, 2, 16], F32, "HLs")
        self.h0 = fw.sbuf([128, 2, NS, 16], F32, "h0")
        self.h0t = fw.sbuf([128, NS, 16], F32, "h0t")
    for reim, nm in ((0, "state_s5_re"), (1, "state_s5_im")):
        self.load_T(I[nm][0].rearrange("s (j g) n -> (s j) (g n)", g=2), NS * 16, self.h0[:, reim].rearrange("p s j -> p (s j)"), self.BGs)
    c1 = bc(self.Ct[:, :, 1:2].rearrange("p j o -> p o j"), [128, NS, 16])
    s1 = bc(self.St[:, :, 1:2].rearrange("p j o -> p o j"), [128, NS, 16])
    hre, him = self.h0[:, 0], self.h0[:, 1]
    B = self.BGs
    dve = lambda fn: fw.op("dve", fn, reads=[B, C], writes=[B])
    dve(lambda h: h.tensor_tensor(out=self.h0t[:], in0=him, in1=s1, op=ALU.mult))
    dve(lambda h: h.tensor_tensor(out=self.Gs[:, :, :, 0], in0=hre, in1=c1, op=ALU.mult))
    dve(lambda h: h.tensor_tensor(out=self.Gs[:, :, :, 0], in0=self.Gs[:, :, :, 0], in1=self.h0t[:], op=ALU.subtract))
    dve(lambda h: h.tensor_tensor(out=self.h0t[:], in0=hre, in1=s1, op=ALU.mult))
    dve(lambda h: h.tensor_tensor(out=self.Gs[:, :, :, 1], in0=him, in1=c1, op=ALU.mult))
    dve(lambda h: h.tensor_tensor(out=self.Gs[:, :, :, 1], in0=self.Gs[:, :, :, 1], in1=self.h0t[:], op=ALU.add))


def load_cache(self, sq):
    fw, I = self.fw, self.I
    C = self.Bconst
    PS, BP = self.PS, self.BP
    stg = self.hbf[:, 0:4096].rearrange("p (b f) -> p b f", b=8)
    Bst = [self.Buf_uf, self.Buf_yf]
    kc = I["cache_sb_k"][0, sq].rearrange("(b p) h d -> p b (h d)", p=128)
    vc = I["cache_sb_v"][0, sq].rearrange("(b p) h d -> p b (h d)", p=128)
    fw.dma("pool", self.VT[:, 0:16, :], vc, self.BVT[0], writes=self.BVT[0:4])
    for half in range(2):
        for q4 in range(2):
            b0 = half * 8 + q4 * 4
            fw.dma("sp", stg[:, q4 * 4:q4 * 4 + 4, :], kc[:, b0:b0 + 4, :], Bst[q4], writes=[Bst[q4]])
        for c in range(4):
            for g in range(2):
                pp, Bp = PS[(2 * c + g) % 4], BP[(2 * c + g) % 4]
                for bb in range(4):
                    b = g * 4 + bb
                    fw.op("pe", lambda h, pp=pp, b=b, bb=bb, c=c: h.transpose(pp[:, bb * 128:(bb + 1) * 128], stg[:, b, c * 128:(c + 1) * 128], self.ident[:]),
                          reads=[Bst[g], C], writes=[Bp])
                col0 = (half * 8 + g * 4) * 128
                eng = "act" if (c + g) % 2 == 0 else "dve"
                if eng == "act":
                    fw.op("act", lambda h, pp=pp, c=c, col0=col0: h.activation(out=self.KT[:, c, col0:col0 + 512], in_=pp[:, :], func=AF.Copy),
                          reads=[Bp], writes=[self.BKT[col0 // 512]])
                else:
                    fw.op("dve", lambda h, pp=pp, c=c, col0=col0: h.tensor_copy(out=self.KT[:, c, col0:col0 + 512], in_=pp[:, :]),
                          reads=[Bp], writes=[self.BKT[col0 // 512]])
    fw.op("dve", lambda h: h.tensor_copy(out=self.KT[:, :, PAST:PAST + DEC_SEQ], in_=self.knew[:, :, sq * DEC_SEQ:(sq + 1) * DEC_SEQ]),
          reads=[self.Bknew], writes=[self.BKT[4]])
    fw.op("dve", lambda h: h.tensor_copy(out=self.VT[0:DEC_SEQ, 16, :], in_=self.vnew[0:DEC_SEQ, sq, :]), reads=[self.Bknew], writes=[self.BVT[4]])

def setup_ssd(self):
    fw, I = self.fw, self.I
    C = self.Bconst
    self.cw = fw.sbuf([128, 96], F32, "cw")
    self.cb = fw.sbuf([128, 24], F32, "cb")
    self.ng = fw.sbuf([128, 16], F32, "ng")
    self.load_T(I["ssd_conv_w"].rearrange("o w (c p) -> (o w c) p", p=128), 96, self.cw[:], C)
    self.load_T(I["ssd_conv_b"].rearrange("o (c p) -> (o c) p", p=128), 24, self.cb[:], C)
    self.load_T(I["ssd_norm_g"].rearrange("o (c p) -> (o c) p", p=128), 16, self.ng[:], C)
    self.dtb = fw.sbuf([128, 32], F32, "dtb")
    self.Arow = fw.sbuf([128, 32], F32, "Arow")
    fw.dma("sp", self.dtb[:], I["ssd_dt_bias"][0:1, :].broadcast_to([128, 32]), C, writes=[C])
    fw.dma("sp", self.Arow[:], I["ssd_a_log"][0:1, :].broadcast_to([128, 32]), C, writes=[C])
    fw.op("act", lambda h: h.activation(out=self.Arow[:], in_=self.Arow[:], func=AF.Exp), reads=[C], writes=[C])
    fw.op("dve", lambda h: h.tensor_scalar(out=self.Arow[:], in0=self.Arow[:], scalar1=-1.0, scalar2=None, op0=ALU.mult), reads=[C], writes=[C])
    self.dvec = fw.sbuf([128, 16], F32, "dvec")
    dd = I["ssd_d"].rearrange("o (c t) -> o c t", t=2)
    fw.dma("sp", self.dvec[0:64, :], dd[0:1, :, 0].broadcast_to([64, 16]), C, writes=[C], slow=True)
    fw.dma("sp", self.dvec[64:128, :], dd[0:1, :, 1].broadcast_to([64, 16]), C, writes=[C], slow=True)
    self.tri = fw.sbuf([64, 64], F32, "tri")
    fw.op("pool", lambda h: h.memset(self.tri[:], 1.0), writes=[C])
    fw.op("pool", lambda h: h.affine_select(out=self.tri[:], in_=self.tri[:], pattern=[[1, 64]], compare_op=ALU.is_ge, fill=0.0,
                                            base=0, channel_multiplier=-1), reads=[C], writes=[C])
    self.ones512 = fw.sbuf([128, 128], BF16, "ones512")
    fw.op("pool", lambda h: h.memset(self.ones512[:], 1.0 / 512.0), writes=[C])
    self.hst = fw.sbuf([128, 2048], F32, "hst")
    self.hstb = fw.sbuf([128, 2048], BF16, "hstb")
    self.Bhst = [Buf("hst%d" % g) for g in range(4)]
    self.Bhstb = [Buf("hstb%d" % g) for g in range(4)]
    self.ctail = fw.sbuf([128, 24, 3], F32, "ctail")
    self.Bctail = Buf("ctail")
    self.dtt = fw.sbuf([64, 8, 4, 32], F32, "dtt")
    self.dend = fw.sbuf([128, 8, 32], F32, "dend")
    self.Bdtt = Buf("dtt")
    self.mcb = fw.sbuf([64, 64], F32, "mcb")
    self.Bmcb = Buf("mcb")


def ssd_core(self, c0, n, nseg, Lc):
    fw = self.fw
    C = self.Bconst
    PS, BP = self.PS, self.BP
    ma, mab, hbf = self.ma, self.mab, self.hbf
    xb, Bxb = self.xb, self.Bxb
    B1 = self.B1
    BT_ = mab[:, 0:2048].rearrange("p (c n) -> p c n", c=4)
    CT_ = mab[:, 2048:4096].rearrange("p (c n) -> p c n", c=4)
    zs = mab[:, 4096:6144].rearrange("p (c n) -> p c n", c=4)
    xsT = mab[:, 6144:8192].rearrange("p (c n) -> p c n", c=4)
    stg = ma[:, 4096:4611]
    DE = ma[:, 4612:5124]
    E2 = ma[:, 5124:5636]
    yg = ma[:, 4096:6144].rearrange("p (c n) -> p c n", c=4)
    yn = self.hb[:, 0:16, :]
    hbb = self.hb[:].rearrange("p m n -> p (m n)")
    sgb0 = self.sg[0][:].bitcast(BF16)
    sgb1 = self.sg[1][:].bitcast(BF16)
    Wt_ = [hbb[:, 8192:8704], sgb0[:, 0:512]]
    Ctl_ = [hbb[:, 8704:9216], sgb0[:, 512:1024]]
    xt_ = [hbb[:, 9216:9728], sgb1[:, 0:512]]
    xh_ = [hbb[:, 9728:10240], sgb1[:, 512:1024]]
    btm_ = [hbb[:, 10240:10368], hbb[:, 10368:10496]]
    self.ssd_it = 0
    win = self.S["ssd_w_in"][0].rearrange("(k p) f -> p k f", p=128)
    dve = lambda fn, r, w: fw.op("dve", fn, reads=list(r), writes=list(w))
    act = lambda fn, r, w: fw.op("act", fn, reads=list(r), writes=list(w))

    def slab_in(ci):
        w = 512 if ci < 10 else 32
        t, B = self.get_slab([(lambda t: t[:, 0:8 * w].rearrange("p (k f) -> p k f", k=8), win[:, :, 512 * ci:512 * ci + w], "ssd_w_in")])
        return t[:, 0:8 * w].rearrange("p (k f) -> p k f", k=8), B

    self.pp_i = 0

    def proj(tw, B, cc):
        i = 5 + self.pp_i % 2
        self.pp_i += 1
        pp, Bp = PS[i], BP[i]
        for k in range(NCH):
            self.mm(pp[:, 0:n], tw[:, k, cc * 128:(cc + 1) * 128], xb[:, k, c0:c0 + n], k == 0, k == NCH - 1, [B, Bxb[k]], [Bp])
        return pp, Bp

    def proj_conv(tw, B, cc, ci, out, Bout):
        pp, Bp = proj(tw, B, cc)
        dve(lambda h: h.tensor_copy(out=stg[:, 0:3], in_=self.ctail[:, ci, :]), [self.Bctail], [B1["stg"]])
        act(lambda h: h.activation(out=stg[:, 3:3 + n], in_=pp[:, 0:n], func=AF.Copy), [Bp], [B1["stg"]])
        dve(lambda h: h.tensor_copy(out=self.ctail[:, ci, :], in_=stg[:, n:n + 3]), [B1["stg"]], [self.Bctail])
        self.acc_i = getattr(self, "acc_i", 0) + 1
        acc, Bacc = (E2[:, 0:n], B1["E2"]) if self.acc_i % 2 == 0 else (DE[:, 0:n], B1["DE"])
        dve(lambda h: h.tensor_scalar(out=acc, in0=stg[:, 0:n], scalar1=self.cw[:, ci:ci + 1], scalar2=self.cb[:, ci:ci + 1], op0=ALU.mult, op1=ALU.add),
            [B1["stg"], C], [Bacc])
        for w in range(1, 4):
            dve(lambda h, w=w: h.scalar_tensor_tensor(out=acc, in0=stg[:, w:w + n], scalar=self.cw[:, w * 24 + ci:w * 24 + ci + 1], in1=acc,
                                                     op0=ALU.mult, op1=ALU.add), [B1["stg"], Bacc, C], [Bacc])
        act(lambda h: h.activation(out=out, in_=acc, func=AF.Silu), [Bacc], [Bout])

    fw.mark("m1_dt")
    tw, B = slab_in(10)
    dtt, Bd = self.dtt, self.Bdtt
    for sg in range(nseg):
        a = c0 + sg * Lc
        pp, Bp = PS[5], BP[5]
        for k in range(NCH):
            self.mm(pp[0:Lc, 0:32], xb[:, k, a:a + Lc], tw[:, k, 0:32], k == 0, k == NCH - 1, [B, Bxb[k]], [Bp])
        dt_, dtA, cs_, te = (dtt[0:Lc, sg, i, :] for i in range(4))
        dve(lambda h: h.tensor_tensor(out=dt_, in0=pp[0:Lc, 0:32], in1=self.dtb[0:Lc, :], op=ALU.add), [Bp, C], [Bd])
        act(lambda h: h.activation(out=dt_, in_=dt_, func=AF.Exp), [Bd], [Bd])
        act(lambda h: h.activation(out=dt_, in_=dt_, func=AF.Ln, bias=1.0), [Bd], [Bd])
        dve(lambda h: h.tensor_tensor(out=dtA, in0=dt_, in1=self.Arow[0:Lc, :], op=ALU.mult), [Bd, C], [Bd])
        pc, Bpc = PS[6], BP[6]
        self.mm(pc[0:Lc, 0:32], self.tri[0:Lc, 0:Lc], dtA, True, True, [C, Bd], [Bpc])
        pe_, Bpe = PS[7], BP[7]
        self.mm(pe_[:, 0:32], self.onesf[0:Lc, :], dtA, True, True, [C, Bd], [Bpe])
        act(lambda h: h.activation(out=cs_, in_=pc[0:Lc, 0:32], func=AF.Copy), [Bpc], [Bd])
        dve(lambda h: h.tensor_tensor(out=te, in0=pe_[0:Lc, 0:32], in1=cs_, op=ALU.subtract), [Bpe, Bd], [Bd])
        act(lambda h: h.activation(out=te, in_=te, func=AF.Exp), [Bd], [Bd])
        dve(lambda h: h.tensor_tensor(out=te, in0=te, in1=dt_, op=ALU.mult), [Bd], [Bd])
        act(lambda h, sg=sg: h.activation(out=self.dend[:, sg, :], in_=pe_[:, 0:32], func=AF.Exp), [Bpe], [Bd])
    fw.mark("m1_bc")
    tw, B = slab_in(8)
    for g in range(4):
        proj_conv(tw, B, g, 16 + g, BT_[:, g, 0:n], B1["BT"])
    tw, B = slab_in(9)
    for g in range(4):
        proj_conv(tw, B, g, 20 + g, CT_[:, g, 0:n], B1["CT"])
    pxt_b = PS[6][:].bitcast(BF16)
    for g in range(4):
        fw.mark("m1_inproj")
        tw, B = slab_in(g)
        for hc in range(4):
            pp, Bp = proj(tw, B, hc)
            act(lambda h, hc=hc, pp=pp: h.activation(out=zs[:, hc, 0:n], in_=pp[:, 0:n], func=AF.Silu), [Bp], [B1["zs"]])
        tw, B = slab_in(4 + g)
        for hc in range(4):
            proj_conv(tw, B, hc, 4 * g + hc, xsT[:, hc, 0:n], B1["xs"])
        fw.mark("m1_seg")
        for sg in range(nseg):
            a = sg * Lc
            par = self.ssd_it % 2
            self.ssd_it += 1
            Wt, Ctl, xt, xh, btm = Wt_[par], Ctl_[par], xt_[par], xh_[par], btm_[par]
            BW, BCt, Bxt, Bxh, Bbt = (B1[k + str(par)] for k in ("W", "Ct", "xt", "xh", "btm"))
            dt_, dtA, cs_, te = (dtt[0:Lc, sg, i, :] for i in range(4))
            pcb, Bpcb = PS[5], BP[5]
            self.mm(pcb[0:Lc, 0:Lc], BT_[:, g, a:a + Lc], CT_[:, g, a:a + Lc], True, True, [B1["BT"], B1["CT"]], [Bpcb])
            dve(lambda h: h.tensor_tensor(out=self.mcb[0:Lc, 0:Lc], in0=pcb[0:Lc, 0:Lc], in1=self.tri[0:Lc, 0:Lc], op=ALU.mult), [Bpcb, C], [self.Bmcb])
            pr, Bpr = PS[4], BP[4]
            for hh in range(8):
                hd = 8 * g + hh
                self.mm(pr[:, hh * Lc:(hh + 1) * Lc], dtA[:, hd:hd + 1].broadcast_to([Lc, 128]), self.tri[0:Lc, 0:Lc], True, True, [Bd, C], [Bpr])
            v3 = lambda ap, P: ap[0:P, 0:8 * Lc].rearrange("p (h t) -> p h t", h=8)
            dve(lambda h: h.tensor_tensor(out=v3(DE, Lc), in0=v3(pr, Lc), in1=bc(cs_[:, 8 * g:8 * g + 8].unsqueeze(2), [Lc, 8, Lc]), op=ALU.subtract),
                [Bpr, Bd], [B1["DE"]])
            act(lambda h: h.activation(out=DE[0:Lc, 0:8 * Lc], in_=DE[0:Lc, 0:8 * Lc], func=AF.Exp), [B1["DE"]], [B1["DE"]])
            dve(lambda h: h.scalar_tensor_tensor(out=v3(Wt, Lc), in0=v3(DE, Lc), scalar=1.0, in1=bc(self.mcb[0:Lc, 0:Lc].unsqueeze(1), [Lc, 8, Lc]),
                                                 op0=ALU.min, op1=ALU.mult), [B1["DE"], self.Bmcb], [BW])
            act(lambda h: h.activation(out=E2[:, 0:8 * Lc], in_=pr[:, 0:8 * Lc], func=AF.Exp), [Bpr], [B1["E2"]])
            dve(lambda h: h.tensor_tensor(out=v3(Ctl, 128), in0=v3(E2, 128), in1=bc(CT_[:, g, a:a + Lc].unsqueeze(1), [128, 8, Lc]), op=ALU.mult),
                [B1["E2"], B1["CT"]], [BCt])
            pxt, Bpxt = pxt_b, BP[6]
            for hc in range(4):
                fw.op("pe", lambda h, hc=hc: h.transpose(pxt[0:Lc, hc * 128:(hc + 1) * 128], xsT[:, hc, a:a + Lc], self.identb[:, :]),
                      reads=[B1["xs"], C], writes=[Bpxt])
            fw.op("pe", lambda h: h.transpose(pxt[0:Lc, 512:640], BT_[:, g, a:a + Lc], self.identb[:, :]), reads=[B1["BT"], C], writes=[Bpxt])
            x3 = lambda ap: ap[0:Lc, 0:512].rearrange("p (h q) -> p h q", h=8)
            dve(lambda h: h.tensor_tensor(out=x3(xt), in0=x3(pxt), in1=bc(dt_[:, 8 * g:8 * g + 8].unsqueeze(2), [Lc, 8, 64]), op=ALU.mult),
                [Bpxt, Bd], [Bxt])
            dve(lambda h: h.tensor_tensor(out=x3(xh), in0=x3(pxt), in1=bc(te[:, 8 * g:8 * g + 8].unsqueeze(2), [Lc, 8, 64]), op=ALU.mult),
                [Bpxt, Bd], [Bxh])
            act(lambda h: h.activation(out=btm[0:Lc, 0:128], in_=pxt[0:Lc, 512:640], func=AF.Copy), [Bpxt], [Bbt])
            for hc in range(4):
                py, Bpy = PS[hc], BP[hc]
                for hh in (2 * hc, 2 * hc + 1):
                    pb = 64 * (hh % 2)
                    self.mm(py[pb:pb + 64, a:a + Lc], xt[0:Lc, hh * 64:(hh + 1) * 64], Wt[0:Lc, hh * Lc:(hh + 1) * Lc], True, False,
                            [Bxt, BW], [Bpy], tile_position=(0, pb))
                for hh in (2 * hc, 2 * hc + 1):
                    pb = 64 * (hh % 2)
                    hd = 8 * g + hh
                    self.mm(py[pb:pb + 64, a:a + Lc], self.hstb[:, hd * 64:(hd + 1) * 64], Ctl[:, hh * Lc:(hh + 1) * Lc], False, True,
                            [self.Bhstb[g], BCt], [Bpy], tile_position=(0, pb))
            pS, BpS = PS[7], BP[7]
            self.mm(pS[:, 0:512], btm[0:Lc, 0:128], xh[0:Lc, 0:512], True, True, [Bbt, Bxh], [BpS])
            hg = self.hst[:, 512 * g:512 * (g + 1)]
            h3 = hg.rearrange("p (h q) -> p h q", h=8)
            dve(lambda h, sg=sg: h.tensor_tensor(out=h3, in0=h3, in1=bc(self.dend[:, sg, 8 * g:8 * g + 8].unsqueeze(2), [128, 8, 64]), op=ALU.mult),
                [self.Bhst[g], Bd], [self.Bhst[g]])
            dve(lambda h: h.tensor_tensor(out=hg, in0=hg, in1=pS[:, 0:512], op=ALU.add), [self.Bhst[g], BpS], [self.Bhst[g]])
            act(lambda h: h.activation(out=self.hstb[:, 512 * g:512 * (g + 1)], in_=hg, func=AF.Copy), [self.Bhst[g]], [self.Bhstb[g]])
        fw.mark("m1_epi")
        fw.barrier()
        for hc in range(4):
            py, Bpy = PS[hc], BP[hc]
            dve(lambda h, hc=hc, py=py: h.scalar_tensor_tensor(out=yg[:, hc, 0:n], in0=xsT[:, hc, 0:n], scalar=self.dvec[:, 4 * g + hc:4 * g + hc + 1],
                                                               in1=py[:, 0:n], op0=ALU.mult, op1=ALU.add), [B1["xs"], Bpy, C], [B1["yg"]])
            dve(lambda h, hc=hc: h.tensor_tensor(out=yg[:, hc, 0:n], in0=yg[:, hc, 0:n], in1=zs[:, hc, 0:n], op=ALU.mult), [B1["yg"], B1["zs"]], [B1["yg"]])
        act(lambda h: h.activation(out=xsT[:, :, 0:n], in_=yg[:, :, 0:n], func=AF.Square), [B1["yg"]], [B1["xs"]])
        pms, Bpms = PS[5], BP[5]
        for hc in range(4):
            self.mm(pms[:, 0:n], self.ones512[:], xsT[:, hc, 0:n], hc == 0, hc == 3, [C, B1["xs"]], [Bpms])
        rstd = self.lnt[:, 1, 0:n]
        Bl = self.Blnt[1]
        dve(lambda h: h.tensor_scalar(out=rstd, in0=pms[:, 0:n], scalar1=RMS_EPS, scalar2=None, op0=ALU.add), [Bpms], [Bl])
        act(lambda h: h.activation(out=rstd, in_=rstd, func=AF.Sqrt), [Bl], [Bl])
        dve(lambda h: h.reciprocal(out=rstd, in_=rstd), [Bl], [Bl])
        for hc in range(4):
            ch = 4 * g + hc
            dve(lambda h, hc=hc, ch=ch: h.scalar_tensor_tensor(out=yn[:, ch, c0:c0 + n], in0=yg[:, hc, 0:n], scalar=self.ng[:, ch:ch + 1], in1=rstd,
                                                               op0=ALU.mult, op1=ALU.mult), [B1["yg"], Bl, C], [B1["yn"]])
        fw.barrier()


def _st_stage(self, r):
    slot = 0 if r % 2 == 0 else 3
    return self.lnt[:, slot, :].rearrange("p (t n) -> p t n", t=4), self.Blnt[slot]


def ssd_state_in(self, sq):
    fw, I = self.fw, self.I
    C = self.Bconst
    src = I["state_ssd"][0, sq].rearrange("h p n -> (h p) n").rearrange("(r t p) n -> r p t n", t=4, p=128)
    for r in range(4):
        st, Bst = _st_stage(self, r)
        ps, Bp = self.PS[6 + r % 2], self.BP[6 + r % 2]
        fw.dma("sp", st, src[r], Bst, writes=[Bst])
        for t in range(4):
            fw.op("pe", lambda h, t=t: h.transpose(ps[:, t * 128:(t + 1) * 128], st[:, t, :], self.ident[:, :]), reads=[Bst, C], writes=[Bp])
        fw.op("dve", lambda h: h.tensor_copy(out=self.hst[:, 512 * r:512 * (r + 1)], in_=ps[:, 0:512]), reads=[Bp], writes=[self.Bhst[r]])
        fw.op("act", lambda h: h.activation(out=self.hstb[:, 512 * r:512 * (r + 1)], in_=ps[:, 0:512], func=AF.Copy), reads=[Bp], writes=[self.Bhstb[r]])
    srcc = I["state_conv"][0, sq].rearrange("w (c p) -> (w c) p", p=128)
    fw.dma("sp", self.stg[0:72, :], srcc, self.Bstg, writes=[self.Bstg])
    fw.op("pe", lambda h: h.transpose(self.PS[7][:, 0:72], self.stg[0:72, :], self.ident[0:72, 0:72]), reads=[self.Bstg, C], writes=[self.BP[7]])
    fw.op("dve", lambda h: h.tensor_copy(out=self.ctail[:].rearrange("p c w -> p w c"), in_=self.PS[7][:, 0:72].rearrange("p (w c) -> p w c", w=3)),
          reads=[self.BP[7]], writes=[self.Bctail])


def ssd_state_out(self, sfx, seq):
    fw = self.fw
    C = self.Bconst
    dst = self.O["ssd_" + sfx][0, seq].rearrange("h p n -> (h p) n").rearrange("(r t p) n -> r p t n", t=4, p=128)
    for r in range(4):
        st, Bst = _st_stage(self, r)
        ps, Bp = self.PS[6 + r % 2], self.BP[6 + r % 2]
        for t in range(4):
            i = 4 * r + t
            fw.op("pe", lambda h, i=i, t=t: h.transpose(ps[:, t * 128:(t + 1) * 128], self.hst[:, i * 128:(i + 1) * 128], self.ident[:, :]),
                  reads=[self.Bhst[r], C], writes=[Bp])
        fw.op("dve", lambda h: h.tensor_copy(out=st, in_=ps[:, 0:512].rearrange("p (t n) -> p t n", t=4)), reads=[Bp], writes=[Bst])
        fw.dma("sp", dst[r], st, Bst, reads=[Bst])
    tmp = self.lnt[:, 2, 0:72]
    fw.op("dve", lambda h: h.tensor_copy(out=tmp.rearrange("p (w c) -> p w c", w=3), in_=self.ctail[:].rearrange("p c w -> p w c")),
          reads=[self.Bctail], writes=[self.Blnt[2]])
    fw.op("pe", lambda h: h.transpose(self.PS[7][0:72, 0:128], tmp, self.ident[:, :]), reads=[self.Blnt[2], C], writes=[self.BP[7]])
    fw.op("dve", lambda h: h.tensor_copy(out=self.stg[0:72, :], in_=self.PS[7][0:72, 0:128]), reads=[self.BP[7]], writes=[self.Bstg])
    dstc = self.O["conv_" + sfx][0, seq].rearrange("w (c p) -> (w c) p", p=128)
    fw.dma("sp", dstc, self.stg[0:72, :], self.Bstg, reads=[self.Bstg])


def mixer1(self, tile, N):
    fw = self.fw
    kind, s, ti = tile
    C = self.Bconst
    PS, BP = self.PS, self.BP
    xf, Bxf = self.xf, self.Bxf
    if not hasattr(self, "B1"):
        self.B1 = {k: Buf(k) for k in ("stg", "E2", "DE", "BT", "CT", "zs", "xs", "yg", "yn")}
        for k in ("W", "Ct", "xt", "xh", "btm"):
            for par in range(2):
                self.B1[k + str(par)] = Buf(k + str(par))
    fw.barrier()
    if kind == "p":
        if ti == 0:
            fw.op("dve", lambda h: h.memset(self.hst[:], 0.0), writes=self.Bhst)
            fw.op("pool", lambda h: h.memset(self.hstb[:], 0.0), writes=self.Bhstb)
            fw.op("pool", lambda h: h.memset(self.ctail[:], 0.0), writes=[self.Bctail])
        ssd_core(self, 0, 512, 8, 64)
        if ti == self.SEQ // 512 - 1:
            ssd_state_out(self, "prompt", s)
    else:
        for sq in range(self.NS):
            ssd_state_in(self, sq)
            ssd_core(self, sq * DEC_SEQ, DEC_SEQ, 1, DEC_SEQ)
            ssd_state_out(self, "sample", sq)
    fw.mark("m1_out")
    yn = self.hb[:, 0:16, :]
    wout = self.S["ssd_w_out"][0].rearrange("(k p) f -> p k f", p=128)
    for o2 in range(4):
        t, B = self.get_slab([(lambda t: t[:, :].rearrange("p (k f) -> p k f", k=16), wout[:, :, 256 * o2:256 * o2 + 256], "ssd_w_out")])
        tw = t[:, :].rearrange("p (k f) -> p k f", k=16)
        for oi in range(2):
            o = 2 * o2 + oi
            pp, Bp = PS[o % 2], BP[o % 2]
            for k in range(16):
                self.mm(pp[:, 0:N], tw[:, k, oi * 128:(oi + 1) * 128], yn[:, k, 0:N], k == 0, k == 15, [B, self.B1["yn"]], [Bp])
            fw.op("dve", lambda h, o=o, pp=pp: h.scalar_tensor_tensor(out=xf[:, o, 0:N], in0=xf[:, o, 0:N], scalar=ALPHA, in1=pp[:, 0:N],
                                                                     op0=ALU.mult, op1=ALU.add), reads=[Bxf[o], Bp], writes=[Bxf[o]])
    fw.barrier([self.Bstg, self.Blnt[0], self.Blnt[3]])
    self.layer_norm(4, N)

Builder.setup_mix0 = setup_mix0
Builder.mixer0 = mixer0
Builder.s5_sample_init = s5_sample_init
Builder.load_cache = load_cache
Builder.setup_ssd = setup_ssd
Builder.mixer1 = mixer1

_OUT_ORDER = ["y_prompt", "y_sample", "s5_re_prompt", "s5_im_prompt", "sb_k_prompt", "sb_v_prompt", "ssd_prompt",
              "conv_prompt", "s5_re_sample", "s5_im_sample", "sb_k_sample", "sb_v_sample", "ssd_sample", "conv_sample"]
_BATCH_AXIS = {"x_prompt": 0, "x_sample": 0, "state_s5_re": 1, "state_s5_im": 1, "cache_sb_k": 1, "cache_sb_v": 1,
               "state_ssd": 1, "state_conv": 1}


def make_in_maps(inputs, n_cores):
    maps = []
    for c in range(n_cores):
        m = {}
        for k, v in inputs.items():
            v = np.asarray(v)
            if k in _BATCH_AXIS:
                ax = _BATCH_AXIS[k]
                n = v.shape[ax] // n_cores
                sl = [slice(None)] * v.ndim
                sl[ax] = slice(c * n, (c + 1) * n)
                m[k] = np.ascontiguousarray(v[tuple(sl)])
            else:
                m[k] = v
        maps.append(m)
    return maps


def gather(results):
    outs = []
    for nm in _OUT_ORDER:
        ax = 0 if nm in ("y_prompt", "y_sample") else 1
        outs.append(np.concatenate([r[nm] for r in results], axis=ax).astype(np.float32))
    return tuple(outs)


def kernel(**inputs):
    n = 8
    b = Builder(SEQ=4096, NP=2, NS=4)
    nc = b.build()
    res = run_bass_kernel_spmd(nc, make_in_maps(inputs, n), core_ids=list(range(n)))
    return gather(res.results)
```

```python
import math
import numpy as np
import concourse.bass as bass
import concourse.mybir as mybir
from concourse.bass_utils import run_bass_kernel_spmd
from contextlib import ExitStack

F32 = mybir.dt.float32
BF16 = mybir.dt.bfloat16
AF = mybir.ActivationFunctionType
ALU = mybir.AluOpType

D = 1024
DFF = 2816
NCH = 8
MCH = 22
ALPHA = (2.0 * 2) ** 0.25
LN_EPS = 1e-5
RMS_EPS = 1e-5
PAST = 2048
DEC_SEQ = 16
LSUB = 64
MAGIC = 12582912.0


class Buf:
    __slots__ = ("name", "w", "r", "dsem", "dtot", "psum", "strict")

    def __init__(self, name="", psum=False, strict=False):
        self.name = name
        self.psum = psum
        self.strict = strict
        self.w = None
        self.r = {}
        self.dsem = None
        self.dtot = 0


class Eng:
    def __init__(self, name, sem):
        self.name = name
        self.sem = sem
        self.n = 0
        self.seen = {}
        self.prog = []


class _Rec:
    def __init__(self):
        self.call = None

    def __getattr__(self, name):
        def f(*args, **kwargs):
            self.call = (name, args, kwargs)
            return None
        return f


class FW:
    def __init__(self, nc, es):
        self.nc = nc
        self.es = es
        self.engs = {}
        for name in ("pe", "dve", "act", "pool", "sp"):
            sem = es.enter_context(nc.semaphore("sem_" + name))
            self.engs[name] = Eng(name, sem)
        self.dma_bufs = []
        self.nsem = 5
        self.uid = 0
        self.nosame = False

    def sbuf(self, shape, dtype, name=None):
        self.uid += 1
        return self.es.enter_context(self.nc.sbuf_tensor(name or ("sb%d" % self.uid), list(shape), dtype))

    def psum(self, shape, dtype=F32, name=None):
        self.uid += 1
        return self.es.enter_context(self.nc.psum_tensor(name or ("ps%d" % self.uid), list(shape), dtype))

    def _wait(self, e, tok, strict=False):
        if tok[0] == "e":
            _, name, seq = tok
            if name == e.name and (name in ("pe", "sp") or (self.nosame and not strict)):
                return
            if e.seen.get(name, 0) >= seq:
                return
            e.prog.append(("w", self.engs[name].sem, seq))
            e.seen[name] = seq
        else:
            _, b, val = tok
            key = ("d", id(b))
            if e.seen.get(key, 0) >= val:
                return
            e.prog.append(("w", b.dsem, val))
            e.seen[key] = val

    def _deps(self, e, reads, writes):
        for b in reads:
            if b.w is not None:
                self._wait(e, b.w, b.strict)
            if b.psum:
                for t in b.r.values():
                    if not (t[0] == "e" and t[1] == e.name):
                        self._wait(e, t)
        for b in writes:
            if b.w is not None:
                self._wait(e, b.w)
            for t in b.r.values():
                if not (t[0] == "e" and t[1] == e.name):
                    self._wait(e, t)

    def op(self, ename, fn, reads=(), writes=()):
        e = self.engs[ename]
        self._deps(e, reads, writes)
        e.n += 1
        rec = _Rec()
        fn(rec)
        e.prog.append(("o", rec.call, e.sem, 1))
        tok = ("e", ename, e.n)
        for b in reads:
            b.r[ename] = tok
        for b in writes:
            b.w = tok
            b.r = {}
        return tok

    def dma(self, qname, out, in_, track, reads=(), writes=(), slow=False):
        e = self.engs[qname]
        self._deps(e, reads, writes)
        if track.dsem is None:
            track.dsem = self.es.enter_context(self.nc.semaphore("dsem_%d" % self.nsem))
            self.nsem += 1
            self.dma_bufs.append(track)
        track.dtot += 16
        if slow:
            e.prog.append(("o", ("dma_start", (), dict(out=out, in_=in_, allow_slow_non_contiguous=True)), track.dsem, 16))
        else:
            e.prog.append(("o", ("dma_start", (), dict(out=out, in_=in_)), track.dsem, 16))
        tok = ("d", track, track.dtot)
        key = ("d", id(track))
        for b in reads:
            b.r[key] = tok
        for b in writes:
            b.w = tok
            b.r = {}
        return tok

    def mark(self, name):
        if not hasattr(self, "marks"):
            self.marks = []
        self.marks.append((name, {k: e.n for k, e in self.engs.items()}))

    def barrier(self, bufs=()):
        names = ("pe", "dve", "act", "pool")
        for a in names:
            ea = self.engs[a]
            for b in bufs:
                if b.dsem is not None:
                    self._wait(ea, ("d", b, b.dtot))
            for bn in names:
                if bn != a and self.engs[bn].n > 0:
                    self._wait(ea, ("e", bn, self.engs[bn].n))

    def finish(self):
        e = self.engs["sp"]
        for b in self.dma_bufs:
            e.prog.append(("w", b.dsem, b.dtot))
        for name, o in self.engs.items():
            if name != "sp" and o.n > 0:
                e.prog.append(("w", o.sem, o.n))
        with self.nc.Block() as block:
            def mk(prog):
                def body(h):
                    for it in prog:
                        if it[0] == "w":
                            h.wait_ge(it[1], it[2])
                        else:
                            name, args, kwargs = it[1]
                            getattr(h, name)(*args, **kwargs).then_inc(it[2], it[3])
                return body
            block.tensor(mk(self.engs["pe"].prog))
            block.vector(mk(self.engs["dve"].prog))
            block.scalar(mk(self.engs["act"].prog))
            block.gpsimd(mk(self.engs["pool"].prog))
            block.sync(mk(self.engs["sp"].prog))


def bc(ap, shape):
    return ap.broadcast_to(list(shape))


class Builder:
    def __init__(self, SEQ=4096, NP=2, NS=4, dbg=False, do_sample=True, layers=2):
        self.SEQ, self.NP, self.NS = SEQ, NP, NS
        self.dbg = dbg
        self.do_sample = do_sample
        self.layers = layers
        self.nc = bass.Bass("TRN2", target_bir_lowering=False)
        self.dbg_outs = {}

    def din(self, name, shape):
        return self.nc.dram_tensor(name, list(shape), F32, kind="ExternalInput").ap()

    def dout(self, name, shape):
        return self.nc.dram_tensor(name, list(shape), F32, kind="ExternalOutput").ap()

    def declare(self):
        NP, NS, SEQ = self.NP, self.NS, self.SEQ
        I = {}
        I["x_prompt"] = self.din("x_prompt", [NP, SEQ, D])
        I["x_sample"] = self.din("x_sample", [NS, DEC_SEQ, D])
        I["state_s5_re"] = self.din("state_s5_re", [1, NS, 32, 64])
        I["state_s5_im"] = self.din("state_s5_im", [1, NS, 32, 64])
        I["cache_sb_k"] = self.din("cache_sb_k", [1, NS, PAST, 8, 64])
        I["cache_sb_v"] = self.din("cache_sb_v", [1, NS, PAST, 8, 64])
        I["state_ssd"] = self.din("state_ssd", [1, NS, 32, 64, 128])
        I["state_conv"] = self.din("state_conv", [1, NS, 3, 3072])
        for nm, shp in (("ln_g", [2, 3, D]), ("ln_b", [2, 3, D]), ("ffn_w_gate", [2, 2, D, DFF]),
                        ("ffn_w_up", [2, 2, D, DFF]), ("ffn_w_down", [2, 2, DFF, D]),
                        ("mix0_w_in", [1, D, 2048]), ("s5_a_re", [1, 32, 64]), ("s5_a_im", [1, 32, 64]),
                        ("s5_log_dt", [1, 32]), ("s5_b_re", [1, 32, 64, 16]), ("s5_b_im", [1, 32, 64, 16]),
                        ("s5_c_re", [1, 32, 16, 64]), ("s5_c_im", [1, 32, 16, 64]), ("s5_d", [1, 512]),
                        ("s5_w_glu", [1, 512, 512]), ("s5_b_glu", [1, 512]), ("mix0_w_out", [1, D, D]),
                        ("ssd_w_in", [1, D, 5152]), ("ssd_conv_w", [1, 4, 3072]), ("ssd_conv_b", [1, 3072]),
                        ("ssd_dt_bias", [1, 32]), ("ssd_a_log", [1, 32]), ("ssd_d", [1, 32]),
                        ("ssd_norm_g", [1, 2048]), ("ssd_w_out", [1, 2048, D])):
            I[nm] = self.din(nm, shp)
        O = {}
        O["y_prompt"] = self.dout("y_prompt", [NP, SEQ, D])
        O["y_sample"] = self.dout("y_sample", [NS, DEC_SEQ, D])
        for sfx, nb, sl in (("prompt", NP, SEQ), ("sample", NS, DEC_SEQ)):
            O["s5_re_" + sfx] = self.dout("s5_re_" + sfx, [1, nb, 32, 64])
            O["s5_im_" + sfx] = self.dout("s5_im_" + sfx, [1, nb, 32, 64])
            O["sb_k_" + sfx] = self.dout("sb_k_" + sfx, [1, nb, sl, 8, 64])
            O["sb_v_" + sfx] = self.dout("sb_v_" + sfx, [1, nb, sl, 8, 64])
            O["ssd_" + sfx] = self.dout("ssd_" + sfx, [1, nb, 32, 64, 128])
            O["conv_" + sfx] = self.dout("conv_" + sfx, [1, nb, 3, 3072])
        self.I, self.O = I, O
        S = {}
        for nm in ("ffn_w_gate", "ffn_w_up", "ffn_w_down", "mix0_w_in", "s5_w_glu", "mix0_w_out",
                   "ssd_w_in", "ssd_w_out"):
            shp = list(I[nm].shape)
            S[nm] = self.nc.dram_tensor("scr_" + nm, shp, BF16, kind="Internal").ap()
        self.S = S
        self.SB = {}
        for nm in S:
            if nm.startswith("ffn"):
                for l in range(2):
                    for j in range(2):
                        self.SB[(nm, l, j)] = Buf("scr")
            else:
                self.SB[nm] = Buf("scr")

    def dbg_out(self, name, shape):
        if name not in self.dbg_outs:
            self.dbg_outs[name] = self.dout("dbg_" + name, shape)
        return self.dbg_outs[name]

    def build(self):
        self.declare()
        with ExitStack() as es:
            self.fw = FW(self.nc, es)
            self.alloc()
            self.setup()
            tiles = []
            for s in range(self.NP):
                for i in range(self.SEQ // 512):
                    tiles.append(("p", s, i))
            for t in tiles:
                self.run_tile(t)
            if self.do_sample:
                self.run_tile(("s", 0, 0))
            self.fw.finish()
        return self.nc

    def alloc(self):
        fw = self.fw
        self.xf = fw.sbuf([128, NCH, 512], F32, "xf")
        self.Bxf = [Buf("xf%d" % c) for c in range(NCH)]
        self.xb = fw.sbuf([128, NCH, 512], BF16, "xb")
        self.Bxb = [Buf("xb%d" % c) for c in range(NCH)]
        self.hb = fw.sbuf([128, MCH, 512], BF16, "hb")
        self.Bh = [Buf("h%d" % m) for m in range(MCH)]
        self.hbf = self.hb[:].rearrange("p m n -> p (m n)").bitcast(F32)
        self.ma = fw.sbuf([128, 6144], F32, "ma")
        self.mab = self.ma[:].bitcast(BF16)
        self.NSLAB = 3
        self.slab = [fw.sbuf([128, 4096], BF16, "slab%d" % i) for i in range(self.NSLAB)]
        self.Bslab = [Buf("slab%d" % i) for i in range(self.NSLAB)]
        self.slab_i = 0
        self.KT = fw.sbuf([128, 4, 4096], BF16, "KT")
        self.VT = fw.sbuf([128, 32, 512], BF16, "VT")
        self.BKT = [Buf("KT%d" % i) for i in range(8)]
        self.BVT = [Buf("VT%d" % i) for i in range(8)]
        self.PS = [fw.psum([128, 512], F32, "psb%d" % i) for i in range(8)]
        self.BP = [Buf("ps%d" % i, psum=True) for i in range(8)]
        self.sg = [fw.sbuf([128, 512], F32, "sg%d" % i) for i in range(2)]
        self.Bsg = [Buf("sg%d" % i) for i in range(2)]
        self.knew = self.KT[:, :, 3072:3136]
        self.vnew = self.VT[0:16, 20:24, :]
        self.lnt = fw.sbuf([128, 4, 512], F32, "lnt")
        self.Blnt = [Buf("lnt%d" % i) for i in range(4)]

    def mm(self, out, lhsT, rhs, start, stop, reads, writes, **kw):
        self.fw.op("pe", lambda h: h.matmul(out, lhsT=lhsT, rhs=rhs, start=start, stop=stop, **kw),
                   reads=reads, writes=writes)

    def load_T(self, rows_ap, R, dest, Bdest):
        fw = self.fw
        fw.dma("sp", self.stg[0:R, :], rows_ap, self.Bstg, writes=[self.Bstg])
        fw.op("pe", lambda h: h.transpose(self.PS[7][:, 0:R], self.stg[0:R, :], self.ident[0:R, 0:R]),
              reads=[self.Bstg, self.Bconst], writes=[self.BP[7]])
        fw.op("dve", lambda h: h.tensor_copy(out=dest, in_=self.PS[7][:, 0:R]), reads=[self.BP[7]], writes=[Bdest])

    def setup(self):
        fw, nc, I, S = self.fw, self.nc, self.I, self.S
        def cast(nm, idx_list):
            for idx in idx_list:
                src = I[nm]
                dst = S[nm]
                for i in idx:
                    src = src[i]
                    dst = dst[i]
                key = (nm,) + tuple(idx) if nm.startswith("ffn") else nm
                fw.dma("pool", dst, src, self.SB[key], writes=[self.SB[key]])
        lj = [(l, j) for l in range(2) for j in range(2)]
        cast("ffn_w_gate", lj[:1]); cast("ffn_w_up", lj[:1]); cast("ffn_w_down", lj[:1])
        cast("mix0_w_in", [(0,)]); cast("s5_w_glu", [(0,)]); cast("mix0_w_out", [(0,)])
        cast("ffn_w_gate", lj[1:]); cast("ffn_w_up", lj[1:]); cast("ffn_w_down", lj[1:])
        cast("ssd_w_in", [(0,)]); cast("ssd_w_out", [(0,)])

        self.Bconst = Buf("const")
        self.Bstg = Buf("stg")
        self.stg = fw.sbuf([128, 128], F32, "stg")
        self.ident = fw.sbuf([128, 128], F32, "ident")
        self.identb = fw.sbuf([128, 128], BF16, "identb")
        self.onesb = fw.sbuf([128, 128], BF16, "onesb")
        self.ones1 = fw.sbuf([128, 128], BF16, "ones1")
        self.uincl = fw.sbuf([128, 128], BF16, "uincl")
        self.onesf = fw.sbuf([128, 128], F32, "onesf")
        C = self.Bconst
        fw.op("pool", lambda h: h.memset(self.ident[:], 1.0), writes=[C])
        fw.op("pool", lambda h: h.affine_select(out=self.ident[:], in_=self.ident[:], pattern=[[-1, 128]],
                                                compare_op=ALU.is_equal, fill=0.0, base=0, channel_multiplier=1),
              reads=[C], writes=[C])
        fw.op("pool", lambda h: h.tensor_copy(out=self.identb[:], in_=self.ident[:]), reads=[C], writes=[C])
        fw.op("pool", lambda h: h.memset(self.onesb[:], 1.0 / 1024.0), writes=[C])
        fw.op("pool", lambda h: h.memset(self.ones1[:], 1.0), writes=[C])
        fw.op("pool", lambda h: h.memset(self.onesf[:], 1.0), writes=[C])
        fw.op("pool", lambda h: h.memset(self.uincl[:], 1.0), writes=[C])
        fw.op("pool", lambda h: h.affine_select(out=self.uincl[:], in_=self.uincl[:], pattern=[[-1, 128]],
                                                compare_op=ALU.is_ge, fill=0.0, base=0, channel_multiplier=1),
              reads=[C], writes=[C])
        self.lng = fw.sbuf([128, 48], F32, "lng")
        self.lnb = fw.sbuf([128, 48], F32, "lnb")
        self.load_T(I["ln_g"].rearrange("l j (c p) -> (l j c) p", p=128), 48, self.lng[:], C)
        self.load_T(I["ln_b"].rearrange("l j (c p) -> (l j c) p", p=128), 48, self.lnb[:], C)
        self.setup_mix0()
        if self.layers > 1:
            self.setup_ssd()
        fw.barrier([self.Bconst, self.Bstg])

    def get_slab(self, pieces):
        i = self.slab_i
        self.slab_i = (i + 1) % self.NSLAB
        t, B = self.slab[i], self.Bslab[i]
        for dst_fn, src, key in pieces:
            self.fw.dma("sp", dst_fn(t), src, B, reads=[self.SB[key]], writes=[B])
        return t, B

    def ln_front(self, c0, nc_, N):
        fw = self.fw
        xf, BP = self.xf, self.BP
        if not hasattr(self, "Blnf"):
            self.Blnf = [Buf("lnf%d" % i) for i in range(16)]
        yb = self.mab[:, 0:4096].rearrange("p (c n) -> p c n", c=8)
        sq = self.mab[:, 4096:8192].rearrange("p (c n) -> p c n", c=8)
        Bf = self.Blnf
        pm, pq = self.PS[0], self.PS[1]
        fw.op("dve", lambda h: h.tensor_copy(out=yb[:, c0:c0 + nc_, 0:N], in_=xf[:, c0:c0 + nc_, 0:N]), reads=self.Bxf[c0:c0 + nc_], writes=Bf[c0:c0 + nc_])
        fw.op("act", lambda h: h.activation(out=sq[:, c0:c0 + nc_, 0:N], in_=xf[:, c0:c0 + nc_, 0:N], func=AF.Square),
              reads=self.Bxf[c0:c0 + nc_], writes=Bf[8 + c0:8 + c0 + nc_])
        for c in range(c0, c0 + nc_):
            self.mm(pm[:, 0:N], self.onesb[:], yb[:, c, 0:N], c == 0, c == NCH - 1, [Bf[c], self.Bconst], [BP[0]])
        for c in range(c0, c0 + nc_):
            self.mm(pq[:, 0:N], self.onesb[:], sq[:, c, 0:N], c == 0, c == NCH - 1, [Bf[8 + c], self.Bconst], [BP[1]])

    def layer_norm(self, idx, N, front=True):
        fw = self.fw
        xf, xb, hb = self.xf, self.xb, self.hb
        Bxf, Bxb, Bh, BP = self.Bxf, self.Bxb, self.Bh, self.BP
        pm, pq = self.PS[0], self.PS[1]
        if front:
            ybf = hb[:, 0:8, 0:N]
            sq = hb[:, 8:16, 0:N]
            fw.op("dve", lambda h: h.tensor_copy(out=ybf, in_=xf[:, :, 0:N]), reads=Bxf, writes=Bh[0:8])
            fw.op("act", lambda h: h.activation(out=sq, in_=xf[:, :, 0:N], func=AF.Square), reads=Bxf, writes=Bh[8:16])
            for c in range(NCH):
                self.mm(pm[:, 0:N], self.onesb[:], hb[:, c, 0:N], c == 0, c == NCH - 1, [Bh[c], self.Bconst], [BP[0]])
            for c in range(NCH):
                self.mm(pq[:, 0:N], self.onesb[:], hb[:, 8 + c, 0:N], c == 0, c == NCH - 1, [Bh[8 + c], self.Bconst], [BP[1]])
        mean, rstd, nmr = self.lnt[:, 0, 0:N], self.lnt[:, 1, 0:N], self.lnt[:, 2, 0:N]
        Bl = self.Blnt
        fw.op("act", lambda h: h.activation(out=mean, in_=pm[:, 0:N], func=AF.Copy), reads=[BP[0]], writes=[Bl[0]])
        fw.op("dve", lambda h: h.tensor_tensor(out=rstd, in0=pm[:, 0:N], in1=mean, op=ALU.mult), reads=[BP[0], Bl[0]], writes=[Bl[1]])
        fw.op("dve", lambda h: h.tensor_tensor(out=rstd, in0=pq[:, 0:N], in1=rstd, op=ALU.subtract), reads=[BP[1], Bl[1]], writes=[Bl[1]])
        fw.op("dve", lambda h: h.tensor_scalar(out=rstd, in0=rstd, scalar1=0.0, scalar2=LN_EPS, op0=ALU.max, op1=ALU.add), reads=[Bl[1]], writes=[Bl[1]])
        fw.op("act", lambda h: h.activation(out=rstd, in_=rstd, func=AF.Sqrt), reads=[Bl[1]], writes=[Bl[1]])
        fw.op("dve", lambda h: h.reciprocal(out=rstd, in_=rstd), reads=[Bl[1]], writes=[Bl[1]])
        fw.op("dve", lambda h: h.scalar_tensor_tensor(out=nmr, in0=mean, scalar=-1.0, in1=rstd, op0=ALU.mult, op1=ALU.mult),
              reads=[Bl[0], Bl[1]], writes=[Bl[2]])
        xv = xf[:, :, 0:N]
        fw.op("dve", lambda h: h.tensor_tensor(out=xv, in0=xv, in1=bc(rstd.unsqueeze(1), [128, NCH, N]), op=ALU.mult),
              reads=Bxf + [Bl[1]], writes=Bxf)
        fw.op("dve", lambda h: h.tensor_tensor(out=xv, in0=xv, in1=bc(nmr.unsqueeze(1), [128, NCH, N]), op=ALU.add),
              reads=Bxf + [Bl[2]], writes=Bxf)
        for c in range(NCH):
            col = idx * 8 + c
            fw.op("act", lambda h, c=c, col=col: h.activation(out=xf[:, c, 0:N], in_=xf[:, c, 0:N], func=AF.Identity,
                                                               scale=self.lng[:, col:col + 1], bias=self.lnb[:, col:col + 1]),
                  reads=[Bxf[c], self.Bconst], writes=[Bxf[c]])
            eng = "dve" if c % 2 == 0 else "pool"
            fw.op(eng, lambda h, c=c: h.tensor_copy(out=xb[:, c, 0:N], in_=xf[:, c, 0:N]), reads=[Bxf[c]], writes=[Bxb[c]])

    def ffn(self, l, j, N):
        fw, S = self.fw, self.S
        xf, xb, hb = self.xf, self.xb, self.hb
        Bxf, Bxb, Bh, BP, PS = self.Bxf, self.Bxb, self.Bh, self.BP, self.PS
        wg = S["ffn_w_gate"][l, j].rearrange("(k p) f -> p k f", p=128)
        wu = S["ffn_w_up"][l, j].rearrange("(k p) f -> p k f", p=128)
        wd = S["ffn_w_down"][l, j].rearrange("(m p) f -> p m f", p=128)
        for s in range(11):
            cols = slice(256 * s, 256 * s + 256)
            t, B = self.get_slab([
                (lambda t: t[:, 0:2048].rearrange("p (k f) -> p k f", k=8), wg[:, :, cols], ("ffn_w_gate", l, j)),
                (lambda t: t[:, 2048:4096].rearrange("p (k f) -> p k f", k=8), wu[:, :, cols], ("ffn_w_up", l, j))])
            tg = t[:, 0:2048].rearrange("p (k f) -> p k f", k=8)
            tu = t[:, 2048:4096].rearrange("p (k f) -> p k f", k=8)
            for mi in range(2):
                m = 2 * s + mi
                pg, pu = PS[m % 2], PS[2 + m % 2]
                Bg, Bu = BP[m % 2], BP[2 + m % 2]
                for k in range(NCH):
                    self.mm(pg[:, 0:N], tg[:, k, mi * 128:(mi + 1) * 128], xb[:, k, 0:N], k == 0, k == NCH - 1, [B, Bxb[k]], [Bg])
                for k in range(NCH):
                    self.mm(pu[:, 0:N], tu[:, k, mi * 128:(mi + 1) * 128], xb[:, k, 0:N], k == 0, k == NCH - 1, [B, Bxb[k]], [Bu])
                sg, Bs = self.sg[m % 2], self.Bsg[m % 2]
                fw.op("act", lambda h, sg=sg, pg=pg: h.activation(out=sg[:, 0:N], in_=pg[:, 0:N], func=AF.Silu), reads=[Bg], writes=[Bs])
                fw.op("dve", lambda h, sg=sg, pu=pu, m=m: h.scalar_tensor_tensor(out=hb[:, m, 0:N], in0=sg[:, 0:N], scalar=0.5, in1=pu[:, 0:N],
                                                                                op0=ALU.mult, op1=ALU.mult),
                      reads=[Bs, Bu], writes=[Bh[m]])
        for o2 in range(4):
            cols = slice(256 * o2, 256 * o2 + 256)
            for half in range(2):
                t, B = self.get_slab([(lambda t: t[:, 0:2816].rearrange("p (m f) -> p m f", m=11),
                                       wd[:, 11 * half:11 * half + 11, cols], ("ffn_w_down", l, j))])
                tv = t[:, 0:2816].rearrange("p (m f) -> p m f", m=11)
                for oi in range(2):
                    o = 2 * o2 + oi
                    pd, Bd = PS[4 + o % 4], BP[4 + o % 4]
                    for mm_ in range(11):
                        m = 11 * half + mm_
                        self.mm(pd[:, 0:N], tv[:, mm_, oi * 128:(oi + 1) * 128], hb[:, m, 0:N], m == 0, m == MCH - 1, [B, Bh[m]], [Bd])
            for oi in range(2):
                o = 2 * o2 + oi
                pd, Bd = PS[4 + o % 4], BP[4 + o % 4]
                fw.op("dve", lambda h, o=o, pd=pd: h.scalar_tensor_tensor(out=xf[:, o, 0:N], in0=xf[:, o, 0:N], scalar=ALPHA, in1=pd[:, 0:N],
                                                                         op0=ALU.mult, op1=ALU.add),
                      reads=[Bxf[o], Bd], writes=[Bxf[o]])
            self.ln_front(2 * o2, 2, N)
        self.layer_norm(l * 3 + (0 if j == 0 else 2), N, front=False)

    def load_x(self, tile, N):
        fw = self.fw
        kind, s, i = tile
        nb = (N + 127) // 128
        stg = self.hbf[:, 0:4096].rearrange("p (b f) -> p b f", b=4)
        Bst = self.Bh[0:16]
        for b in range(nb):
            rows = min(128, N - b * 128)
            if kind == "p":
                src = self.I["x_prompt"][s, i * 512 + b * 128:i * 512 + b * 128 + rows, :]
            else:
                src = self.I["x_sample"].rearrange("s t d -> (s t) d")[b * 128:b * 128 + rows, :]
            fw.dma("sp", stg[0:rows, b, :], src, Bst[4 * b], writes=Bst[4 * b:4 * b + 4])
        for c in range(NCH):
            ps, Bp = self.PS[c % 4], self.BP[c % 4]
            for b in range(nb):
                rows = min(128, N - b * 128)
                fw.op("pe", lambda h, ps=ps, b=b, c=c, rows=rows: h.transpose(ps[:, b * 128:b * 128 + rows], stg[0:rows, b, c * 128:(c + 1) * 128],
                                                                           self.ident[0:rows, 0:rows]),
                      reads=Bst[4 * b:4 * b + 4] + [self.Bconst], writes=[Bp])
            fw.op("act", lambda h, ps=ps, c=c: h.activation(out=self.xf[:, c, 0:N], in_=ps[:, 0:N], func=AF.Copy), reads=[Bp], writes=[self.Bxf[c]])
            fw.op("dve", lambda h, ps=ps, c=c: h.tensor_copy(out=self.xb[:, c, 0:N], in_=ps[:, 0:N]), reads=[Bp], writes=[self.Bxb[c]])

    def store_y(self, tile, N):
        fw = self.fw
        kind, s, i = tile
        nb = (N + 127) // 128
        stg = self.hbf[:, 0:4096].rearrange("p (b f) -> p b f", b=4)
        Bst = self.Bh[0:16]
        for b in range(nb):
            rows = min(128, N - b * 128)
            for half in range(2):
                ps, Bp = self.PS[(2 * b + half) % 4], self.BP[(2 * b + half) % 4]
                for cc in range(4):
                    c = 4 * half + cc
                    fw.op("pe", lambda h, ps=ps, b=b, c=c, cc=cc, rows=rows: h.transpose(ps[0:rows, cc * 128:(cc + 1) * 128],
                                                                                     self.xf[:, c, b * 128:b * 128 + rows], self.ident[:, :]),
                          reads=[self.Bxf[c], self.Bconst], writes=[Bp])
                eng = "act" if half == 0 else "dve"
                if eng == "act":
                    fw.op("act", lambda h, ps=ps, b=b, half=half, rows=rows: h.activation(out=stg[0:rows, b, half * 512:(half + 1) * 512], in_=ps[0:rows, :], func=AF.Copy),
                          reads=[Bp], writes=[Bst[4 * b + 2 * half], Bst[4 * b + 2 * half + 1]])
                else:
                    fw.op("dve", lambda h, ps=ps, b=b, half=half, rows=rows: h.tensor_copy(out=stg[0:rows, b, half * 512:(half + 1) * 512], in_=ps[0:rows, :]),
                          reads=[Bp], writes=[Bst[4 * b + 2 * half], Bst[4 * b + 2 * half + 1]])
            if kind == "p":
                dst = self.O["y_prompt"][s, i * 512 + b * 128:i * 512 + b * 128 + rows, :]
            else:
                dst = self.O["y_sample"].rearrange("s t d -> (s t) d")[b * 128:b * 128 + rows, :]
            fw.dma("sp", dst, stg[0:rows, b, :], Bst[4 * b], reads=Bst[4 * b:4 * b + 4])

    def dbg_sb(self, name, ap, B, shape):
        if not self.dbg or name in self.dbg_outs:
            return
        o = self.dbg_out(name, shape)
        self.fw.dma("sp", o, ap, B, reads=[B])

    def dump_x(self, name, tile, N):
        if not self.dbg:
            return
        kind, s, i = tile
        ntile = self.NP * (self.SEQ // 512) + 1
        o = self.dbg_out(name, [ntile, NCH, 128, 512])
        ti = (s * (self.SEQ // 512) + i) if kind == "p" else ntile - 1
        for c in range(NCH):
            self.fw.dma("sp", o[ti, c, :, 0:N], self.xf[:, c, 0:N], self.Bxf[c], reads=[self.Bxf[c]])

    def run_tile(self, tile):
        import os
        self.stop = int(os.environ.get("KSTOP", "9"))
        kind = tile[0]
        N = 512 if kind == "p" else self.NS * DEC_SEQ
        if self.stop < 1:
            return
        self.fw.mark("tile %s %d %d" % tile)
        self.load_x(tile, N)
        self.fw.mark("ffn00")
        ksub = os.environ.get("KSUB", "")
        if ksub == "load":
            self.dump_x("ffn00", tile, N)
            return
        if ksub == "ln":
            self.layer_norm(0, N)
            self.dump_x("ffn00", tile, N)
            return
        self.ffn(0, 0, N)
        self.dump_x("ffn00", tile, N)
        if self.stop < 2:
            return
        self.fw.mark("mixer0")
        self.mixer0(tile, N)
        self.dump_x("mix0", tile, N)
        self.fw.mark("ffn01")
        self.ffn(0, 1, N)
        self.dump_x("l0", tile, N)
        if self.layers > 1:
            self.fw.mark("ffn10")
            self.ffn(1, 0, N)
            self.fw.mark("mixer1")
            self.mixer1(tile, N)
            self.dump_x("mix1", tile, N)
            self.fw.mark("ffn11")
            self.ffn(1, 1, N)
        self.fw.mark("store")
        self.store_y(tile, N)
        self.fw.mark("end")


def _ts(h, out, in0, s1, s2, op0, op1=None):
    if op1 is None:
        return h.tensor_scalar(out=out, in0=in0, scalar1=s1, scalar2=None, op0=op0)
    return h.tensor_scalar(out=out, in0=in0, scalar1=s1, scalar2=s2, op0=op0, op1=op1)


def setup_mix0(self):
    fw, I = self.fw, self.I
    C = self.Bconst
    dve = lambda fn, r=(C,), w=(C,): fw.op("dve", fn, reads=list(r), writes=list(w))
    self.s5d = fw.sbuf([128, 4], F32, "s5d")
    self.bglu = fw.sbuf([128, 4], F32, "bglu")
    self.load_T(I["s5_d"].rearrange("o (c p) -> (o c) p", p=128), 4, self.s5d[:], C)
    self.load_T(I["s5_b_glu"].rearrange("o (c p) -> (o c) p", p=128), 4, self.bglu[:], C)
    self.negones = fw.sbuf([128, 128], BF16, "negones")
    self.nuincl = fw.sbuf([128, 128], BF16, "nuincl")
    fw.op("pool", lambda h: h.memset(self.negones[:], -1.0), writes=[C])
    fw.op("pool", lambda h: h.tensor_scalar(out=self.nuincl[:], in0=self.uincl[:], scalar1=-1.0, scalar2=None, op0=ALU.mult),
          reads=[C], writes=[C])
    L = LSUB
    pr = self.ma[:, 3712:3904].rearrange("p (i j) -> p i j", i=12)
    P = lambda i: pr[:, i, :]
    self.load_T(I["s5_a_re"].rearrange("o (j g) n -> (o j) (g n)", g=2), 16, P(0), C)
    self.load_T(I["s5_a_im"].rearrange("o (j g) n -> (o j) (g n)", g=2), 16, P(1), C)
    ld = I["s5_log_dt"].rearrange("o (j t) -> o j t", t=2)
    fw.dma("sp", pr[0:64, 2, :], ld[0:1, :, 0].broadcast_to([64, 16]), C, writes=[C], slow=True)
    fw.dma("sp", pr[64:128, 2, :], ld[0:1, :, 1].broadcast_to([64, 16]), C, writes=[C], slow=True)
    fw.op("act", lambda h: h.activation(out=P(2), in_=P(2), func=AF.Exp), reads=[C], writes=[C])
    dve(lambda h: h.tensor_tensor(out=P(3), in0=P(0), in1=P(2), op=ALU.mult))
    dve(lambda h: h.tensor_tensor(out=P(4), in0=P(1), in1=P(2), op=ALU.mult))
    self.s5r = fw.sbuf([128, 16], F32, "s5r")
    fw.op("act", lambda h: h.activation(out=self.s5r[:], in_=P(3), func=AF.Exp), reads=[C], writes=[C])
    dve(lambda h: _ts(h, P(5), P(4), 1.0 / (2 * math.pi), MAGIC, ALU.mult, ALU.add))
    dve(lambda h: _ts(h, P(5), P(5), MAGIC, None, ALU.subtract))
    dve(lambda h: h.scalar_tensor_tensor(out=P(5), in0=P(5), scalar=-2 * math.pi, in1=P(4), op0=ALU.mult, op1=ALU.add))
    dve(lambda h: _ts(h, P(5), P(5), 0.125, None, ALU.mult))
    dve(lambda h: h.tensor_tensor(out=P(6), in0=P(5), in1=P(5), op=ALU.mult))
    sc = [-1.0 / 6, 1.0 / 120, -1.0 / 5040, 1.0 / 362880]
    cc = [-0.5, 1.0 / 24, -1.0 / 720, 1.0 / 40320, -1.0 / 3628800]
    dve(lambda h: _ts(h, P(7), P(6), sc[3], sc[2], ALU.mult, ALU.add))
    for co in (sc[1], sc[0], 1.0):
        dve(lambda h: h.tensor_tensor(out=P(7), in0=P(7), in1=P(6), op=ALU.mult))
        dve(lambda h, co=co: _ts(h, P(7), P(7), co, None, ALU.add))
    dve(lambda h: h.tensor_tensor(out=P(7), in0=P(7), in1=P(5), op=ALU.mult))
    dve(lambda h: _ts(h, P(8), P(6), cc[4], cc[3], ALU.mult, ALU.add))
    for co in (cc[2], cc[1], cc[0], 1.0):
        dve(lambda h: h.tensor_tensor(out=P(8), in0=P(8), in1=P(6), op=ALU.mult))
        dve(lambda h, co=co: _ts(h, P(8), P(8), co, None, ALU.add))
    for _ in range(3):
        dve(lambda h: h.tensor_tensor(out=P(9), in0=P(8), in1=P(7), op=ALU.mult))
        dve(lambda h: h.tensor_tensor(out=P(8), in0=P(8), in1=P(8), op=ALU.mult))
        dve(lambda h: h.tensor_tensor(out=P(7), in0=P(7), in1=P(7), op=ALU.mult))
        dve(lambda h: h.tensor_tensor(out=P(8), in0=P(8), in1=P(7), op=ALU.subtract))
        dve(lambda h: _ts(h, P(7), P(9), 2.0, None, ALU.mult))
    self.Ct = fw.sbuf([128, 16, L + 1], F32, "Ct")
    self.St = fw.sbuf([128, 16, L + 1], F32, "St")
    Ct, St = self.Ct, self.St
    dve(lambda h: h.memset(Ct[:, :, 0:1], 1.0))
    dve(lambda h: h.memset(St[:, :, 0:1], 0.0))
    dve(lambda h: h.tensor_copy(out=Ct[:, :, 1:2], in_=P(8).unsqueeze(2)))
    dve(lambda h: h.tensor_copy(out=St[:, :, 1:2], in_=P(7).unsqueeze(2)))
    tmpA = self.ma[:, 0:2048].rearrange("p (j l) -> p j l", j=16)
    n = 2
    while n <= L:
        hn = n // 2
        ch, sh = Ct[:, :, hn:hn + 1], St[:, :, hn:hn + 1]
        dve(lambda h, ch=ch, sh=sh: h.tensor_tensor(out=P(9).unsqueeze(2), in0=ch, in1=sh, op=ALU.mult))
        dve(lambda h, ch=ch: h.tensor_tensor(out=P(10).unsqueeze(2), in0=ch, in1=ch, op=ALU.mult))
        dve(lambda h, sh=sh: h.tensor_tensor(out=P(11).unsqueeze(2), in0=sh, in1=sh, op=ALU.mult))
        dve(lambda h, n=n: h.tensor_tensor(out=Ct[:, :, n:n + 1], in0=P(10).unsqueeze(2), in1=P(11).unsqueeze(2), op=ALU.subtract))
        dve(lambda h, n=n: _ts(h, St[:, :, n:n + 1], P(9).unsqueeze(2), 2.0, None, ALU.mult))
        m = min(n, L + 1 - n)
        if m > 1:
            cn = bc(Ct[:, :, n:n + 1], [128, 16, m - 1])
            sn = bc(St[:, :, n:n + 1], [128, 16, m - 1])
            c0, s0 = Ct[:, :, 1:m], St[:, :, 1:m]
            tA = tmpA[:, :, 0:m - 1]
            dve(lambda h, c0=c0, cn=cn, tA=tA: h.tensor_tensor(out=tA, in0=c0, in1=cn, op=ALU.mult))
            dve(lambda h, s0=s0, sn=sn, n=n, m=m: h.tensor_tensor(out=Ct[:, :, n + 1:n + m], in0=s0, in1=sn, op=ALU.mult))
            dve(lambda h, tA=tA, n=n, m=m: h.tensor_tensor(out=Ct[:, :, n + 1:n + m], in0=tA, in1=Ct[:, :, n + 1:n + m], op=ALU.subtract))
            dve(lambda h, s0=s0, cn=cn, tA=tA: h.tensor_tensor(out=tA, in0=s0, in1=cn, op=ALU.mult))
            dve(lambda h, c0=c0, sn=sn, n=n, m=m: h.tensor_tensor(out=St[:, :, n + 1:n + m], in0=c0, in1=sn, op=ALU.mult))
            dve(lambda h, tA=tA, n=n, m=m: h.tensor_tensor(out=St[:, :, n + 1:n + m], in0=tA, in1=St[:, :, n + 1:n + m], op=ALU.add))
        n *= 2
    dve(lambda h: h.tensor_tensor(out=P(9), in0=self.s5r[:], in1=P(8), op=ALU.mult))
    dve(lambda h: h.tensor_tensor(out=P(10), in0=self.s5r[:], in1=P(7), op=ALU.mult))
    dve(lambda h: _ts(h, P(9), P(9), -1.0, None, ALU.add))
    dve(lambda h: h.tensor_tensor(out=P(2), in0=P(0), in1=P(0), op=ALU.mult))
    dve(lambda h: h.tensor_tensor(out=P(3), in0=P(1), in1=P(1), op=ALU.mult))
    dve(lambda h: h.tensor_tensor(out=P(2), in0=P(2), in1=P(3), op=ALU.add))
    dve(lambda h: h.reciprocal(out=P(2), in_=P(2)))
    dve(lambda h: h.tensor_tensor(out=P(3), in0=P(9), in1=P(0), op=ALU.mult))
    dve(lambda h: h.tensor_tensor(out=P(4), in0=P(10), in1=P(1), op=ALU.mult))
    dve(lambda h: h.tensor_tensor(out=P(3), in0=P(3), in1=P(4), op=ALU.add))
    dve(lambda h: h.tensor_tensor(out=P(5), in0=P(3), in1=P(2), op=ALU.mult))
    dve(lambda h: h.tensor_tensor(out=P(3), in0=P(10), in1=P(0), op=ALU.mult))
    dve(lambda h: h.tensor_tensor(out=P(4), in0=P(9), in1=P(1), op=ALU.mult))
    dve(lambda h: h.tensor_tensor(out=P(3), in0=P(3), in1=P(4), op=ALU.subtract))
    dve(lambda h: h.tensor_tensor(out=P(6), in0=P(3), in1=P(2), op=ALU.mult))
    braw = self.ma[:, 2048:3072].rearrange("p (t j q) -> p t j q", t=4, j=16)
    for t, nm in ((0, "s5_b_re"), (1, "s5_b_im")):
        fw.dma("sp", braw[:, t, :, :], I[nm][0].rearrange("g n q -> (g n) q").rearrange("(j p) q -> p j q", p=128), C, writes=[C])
    fre = bc(P(5).unsqueeze(2), [128, 16, 16])
    fim = bc(P(6).unsqueeze(2), [128, 16, 16])
    t3 = tmpA[:, :, 0:16]
    dve(lambda h: h.tensor_tensor(out=braw[:, 2], in0=braw[:, 0], in1=fre, op=ALU.mult))
    dve(lambda h: h.tensor_tensor(out=t3, in0=braw[:, 1], in1=fim, op=ALU.mult))
    dve(lambda h: h.tensor_tensor(out=braw[:, 2], in0=braw[:, 2], in1=t3, op=ALU.subtract))
    dve(lambda h: h.tensor_tensor(out=braw[:, 3], in0=braw[:, 1], in1=fre, op=ALU.mult))
    dve(lambda h: h.tensor_tensor(out=t3, in0=braw[:, 0], in1=fim, op=ALU.mult))
    dve(lambda h: h.tensor_tensor(out=braw[:, 3], in0=braw[:, 3], in1=t3, op=ALU.add))
    if self.dbg:
        o = self.dbg_out("s5par", [128, 12, 16])
        fw.dma("sp", o, pr, C, reads=[C])
        o = self.dbg_out("s5ct", [128, 16, L + 1])
        fw.dma("sp", o, Ct[:], C, reads=[C])
        o = self.dbg_out("s5st", [128, 16, L + 1])
        fw.dma("sp", o, St[:], C, reads=[C])
        o = self.dbg_out("s5r", [128, 16])
        fw.dma("sp", o, self.s5r[:], C, reads=[C])
        o = self.dbg_out("s5braw", [128, 4, 16, 16])
        fw.dma("sp", o, braw, C, reads=[C])
    self.LBc = fw.sbuf([128, 8, 128], BF16, "LBc")
    self.LCc = fw.sbuf([128, 16, 2, 32], BF16, "LCc")
    mj = self.ma[:, 3584:3712]
    Bmj = Buf("mj")
    for j in range(16):
        c, q = j // 4, j % 4
        for reim in range(2):
            fw.op("dve", lambda h: h.memset(mj[:], 0.0), writes=[Bmj])
            g0 = (2 * j) % 8
            fw.op("dve", lambda h, j=j, reim=reim, g0=g0: h.tensor_copy(out=mj[0:64, g0 * 16:g0 * 16 + 16], in_=braw[0:64, 2 + reim, j, :]),
                  reads=[C], writes=[Bmj])
            fw.op("dve", lambda h, j=j, reim=reim, g0=g0: h.tensor_copy(out=mj[64:128, (g0 + 1) * 16:(g0 + 1) * 16 + 16], in_=braw[64:128, 2 + reim, j, :]),
                  reads=[C], writes=[Bmj])
            fw.op("pe", lambda h: h.transpose(self.PS[7][:, 0:128], mj[:], self.ident[:]), reads=[Bmj, C], writes=[self.BP[7]])
            fw.op("dve", lambda h, c=c, q=q, reim=reim: h.tensor_copy(out=self.LBc[32 * q:32 * q + 32, c * 2 + reim, :], in_=self.PS[7][32 * q:32 * q + 32, 0:128]),
                  reads=[self.BP[7]], writes=[C])
    craw = self.ma[:, 3072:3584].rearrange("p (t c n) -> p t c n", t=2, c=4)
    for t, nm in ((0, "s5_c_re"), (1, "s5_c_im")):
        fw.dma("sp", craw[:, t, :, :], I[nm][0].rearrange("g p n -> (g p) n").rearrange("(c q) n -> q c n", q=128), C, writes=[C])
    e8 = fw.sbuf([128, 2, 8], F32, "e8")
    fw.op("pool", lambda h: h.memset(e8[:, 0, :], 1.0), reads=[C], writes=[C])
    fw.op("pool", lambda h: h.affine_select(out=e8[:, 0, :], in_=e8[:, 0, :], pattern=[[-16, 8]], compare_op=ALU.is_ge, fill=0.0,
                                            base=0, channel_multiplier=1), reads=[C], writes=[C])
    fw.op("pool", lambda h: h.affine_select(out=e8[:, 0, :], in_=e8[:, 0, :], pattern=[[16, 8]], compare_op=ALU.is_ge, fill=0.0,
                                            base=15, channel_multiplier=-1), reads=[C], writes=[C])
    fw.op("pool", lambda h: h.tensor_scalar(out=e8[:, 1, :], in0=e8[:, 0, :], scalar1=-1.0, scalar2=None, op0=ALU.mult), reads=[C], writes=[C])
    for j in range(16):
        c, q = j // 4, j % 4
        g0 = (2 * j) % 8
        for reim in range(2):
            for gl in range(2):
                fw.op("dve", lambda h, c=c, reim=reim, gl=gl, g0=g0: h.tensor_scalar(
                    out=mj[:, gl * 64:(gl + 1) * 64], in0=craw[:, reim, c, :], scalar1=e8[:, reim, g0 + gl:g0 + gl + 1], scalar2=None, op0=ALU.mult),
                    reads=[C], writes=[Bmj])
            fw.op("pe", lambda h: h.transpose(self.PS[7][:, 0:128], mj[:], self.ident[:]), reads=[Bmj, C], writes=[self.BP[7]])
            fw.op("dve", lambda h, j=j, q=q, reim=reim: h.tensor_copy(out=self.LCc[:, j, reim, :], in_=self.PS[7][:, 32 * q:32 * q + 32]),
                  reads=[self.BP[7]], writes=[C])
    self.G = fw.sbuf([128, 16, 2], F32, "s5G")
    self.BG = Buf("s5G", strict=True)
    self.HL = fw.sbuf([128, 2, 16], F32, "s5HL")
    self.BHL = Buf("s5HL")
    self.s5tmp = fw.sbuf([128, 4], F32, "s5tmp")
    self.Bs5tmp = Buf("s5tmp")
    self.crowb2 = [self.lnt[0:1, 2 + i, :].bitcast(BF16).rearrange("p (a n) -> p a n", a=2) for i in range(2)]
    self.Bcrowb2 = [Buf("crowb0"), Buf("crowb1")]
    self.Bcrow = Buf("crow")
    self.Bcrowb = Buf("crowb")


def s5_phase(self, tile, N):
    fw = self.fw
    kind, s, ti = tile
    C = self.Bconst
    PS, BP = self.PS, self.BP
    ma, mab, hbf = self.ma, self.mab, self.hbf
    uf = hbf[:, 0:2048].rearrange("p (c n) -> p c n", c=4)
    yf = hbf[:, 2048:4096].rearrange("p (c n) -> p c n", c=4)
    ub = self.hb[:, 16:20, :]
    s5out = mab[:, 2048:4096].rearrange("p (c n) -> p c n", c=4)
    T = [ma[:, 3072 + 512 * i:3072 + 512 * (i + 1)] for i in range(4)]
    BT = self.BT
    hre = [mab[:, 10240 + 1024 * i:10240 + 1024 * i + 512] for i in range(2)]
    him = [mab[:, 10240 + 1024 * i + 512:10240 + 1024 * (i + 1)] for i in range(2)]
    Bhh = self.Bhh
    Ct, St = self.Ct, self.St
    if kind == "p":
        nsub, L = 512 // LSUB, LSUB
    else:
        nsub, L = self.NS, DEC_SEQ
    dve = lambda fn, r, w: fw.op("dve", fn, reads=list(r), writes=list(w))
    last_tile = (kind == "s") or (ti == self.SEQ // 512 - 1)
    if kind == "p" and ti == 0:
        dve(lambda h: h.memset(self.G[:], 0.0), [], [self.BG])
    for j in range(16):
        c, q = j // 4, j % 4
        pa, pb = PS[0], PS[1]
        rhs = ub[32 * q:32 * q + 32, c, 0:N]
        self.mm(pa[:, 0:N], self.LBc[32 * q:32 * q + 32, c * 2 + 0, :], rhs, True, True, [C, self.Bub], [BP[0]], tile_position=(32 * q, 0))
        self.mm(pb[:, 0:N], self.LBc[32 * q:32 * q + 32, c * 2 + 1, :], rhs, True, True, [C, self.Bub], [BP[1]], tile_position=(32 * q, 0))
        v3 = lambda ap: ap[:, 0:N].rearrange("p (s l) -> p s l", s=nsub)
        ct = bc(Ct[:, j:j + 1, 0:L], [128, nsub, L])
        st = bc(St[:, j:j + 1, 0:L], [128, nsub, L])
        A, Bb, Dd, E = T
        dve(lambda h: h.tensor_tensor(out=v3(A), in0=v3(pa), in1=ct, op=ALU.mult), [BP[0], C], [BT[0]])
        dve(lambda h: h.tensor_tensor(out=v3(Bb), in0=v3(pb), in1=st, op=ALU.mult), [BP[1], C], [BT[1]])
        dve(lambda h: h.tensor_tensor(out=A[:, 0:N], in0=A[:, 0:N], in1=Bb[:, 0:N], op=ALU.add), [BT[0], BT[1]], [BT[0]])
        dve(lambda h: h.tensor_tensor(out=v3(Dd), in0=v3(pb), in1=ct, op=ALU.mult), [BP[1], C], [BT[2]])
        dve(lambda h: h.tensor_tensor(out=v3(Bb), in0=v3(pa), in1=st, op=ALU.mult), [BP[0], C], [BT[1]])
        dve(lambda h: h.tensor_tensor(out=Dd[:, 0:N], in0=Dd[:, 0:N], in1=Bb[:, 0:N], op=ALU.subtract), [BT[2], BT[1]], [BT[2]])
        if j == 5 and kind == "p" and self.dbg:
            dve(lambda h: h.tensor_copy(out=E[:, 0:N], in_=pa[:, 0:N]), [BP[0]], [BT[3]])
            self.dbg_sb("bu_re", E[:, 0:N], BT[3], [128, N])
            dve(lambda h: h.tensor_copy(out=self.sg[0][:, 0:256].rearrange("p (a b) -> p a b", a=2), in_=self.LBc[:, 2:4, :]), [C], [self.Bsg[0]])
            self.dbg_sb("lbc", self.sg[0][:, 0:256], self.Bsg[0], [128, 256])
            self.dbg_sb("ub", ub[:, :, 0:N].bitcast(mybir.dt.uint16), self.Bub, [128, 4, N]) if False else None
        if j == 5 and kind == "p":
            self.dbg_sb("ut_re", A[:, 0:N], BT[0], [128, N])
            self.dbg_sb("ut_im", Dd[:, 0:N], BT[2], [128, N])
            self.dbg_sb("uf", uf[:, :, 0:N], self.Buf_uf, [128, 4, N])
        rr = bc(self.s5r[:, j:j + 1], [128, L])
        for sub in range(nsub):
            sl = slice(sub * L, (sub + 1) * L)
            if kind == "s":
                gi = self.Gs[:, sub, j, :]
                BGi = self.BGs
            else:
                gi = self.G[:, j, :]
                BGi = self.BG
            dve(lambda h, sl=sl, gi=gi: h.tensor_tensor_scan(out=Bb[:, sl], data0=rr, data1=A[:, sl], initial=gi[:, 0:1], op0=ALU.mult, op1=ALU.add),
                [BT[0], BGi, C], [BT[1]])
            dve(lambda h, sl=sl, gi=gi: h.tensor_tensor_scan(out=E[:, sl], data0=rr, data1=Dd[:, sl], initial=gi[:, 1:2], op0=ALU.mult, op1=ALU.add),
                [BT[2], BGi, C], [BT[3]])
            e0 = (sub + 1) * L - 1
            gre_l, gim_l = Bb[:, e0:e0 + 1], E[:, e0:e0 + 1]
            tmp = self.s5tmp
            if kind == "p":
                cL, sL = Ct[:, j, L:L + 1], St[:, j, L:L + 1]
                dve(lambda h, gim_l=gim_l, sL=sL: h.tensor_tensor(out=tmp[:, 0:1], in0=gim_l, in1=sL, op=ALU.mult), [BT[3], C], [self.Bs5tmp])
                dve(lambda h, gre_l=gre_l, sL=sL: h.tensor_tensor(out=tmp[:, 1:2], in0=gre_l, in1=sL, op=ALU.mult), [BT[1], C], [self.Bs5tmp])
                dve(lambda h, gre_l=gre_l, cL=cL, j=j: h.scalar_tensor_tensor(out=self.G[:, j, 0:1], in0=gre_l, scalar=cL, in1=tmp[:, 0:1], op0=ALU.mult, op1=ALU.subtract),
                    [BT[1], C, self.Bs5tmp], [self.BG])
                dve(lambda h, gim_l=gim_l, cL=cL, j=j: h.scalar_tensor_tensor(out=self.G[:, j, 1:2], in0=gim_l, scalar=cL, in1=tmp[:, 1:2], op0=ALU.mult, op1=ALU.add),
                    [BT[3], C, self.Bs5tmp], [self.BG])
            if last_tile and (kind == "s" or sub == nsub - 1):
                c1, s1 = Ct[:, j, L - 1:L], St[:, j, L - 1:L]
                if kind == "s":
                    ore, oim = self.HLs[:, sub, 0, j:j + 1], self.HLs[:, sub, 1, j:j + 1]
                else:
                    ore, oim = self.HL[:, 0, j:j + 1], self.HL[:, 1, j:j + 1]
                dve(lambda h, gim_l=gim_l, s1=s1: h.tensor_tensor(out=tmp[:, 2:3], in0=gim_l, in1=s1, op=ALU.mult), [BT[3], C], [self.Bs5tmp])
                dve(lambda h, gre_l=gre_l, s1=s1: h.tensor_tensor(out=tmp[:, 3:4], in0=gre_l, in1=s1, op=ALU.mult), [BT[1], C], [self.Bs5tmp])
                dve(lambda h, gre_l=gre_l, c1=c1, ore=ore: h.scalar_tensor_tensor(out=ore, in0=gre_l, scalar=c1, in1=tmp[:, 2:3], op0=ALU.mult, op1=ALU.subtract),
                    [BT[1], C, self.Bs5tmp], [self.BHL])
                dve(lambda h, gim_l=gim_l, c1=c1, oim=oim: h.scalar_tensor_tensor(out=oim, in0=gim_l, scalar=c1, in1=tmp[:, 3:4], op0=ALU.mult, op1=ALU.add),
                    [BT[3], C, self.Bs5tmp], [self.BHL])
        if j == 5 and kind == "p":
            self.dbg_sb("g_re", Bb[:, 0:N], BT[1], [128, N])
            self.dbg_sb("g_im", E[:, 0:N], BT[3], [128, N])
        hr, hi = hre[j % 2], him[j % 2]
        Bh2 = Bhh[j % 2]
        dve(lambda h: h.tensor_tensor(out=v3(A), in0=v3(Bb), in1=ct, op=ALU.mult), [BT[1], C], [BT[0]])
        dve(lambda h: h.tensor_tensor(out=v3(Dd), in0=v3(E), in1=st, op=ALU.mult), [BT[3], C], [BT[2]])
        dve(lambda h, hr=hr: h.tensor_tensor(out=hr[:, 0:N], in0=A[:, 0:N], in1=Dd[:, 0:N], op=ALU.subtract), [BT[0], BT[2]], [Bh2])
        dve(lambda h: h.tensor_tensor(out=v3(A), in0=v3(E), in1=ct, op=ALU.mult), [BT[3], C], [BT[0]])
        dve(lambda h: h.tensor_tensor(out=v3(Dd), in0=v3(Bb), in1=st, op=ALU.mult), [BT[1], C], [BT[2]])
        dve(lambda h, hi=hi: h.tensor_tensor(out=hi[:, 0:N], in0=A[:, 0:N], in1=Dd[:, 0:N], op=ALU.add), [BT[0], BT[2]], [Bh2])
        py, Bpy = PS[2 + c % 2], BP[2 + c % 2]
        self.mm(py[32 * q:32 * q + 32, 0:N], self.LCc[:, j, 0, :], hr[:, 0:N], True, False, [C, Bh2], [Bpy], tile_position=(0, 32 * q))
        self.mm(py[32 * q:32 * q + 32, 0:N], self.LCc[:, j, 1, :], hi[:, 0:N], False, True, [C, Bh2], [Bpy], tile_position=(0, 32 * q))
        if q == 3:
            dve(lambda h, c=c, py=py: h.scalar_tensor_tensor(out=yf[:, c, 0:N], in0=uf[:, c, 0:N], scalar=self.s5d[:, c:c + 1], in1=py[:, 0:N],
                                                            op0=ALU.mult, op1=ALU.add), [self.Buf_uf, Bpy, C], [self.Buf_yf])
    if last_tile:
        nseq = self.NS if kind == "s" else 1
        for sq in range(nseq):
            for reim in range(2):
                src = self.HLs[:, sq, reim, :] if kind == "s" else self.HL[:, reim, :]
                fw.op("pe", lambda h, src=src: h.transpose(PS[7][0:16, 0:128], src, self.ident[:]), reads=[self.BHL, C], writes=[BP[7]])
                fw.op("dve", lambda h: h.tensor_copy(out=self.stg[0:16, :], in_=PS[7][0:16, 0:128]), reads=[BP[7]], writes=[self.Bstg])
                nm = ("s5_re_" if reim == 0 else "s5_im_") + ("sample" if kind == "s" else "prompt")
                seq = sq if kind == "s" else s
                dst = self.O[nm][0, seq].rearrange("(j g) n -> j (g n)", g=2)
                fw.dma("sp", dst, self.stg[0:16, :], self.Bstg, reads=[self.Bstg])
    tmpf = ma[:, 3072:5120].rearrange("p (c n) -> p c n", c=4)
    yv, tv = yf[:, :, 0:N], tmpf[:, :, 0:N]
    gb = ub
    dve(lambda h: h.tensor_tensor(out=tv, in0=yv, in1=yv, op=ALU.mult), [self.Buf_yf], BT)
    dve(lambda h: _ts(h, tv, tv, 0.044715, 1.0, ALU.mult, ALU.add), BT, BT)
    dve(lambda h: h.tensor_tensor(out=tv, in0=tv, in1=yv, op=ALU.mult), BT + [self.Buf_yf], BT)
    fw.op("act", lambda h: h.activation(out=tv, in_=tv, func=AF.Sigmoid, scale=2.0 * math.sqrt(2.0 / math.pi)), reads=BT, writes=BT)
    dve(lambda h: h.tensor_tensor(out=yv, in0=yv, in1=tv, op=ALU.mult), BT + [self.Buf_yf], [self.Buf_yf])
    fw.op("pool", lambda h: h.tensor_copy(out=gb[:, :, 0:N], in_=yv), reads=[self.Buf_yf], writes=[self.Bub])
    wglu = self.S["s5_w_glu"][0].rearrange("(k p) f -> p k f", p=128)
    t, B = self.get_slab([(lambda t: t[:, 0:2048].rearrange("p (k f) -> p k f", k=4), wglu, "s5_w_glu")])
    tg = t[:, 0:2048].rearrange("p (k f) -> p k f", k=4)
    for oc in range(4):
        pg, Bg = PS[oc % 2], BP[oc % 2]
        for k in range(4):
            self.mm(pg[:, 0:N], tg[:, k, oc * 128:(oc + 1) * 128], gb[:, k, 0:N], k == 0, k == 3, [B, self.Bub], [Bg])
        sg, Bs = self.sg[oc % 2], self.Bsg[oc % 2]
        fw.op("act", lambda h, sg=sg, pg=pg, oc=oc: h.activation(out=sg[:, 0:N], in_=pg[:, 0:N], func=AF.Sigmoid, bias=self.bglu[:, oc:oc + 1]),
              reads=[Bg, C], writes=[Bs])
        dve(lambda h, sg=sg, oc=oc: h.tensor_tensor(out=s5out[:, oc, 0:N], in0=yf[:, oc, 0:N], in1=sg[:, 0:N], op=ALU.mult),
            [Bs, self.Buf_yf], [self.Bs5out])


def attn_head_loop(self, N, qcols, blocks, diag_base):
    fw = self.fw
    C = self.Bconst
    PS, BP = self.PS, self.BP
    ma, mab = self.ma, self.mab
    QT = mab[:, 0:2048].rearrange("p (c n) -> p c n", c=4)
    attno = mab[:, 4096:6144].rearrange("p (c n) -> p c n", c=4)
    e_ = [ma[:, 3072 + 512 * i:3072 + 512 * (i + 1)] for i in range(2)]
    spb = [mab[:, 8192 + 512 * i:8192 + 512 * (i + 1)] for i in range(2)]
    wb = [mab[:, 9216 + 512 * i:9216 + 512 * (i + 1)] for i in range(2)]
    Be, Bsp, Bw = self.Be, self.Bsp, self.Bw
    crb = [self.lnt[0:33, 2 + i, :].bitcast(BF16)[:, 0:512] for i in range(2)]
    Bcr = self.Bcrowb2
    q0, q1 = qcols
    nblk = len(blocks)
    for hp in range(4):
        c = hp
        po, Bpo = PS[7], BP[7]
        pst, Bpt = PS[6], BP[6]
        pss = [PS[0], PS[1]]
        Bps = [BP[0], BP[1]]

        def qk(bi):
            k0, kr, vblk, jj = blocks[bi]
            Bkt = self.BKT[min(k0 // 512, 7)]
            for x in range(2):
                hb_ = 64 * x
                self.mm(pss[x][0:kr, 0:N], self.KT[hb_:hb_ + 64, c, k0:k0 + kr], QT[hb_:hb_ + 64, c, q0:q1], True, True,
                        [Bkt, self.BQT], [Bps[x]], tile_position=(hb_, 0))

        qk(0)
        for bi, (k0, kr, vblk, jj) in enumerate(blocks):
            Bvt = self.BVT[min(vblk // 4, 7)]
            psc = [PS[2 + 2 * (bi % 2)], PS[3 + 2 * (bi % 2)]]
            Bpc = [BP[2 + 2 * (bi % 2)], BP[3 + 2 * (bi % 2)]]
            for x in range(2):
                fw.op("act", lambda h, x=x: h.activation(out=e_[x][0:kr, 0:N], in_=pss[x][0:kr, 0:N], func=AF.Exp), reads=[Bps[x]], writes=[Be[x]])
                if jj is not None:
                    fw.op("pool", lambda h, x=x: h.affine_select(out=e_[x][0:kr, 0:N], in_=e_[x][0:kr, 0:N], pattern=[[1, N]], compare_op=ALU.is_gt,
                                                                 fill=0.0, base=-128 * jj, channel_multiplier=-1), reads=[Be[x]], writes=[Be[x]])
                fw.op("act", lambda h, x=x: h.activation(out=spb[x][0:kr, 0:N], in_=e_[x][0:kr, 0:N], func=AF.Ln, bias=1.0), reads=[Be[x]], writes=[Bsp[x]])
                self.mm(psc[x][0:kr, 0:N], self.nuincl[0:kr, 0:kr], spb[x][0:kr, 0:N], True, bi == 0, [C, Bsp[x]], [Bpc[x]])
                if bi > 0:
                    cbp, Bcbp = crb[(bi - 1) % 2], Bcr[(bi - 1) % 2]
                    self.mm(psc[x][0:kr, 0:N], self.negones[32 * x:32 * x + 1, 0:kr], cbp[32 * x:32 * x + 1, 0:N], False, True,
                            [C, Bcbp], [Bpc[x]], tile_position=(32 * x, 0))
            if bi < nblk - 1:
                for x in range(2):
                    self.mm(pst[32 * x:32 * x + 1, 0:N], self.ones1[0:kr, 0:1], spb[x][0:kr, 0:N], bi == 0, bi == nblk - 2,
                            [C, Bsp[x]], [Bpt], tile_position=(0, 32 * x))
                cb_, Bcb = crb[bi % 2], Bcr[bi % 2]
                fw.op("dve", lambda h, cb_=cb_: h.tensor_copy(out=cb_[0:33, 0:N], in_=pst[0:33, 0:N]), reads=[Bpt], writes=[Bcb])
                qk(bi + 1)
            for x in range(2):
                fw.op("act", lambda h, x=x: h.activation(out=psc[x][0:kr, 0:N], in_=psc[x][0:kr, 0:N], func=AF.Exp), reads=[Bpc[x]], writes=[Bpc[x]])
                fw.op("dve", lambda h, x=x: h.tensor_tensor(out=wb[x][0:kr, 0:N], in0=psc[x][0:kr, 0:N], in1=e_[x][0:kr, 0:N], op=ALU.mult),
                      reads=[Bpc[x], Be[x]], writes=[Bw[x]])
            for x in range(2):
                hd = 2 * hp + x
                self.mm(po[64 * x:64 * x + 64, 0:N], self.VT[0:kr, vblk, hd * 64:(hd + 1) * 64], wb[x][0:kr, 0:N], bi == 0, bi == nblk - 1,
                        [Bvt, Bw[x]], [Bpo], tile_position=(0, 64 * x))
        fw.op("dve", lambda h, c=c: h.tensor_copy(out=attno[:, c, q0:q1], in_=po[:, 0:N]), reads=[Bpo], writes=[self.Battno])


def attn_sample_loop(self, qcols, blocks):
    fw = self.fw
    C = self.Bconst
    PS, BP = self.PS, self.BP
    ma, mab = self.ma, self.mab
    QT = mab[:, 0:2048].rearrange("p (c n) -> p c n", c=4)
    attno = mab[:, 4096:6144].rearrange("p (c n) -> p c n", c=4)
    e_ = [ma[:, 3072 + 512 * i:3072 + 512 * (i + 1)] for i in range(2)]
    spb = [mab[:, 8192 + 512 * i:8192 + 512 * (i + 1)] for i in range(2)]
    wb = [mab[:, 9216 + 512 * i:9216 + 512 * (i + 1)] for i in range(2)]
    Be, Bsp, Bw = self.Be, self.Bsp, self.Bw
    crb = [self.lnt[0:1, 2 + i, :].bitcast(BF16)[:, 0:512] for i in range(2)]
    Bcr = self.Bcrowb2
    q0, q1 = qcols
    NQ = q1 - q0
    W = 8 * NQ
    nblk = len(blocks)
    po, Bpo = PS[7], BP[7]
    pst, Bpt = PS[6], BP[6]

    def qk(bi):
        k0, kr, vblk, jj = blocks[bi]
        Bkt = self.BKT[min(k0 // 512, 7)]
        for hd in range(8):
            c, xx = hd // 2, hd % 2
            hb_ = 64 * xx
            pss, Bps = PS[2 * (bi % 2) + xx], BP[2 * (bi % 2) + xx]
            self.mm(pss[0:kr, c * NQ:(c + 1) * NQ], self.KT[hb_:hb_ + 64, c, k0:k0 + kr], QT[hb_:hb_ + 64, c, q0:q1], True, True,
                    [Bkt, self.BQT], [Bps], tile_position=(hb_, 0))

    qk(0)
    for bi, (k0, kr, vblk, jj) in enumerate(blocks):
        x = bi % 2
        Bvt = self.BVT[min(vblk // 4, 7)]
        psc, Bpc = PS[4 + x], BP[4 + x]
        H4 = 4 * NQ
        for xx in range(2):
            pss, Bps = PS[2 * x + xx], BP[2 * x + xx]
            fw.op("act", lambda h, xx=xx, pss=pss: h.activation(out=e_[x][0:kr, xx * H4:(xx + 1) * H4], in_=pss[0:kr, 0:H4], func=AF.Exp),
                  reads=[Bps], writes=[Be[x]])
        if jj is not None:
            fw.op("pool", lambda h: h.affine_select(out=e_[x][0:kr, 0:W], in_=e_[x][0:kr, 0:W], pattern=[[0, 8], [1, NQ]], compare_op=ALU.is_gt,
                                                    fill=0.0, base=-128 * jj, channel_multiplier=-1), reads=[Be[x]], writes=[Be[x]])
        fw.op("act", lambda h: h.activation(out=spb[x][0:kr, 0:W], in_=e_[x][0:kr, 0:W], func=AF.Ln, bias=1.0), reads=[Be[x]], writes=[Bsp[x]])
        self.mm(psc[0:kr, 0:W], self.nuincl[0:kr, 0:kr], spb[x][0:kr, 0:W], True, bi == 0, [C, Bsp[x]], [Bpc])
        if bi > 0:
            self.mm(psc[0:kr, 0:W], self.negones[0:1, 0:kr], crb[(bi - 1) % 2][0:1, 0:W], False, True, [C, Bcr[(bi - 1) % 2]], [Bpc])
        if bi < nblk - 1:
            self.mm(pst[0:1, 0:W], self.ones1[0:kr, 0:1], spb[x][0:kr, 0:W], bi == 0, bi == nblk - 2, [C, Bsp[x]], [Bpt])
            fw.op("dve", lambda h: h.tensor_copy(out=crb[x][0:1, 0:W], in_=pst[0:1, 0:W]), reads=[Bpt], writes=[Bcr[x]])
            qk(bi + 1)
        fw.op("act", lambda h: h.activation(out=psc[0:kr, 0:W], in_=psc[0:kr, 0:W], func=AF.Exp), reads=[Bpc], writes=[Bpc])
        fw.op("dve", lambda h: h.tensor_tensor(out=wb[x][0:kr, 0:W], in0=psc[0:kr, 0:W], in1=e_[x][0:kr, 0:W], op=ALU.mult),
              reads=[Bpc, Be[x]], writes=[Bw[x]])
        for hd in range(8):
            hp, xx = hd // 2, hd % 2
            slot = xx * 4 + hp
            self.mm(po[64 * xx:64 * xx + 64, hp * NQ:(hp + 1) * NQ], self.VT[0:kr, vblk, hd * 64:(hd + 1) * 64], wb[x][0:kr, slot * NQ:(slot + 1) * NQ],
                    bi == 0 and hp == 0, bi == nblk - 1, [Bvt, Bw[x]], [Bpo], tile_position=(0, 64 * xx), skip_group_check=True)
    fw.op("dve", lambda h: h.tensor_copy(out=attno[:, :, q0:q1], in_=po[:, 0:4 * NQ].rearrange("p (c q) -> p c q", c=4)), reads=[Bpo], writes=[self.Battno])


def mixer0(self, tile, N):
    fw = self.fw
    kind, s, ti = tile
    C = self.Bconst
    PS, BP = self.PS, self.BP
    ma, mab, hbf = self.ma, self.mab, self.hbf
    xf, xb, Bxf, Bxb = self.xf, self.xb, self.Bxf, self.Bxb
    if not hasattr(self, "Buf_uf"):
        self.Buf_uf, self.Buf_yf, self.Bub, self.Bkvst = Buf("uf"), Buf("yf"), Buf("ub"), Buf("kvst")
        self.BQT, self.Bs5out, self.Battno = Buf("QT"), Buf("s5out"), Buf("attno")
        self.BT = [Buf("T%d" % i) for i in range(4)]
        self.Bhh = [Buf("hh%d" % i) for i in range(2)]
        self.Be = [Buf("e%d" % i) for i in range(2)]
        self.Bsp = [Buf("sp%d" % i) for i in range(2)]
        self.Bw = [Buf("w%d" % i) for i in range(2)]
        self.Bknew = Buf("knew")
    fw.barrier()
    uf = hbf[:, 0:2048].rearrange("p (c n) -> p c n", c=4)
    ub = self.hb[:, 16:20, :]
    kvst = hbf[:, 5120:5632]
    QT = mab[:, 0:2048].rearrange("p (c n) -> p c n", c=4)
    s5out = mab[:, 2048:4096].rearrange("p (c n) -> p c n", c=4)
    attno = mab[:, 4096:6144].rearrange("p (c n) -> p c n", c=4)
    win = self.S["mix0_w_in"][0].rearrange("(k p) f -> p k f", p=128)
    pos = ti * 512

    def slab_in(ci):
        t, B = self.get_slab([(lambda t: t[:, :].rearrange("p (k f) -> p k f", k=8), win[:, :, 512 * ci:512 * ci + 512], "mix0_w_in")])
        return t[:, :].rearrange("p (k f) -> p k f", k=8), B

    tw, B = slab_in(0)
    for oc in range(4):
        pp, Bp = PS[oc % 2], BP[oc % 2]
        for k in range(NCH):
            self.mm(pp[:, 0:N], tw[:, k, oc * 128:(oc + 1) * 128], xb[:, k, 0:N], k == 0, k == NCH - 1, [B, Bxb[k]], [Bp])
        fw.op("act", lambda h, pp=pp, oc=oc: h.activation(out=uf[:, oc, 0:N], in_=pp[:, 0:N], func=AF.Copy), reads=[Bp], writes=[self.Buf_uf])
        fw.op("dve", lambda h, pp=pp, oc=oc: h.tensor_copy(out=ub[:, oc, 0:N], in_=pp[:, 0:N]), reads=[Bp], writes=[self.Bub])
    tw, B = slab_in(1)
    for oc in range(4):
        pp, Bp = PS[2 + oc % 2], BP[2 + oc % 2]
        for k in range(NCH):
            self.mm(pp[:, 0:N], tw[:, k, oc * 128:(oc + 1) * 128], xb[:, k, 0:N], k == 0, k == NCH - 1, [B, Bxb[k]], [Bp])
        fw.op("act", lambda h, pp=pp, oc=oc: h.activation(out=QT[:, oc, 0:N], in_=pp[:, 0:N], func=AF.Copy, scale=0.125), reads=[Bp], writes=[self.BQT])
    tw, B = slab_in(2)
    for oc in range(4):
        pp, Bp = PS[oc % 2], BP[oc % 2]
        for k in range(NCH):
            self.mm(pp[:, 0:N], tw[:, k, oc * 128:(oc + 1) * 128], xb[:, k, 0:N], k == 0, k == NCH - 1, [B, Bxb[k]], [Bp])
        if kind == "p":
            fw.op("act", lambda h, pp=pp, oc=oc: h.activation(out=self.KT[:, oc, pos:pos + N], in_=pp[:, 0:N], func=AF.Copy), reads=[Bp], writes=[self.BKT[ti]])
        else:
            fw.op("act", lambda h, pp=pp, oc=oc: h.activation(out=self.knew[:, oc, 0:N], in_=pp[:, 0:N], func=AF.Copy), reads=[Bp], writes=[self.Bknew])
    twv, Bv = slab_in(3)
    if kind == "p":
        segs = [(b * 128, 128, s, pos + b * 128) for b in range(4)]
    else:
        segs = [(sq * DEC_SEQ, DEC_SEQ, sq, 0) for sq in range(self.NS)]
    sfx = "prompt" if kind == "p" else "sample"
    for (c0, rows, seq, sp_) in segs:
        for which, (tws, Bs_) in enumerate(((tw, B), (twv, Bv))):
            pp, Bp = PS[2 + which], BP[2 + which]
            for k in range(NCH):
                self.mm(pp[0:rows, :], xb[:, k, c0:c0 + rows], tws[:, k, :], k == 0, k == NCH - 1, [Bs_, Bxb[k]], [Bp])
            fw.op("act", lambda h, pp=pp, rows=rows: h.activation(out=kvst[0:rows, :], in_=pp[0:rows, :], func=AF.Copy), reads=[Bp], writes=[self.Bkvst])
            if which == 1:
                if kind == "p":
                    blk = sp_ // 128
                    fw.op("dve", lambda h, pp=pp, blk=blk: h.tensor_copy(out=self.VT[:, blk, :], in_=pp[:, :]), reads=[Bp], writes=[self.BVT[blk // 4]])
                else:
                    fw.op("dve", lambda h, pp=pp, seq=seq, rows=rows: h.tensor_copy(out=self.vnew[0:rows, seq, :], in_=pp[0:rows, :]), reads=[Bp], writes=[self.Bknew])
            nm = ("sb_k_" if which == 0 else "sb_v_") + sfx
            dst = self.O[nm][0, seq, sp_:sp_ + rows].rearrange("t h d -> t (h d)")
            fw.dma("sp", dst, kvst[0:rows, :], self.Bkvst, reads=[self.Bkvst])
    if self.stop < 3:
        return
    if kind == "s":
        self.s5_sample_init()
    fw.mark("s5")
    s5_phase(self, tile, N)
    fw.barrier()
    fw.mark("attn")
    if self.stop < 4:
        return
    if kind == "p":
        nkb = 4 * (ti + 1)
        blocks = []
        for kb in reversed(range(nkb)):
            jj = kb - 4 * ti if kb >= 4 * ti else None
            blocks.append((kb * 128, 128, kb, jj))
        attn_head_loop(self, N, (0, N), blocks, 0)
    else:
        for sq in range(self.NS):
            self.load_cache(sq)
            blocks = [(PAST, DEC_SEQ, 16, 0)] + [(kb * 128, 128, kb, None) for kb in reversed(range(PAST // 128))]
            attn_sample_loop(self, (sq * DEC_SEQ, (sq + 1) * DEC_SEQ), blocks)
    fw.mark("out0")
    wout = self.S["mix0_w_out"][0].rearrange("(k p) f -> p k f", p=128)
    for half in range(2):
        t, B = self.get_slab([(lambda t: t[:, :].rearrange("p (k f) -> p k f", k=8), wout[:, :, 512 * half:512 * half + 512], "mix0_w_out")])
        tw = t[:, :].rearrange("p (k f) -> p k f", k=8)
        for oc in range(4):
            o = 4 * half + oc
            pp, Bp = PS[o % 2], BP[o % 2]
            for k in range(NCH):
                rhs = s5out[:, k, 0:N] if k < 4 else attno[:, k - 4, 0:N]
                Br = self.Bs5out if k < 4 else self.Battno
                self.mm(pp[:, 0:N], tw[:, k, oc * 128:(oc + 1) * 128], rhs, k == 0, k == NCH - 1, [B, Br], [Bp])
            fw.op("dve", lambda h, o=o, pp=pp: h.scalar_tensor_tensor(out=xf[:, o, 0:N], in0=xf[:, o, 0:N], scalar=ALPHA, in1=pp[:, 0:N],
                                                                     op0=ALU.mult, op1=ALU.add), reads=[Bxf[o], Bp], writes=[Bxf[o]])
    fw.barrier([self.Bkvst])
    self.layer_norm(1, N)


def s5_sample_init(self):
    fw, I = self.fw, self.I
    C = self.Bconst
    NS = self.NS
    if not hasattr(self, "Gs"):
        self.Gs = fw.sbuf([128, NS, 16, 2], F32, "Gs")
        self.BGs = Buf("Gs", strict=True)
        self.HLs = fw.sbuf([128, NS, 2, 16], F32, "HLs")
        self.h0 = fw.sbuf([128, 2, NS, 16], F32, "h0")
        self.h0t = fw.sbuf([128, NS, 16], F32, "h0t")
    for reim, nm in ((0, "state_s5_re"), (1, "state_s5_im")):
        self.load_T(I[nm][0].rearrange("s (j g) n -> (s j) (g n)", g=2), NS * 16, self.h0[:, reim].rearrange("p s j -> p (s j)"), self.BGs)
    c1 = bc(self.Ct[:, :, 1:2].rearrange("p j o -> p o j"), [128, NS, 16])
    s1 = bc(self.St[:, :, 1:2].rearrange("p j o -> p o j"), [128, NS, 16])
    hre, him = self.h0[:, 0], self.h0[:, 1]
    B = self.BGs
    dve = lambda fn: fw.op("dve", fn, reads=[B, C], writes=[B])
    dve(lambda h: h.tensor_tensor(out=self.h0t[:], in0=him, in1=s1, op=ALU.mult))
    dve(lambda h: h.tensor_tensor(out=self.Gs[:, :, :, 0], in0=hre, in1=c1, op=ALU.mult))
    dve(lambda h: h.tensor_tensor(out=self.Gs[:, :, :, 0], in0=self.Gs[:, :, :, 0], in1=self.h0t[:], op=ALU.subtract))
    dve(lambda h: h.tensor_tensor(out=self.h0t[:], in0=hre, in1=s1, op=ALU.mult))
    dve(lambda h: h.tensor_tensor(out=self.Gs[:, :, :, 1], in0=him, in1=c1, op=ALU.mult))
    dve(lambda h: h.tensor_tensor(out=self.Gs[:, :, :, 1], in0=self.Gs[:, :, :, 1], in1=self.h0t[:], op=ALU.add))


def load_cache(self, sq):
    fw, I = self.fw, self.I
    C = self.Bconst
    PS, BP = self.PS, self.BP
    stg = self.hbf[:, 0:4096].rearrange("p (b f) -> p b f", b=8)
    Bst = [self.Buf_uf, self.Buf_yf]
    kc = I["cache_sb_k"][0, sq].rearrange("(b p) h d -> p b (h d)", p=128)
    vc = I["cache_sb_v"][0, sq].rearrange("(b p) h d -> p b (h d)", p=128)
    fw.dma("pool", self.VT[:, 0:16, :], vc, self.BVT[0], writes=self.BVT[0:4])
    for half in range(2):
        for q4 in range(2):
            b0 = half * 8 + q4 * 4
            fw.dma("sp", stg[:, q4 * 4:q4 * 4 + 4, :], kc[:, b0:b0 + 4, :], Bst[q4], writes=[Bst[q4]])
        for c in range(4):
            for g in range(2):
                pp, Bp = PS[(2 * c + g) % 4], BP[(2 * c + g) % 4]
                for bb in range(4):
                    b = g * 4 + bb
                    fw.op("pe", lambda h, pp=pp, b=b, bb=bb, c=c: h.transpose(pp[:, bb * 128:(bb + 1) * 128], stg[:, b, c * 128:(c + 1) * 128], self.ident[:]),
                          reads=[Bst[g], C], writes=[Bp])
                col0 = (half * 8 + g * 4) * 128
                eng = "act" if (c + g) % 2 == 0 else "dve"
                if eng == "act":
                    fw.op("act", lambda h, pp=pp, c=c, col0=col0: h.activation(out=self.KT[:, c, col0:col0 + 512], in_=pp[:, :], func=AF.Copy),
                          reads=[Bp], writes=[self.BKT[col0 // 512]])
                else:
                    fw.op("dve", lambda h, pp=pp, c=c, col0=col0: h.tensor_copy(out=self.KT[:, c, col0:col0 + 512], in_=pp[:, :]),
                          reads=[Bp], writes=[self.BKT[col0 // 512]])
    fw.op("dve", lambda h: h.tensor_copy(out=self.KT[:, :, PAST:PAST + DEC_SEQ], in_=self.knew[:, :, sq * DEC_SEQ:(sq + 1) * DEC_SEQ]),
          reads=[self.Bknew], writes=[self.BKT[4]])
    fw.op("dve", lambda h: h.tensor_copy(out=self.VT[0:DEC_SEQ, 16, :], in_=self.vnew[0:DEC_SEQ, sq, :]), reads=[self.Bknew], writes=[self.BVT[4]])

def setup_ssd(self):
    fw, I = self.fw, self.I
    C = self.Bconst
    self.cw = fw.sbuf([128, 96], F32, "cw")
    self.cb = fw.sbuf([128, 24], F32, "cb")
    self.ng = fw.sbuf([128, 16], F32, "ng")
    self.load_T(I["ssd_conv_w"].rearrange("o w (c p) -> (o w c) p", p=128), 96, self.cw[:], C)
    self.load_T(I["ssd_conv_b"].rearrange("o (c p) -> (o c) p", p=128), 24, self.cb[:], C)
    self.load_T(I["ssd_norm_g"].rearrange("o (c p) -> (o c) p", p=128), 16, self.ng[:], C)
    self.dtb = fw.sbuf([128, 32], F32, "dtb")
    self.Arow = fw.sbuf([128, 32], F32, "Arow")
    fw.dma("sp", self.dtb[:], I["ssd_dt_bias"][0:1, :].broadcast_to([128, 32]), C, writes=[C])
    fw.dma("sp", self.Arow[:], I["ssd_a_log"][0:1, :].broadcast_to([128, 32]), C, writes=[C])
    fw.op("act", lambda h: h.activation(out=self.Arow[:], in_=self.Arow[:], func=AF.Exp), reads=[C], writes=[C])
    fw.op("dve", lambda h: h.tensor_scalar(out=self.Arow[:], in0=self.Arow[:], scalar1=-1.0, scalar2=None, op0=ALU.mult), reads=[C], writes=[C])
    self.dvec = fw.sbuf([128, 16], F32, "dvec")
    dd = I["ssd_d"].rearrange("o (c t) -> o c t", t=2)
    fw.dma("sp", self.dvec[0:64, :], dd[0:1, :, 0].broadcast_to([64, 16]), C, writes=[C], slow=True)
    fw.dma("sp", self.dvec[64:128, :], dd[0:1, :, 1].broadcast_to([64, 16]), C, writes=[C], slow=True)
    self.tri = fw.sbuf([64, 64], F32, "tri")
    fw.op("pool", lambda h: h.memset(self.tri[:], 1.0), writes=[C])
    fw.op("pool", lambda h: h.affine_select(out=self.tri[:], in_=self.tri[:], pattern=[[1, 64]], compare_op=ALU.is_ge, fill=0.0,
                                            base=0, channel_multiplier=-1), reads=[C], writes=[C])
    self.ones512 = fw.sbuf([128, 128], BF16, "ones512")
    fw.op("pool", lambda h: h.memset(self.ones512[:], 1.0 / 512.0), writes=[C])
    self.hst = fw.sbuf([128, 2048], F32, "hst")
    self.hstb = fw.sbuf([128, 2048], BF16, "hstb")
    self.Bhst = [Buf("hst%d" % g) for g in range(4)]
    self.Bhstb = [Buf("hstb%d" % g) for g in range(4)]
    self.ctail = fw.sbuf([128, 24, 3], F32, "ctail")
    self.Bctail = Buf("ctail")
    self.dtt = fw.sbuf([64, 8, 4, 32], F32, "dtt")
    self.dend = fw.sbuf([128, 8, 32], F32, "dend")
    self.Bdtt = Buf("dtt")
    self.mcb = fw.sbuf([64, 64], F32, "mcb")
    self.Bmcb = Buf("mcb")


def ssd_core(self, c0, n, nseg, Lc):
    fw = self.fw
    C = self.Bconst
    PS, BP = self.PS, self.BP
    ma, mab, hbf = self.ma, self.mab, self.hbf
    xb, Bxb = self.xb, self.Bxb
    B1 = self.B1
    BT_ = mab[:, 0:2048].rearrange("p (c n) -> p c n", c=4)
    CT_ = mab[:, 2048:4096].rearrange("p (c n) -> p c n", c=4)
    zs = mab[:, 4096:6144].rearrange("p (c n) -> p c n", c=4)
    xsT = mab[:, 6144:8192].rearrange("p (c n) -> p c n", c=4)
    stg = ma[:, 4096:4611]
    DE = ma[:, 4612:5124]
    E2 = ma[:, 5124:5636]
    yg = ma[:, 4096:6144].rearrange("p (c n) -> p c n", c=4)
    yn = self.hb[:, 0:16, :]
    hbb = self.hb[:].rearrange("p m n -> p (m n)")
    sgb0 = self.sg[0][:].bitcast(BF16)
    sgb1 = self.sg[1][:].bitcast(BF16)
    Wt_ = [hbb[:, 8192:8704], sgb0[:, 0:512]]
    Ctl_ = [hbb[:, 8704:9216], sgb0[:, 512:1024]]
    xt_ = [hbb[:, 9216:9728], sgb1[:, 0:512]]
    xh_ = [hbb[:, 9728:10240], sgb1[:, 512:1024]]
    btm_ = [hbb[:, 10240:10368], hbb[:, 10368:10496]]
    self.ssd_it = 0
    win = self.S["ssd_w_in"][0].rearrange("(k p) f -> p k f", p=128)
    dve = lambda fn, r, w: fw.op("dve", fn, reads=list(r), writes=list(w))
    act = lambda fn, r, w: fw.op("act", fn, reads=list(r), writes=list(w))

    def slab_in(ci):
        w = 512 if ci < 10 else 32
        t, B = self.get_slab([(lambda t: t[:, 0:8 * w].rearrange("p (k f) -> p k f", k=8), win[:, :, 512 * ci:512 * ci + w], "ssd_w_in")])
        return t[:, 0:8 * w].rearrange("p (k f) -> p k f", k=8), B

    self.pp_i = 0

    def proj(tw, B, cc):
        i = 5 + self.pp_i % 2
        self.pp_i += 1
        pp, Bp = PS[i], BP[i]
        for k in range(NCH):
            self.mm(pp[:, 0:n], tw[:, k, cc * 128:(cc + 1) * 128], xb[:, k, c0:c0 + n], k == 0, k == NCH - 1, [B, Bxb[k]], [Bp])
        return pp, Bp

    def proj_conv(tw, B, cc, ci, out, Bout):
        pp, Bp = proj(tw, B, cc)
        dve(lambda h: h.tensor_copy(out=stg[:, 0:3], in_=self.ctail[:, ci, :]), [self.Bctail], [B1["stg"]])
        act(lambda h: h.activation(out=stg[:, 3:3 + n], in_=pp[:, 0:n], func=AF.Copy), [Bp], [B1["stg"]])
        dve(lambda h: h.tensor_copy(out=self.ctail[:, ci, :], in_=stg[:, n:n + 3]), [B1["stg"]], [self.Bctail])
        self.acc_i = getattr(self, "acc_i", 0) + 1
        acc, Bacc = (E2[:, 0:n], B1["E2"]) if self.acc_i % 2 == 0 else (DE[:, 0:n], B1["DE"])
        dve(lambda h: h.tensor_scalar(out=acc, in0=stg[:, 0:n], scalar1=self.cw[:, ci:ci + 1], scalar2=self.cb[:, ci:ci + 1], op0=ALU.mult, op1=ALU.add),
            [B1["stg"], C], [Bacc])
        for w in range(1, 4):
            dve(lambda h, w=w: h.scalar_tensor_tensor(out=acc, in0=stg[:, w:w + n], scalar=self.cw[:, w * 24 + ci:w * 24 + ci + 1], in1=acc,
                                                     op0=ALU.mult, op1=ALU.add), [B1["stg"], Bacc, C], [Bacc])
        act(lambda h: h.activation(out=out, in_=acc, func=AF.Silu), [Bacc], [Bout])

    fw.mark("m1_dt")
    tw, B = slab_in(10)
    dtt, Bd = self.dtt, self.Bdtt
    for sg in range(nseg):
        a = c0 + sg * Lc
        pp, Bp = PS[5], BP[5]
        for k in range(NCH):
            self.mm(pp[0:Lc, 0:32], xb[:, k, a:a + Lc], tw[:, k, 0:32], k == 0, k == NCH - 1, [B, Bxb[k]], [Bp])
        dt_, dtA, cs_, te = (dtt[0:Lc, sg, i, :] for i in range(4))
        dve(lambda h: h.tensor_tensor(out=dt_, in0=pp[0:Lc, 0:32], in1=self.dtb[0:Lc, :], op=ALU.add), [Bp, C], [Bd])
        act(lambda h: h.activation(out=dt_, in_=dt_, func=AF.Exp), [Bd], [Bd])
        act(lambda h: h.activation(out=dt_, in_=dt_, func=AF.Ln, bias=1.0), [Bd], [Bd])
        dve(lambda h: h.tensor_tensor(out=dtA, in0=dt_, in1=self.Arow[0:Lc, :], op=ALU.mult), [Bd, C], [Bd])
        pc, Bpc = PS[6], BP[6]
        self.mm(pc[0:Lc, 0:32], self.tri[0:Lc, 0:Lc], dtA, True, True, [C, Bd], [Bpc])
        pe_, Bpe = PS[7], BP[7]
        self.mm(pe_[:, 0:32], self.onesf[0:Lc, :], dtA, True, True, [C, Bd], [Bpe])
        act(lambda h: h.activation(out=cs_, in_=pc[0:Lc, 0:32], func=AF.Copy), [Bpc], [Bd])
        dve(lambda h: h.tensor_tensor(out=te, in0=pe_[0:Lc, 0:32], in1=cs_, op=ALU.subtract), [Bpe, Bd], [Bd])
        act(lambda h: h.activation(out=te, in_=te, func=AF.Exp), [Bd], [Bd])
        dve(lambda h: h.tensor_tensor(out=te, in0=te, in1=dt_, op=ALU.mult), [Bd], [Bd])
        act(lambda h, sg=sg: h.activation(out=self.dend[:, sg, :], in_=pe_[:, 0:32], func=AF.Exp), [Bpe], [Bd])
    fw.mark("m1_bc")
    tw, B = slab_in(8)
    for g in range(4):
        proj_conv(tw, B, g, 16 + g, BT_[:, g, 0:n], B1["BT"])
    tw, B = slab_in(9)
    for g in range(4):
        proj_conv(tw, B, g, 20 + g, CT_[:, g, 0:n], B1["CT"])
    pxt_b = PS[6][:].bitcast(BF16)
    for g in range(4):
        fw.mark("m1_inproj")
        tw, B = slab_in(g)
        for hc in range(4):
            pp, Bp = proj(tw, B, hc)
            act(lambda h, hc=hc, pp=pp: h.activation(out=zs[:, hc, 0:n], in_=pp[:, 0:n], func=AF.Silu), [Bp], [B1["zs"]])
        tw, B = slab_in(4 + g)
        for hc in range(4):
            proj_conv(tw, B, hc, 4 * g + hc, xsT[:, hc, 0:n], B1["xs"])
        fw.mark("m1_seg")
        for sg in range(nseg):
            a = sg * Lc
            par = self.ssd_it % 2
            self.ssd_it += 1
            Wt, Ctl, xt, xh, btm = Wt_[par], Ctl_[par], xt_[par], xh_[par], btm_[par]
            BW, BCt, Bxt, Bxh, Bbt = (B1[k + str(par)] for k in ("W", "Ct", "xt", "xh", "btm"))
            dt_, dtA, cs_, te = (dtt[0:Lc, sg, i, :] for i in range(4))
            pcb, Bpcb = PS[5], BP[5]
            self.mm(pcb[0:Lc, 0:Lc], BT_[:, g, a:a + Lc], CT_[:, g, a:a + Lc], True, True, [B1["BT"], B1["CT"]], [Bpcb])
            dve(lambda h: h.tensor_tensor(out=self.mcb[0:Lc, 0:Lc], in0=pcb[0:Lc, 0:Lc], in1=self.tri[0:Lc, 0:Lc], op=ALU.mult), [Bpcb, C], [self.Bmcb])
            pr, Bpr = PS[4], BP[4]
            for hh in range(8):
                hd = 8 * g + hh
                self.mm(pr[:, hh * Lc:(hh + 1) * Lc], dtA[:, hd:hd + 1].broadcast_to([Lc, 128]), self.tri[0:Lc, 0:Lc], True, True, [Bd, C], [Bpr])
            v3 = lambda ap, P: ap[0:P, 0:8 * Lc].rearrange("p (h t) -> p h t", h=8)
            dve(lambda h: h.tensor_tensor(out=v3(DE, Lc), in0=v3(pr, Lc), in1=bc(cs_[:, 8 * g:8 * g + 8].unsqueeze(2), [Lc, 8, Lc]), op=ALU.subtract),
                [Bpr, Bd], [B1["DE"]])
            act(lambda h: h.activation(out=DE[0:Lc, 0:8 * Lc], in_=DE[0:Lc, 0:8 * Lc], func=AF.Exp), [B1["DE"]], [B1["DE"]])
            dve(lambda h: h.scalar_tensor_tensor(out=v3(Wt, Lc), in0=v3(DE, Lc), scalar=1.0, in1=bc(self.mcb[0:Lc, 0:Lc].unsqueeze(1), [Lc, 8, Lc]),
                                                 op0=ALU.min, op1=ALU.mult), [B1["DE"], self.Bmcb], [BW])
            act(lambda h: h.activation(out=E2[:, 0:8 * Lc], in_=pr[:, 0:8 * Lc], func=AF.Exp), [Bpr], [B1["E2"]])
            dve(lambda h: h.tensor_tensor(out=v3(Ctl, 128), in0=v3(E2, 128), in1=bc(CT_[:, g, a:a + Lc].unsqueeze(1), [128, 8, Lc]), op=ALU.mult),
                [B1["E2"], B1["CT"]], [BCt])
            pxt, Bpxt = pxt_b, BP[6]
            for hc in range(4):
                fw.op("pe", lambda h, hc=hc: h.transpose(pxt[0:Lc, hc * 128:(hc + 1) * 128], xsT[:, hc, a:a + Lc], self.identb[:, :]),
                      reads=[B1["xs"], C], writes=[Bpxt])
            fw.op("pe", lambda h: h.transpose(pxt[0:Lc, 512:640], BT_[:, g, a:a + Lc], self.identb[:, :]), reads=[B1["BT"], C], writes=[Bpxt])
            x3 = lambda ap: ap[0:Lc, 0:512].rearrange("p (h q) -> p h q", h=8)
            dve(lambda h: h.tensor_tensor(out=x3(xt), in0=x3(pxt), in1=bc(dt_[:, 8 * g:8 * g + 8].unsqueeze(2), [Lc, 8, 64]), op=ALU.mult),
                [Bpxt, Bd], [Bxt])
            dve(lambda h: h.tensor_tensor(out=x3(xh), in0=x3(pxt), in1=bc(te[:, 8 * g:8 * g + 8].unsqueeze(2), [Lc, 8, 64]), op=ALU.mult),
                [Bpxt, Bd], [Bxh])
            act(lambda h: h.activation(out=btm[0:Lc, 0:128], in_=pxt[0:Lc, 512:640], func=AF.Copy), [Bpxt], [Bbt])
            for hc in range(4):
                py, Bpy = PS[hc], BP[hc]
                for hh in (2 * hc, 2 * hc + 1):
                    pb = 64 * (hh % 2)
                    self.mm(py[pb:pb + 64, a:a + Lc], xt[0:Lc, hh * 64:(hh + 1) * 64], Wt[0:Lc, hh * Lc:(hh + 1) * Lc], True, False,
                            [Bxt, BW], [Bpy], tile_position=(0, pb))
                for hh in (2 * hc, 2 * hc + 1):
                    pb = 64 * (hh % 2)
                    hd = 8 * g + hh
                    self.mm(py[pb:pb + 64, a:a + Lc], self.hstb[:, hd * 64:(hd + 1) * 64], Ctl[:, hh * Lc:(hh + 1) * Lc], False, True,
                            [self.Bhstb[g], BCt], [Bpy], tile_position=(0, pb))
            pS, BpS = PS[7], BP[7]
            self.mm(pS[:, 0:512], btm[0:Lc, 0:128], xh[0:Lc, 0:512], True, True, [Bbt, Bxh], [BpS])
            hg = self.hst[:, 512 * g:512 * (g + 1)]
            h3 = hg.rearrange("p (h q) -> p h q", h=8)
            dve(lambda h, sg=sg: h.tensor_tensor(out=h3, in0=h3, in1=bc(self.dend[:, sg, 8 * g:8 * g + 8].unsqueeze(2), [128, 8, 64]), op=ALU.mult),
                [self.Bhst[g], Bd], [self.Bhst[g]])
            dve(lambda h: h.tensor_tensor(out=hg, in0=hg, in1=pS[:, 0:512], op=ALU.add), [self.Bhst[g], BpS], [self.Bhst[g]])
            act(lambda h: h.activation(out=self.hstb[:, 512 * g:512 * (g + 1)], in_=hg, func=AF.Copy), [self.Bhst[g]], [self.Bhstb[g]])
        fw.mark("m1_epi")
        fw.barrier()
        for hc in range(4):
            py, Bpy = PS[hc], BP[hc]
            dve(lambda h, hc=hc, py=py: h.scalar_tensor_tensor(out=yg[:, hc, 0:n], in0=xsT[:, hc, 0:n], scalar=self.dvec[:, 4 * g + hc:4 * g + hc + 1],
                                                               in1=py[:, 0:n], op0=ALU.mult, op1=ALU.add), [B1["xs"], Bpy, C], [B1["yg"]])
            dve(lambda h, hc=hc: h.tensor_tensor(out=yg[:, hc, 0:n], in0=yg[:, hc, 0:n], in1=zs[:, hc, 0:n], op=ALU.mult), [B1["yg"], B1["zs"]], [B1["yg"]])
        act(lambda h: h.activation(out=xsT[:, :, 0:n], in_=yg[:, :, 0:n], func=AF.Square), [B1["yg"]], [B1["xs"]])
        pms, Bpms = PS[5], BP[5]
        for hc in range(4):
            self.mm(pms[:, 0:n], self.ones512[:], xsT[:, hc, 0:n], hc == 0, hc == 3, [C, B1["xs"]], [Bpms])
        rstd = self.lnt[:, 1, 0:n]
        Bl = self.Blnt[1]
        dve(lambda h: h.tensor_scalar(out=rstd, in0=pms[:, 0:n], scalar1=RMS_EPS, scalar2=None, op0=ALU.add), [Bpms], [Bl])
        act(lambda h: h.activation(out=rstd, in_=rstd, func=AF.Sqrt), [Bl], [Bl])
        dve(lambda h: h.reciprocal(out=rstd, in_=rstd), [Bl], [Bl])
        for hc in range(4):
            ch = 4 * g + hc
            dve(lambda h, hc=hc, ch=ch: h.scalar_tensor_tensor(out=yn[:, ch, c0:c0 + n], in0=yg[:, hc, 0:n], scalar=self.ng[:, ch:ch + 1], in1=rstd,
                                                               op0=ALU.mult, op1=ALU.mult), [B1["yg"], Bl, C], [B1["yn"]])
        fw.barrier()


def _st_stage(self, r):
    slot = 0 if r % 2 == 0 else 3
    return self.lnt[:, slot, :].rearrange("p (t n) -> p t n", t=4), self.Blnt[slot]


def ssd_state_in(self, sq):
    fw, I = self.fw, self.I
    C = self.Bconst
    src = I["state_ssd"][0, sq].rearrange("h p n -> (h p) n").rearrange("(r t p) n -> r p t n", t=4, p=128)
    for r in range(4):
        st, Bst = _st_stage(self, r)
        ps, Bp = self.PS[6 + r % 2], self.BP[6 + r % 2]
        fw.dma("sp", st, src[r], Bst, writes=[Bst])
        for t in range(4):
            fw.op("pe", lambda h, t=t: h.transpose(ps[:, t * 128:(t + 1) * 128], st[:, t, :], self.ident[:, :]), reads=[Bst, C], writes=[Bp])
        fw.op("dve", lambda h: h.tensor_copy(out=self.hst[:, 512 * r:512 * (r + 1)], in_=ps[:, 0:512]), reads=[Bp], writes=[self.Bhst[r]])
        fw.op("act", lambda h: h.activation(out=self.hstb[:, 512 * r:512 * (r + 1)], in_=ps[:, 0:512], func=AF.Copy), reads=[Bp], writes=[self.Bhstb[r]])
    srcc = I["state_conv"][0, sq].rearrange("w (c p) -> (w c) p", p=128)
    fw.dma("sp", self.stg[0:72, :], srcc, self.Bstg, writes=[self.Bstg])
    fw.op("pe", lambda h: h.transpose(self.PS[7][:, 0:72], self.stg[0:72, :], self.ident[0:72, 0:72]), reads=[self.Bstg, C], writes=[self.BP[7]])
    fw.op("dve", lambda h: h.tensor_copy(out=self.ctail[:].rearrange("p c w -> p w c"), in_=self.PS[7][:, 0:72].rearrange("p (w c) -> p w c", w=3)),
          reads=[self.BP[7]], writes=[self.Bctail])


def ssd_state_out(self, sfx, seq):
    fw = self.fw
    C = self.Bconst
    dst = self.O["ssd_" + sfx][0, seq].rearrange("h p n -> (h p) n").rearrange("(r t p) n -> r p t n", t=4, p=128)
    for r in range(4):
        st, Bst = _st_stage(self, r)
        ps, Bp = self.PS[6 + r % 2], self.BP[6 + r % 2]
        for t in range(4):
            i = 4 * r + t
            fw.op("pe", lambda h, i=i, t=t: h.transpose(ps[:, t * 128:(t + 1) * 128], self.hst[:, i * 128:(i + 1) * 128], self.ident[:, :]),
                  reads=[self.Bhst[r], C], writes=[Bp])
        fw.op("dve", lambda h: h.tensor_copy(out=st, in_=ps[:, 0:512].rearrange("p (t n) -> p t n", t=4)), reads=[Bp], writes=[Bst])
        fw.dma("sp", dst[r], st, Bst, reads=[Bst])
    tmp = self.lnt[:, 2, 0:72]
    fw.op("dve", lambda h: h.tensor_copy(out=tmp.rearrange("p (w c) -> p w c", w=3), in_=self.ctail[:].rearrange("p c w -> p w c")),
          reads=[self.Bctail], writes=[self.Blnt[2]])
    fw.op("pe", lambda h: h.transpose(self.PS[7][0:72, 0:128], tmp, self.ident[:, :]), reads=[self.Blnt[2], C], writes=[self.BP[7]])
    fw.op("dve", lambda h: h.tensor_copy(out=self.stg[0:72, :], in_=self.PS[7][0:72, 0:128]), reads=[self.BP[7]], writes=[self.Bstg])
    dstc = self.O["conv_" + sfx][0, seq].rearrange("w (c p) -> (w c) p", p=128)
    fw.dma("sp", dstc, self.stg[0:72, :], self.Bstg, reads=[self.Bstg])


def mixer1(self, tile, N):
    fw = self.fw
    kind, s, ti = tile
    C = self.Bconst
    PS, BP = self.PS, self.BP
    xf, Bxf = self.xf, self.Bxf
    if not hasattr(self, "B1"):
        self.B1 = {k: Buf(k) for k in ("stg", "E2", "DE", "BT", "CT", "zs", "xs", "yg", "yn")}
        for k in ("W", "Ct", "xt", "xh", "btm"):
            for par in range(2):
                self.B1[k + str(par)] = Buf(k + str(par))
    fw.barrier()
    if kind == "p":
        if ti == 0:
            fw.op("dve", lambda h: h.memset(self.hst[:], 0.0), writes=self.Bhst)
            fw.op("pool", lambda h: h.memset(self.hstb[:], 0.0), writes=self.Bhstb)
            fw.op("pool", lambda h: h.memset(self.ctail[:], 0.0), writes=[self.Bctail])
        ssd_core(self, 0, 512, 8, 64)
        if ti == self.SEQ // 512 - 1:
            ssd_state_out(self, "prompt", s)
    else:
        for sq in range(self.NS):
            ssd_state_in(self, sq)
            ssd_core(self, sq * DEC_SEQ, DEC_SEQ, 1, DEC_SEQ)
            ssd_state_out(self, "sample", sq)
    fw.mark("m1_out")
    yn = self.hb[:, 0:16, :]
    wout = self.S["ssd_w_out"][0].rearrange("(k p) f -> p k f", p=128)
    for o2 in range(4):
        t, B = self.get_slab([(lambda t: t[:, :].rearrange("p (k f) -> p k f", k=16), wout[:, :, 256 * o2:256 * o2 + 256], "ssd_w_out")])
        tw = t[:, :].rearrange("p (k f) -> p k f", k=16)
        for oi in range(2):
            o = 2 * o2 + oi
            pp, Bp = PS[o % 2], BP[o % 2]
            for k in range(16):
                self.mm(pp[:, 0:N], tw[:, k, oi * 128:(oi + 1) * 128], yn[:, k, 0:N], k == 0, k == 15, [B, self.B1["yn"]], [Bp])
            fw.op("dve", lambda h, o=o, pp=pp: h.scalar_tensor_tensor(out=xf[:, o, 0:N], in0=xf[:, o, 0:N], scalar=ALPHA, in1=pp[:, 0:N],
                                                                     op0=ALU.mult, op1=ALU.add), reads=[Bxf[o], Bp], writes=[Bxf[o]])
    fw.barrier([self.Bstg, self.Blnt[0], self.Blnt[3]])
    self.layer_norm(4, N)

Builder.setup_mix0 = setup_mix0
Builder.mixer0 = mixer0
Builder.s5_sample_init = s5_sample_init
Builder.load_cache = load_cache
Builder.setup_ssd = setup_ssd
Builder.mixer1 = mixer1

_OUT_ORDER = ["y_prompt", "y_sample", "s5_re_prompt", "s5_im_prompt", "sb_k_prompt", "sb_v_prompt", "ssd_prompt",
              "conv_prompt", "s5_re_sample", "s5_im_sample", "sb_k_sample", "sb_v_sample", "ssd_sample", "conv_sample"]
_BATCH_AXIS = {"x_prompt": 0, "x_sample": 0, "state_s5_re": 1, "state_s5_im": 1, "cache_sb_k": 1, "cache_sb_v": 1,
               "state_ssd": 1, "state_conv": 1}


def make_in_maps(inputs, n_cores):
    maps = []
    for c in range(n_cores):
        m = {}
        for k, v in inputs.items():
            v = np.asarray(v)
            if k in _BATCH_AXIS:
                ax = _BATCH_AXIS[k]
                n = v.shape[ax] // n_cores
                sl = [slice(None)] * v.ndim
                sl[ax] = slice(c * n, (c + 1) * n)
                m[k] = np.ascontiguousarray(v[tuple(sl)])
            else:
                m[k] = v
        maps.append(m)
    return maps


def gather(results):
    outs = []
    for nm in _OUT_ORDER:
        ax = 0 if nm in ("y_prompt", "y_sample") else 1
        outs.append(np.concatenate([r[nm] for r in results], axis=ax).astype(np.float32))
    return tuple(outs)


def kernel(**inputs):
    n = 8
    b = Builder(SEQ=4096, NP=2, NS=4)
    nc = b.build()
    res = run_bass_kernel_spmd(nc, make_in_maps(inputs, n), core_ids=list(range(n)))
    return gather(res.results)
```

```python
import math
import numpy as np
import concourse.bass as bass
import concourse.mybir as mybir
from concourse.bass_utils import run_bass_kernel_spmd
from contextlib import ExitStack

F32 = mybir.dt.float32
BF16 = mybir.dt.bfloat16
AF = mybir.ActivationFunctionType
ALU = mybir.AluOpType

D = 1024
DFF = 2816
NCH = 8
MCH = 22
ALPHA = (2.0 * 2) ** 0.25
LN_EPS = 1e-5
RMS_EPS = 1e-5
PAST = 2048
DEC_SEQ = 16
LSUB = 64
MAGIC = 12582912.0


class Buf:
    __slots__ = ("name", "w", "r", "dsem", "dtot", "psum", "strict")

    def __init__(self, name="", psum=False, strict=False):
        self.name = name
        self.psum = psum
        self.strict = strict
        self.w = None
        self.r = {}
        self.dsem = None
        self.dtot = 0


class Eng:
    def __init__(self, name, sem):
        self.name = name
        self.sem = sem
        self.n = 0
        self.seen = {}
        self.prog = []


class _Rec:
    def __init__(self):
        self.call = None

    def __getattr__(self, name):
        def f(*args, **kwargs):
            self.call = (name, args, kwargs)
            return None
        return f


class FW:
    def __init__(self, nc, es):
        self.nc = nc
        self.es = es
        self.engs = {}
        for name in ("pe", "dve", "act", "pool", "sp"):
            sem = es.enter_context(nc.semaphore("sem_" + name))
            self.engs[name] = Eng(name, sem)
        self.dma_bufs = []
        self.nsem = 5
        self.uid = 0
        self.nosame = False

    def sbuf(self, shape, dtype, name=None):
        self.uid += 1
        return self.es.enter_context(self.nc.sbuf_tensor(name or ("sb%d" % self.uid), list(shape), dtype))

    def psum(self, shape, dtype=F32, name=None):
        self.uid += 1
        return self.es.enter_context(self.nc.psum_tensor(name or ("ps%d" % self.uid), list(shape), dtype))

    def _wait(self, e, tok, strict=False):
        if tok[0] == "e":
            _, name, seq = tok
            if name == e.name and (name in ("pe", "sp") or (self.nosame and not strict)):
                return
            if e.seen.get(name, 0) >= seq:
                return
            e.prog.append(("we", name, seq))
            e.seen[name] = seq
        else:
            _, b, val = tok
            key = ("d", id(b))
            if e.seen.get(key, 0) >= val:
                return
            e.prog.append(("w", b.dsem, val))
            e.seen[key] = val

    def _deps(self, e, reads, writes):
        for b in reads:
            if b.w is not None:
                self._wait(e, b.w, b.strict)
            if b.psum:
                for t in b.r.values():
                    if not (t[0] == "e" and t[1] == e.name):
                        self._wait(e, t)
        for b in writes:
            if b.w is not None:
                self._wait(e, b.w)
            for t in b.r.values():
                if not (t[0] == "e" and t[1] == e.name):
                    self._wait(e, t)

    def op(self, ename, fn, reads=(), writes=()):
        e = self.engs[ename]
        self._deps(e, reads, writes)
        e.n += 1
        rec = _Rec()
        fn(rec)
        e.prog.append(("o", rec.call, e.sem, 1))
        tok = ("e", ename, e.n)
        for b in reads:
            b.r[ename] = tok
        for b in writes:
            b.w = tok
            b.r = {}
        return tok

    def dma(self, qname, out, in_, track, reads=(), writes=(), slow=False):
        e = self.engs[qname]
        self._deps(e, reads, writes)
        if track.dsem is None:
            track.dsem = self.es.enter_context(self.nc.semaphore("dsem_%d" % self.nsem))
            self.nsem += 1
            self.dma_bufs.append(track)
        track.dtot += 16
        if slow:
            e.prog.append(("o", ("dma_start", (), dict(out=out, in_=in_, allow_slow_non_contiguous=True)), track.dsem, 16))
        else:
            e.prog.append(("o", ("dma_start", (), dict(out=out, in_=in_)), track.dsem, 16))
        tok = ("d", track, track.dtot)
        key = ("d", id(track))
        for b in reads:
            b.r[key] = tok
        for b in writes:
            b.w = tok
            b.r = {}
        return tok

    def mark(self, name):
        if not hasattr(self, "marks"):
            self.marks = []
        self.marks.append((name, {k: e.n for k, e in self.engs.items()}))

    def barrier(self, bufs=()):
        names = ("pe", "dve", "act", "pool")
        for a in names:
            ea = self.engs[a]
            for b in bufs:
                if b.dsem is not None:
                    self._wait(ea, ("d", b, b.dtot))
            for bn in names:
                if bn != a and self.engs[bn].n > 0:
                    self._wait(ea, ("e", bn, self.engs[bn].n))

    def finish(self):
        e = self.engs["sp"]
        for b in self.dma_bufs:
            e.prog.append(("w", b.dsem, b.dtot))
        for name, o in self.engs.items():
            if name != "sp" and o.n > 0:
                e.prog.append(("we", name, o.n))
        ref = {name: set() for name in self.engs}
        for o in self.engs.values():
            for it in o.prog:
                if it[0] == "we":
                    ref[it[1]].add(it[2])
        rank = {name: {s_: i + 1 for i, s_ in enumerate(sorted(v))} for name, v in ref.items()}
        with self.nc.Block() as block:
            def mk(ename, prog):
                def body(h):
                    k = 0
                    for it in prog:
                        if it[0] == "w":
                            h.wait_ge(it[1], it[2])
                        elif it[0] == "we":
                            h.wait_ge(self.engs[it[1]].sem, rank[it[1]][it[2]])
                        else:
                            name, args, kwargs = it[1]
                            ins = getattr(h, name)(*args, **kwargs)
                            if it[3] == 16:
                                ins.then_inc(it[2], 16)
                            else:
                                k += 1
                                if k in rank[ename]:
                                    ins.then_inc(it[2], 1)
                return body
            block.tensor(mk("pe", self.engs["pe"].prog))
            block.vector(mk("dve", self.engs["dve"].prog))
            block.scalar(mk("act", self.engs["act"].prog))
            block.gpsimd(mk("pool", self.engs["pool"].prog))
            block.sync(mk("sp", self.engs["sp"].prog))


def bc(ap, shape):
    return ap.broadcast_to(list(shape))


class Builder:
    def __init__(self, SEQ=4096, NP=2, NS=4, dbg=False, do_sample=True, layers=2):
        self.SEQ, self.NP, self.NS = SEQ, NP, NS
        self.dbg = dbg
        self.do_sample = do_sample
        self.layers = layers
        self.nc = bass.Bass("TRN2", target_bir_lowering=False)
        self.dbg_outs = {}

    def din(self, name, shape):
        return self.nc.dram_tensor(name, list(shape), F32, kind="ExternalInput").ap()

    def dout(self, name, shape):
        return self.nc.dram_tensor(name, list(shape), F32, kind="ExternalOutput").ap()

    def declare(self):
        NP, NS, SEQ = self.NP, self.NS, self.SEQ
        I = {}
        I["x_prompt"] = self.din("x_prompt", [NP, SEQ, D])
        I["x_sample"] = self.din("x_sample", [NS, DEC_SEQ, D])
        I["state_s5_re"] = self.din("state_s5_re", [1, NS, 32, 64])
        I["state_s5_im"] = self.din("state_s5_im", [1, NS, 32, 64])
        I["cache_sb_k"] = self.din("cache_sb_k", [1, NS, PAST, 8, 64])
        I["cache_sb_v"] = self.din("cache_sb_v", [1, NS, PAST, 8, 64])
        I["state_ssd"] = self.din("state_ssd", [1, NS, 32, 64, 128])
        I["state_conv"] = self.din("state_conv", [1, NS, 3, 3072])
        for nm, shp in (("ln_g", [2, 3, D]), ("ln_b", [2, 3, D]), ("ffn_w_gate", [2, 2, D, DFF]),
                        ("ffn_w_up", [2, 2, D, DFF]), ("ffn_w_down", [2, 2, DFF, D]),
                        ("mix0_w_in", [1, D, 2048]), ("s5_a_re", [1, 32, 64]), ("s5_a_im", [1, 32, 64]),
                        ("s5_log_dt", [1, 32]), ("s5_b_re", [1, 32, 64, 16]), ("s5_b_im", [1, 32, 64, 16]),
                        ("s5_c_re", [1, 32, 16, 64]), ("s5_c_im", [1, 32, 16, 64]), ("s5_d", [1, 512]),
                        ("s5_w_glu", [1, 512, 512]), ("s5_b_glu", [1, 512]), ("mix0_w_out", [1, D, D]),
                        ("ssd_w_in", [1, D, 5152]), ("ssd_conv_w", [1, 4, 3072]), ("ssd_conv_b", [1, 3072]),
                        ("ssd_dt_bias", [1, 32]), ("ssd_a_log", [1, 32]), ("ssd_d", [1, 32]),
                        ("ssd_norm_g", [1, 2048]), ("ssd_w_out", [1, 2048, D])):
            I[nm] = self.din(nm, shp)
        O = {}
        O["y_prompt"] = self.dout("y_prompt", [NP, SEQ, D])
        O["y_sample"] = self.dout("y_sample", [NS, DEC_SEQ, D])
        for sfx, nb, sl in (("prompt", NP, SEQ), ("sample", NS, DEC_SEQ)):
            O["s5_re_" + sfx] = self.dout("s5_re_" + sfx, [1, nb, 32, 64])
            O["s5_im_" + sfx] = self.dout("s5_im_" + sfx, [1, nb, 32, 64])
            O["sb_k_" + sfx] = self.dout("sb_k_" + sfx, [1, nb, sl, 8, 64])
            O["sb_v_" + sfx] = self.dout("sb_v_" + sfx, [1, nb, sl, 8, 64])
            O["ssd_" + sfx] = self.dout("ssd_" + sfx, [1, nb, 32, 64, 128])
            O["conv_" + sfx] = self.dout("conv_" + sfx, [1, nb, 3, 3072])
        self.I, self.O = I, O
        S = {}
        for nm in ("ffn_w_gate", "ffn_w_up", "ffn_w_down", "mix0_w_in", "s5_w_glu", "mix0_w_out",
                   "ssd_w_in", "ssd_w_out"):
            shp = list(I[nm].shape)
            S[nm] = self.nc.dram_tensor("scr_" + nm, shp, BF16, kind="Internal").ap()
        self.S = S
        self.SB = {}
        for nm in S:
            if nm.startswith("ffn"):
                for l in range(2):
                    for j in range(2):
                        self.SB[(nm, l, j)] = Buf("scr")
            else:
                self.SB[nm] = Buf("scr")

    def dbg_out(self, name, shape):
        if name not in self.dbg_outs:
            self.dbg_outs[name] = self.dout("dbg_" + name, shape)
        return self.dbg_outs[name]

    def build(self):
        self.declare()
        with ExitStack() as es:
            self.fw = FW(self.nc, es)
            self.alloc()
            self.setup()
            tiles = []
            for s in range(self.NP):
                for i in range(self.SEQ // 512):
                    tiles.append(("p", s, i))
            for t in tiles:
                self.run_tile(t)
            if self.do_sample:
                self.run_tile(("s", 0, 0))
            self.fw.finish()
        return self.nc

    def alloc(self):
        fw = self.fw
        self.xf = fw.sbuf([128, NCH, 512], F32, "xf")
        self.Bxf = [Buf("xf%d" % c) for c in range(NCH)]
        self.xb = fw.sbuf([128, NCH, 512], BF16, "xb")
        self.Bxb = [Buf("xb%d" % c) for c in range(NCH)]
        self.hb = fw.sbuf([128, MCH, 512], BF16, "hb")
        self.Bh = [Buf("h%d" % m) for m in range(MCH)]
        self.hbf = self.hb[:].rearrange("p m n -> p (m n)").bitcast(F32)
        self.ma = fw.sbuf([128, 6144], F32, "ma")
        self.mab = self.ma[:].bitcast(BF16)
        self.NSLAB = 3
        self.slab = [fw.sbuf([128, 4096], BF16, "slab%d" % i) for i in range(self.NSLAB)]
        self.Bslab = [Buf("slab%d" % i) for i in range(self.NSLAB)]
        self.slab_i = 0
        self.KT = fw.sbuf([128, 4, 4096], BF16, "KT")
        self.VT = fw.sbuf([128, 32, 512], BF16, "VT")
        self.BKT = [Buf("KT%d" % i) for i in range(8)]
        self.BVT = [Buf("VT%d" % i) for i in range(8)]
        self.PS = [fw.psum([128, 512], F32, "psb%d" % i) for i in range(8)]
        self.BP = [Buf("ps%d" % i, psum=True) for i in range(8)]
        self.sg = [fw.sbuf([128, 512], F32, "sg%d" % i) for i in range(2)]
        self.Bsg = [Buf("sg%d" % i) for i in range(2)]
        self.knew = self.KT[:, :, 3072:3136]
        self.vnew = self.VT[0:16, 20:24, :]
        self.lnt = fw.sbuf([128, 4, 512], F32, "lnt")
        self.Blnt = [Buf("lnt%d" % i) for i in range(4)]

    def mm(self, out, lhsT, rhs, start, stop, reads, writes, **kw):
        self.fw.op("pe", lambda h: h.matmul(out, lhsT=lhsT, rhs=rhs, start=start, stop=stop, **kw),
                   reads=reads, writes=writes)

    def load_T(self, rows_ap, R, dest, Bdest):
        fw = self.fw
        fw.dma("sp", self.stg[0:R, :], rows_ap, self.Bstg, writes=[self.Bstg])
        fw.op("pe", lambda h: h.transpose(self.PS[7][:, 0:R], self.stg[0:R, :], self.ident[0:R, 0:R]),
              reads=[self.Bstg, self.Bconst], writes=[self.BP[7]])
        fw.op("dve", lambda h: h.tensor_copy(out=dest, in_=self.PS[7][:, 0:R]), reads=[self.BP[7]], writes=[Bdest])

    def setup(self):
        fw, nc, I, S = self.fw, self.nc, self.I, self.S
        def cast(nm, idx_list):
            for idx in idx_list:
                src = I[nm]
                dst = S[nm]
                for i in idx:
                    src = src[i]
                    dst = dst[i]
                key = (nm,) + tuple(idx) if nm.startswith("ffn") else nm
                fw.dma("pool", dst, src, self.SB[key], writes=[self.SB[key]])
        lj = [(l, j) for l in range(2) for j in range(2)]
        cast("ffn_w_gate", lj[:1]); cast("ffn_w_up", lj[:1]); cast("ffn_w_down", lj[:1])
        cast("mix0_w_in", [(0,)]); cast("s5_w_glu", [(0,)]); cast("mix0_w_out", [(0,)])
        cast("ffn_w_gate", lj[1:]); cast("ffn_w_up", lj[1:]); cast("ffn_w_down", lj[1:])
        cast("ssd_w_in", [(0,)]); cast("ssd_w_out", [(0,)])

        self.Bconst = Buf("const")
        self.Bstg = Buf("stg")
        self.stg = fw.sbuf([128, 128], F32, "stg")
        self.ident = fw.sbuf([128, 128], F32, "ident")
        self.identb = fw.sbuf([128, 128], BF16, "identb")
        self.onesb = fw.sbuf([128, 128], BF16, "onesb")
        self.ones1 = fw.sbuf([128, 128], BF16, "ones1")
        self.uincl = fw.sbuf([128, 128], BF16, "uincl")
        self.onesf = fw.sbuf([128, 128], F32, "onesf")
        C = self.Bconst
        fw.op("pool", lambda h: h.memset(self.ident[:], 1.0), writes=[C])
        fw.op("pool", lambda h: h.affine_select(out=self.ident[:], in_=self.ident[:], pattern=[[-1, 128]],
                                                compare_op=ALU.is_equal, fill=0.0, base=0, channel_multiplier=1),
              reads=[C], writes=[C])
        fw.op("pool", lambda h: h.tensor_copy(out=self.identb[:], in_=self.ident[:]), reads=[C], writes=[C])
        fw.op("pool", lambda h: h.memset(self.onesb[:], 1.0 / 1024.0), writes=[C])
        fw.op("pool", lambda h: h.memset(self.ones1[:], 1.0), writes=[C])
        fw.op("pool", lambda h: h.memset(self.onesf[:], 1.0), writes=[C])
        fw.op("pool", lambda h: h.memset(self.uincl[:], 1.0), writes=[C])
        fw.op("pool", lambda h: h.affine_select(out=self.uincl[:], in_=self.uincl[:], pattern=[[-1, 128]],
                                                compare_op=ALU.is_ge, fill=0.0, base=0, channel_multiplier=1),
              reads=[C], writes=[C])
        self.lng = fw.sbuf([128, 48], F32, "lng")
        self.lnb = fw.sbuf([128, 48], F32, "lnb")
        self.load_T(I["ln_g"].rearrange("l j (c p) -> (l j c) p", p=128), 48, self.lng[:], C)
        self.load_T(I["ln_b"].rearrange("l j (c p) -> (l j c) p", p=128), 48, self.lnb[:], C)
        self.setup_mix0()
        if self.layers > 1:
            self.setup_ssd()
        fw.barrier([self.Bconst, self.Bstg])

    def get_slab(self, pieces):
        i = self.slab_i
        self.slab_i = (i + 1) % self.NSLAB
        t, B = self.slab[i], self.Bslab[i]
        for dst_fn, src, key in pieces:
            self.fw.dma("sp", dst_fn(t), src, B, reads=[self.SB[key]], writes=[B])
        return t, B

    def layer_norm(self, idx, N):
        fw = self.fw
        xf, xb, hb = self.xf, self.xb, self.hb
        Bxf, Bxb, Bh, BP = self.Bxf, self.Bxb, self.Bh, self.BP
        ybf = hb[:, 0:8, 0:N]
        sq = hb[:, 8:16, 0:N]
        fw.op("dve", lambda h: h.tensor_copy(out=ybf, in_=xf[:, :, 0:N]), reads=Bxf, writes=Bh[0:8])
        fw.op("act", lambda h: h.activation(out=sq, in_=xf[:, :, 0:N], func=AF.Square), reads=Bxf, writes=Bh[8:16])
        pm, pq = self.PS[0], self.PS[1]
        for c in range(NCH):
            self.mm(pm[:, 0:N], self.onesb[:], hb[:, c, 0:N], c == 0, c == NCH - 1, [Bh[c], self.Bconst], [BP[0]])
        for c in range(NCH):
            self.mm(pq[:, 0:N], self.onesb[:], hb[:, 8 + c, 0:N], c == 0, c == NCH - 1, [Bh[8 + c], self.Bconst], [BP[1]])
        mean, rstd, nmr = self.lnt[:, 0, 0:N], self.lnt[:, 1, 0:N], self.lnt[:, 2, 0:N]
        Bl = self.Blnt
        fw.op("act", lambda h: h.activation(out=mean, in_=pm[:, 0:N], func=AF.Copy), reads=[BP[0]], writes=[Bl[0]])
        fw.op("dve", lambda h: h.tensor_tensor(out=rstd, in0=pm[:, 0:N], in1=mean, op=ALU.mult), reads=[BP[0], Bl[0]], writes=[Bl[1]])
        fw.op("dve", lambda h: h.tensor_tensor(out=rstd, in0=pq[:, 0:N], in1=rstd, op=ALU.subtract), reads=[BP[1], Bl[1]], writes=[Bl[1]])
        fw.op("dve", lambda h: h.tensor_scalar(out=rstd, in0=rstd, scalar1=0.0, scalar2=LN_EPS, op0=ALU.max, op1=ALU.add), reads=[Bl[1]], writes=[Bl[1]])
        fw.op("act", lambda h: h.activation(out=rstd, in_=rstd, func=AF.Sqrt), reads=[Bl[1]], writes=[Bl[1]])
        fw.op("dve", lambda h: h.reciprocal(out=rstd, in_=rstd), reads=[Bl[1]], writes=[Bl[1]])
        fw.op("dve", lambda h: h.scalar_tensor_tensor(out=nmr, in0=mean, scalar=-1.0, in1=rstd, op0=ALU.mult, op1=ALU.mult),
              reads=[Bl[0], Bl[1]], writes=[Bl[2]])
        xv = xf[:, :, 0:N]
        fw.op("dve", lambda h: h.tensor_tensor(out=xv, in0=xv, in1=bc(rstd.unsqueeze(1), [128, NCH, N]), op=ALU.mult),
              reads=Bxf + [Bl[1]], writes=Bxf)
        fw.op("dve", lambda h: h.tensor_tensor(out=xv, in0=xv, in1=bc(nmr.unsqueeze(1), [128, NCH, N]), op=ALU.add),
              reads=Bxf + [Bl[2]], writes=Bxf)
        for c in range(NCH):
            col = idx * 8 + c
            fw.op("act", lambda h, c=c, col=col: h.activation(out=xf[:, c, 0:N], in_=xf[:, c, 0:N], func=AF.Identity,
                                                               scale=self.lng[:, col:col + 1], bias=self.lnb[:, col:col + 1]),
                  reads=[Bxf[c], self.Bconst], writes=[Bxf[c]])
            eng = "dve" if c % 2 == 0 else "pool"
            fw.op(eng, lambda h, c=c: h.tensor_copy(out=xb[:, c, 0:N], in_=xf[:, c, 0:N]), reads=[Bxf[c]], writes=[Bxb[c]])

    def ffn(self, l, j, N):
        fw, S = self.fw, self.S
        xf, xb, hb = self.xf, self.xb, self.hb
        Bxf, Bxb, Bh, BP, PS = self.Bxf, self.Bxb, self.Bh, self.BP, self.PS
        wg = S["ffn_w_gate"][l, j].rearrange("(k p) f -> p k f", p=128)
        wu = S["ffn_w_up"][l, j].rearrange("(k p) f -> p k f", p=128)
        wd = S["ffn_w_down"][l, j].rearrange("(m p) f -> p m f", p=128)
        for s in range(11):
            cols = slice(256 * s, 256 * s + 256)
            t, B = self.get_slab([
                (lambda t: t[:, 0:2048].rearrange("p (k f) -> p k f", k=8), wg[:, :, cols], ("ffn_w_gate", l, j)),
                (lambda t: t[:, 2048:4096].rearrange("p (k f) -> p k f", k=8), wu[:, :, cols], ("ffn_w_up", l, j))])
            tg = t[:, 0:2048].rearrange("p (k f) -> p k f", k=8)
            tu = t[:, 2048:4096].rearrange("p (k f) -> p k f", k=8)
            for mi in range(2):
                m = 2 * s + mi
                pg, pu = PS[m % 2], PS[2 + m % 2]
                Bg, Bu = BP[m % 2], BP[2 + m % 2]
                for k in range(NCH):
                    self.mm(pg[:, 0:N], tg[:, k, mi * 128:(mi + 1) * 128], xb[:, k, 0:N], k == 0, k == NCH - 1, [B, Bxb[k]], [Bg])
                for k in range(NCH):
                    self.mm(pu[:, 0:N], tu[:, k, mi * 128:(mi + 1) * 128], xb[:, k, 0:N], k == 0, k == NCH - 1, [B, Bxb[k]], [Bu])
                sg, Bs = self.sg[m % 2], self.Bsg[m % 2]
                fw.op("act", lambda h, sg=sg, pg=pg: h.activation(out=sg[:, 0:N], in_=pg[:, 0:N], func=AF.Silu), reads=[Bg], writes=[Bs])
                fw.op("dve", lambda h, sg=sg, pu=pu, m=m: h.scalar_tensor_tensor(out=hb[:, m, 0:N], in0=sg[:, 0:N], scalar=0.5, in1=pu[:, 0:N],
                                                                                op0=ALU.mult, op1=ALU.mult),
                      reads=[Bs, Bu], writes=[Bh[m]])
        for o2 in range(4):
            cols = slice(256 * o2, 256 * o2 + 256)
            for half in range(2):
                t, B = self.get_slab([(lambda t: t[:, 0:2816].rearrange("p (m f) -> p m f", m=11),
                                       wd[:, 11 * half:11 * half + 11, cols], ("ffn_w_down", l, j))])
                tv = t[:, 0:2816].rearrange("p (m f) -> p m f", m=11)
                for oi in range(2):
                    o = 2 * o2 + oi
                    pd, Bd = PS[4 + o % 4], BP[4 + o % 4]
                    for mm_ in range(11):
                        m = 11 * half + mm_
                        self.mm(pd[:, 0:N], tv[:, mm_, oi * 128:(oi + 1) * 128], hb[:, m, 0:N], m == 0, m == MCH - 1, [B, Bh[m]], [Bd])
            for oi in range(2):
                o = 2 * o2 + oi
                pd, Bd = PS[4 + o % 4], BP[4 + o % 4]
                fw.op("dve", lambda h, o=o, pd=pd: h.scalar_tensor_tensor(out=xf[:, o, 0:N], in0=xf[:, o, 0:N], scalar=ALPHA, in1=pd[:, 0:N],
                                                                         op0=ALU.mult, op1=ALU.add),
                      reads=[Bxf[o], Bd], writes=[Bxf[o]])
        self.layer_norm(l * 3 + (0 if j == 0 else 2), N)

    def load_x(self, tile, N):
        fw = self.fw
        kind, s, i = tile
        nb = (N + 127) // 128
        stg = self.hbf[:, 0:4096].rearrange("p (b f) -> p b f", b=4)
        Bst = self.Bh[0:16]
        for b in range(nb):
            rows = min(128, N - b * 128)
            if kind == "p":
                src = self.I["x_prompt"][s, i * 512 + b * 128:i * 512 + b * 128 + rows, :]
            else:
                src = self.I["x_sample"].rearrange("s t d -> (s t) d")[b * 128:b * 128 + rows, :]
            fw.dma("sp", stg[0:rows, b, :], src, Bst[4 * b], writes=Bst[4 * b:4 * b + 4])
        for c in range(NCH):
            ps, Bp = self.PS[c % 4], self.BP[c % 4]
            for b in range(nb):
                rows = min(128, N - b * 128)
                fw.op("pe", lambda h, ps=ps, b=b, c=c, rows=rows: h.transpose(ps[:, b * 128:b * 128 + rows], stg[0:rows, b, c * 128:(c + 1) * 128],
                                                                           self.ident[0:rows, 0:rows]),
                      reads=Bst[4 * b:4 * b + 4] + [self.Bconst], writes=[Bp])
            fw.op("act", lambda h, ps=ps, c=c: h.activation(out=self.xf[:, c, 0:N], in_=ps[:, 0:N], func=AF.Copy), reads=[Bp], writes=[self.Bxf[c]])
            fw.op("dve", lambda h, ps=ps, c=c: h.tensor_copy(out=self.xb[:, c, 0:N], in_=ps[:, 0:N]), reads=[Bp], writes=[self.Bxb[c]])

    def store_y(self, tile, N):
        fw = self.fw
        kind, s, i = tile
        nb = (N + 127) // 128
        stg = self.hbf[:, 0:4096].rearrange("p (b f) -> p b f", b=4)
        Bst = self.Bh[0:16]
        for b in range(nb):
            rows = min(128, N - b * 128)
            for half in range(2):
                ps, Bp = self.PS[(2 * b + half) % 4], self.BP[(2 * b + half) % 4]
                for cc in range(4):
                    c = 4 * half + cc
                    fw.op("pe", lambda h, ps=ps, b=b, c=c, cc=cc, rows=rows: h.transpose(ps[0:rows, cc * 128:(cc + 1) * 128],
                                                                                     self.xf[:, c, b * 128:b * 128 + rows], self.ident[:, :]),
                          reads=[self.Bxf[c], self.Bconst], writes=[Bp])
                eng = "act" if half == 0 else "dve"
                if eng == "act":
                    fw.op("act", lambda h, ps=ps, b=b, half=half, rows=rows: h.activation(out=stg[0:rows, b, half * 512:(half + 1) * 512], in_=ps[0:rows, :], func=AF.Copy),
                          reads=[Bp], writes=[Bst[4 * b + 2 * half], Bst[4 * b + 2 * half + 1]])
                else:
                    fw.op("dve", lambda h, ps=ps, b=b, half=half, rows=rows: h.tensor_copy(out=stg[0:rows, b, half * 512:(half + 1) * 512], in_=ps[0:rows, :]),
                          reads=[Bp], writes=[Bst[4 * b + 2 * half], Bst[4 * b + 2 * half + 1]])
            if kind == "p":
                dst = self.O["y_prompt"][s, i * 512 + b * 128:i * 512 + b * 128 + rows, :]
            else:
                dst = self.O["y_sample"].rearrange("s t d -> (s t) d")[b * 128:b * 128 + rows, :]
            fw.dma("sp", dst, stg[0:rows, b, :], Bst[4 * b], reads=Bst[4 * b:4 * b + 4])

    def dbg_sb(self, name, ap, B, shape):
        if not self.dbg or name in self.dbg_outs:
            return
        o = self.dbg_out(name, shape)
        self.fw.dma("sp", o, ap, B, reads=[B])

    def dump_x(self, name, tile, N):
        if not self.dbg:
            return
        kind, s, i = tile
        ntile = self.NP * (self.SEQ // 512) + 1
        o = self.dbg_out(name, [ntile, NCH, 128, 512])
        ti = (s * (self.SEQ // 512) + i) if kind == "p" else ntile - 1
        for c in range(NCH):
            self.fw.dma("sp", o[ti, c, :, 0:N], self.xf[:, c, 0:N], self.Bxf[c], reads=[self.Bxf[c]])

    def run_tile(self, tile):
        import os
        self.stop = int(os.environ.get("KSTOP", "9"))
        kind = tile[0]
        N = 512 if kind == "p" else self.NS * DEC_SEQ
        if self.stop < 1:
            return
        self.fw.mark("tile %s %d %d" % tile)
        self.load_x(tile, N)
        self.fw.mark("ffn00")
        ksub = os.environ.get("KSUB", "")
        if ksub == "load":
            self.dump_x("ffn00", tile, N)
            return
        if ksub == "ln":
            self.layer_norm(0, N)
            self.dump_x("ffn00", tile, N)
            return
        self.ffn(0, 0, N)
        self.dump_x("ffn00", tile, N)
        if self.stop < 2:
            return
        self.fw.mark("mixer0")
        self.mixer0(tile, N)
        self.dump_x("mix0", tile, N)
        self.fw.mark("ffn01")
        self.ffn(0, 1, N)
        self.dump_x("l0", tile, N)
        if self.layers > 1:
            self.fw.mark("ffn10")
            self.ffn(1, 0, N)
            self.fw.mark("mixer1")
            self.mixer1(tile, N)
            self.dump_x("mix1", tile, N)
            self.fw.mark("ffn11")
            self.ffn(1, 1, N)
        self.fw.mark("store")
        self.store_y(tile, N)
        self.fw.mark("end")


def _ts(h, out, in0, s1, s2, op0, op1=None):
    if op1 is None:
        return h.tensor_scalar(out=out, in0=in0, scalar1=s1, scalar2=None, op0=op0)
    return h.tensor_scalar(out=out, in0=in0, scalar1=s1, scalar2=s2, op0=op0, op1=op1)


def setup_mix0(self):
    fw, I = self.fw, self.I
    C = self.Bconst
    dve = lambda fn, r=(C,), w=(C,): fw.op("dve", fn, reads=list(r), writes=list(w))
    self.s5d = fw.sbuf([128, 4], F32, "s5d")
    self.bglu = fw.sbuf([128, 4], F32, "bglu")
    self.load_T(I["s5_d"].rearrange("o (c p) -> (o c) p", p=128), 4, self.s5d[:], C)
    self.load_T(I["s5_b_glu"].rearrange("o (c p) -> (o c) p", p=128), 4, self.bglu[:], C)
    self.negones = fw.sbuf([128, 128], BF16, "negones")
    self.nuincl = fw.sbuf([128, 128], BF16, "nuincl")
    fw.op("pool", lambda h: h.memset(self.negones[:], -1.0), writes=[C])
    fw.op("pool", lambda h: h.tensor_scalar(out=self.nuincl[:], in0=self.uincl[:], scalar1=-1.0, scalar2=None, op0=ALU.mult),
          reads=[C], writes=[C])
    L = LSUB
    pr = self.ma[:, 3712:3904].rearrange("p (i j) -> p i j", i=12)
    P = lambda i: pr[:, i, :]
    self.load_T(I["s5_a_re"].rearrange("o (j g) n -> (o j) (g n)", g=2), 16, P(0), C)
    self.load_T(I["s5_a_im"].rearrange("o (j g) n -> (o j) (g n)", g=2), 16, P(1), C)
    ld = I["s5_log_dt"].rearrange("o (j t) -> o j t", t=2)
    fw.dma("sp", pr[0:64, 2, :], ld[0:1, :, 0].broadcast_to([64, 16]), C, writes=[C], slow=True)
    fw.dma("sp", pr[64:128, 2, :], ld[0:1, :, 1].broadcast_to([64, 16]), C, writes=[C], slow=True)
    fw.op("act", lambda h: h.activation(out=P(2), in_=P(2), func=AF.Exp), reads=[C], writes=[C])
    dve(lambda h: h.tensor_tensor(out=P(3), in0=P(0), in1=P(2), op=ALU.mult))
    dve(lambda h: h.tensor_tensor(out=P(4), in0=P(1), in1=P(2), op=ALU.mult))
    self.s5r = fw.sbuf([128, 16], F32, "s5r")
    fw.op("act", lambda h: h.activation(out=self.s5r[:], in_=P(3), func=AF.Exp), reads=[C], writes=[C])
    dve(lambda h: _ts(h, P(5), P(4), 1.0 / (2 * math.pi), MAGIC, ALU.mult, ALU.add))
    dve(lambda h: _ts(h, P(5), P(5), MAGIC, None, ALU.subtract))
    dve(lambda h: h.scalar_tensor_tensor(out=P(5), in0=P(5), scalar=-2 * math.pi, in1=P(4), op0=ALU.mult, op1=ALU.add))
    dve(lambda h: _ts(h, P(5), P(5), 0.125, None, ALU.mult))
    dve(lambda h: h.tensor_tensor(out=P(6), in0=P(5), in1=P(5), op=ALU.mult))
    sc = [-1.0 / 6, 1.0 / 120, -1.0 / 5040, 1.0 / 362880]
    cc = [-0.5, 1.0 / 24, -1.0 / 720, 1.0 / 40320, -1.0 / 3628800]
    dve(lambda h: _ts(h, P(7), P(6), sc[3], sc[2], ALU.mult, ALU.add))
    for co in (sc[1], sc[0], 1.0):
        dve(lambda h: h.tensor_tensor(out=P(7), in0=P(7), in1=P(6), op=ALU.mult))
        dve(lambda h, co=co: _ts(h, P(7), P(7), co, None, ALU.add))
    dve(lambda h: h.tensor_tensor(out=P(7), in0=P(7), in1=P(5), op=ALU.mult))
    dve(lambda h: _ts(h, P(8), P(6), cc[4], cc[3], ALU.mult, ALU.add))
    for co in (cc[2], cc[1], cc[0], 1.0):
        dve(lambda h: h.tensor_tensor(out=P(8), in0=P(8), in1=P(6), op=ALU.mult))
        dve(lambda h, co=co: _ts(h, P(8), P(8), co, None, ALU.add))
    for _ in range(3):
        dve(lambda h: h.tensor_tensor(out=P(9), in0=P(8), in1=P(7), op=ALU.mult))
        dve(lambda h: h.tensor_tensor(out=P(8), in0=P(8), in1=P(8), op=ALU.mult))
        dve(lambda h: h.tensor_tensor(out=P(7), in0=P(7), in1=P(7), op=ALU.mult))
        dve(lambda h: h.tensor_tensor(out=P(8), in0=P(8), in1=P(7), op=ALU.subtract))
        dve(lambda h: _ts(h, P(7), P(9), 2.0, None, ALU.mult))
    self.Ct = fw.sbuf([128, 16, L + 1], F32, "Ct")
    self.St = fw.sbuf([128, 16, L + 1], F32, "St")
    Ct, St = self.Ct, self.St
    dve(lambda h: h.memset(Ct[:, :, 0:1], 1.0))
    dve(lambda h: h.memset(St[:, :, 0:1], 0.0))
    dve(lambda h: h.tensor_copy(out=Ct[:, :, 1:2], in_=P(8).unsqueeze(2)))
    dve(lambda h: h.tensor_copy(out=St[:, :, 1:2], in_=P(7).unsqueeze(2)))
    tmpA = self.ma[:, 0:2048].rearrange("p (j l) -> p j l", j=16)
    n = 2
    while n <= L:
        hn = n // 2
        ch, sh = Ct[:, :, hn:hn + 1], St[:, :, hn:hn + 1]
        dve(lambda h, ch=ch, sh=sh: h.tensor_tensor(out=P(9).unsqueeze(2), in0=ch, in1=sh, op=ALU.mult))
        dve(lambda h, ch=ch: h.tensor_tensor(out=P(10).unsqueeze(2), in0=ch, in1=ch, op=ALU.mult))
        dve(lambda h, sh=sh: h.tensor_tensor(out=P(11).unsqueeze(2), in0=sh, in1=sh, op=ALU.mult))
        dve(lambda h, n=n: h.tensor_tensor(out=Ct[:, :, n:n + 1], in0=P(10).unsqueeze(2), in1=P(11).unsqueeze(2), op=ALU.subtract))
        dve(lambda h, n=n: _ts(h, St[:, :, n:n + 1], P(9).unsqueeze(2), 2.0, None, ALU.mult))
        m = min(n, L + 1 - n)
        if m > 1:
            cn = bc(Ct[:, :, n:n + 1], [128, 16, m - 1])
            sn = bc(St[:, :, n:n + 1], [128, 16, m - 1])
            c0, s0 = Ct[:, :, 1:m], St[:, :, 1:m]
            tA = tmpA[:, :, 0:m - 1]
            dve(lambda h, c0=c0, cn=cn, tA=tA: h.tensor_tensor(out=tA, in0=c0, in1=cn, op=ALU.mult))
            dve(lambda h, s0=s0, sn=sn, n=n, m=m: h.tensor_tensor(out=Ct[:, :, n + 1:n + m], in0=s0, in1=sn, op=ALU.mult))
            dve(lambda h, tA=tA, n=n, m=m: h.tensor_tensor(out=Ct[:, :, n + 1:n + m], in0=tA, in1=Ct[:, :, n + 1:n + m], op=ALU.subtract))
            dve(lambda h, s0=s0, cn=cn, tA=tA: h.tensor_tensor(out=tA, in0=s0, in1=cn, op=ALU.mult))
            dve(lambda h, c0=c0, sn=sn, n=n, m=m: h.tensor_tensor(out=St[:, :, n + 1:n + m], in0=c0, in1=sn, op=ALU.mult))
            dve(lambda h, tA=tA, n=n, m=m: h.tensor_tensor(out=St[:, :, n + 1:n + m], in0=tA, in1=St[:, :, n + 1:n + m], op=ALU.add))
        n *= 2
    dve(lambda h: h.tensor_tensor(out=P(9), in0=self.s5r[:], in1=P(8), op=ALU.mult))
    dve(lambda h: h.tensor_tensor(out=P(10), in0=self.s5r[:], in1=P(7), op=ALU.mult))
    dve(lambda h: _ts(h, P(9), P(9), -1.0, None, ALU.add))
    dve(lambda h: h.tensor_tensor(out=P(2), in0=P(0), in1=P(0), op=ALU.mult))
    dve(lambda h: h.tensor_tensor(out=P(3), in0=P(1), in1=P(1), op=ALU.mult))
    dve(lambda h: h.tensor_tensor(out=P(2), in0=P(2), in1=P(3), op=ALU.add))
    dve(lambda h: h.reciprocal(out=P(2), in_=P(2)))
    dve(lambda h: h.tensor_tensor(out=P(3), in0=P(9), in1=P(0), op=ALU.mult))
    dve(lambda h: h.tensor_tensor(out=P(4), in0=P(10), in1=P(1), op=ALU.mult))
    dve(lambda h: h.tensor_tensor(out=P(3), in0=P(3), in1=P(4), op=ALU.add))
    dve(lambda h: h.tensor_tensor(out=P(5), in0=P(3), in1=P(2), op=ALU.mult))
    dve(lambda h: h.tensor_tensor(out=P(3), in0=P(10), in1=P(0), op=ALU.mult))
    dve(lambda h: h.tensor_tensor(out=P(4), in0=P(9), in1=P(1), op=ALU.mult))
    dve(lambda h: h.tensor_tensor(out=P(3), in0=P(3), in1=P(4), op=ALU.subtract))
    dve(lambda h: h.tensor_tensor(out=P(6), in0=P(3), in1=P(2), op=ALU.mult))
    braw = self.ma[:, 2048:3072].rearrange("p (t j q) -> p t j q", t=4, j=16)
    for t, nm in ((0, "s5_b_re"), (1, "s5_b_im")):
        fw.dma("sp", braw[:, t, :, :], I[nm][0].rearrange("g n q -> (g n) q").rearrange("(j p) q -> p j q", p=128), C, writes=[C])
    fre = bc(P(5).unsqueeze(2), [128, 16, 16])
    fim = bc(P(6).unsqueeze(2), [128, 16, 16])
    t3 = tmpA[:, :, 0:16]
    dve(lambda h: h.tensor_tensor(out=braw[:, 2], in0=braw[:, 0], in1=fre, op=ALU.mult))
    dve(lambda h: h.tensor_tensor(out=t3, in0=braw[:, 1], in1=fim, op=ALU.mult))
    dve(lambda h: h.tensor_tensor(out=braw[:, 2], in0=braw[:, 2], in1=t3, op=ALU.subtract))
    dve(lambda h: h.tensor_tensor(out=braw[:, 3], in0=braw[:, 1], in1=fre, op=ALU.mult))
    dve(lambda h: h.tensor_tensor(out=t3, in0=braw[:, 0], in1=fim, op=ALU.mult))
    dve(lambda h: h.tensor_tensor(out=braw[:, 3], in0=braw[:, 3], in1=t3, op=ALU.add))
    if self.dbg:
        o = self.dbg_out("s5par", [128, 12, 16])
        fw.dma("sp", o, pr, C, reads=[C])
        o = self.dbg_out("s5ct", [128, 16, L + 1])
        fw.dma("sp", o, Ct[:], C, reads=[C])
        o = self.dbg_out("s5st", [128, 16, L + 1])
        fw.dma("sp", o, St[:], C, reads=[C])
        o = self.dbg_out("s5r", [128, 16])
        fw.dma("sp", o, self.s5r[:], C, reads=[C])
        o = self.dbg_out("s5braw", [128, 4, 16, 16])
        fw.dma("sp", o, braw, C, reads=[C])
    self.LBc = fw.sbuf([128, 8, 128], BF16, "LBc")
    self.LCc = fw.sbuf([128, 16, 2, 32], BF16, "LCc")
    mj = self.ma[:, 3584:3712]
    Bmj = Buf("mj")
    for j in range(16):
        c, q = j // 4, j % 4
        for reim in range(2):
            fw.op("dve", lambda h: h.memset(mj[:], 0.0), writes=[Bmj])
            g0 = (2 * j) % 8
            fw.op("dve", lambda h, j=j, reim=reim, g0=g0: h.tensor_copy(out=mj[0:64, g0 * 16:g0 * 16 + 16], in_=braw[0:64, 2 + reim, j, :]),
                  reads=[C], writes=[Bmj])
            fw.op("dve", lambda h, j=j, reim=reim, g0=g0: h.tensor_copy(out=mj[64:128, (g0 + 1) * 16:(g0 + 1) * 16 + 16], in_=braw[64:128, 2 + reim, j, :]),
                  reads=[C], writes=[Bmj])
            fw.op("pe", lambda h: h.transpose(self.PS[7][:, 0:128], mj[:], self.ident[:]), reads=[Bmj, C], writes=[self.BP[7]])
            fw.op("dve", lambda h, c=c, q=q, reim=reim: h.tensor_copy(out=self.LBc[32 * q:32 * q + 32, c * 2 + reim, :], in_=self.PS[7][32 * q:32 * q + 32, 0:128]),
                  reads=[self.BP[7]], writes=[C])
    craw = self.ma[:, 3072:3584].rearrange("p (t c n) -> p t c n", t=2, c=4)
    for t, nm in ((0, "s5_c_re"), (1, "s5_c_im")):
        fw.dma("sp", craw[:, t, :, :], I[nm][0].rearrange("g p n -> (g p) n").rearrange("(c q) n -> q c n", q=128), C, writes=[C])
    e8 = fw.sbuf([128, 2, 8], F32, "e8")
    fw.op("pool", lambda h: h.memset(e8[:, 0, :], 1.0), reads=[C], writes=[C])
    fw.op("pool", lambda h: h.affine_select(out=e8[:, 0, :], in_=e8[:, 0, :], pattern=[[-16, 8]], compare_op=ALU.is_ge, fill=0.0,
                                            base=0, channel_multiplier=1), reads=[C], writes=[C])
    fw.op("pool", lambda h: h.affine_select(out=e8[:, 0, :], in_=e8[:, 0, :], pattern=[[16, 8]], compare_op=ALU.is_ge, fill=0.0,
                                            base=15, channel_multiplier=-1), reads=[C], writes=[C])
    fw.op("pool", lambda h: h.tensor_scalar(out=e8[:, 1, :], in0=e8[:, 0, :], scalar1=-1.0, scalar2=None, op0=ALU.mult), reads=[C], writes=[C])
    for j in range(16):
        c, q = j // 4, j % 4
        g0 = (2 * j) % 8
        for reim in range(2):
            for gl in range(2):
                fw.op("dve", lambda h, c=c, reim=reim, gl=gl, g0=g0: h.tensor_scalar(
                    out=mj[:, gl * 64:(gl + 1) * 64], in0=craw[:, reim, c, :], scalar1=e8[:, reim, g0 + gl:g0 + gl + 1], scalar2=None, op0=ALU.mult),
                    reads=[C], writes=[Bmj])
            fw.op("pe", lambda h: h.transpose(self.PS[7][:, 0:128], mj[:], self.ident[:]), reads=[Bmj, C], writes=[self.BP[7]])
            fw.op("dve", lambda h, j=j, q=q, reim=reim: h.tensor_copy(out=self.LCc[:, j, reim, :], in_=self.PS[7][:, 32 * q:32 * q + 32]),
                  reads=[self.BP[7]], writes=[C])
    self.G = fw.sbuf([128, 16, 2], F32, "s5G")
    self.BG = Buf("s5G", strict=True)
    self.HL = fw.sbuf([128, 2, 16], F32, "s5HL")
    self.BHL = Buf("s5HL")
    self.s5tmp = fw.sbuf([128, 4], F32, "s5tmp")
    self.Bs5tmp = Buf("s5tmp")
    self.crowb2 = [self.lnt[0:1, 2 + i, :].bitcast(BF16).rearrange("p (a n) -> p a n", a=2) for i in range(2)]
    self.Bcrowb2 = [Buf("crowb0"), Buf("crowb1")]
    self.Bcrow = Buf("crow")
    self.Bcrowb = Buf("crowb")


def s5_phase(self, tile, N):
    fw = self.fw
    kind, s, ti = tile
    C = self.Bconst
    PS, BP = self.PS, self.BP
    ma, mab, hbf = self.ma, self.mab, self.hbf
    uf = hbf[:, 0:2048].rearrange("p (c n) -> p c n", c=4)
    yf = hbf[:, 2048:4096].rearrange("p (c n) -> p c n", c=4)
    ub = self.hb[:, 16:20, :]
    s5out = mab[:, 2048:4096].rearrange("p (c n) -> p c n", c=4)
    T = [ma[:, 3072 + 512 * i:3072 + 512 * (i + 1)] for i in range(4)]
    BT = self.BT
    hre = [mab[:, 10240 + 1024 * i:10240 + 1024 * i + 512] for i in range(2)]
    him = [mab[:, 10240 + 1024 * i + 512:10240 + 1024 * (i + 1)] for i in range(2)]
    Bhh = self.Bhh
    Ct, St = self.Ct, self.St
    if kind == "p":
        nsub, L = 512 // LSUB, LSUB
    else:
        nsub, L = self.NS, DEC_SEQ
    dve = lambda fn, r, w: fw.op("dve", fn, reads=list(r), writes=list(w))
    last_tile = (kind == "s") or (ti == self.SEQ // 512 - 1)
    if kind == "p" and ti == 0:
        dve(lambda h: h.memset(self.G[:], 0.0), [], [self.BG])
    for j in range(16):
        c, q = j // 4, j % 4
        pa, pb = PS[0], PS[1]
        rhs = ub[32 * q:32 * q + 32, c, 0:N]
        self.mm(pa[:, 0:N], self.LBc[32 * q:32 * q + 32, c * 2 + 0, :], rhs, True, True, [C, self.Bub], [BP[0]], tile_position=(32 * q, 0))
        self.mm(pb[:, 0:N], self.LBc[32 * q:32 * q + 32, c * 2 + 1, :], rhs, True, True, [C, self.Bub], [BP[1]], tile_position=(32 * q, 0))
        v3 = lambda ap: ap[:, 0:N].rearrange("p (s l) -> p s l", s=nsub)
        ct = bc(Ct[:, j:j + 1, 0:L], [128, nsub, L])
        st = bc(St[:, j:j + 1, 0:L], [128, nsub, L])
        A, Bb, Dd, E = T
        dve(lambda h: h.tensor_tensor(out=v3(A), in0=v3(pa), in1=ct, op=ALU.mult), [BP[0], C], [BT[0]])
        dve(lambda h: h.tensor_tensor(out=v3(Bb), in0=v3(pb), in1=st, op=ALU.mult), [BP[1], C], [BT[1]])
        dve(lambda h: h.tensor_tensor(out=A[:, 0:N], in0=A[:, 0:N], in1=Bb[:, 0:N], op=ALU.add), [BT[0], BT[1]], [BT[0]])
        dve(lambda h: h.tensor_tensor(out=v3(Dd), in0=v3(pb), in1=ct, op=ALU.mult), [BP[1], C], [BT[2]])
        dve(lambda h: h.tensor_tensor(out=v3(Bb), in0=v3(pa), in1=st, op=ALU.mult), [BP[0], C], [BT[1]])
        dve(lambda h: h.tensor_tensor(out=Dd[:, 0:N], in0=Dd[:, 0:N], in1=Bb[:, 0:N], op=ALU.subtract), [BT[2], BT[1]], [BT[2]])
        if j == 5 and kind == "p" and self.dbg:
            dve(lambda h: h.tensor_copy(out=E[:, 0:N], in_=pa[:, 0:N]), [BP[0]], [BT[3]])
            self.dbg_sb("bu_re", E[:, 0:N], BT[3], [128, N])
            dve(lambda h: h.tensor_copy(out=self.sg[0][:, 0:256].rearrange("p (a b) -> p a b", a=2), in_=self.LBc[:, 2:4, :]), [C], [self.Bsg[0]])
            self.dbg_sb("lbc", self.sg[0][:, 0:256], self.Bsg[0], [128, 256])
            self.dbg_sb("ub", ub[:, :, 0:N].bitcast(mybir.dt.uint16), self.Bub, [128, 4, N]) if False else None
        if j == 5 and kind == "p":
            self.dbg_sb("ut_re", A[:, 0:N], BT[0], [128, N])
            self.dbg_sb("ut_im", Dd[:, 0:N], BT[2], [128, N])
            self.dbg_sb("uf", uf[:, :, 0:N], self.Buf_uf, [128, 4, N])
        rr = bc(self.s5r[:, j:j + 1], [128, L])
        for sub in range(nsub):
            sl = slice(sub * L, (sub + 1) * L)
            if kind == "s":
                gi = self.Gs[:, sub, j, :]
                BGi = self.BGs
            else:
                gi = self.G[:, j, :]
                BGi = self.BG
            dve(lambda h, sl=sl, gi=gi: h.tensor_tensor_scan(out=Bb[:, sl], data0=rr, data1=A[:, sl], initial=gi[:, 0:1], op0=ALU.mult, op1=ALU.add),
                [BT[0], BGi, C], [BT[1]])
            dve(lambda h, sl=sl, gi=gi: h.tensor_tensor_scan(out=E[:, sl], data0=rr, data1=Dd[:, sl], initial=gi[:, 1:2], op0=ALU.mult, op1=ALU.add),
                [BT[2], BGi, C], [BT[3]])
            e0 = (sub + 1) * L - 1
            gre_l, gim_l = Bb[:, e0:e0 + 1], E[:, e0:e0 + 1]
            tmp = self.s5tmp
            if kind == "p":
                cL, sL = Ct[:, j, L:L + 1], St[:, j, L:L + 1]
                dve(lambda h, gim_l=gim_l, sL=sL: h.tensor_tensor(out=tmp[:, 0:1], in0=gim_l, in1=sL, op=ALU.mult), [BT[3], C], [self.Bs5tmp])
                dve(lambda h, gre_l=gre_l, sL=sL: h.tensor_tensor(out=tmp[:, 1:2], in0=gre_l, in1=sL, op=ALU.mult), [BT[1], C], [self.Bs5tmp])
                dve(lambda h, gre_l=gre_l, cL=cL, j=j: h.scalar_tensor_tensor(out=self.G[:, j, 0:1], in0=gre_l, scalar=cL, in1=tmp[:, 0:1], op0=ALU.mult, op1=ALU.subtract),
                    [BT[1], C, self.Bs5tmp], [self.BG])
                dve(lambda h, gim_l=gim_l, cL=cL, j=j: h.scalar_tensor_tensor(out=self.G[:, j, 1:2], in0=gim_l, scalar=cL, in1=tmp[:, 1:2], op0=ALU.mult, op1=ALU.add),
                    [BT[3], C, self.Bs5tmp], [self.BG])
            if last_tile and (kind == "s" or sub == nsub - 1):
                c1, s1 = Ct[:, j, L - 1:L], St[:, j, L - 1:L]
                if kind == "s":
                    ore, oim = self.HLs[:, sub, 0, j:j + 1], self.HLs[:, sub, 1, j:j + 1]
                else:
                    ore, oim = self.HL[:, 0, j:j + 1], self.HL[:, 1, j:j + 1]
                dve(lambda h, gim_l=gim_l, s1=s1: h.tensor_tensor(out=tmp[:, 2:3], in0=gim_l, in1=s1, op=ALU.mult), [BT[3], C], [self.Bs5tmp])
                dve(lambda h, gre_l=gre_l, s1=s1: h.tensor_tensor(out=tmp[:, 3:4], in0=gre_l, in1=s1, op=ALU.mult), [BT[1], C], [self.Bs5tmp])
                dve(lambda h, gre_l=gre_l, c1=c1, ore=ore: h.scalar_tensor_tensor(out=ore, in0=gre_l, scalar=c1, in1=tmp[:, 2:3], op0=ALU.mult, op1=ALU.subtract),
                    [BT[1], C, self.Bs5tmp], [self.BHL])
                dve(lambda h, gim_l=gim_l, c1=c1, oim=oim: h.scalar_tensor_tensor(out=oim, in0=gim_l, scalar=c1, in1=tmp[:, 3:4], op0=ALU.mult, op1=ALU.add),
                    [BT[3], C, self.Bs5tmp], [self.BHL])
        if j == 5 and kind == "p":
            self.dbg_sb("g_re", Bb[:, 0:N], BT[1], [128, N])
            self.dbg_sb("g_im", E[:, 0:N], BT[3], [128, N])
        hr, hi = hre[j % 2], him[j % 2]
        Bh2 = Bhh[j % 2]
        dve(lambda h: h.tensor_tensor(out=v3(A), in0=v3(Bb), in1=ct, op=ALU.mult), [BT[1], C], [BT[0]])
        dve(lambda h: h.tensor_tensor(out=v3(Dd), in0=v3(E), in1=st, op=ALU.mult), [BT[3], C], [BT[2]])
        dve(lambda h, hr=hr: h.tensor_tensor(out=hr[:, 0:N], in0=A[:, 0:N], in1=Dd[:, 0:N], op=ALU.subtract), [BT[0], BT[2]], [Bh2])
        dve(lambda h: h.tensor_tensor(out=v3(A), in0=v3(E), in1=ct, op=ALU.mult), [BT[3], C], [BT[0]])
        dve(lambda h: h.tensor_tensor(out=v3(Dd), in0=v3(Bb), in1=st, op=ALU.mult), [BT[1], C], [BT[2]])
        dve(lambda h, hi=hi: h.tensor_tensor(out=hi[:, 0:N], in0=A[:, 0:N], in1=Dd[:, 0:N], op=ALU.add), [BT[0], BT[2]], [Bh2])
        py, Bpy = PS[2 + c % 2], BP[2 + c % 2]
        self.mm(py[32 * q:32 * q + 32, 0:N], self.LCc[:, j, 0, :], hr[:, 0:N], True, False, [C, Bh2], [Bpy], tile_position=(0, 32 * q))
        self.mm(py[32 * q:32 * q + 32, 0:N], self.LCc[:, j, 1, :], hi[:, 0:N], False, True, [C, Bh2], [Bpy], tile_position=(0, 32 * q))
        if q == 3:
            dve(lambda h, c=c, py=py: h.scalar_tensor_tensor(out=yf[:, c, 0:N], in0=uf[:, c, 0:N], scalar=self.s5d[:, c:c + 1], in1=py[:, 0:N],
                                                            op0=ALU.mult, op1=ALU.add), [self.Buf_uf, Bpy, C], [self.Buf_yf])
    if last_tile:
        nseq = self.NS if kind == "s" else 1
        for sq in range(nseq):
            for reim in range(2):
                src = self.HLs[:, sq, reim, :] if kind == "s" else self.HL[:, reim, :]
                fw.op("pe", lambda h, src=src: h.transpose(PS[7][0:16, 0:128], src, self.ident[:]), reads=[self.BHL, C], writes=[BP[7]])
                fw.op("dve", lambda h: h.tensor_copy(out=self.stg[0:16, :], in_=PS[7][0:16, 0:128]), reads=[BP[7]], writes=[self.Bstg])
                nm = ("s5_re_" if reim == 0 else "s5_im_") + ("sample" if kind == "s" else "prompt")
                seq = sq if kind == "s" else s
                dst = self.O[nm][0, seq].rearrange("(j g) n -> j (g n)", g=2)
                fw.dma("sp", dst, self.stg[0:16, :], self.Bstg, reads=[self.Bstg])
    tmpf = ma[:, 3072:5120].rearrange("p (c n) -> p c n", c=4)
    yv, tv = yf[:, :, 0:N], tmpf[:, :, 0:N]
    gb = ub
    dve(lambda h: h.tensor_tensor(out=tv, in0=yv, in1=yv, op=ALU.mult), [self.Buf_yf], BT)
    dve(lambda h: _ts(h, tv, tv, 0.044715, 1.0, ALU.mult, ALU.add), BT, BT)
    dve(lambda h: h.tensor_tensor(out=tv, in0=tv, in1=yv, op=ALU.mult), BT + [self.Buf_yf], BT)
    fw.op("act", lambda h: h.activation(out=tv, in_=tv, func=AF.Sigmoid, scale=2.0 * math.sqrt(2.0 / math.pi)), reads=BT, writes=BT)
    dve(lambda h: h.tensor_tensor(out=yv, in0=yv, in1=tv, op=ALU.mult), BT + [self.Buf_yf], [self.Buf_yf])
    fw.op("pool", lambda h: h.tensor_copy(out=gb[:, :, 0:N], in_=yv), reads=[self.Buf_yf], writes=[self.Bub])
    wglu = self.S["s5_w_glu"][0].rearrange("(k p) f -> p k f", p=128)
    t, B = self.get_slab([(lambda t: t[:, 0:2048].rearrange("p (k f) -> p k f", k=4), wglu, "s5_w_glu")])
    tg = t[:, 0:2048].rearrange("p (k f) -> p k f", k=4)
    for oc in range(4):
        pg, Bg = PS[oc % 2], BP[oc % 2]
        for k in range(4):
            self.mm(pg[:, 0:N], tg[:, k, oc * 128:(oc + 1) * 128], gb[:, k, 0:N], k == 0, k == 3, [B, self.Bub], [Bg])
        sg, Bs = self.sg[oc % 2], self.Bsg[oc % 2]
        fw.op("act", lambda h, sg=sg, pg=pg, oc=oc: h.activation(out=sg[:, 0:N], in_=pg[:, 0:N], func=AF.Sigmoid, bias=self.bglu[:, oc:oc + 1]),
              reads=[Bg, C], writes=[Bs])
        dve(lambda h, sg=sg, oc=oc: h.tensor_tensor(out=s5out[:, oc, 0:N], in0=yf[:, oc, 0:N], in1=sg[:, 0:N], op=ALU.mult),
            [Bs, self.Buf_yf], [self.Bs5out])


def attn_head_loop(self, N, qcols, blocks, diag_base):
    fw = self.fw
    C = self.Bconst
    PS, BP = self.PS, self.BP
    ma, mab = self.ma, self.mab
    QT = mab[:, 0:2048].rearrange("p (c n) -> p c n", c=4)
    attno = mab[:, 4096:6144].rearrange("p (c n) -> p c n", c=4)
    e_ = [ma[:, 3072 + 512 * i:3072 + 512 * (i + 1)] for i in range(2)]
    spb = [mab[:, 8192 + 512 * i:8192 + 512 * (i + 1)] for i in range(2)]
    wb = [mab[:, 9216 + 512 * i:9216 + 512 * (i + 1)] for i in range(2)]
    Be, Bsp, Bw = self.Be, self.Bsp, self.Bw
    crb = [self.lnt[0:33, 2 + i, :].bitcast(BF16)[:, 0:512] for i in range(2)]
    Bcr = self.Bcrowb2
    q0, q1 = qcols
    nblk = len(blocks)
    for hp in range(4):
        c = hp
        po, Bpo = PS[7], BP[7]
        pst, Bpt = PS[6], BP[6]
        pss = [PS[0], PS[1]]
        Bps = [BP[0], BP[1]]

        def qk(bi):
            k0, kr, vblk, jj = blocks[bi]
            Bkt = self.BKT[min(k0 // 512, 7)]
            for x in range(2):
                hb_ = 64 * x
                self.mm(pss[x][0:kr, 0:N], self.KT[hb_:hb_ + 64, c, k0:k0 + kr], QT[hb_:hb_ + 64, c, q0:q1], True, True,
                        [Bkt, self.BQT], [Bps[x]], tile_position=(hb_, 0))

        qk(0)
        for bi, (k0, kr, vblk, jj) in enumerate(blocks):
            Bvt = self.BVT[min(vblk // 4, 7)]
            psc = [PS[2 + 2 * (bi % 2)], PS[3 + 2 * (bi % 2)]]
            Bpc = [BP[2 + 2 * (bi % 2)], BP[3 + 2 * (bi % 2)]]
            for x in range(2):
                fw.op("act", lambda h, x=x: h.activation(out=e_[x][0:kr, 0:N], in_=pss[x][0:kr, 0:N], func=AF.Exp), reads=[Bps[x]], writes=[Be[x]])
                if jj is not None:
                    fw.op("pool", lambda h, x=x: h.affine_select(out=e_[x][0:kr, 0:N], in_=e_[x][0:kr, 0:N], pattern=[[1, N]], compare_op=ALU.is_gt,
                                                                 fill=0.0, base=-128 * jj, channel_multiplier=-1), reads=[Be[x]], writes=[Be[x]])
                fw.op("act", lambda h, x=x: h.activation(out=spb[x][0:kr, 0:N], in_=e_[x][0:kr, 0:N], func=AF.Ln, bias=1.0), reads=[Be[x]], writes=[Bsp[x]])
                self.mm(psc[x][0:kr, 0:N], self.nuincl[0:kr, 0:kr], spb[x][0:kr, 0:N], True, bi == 0, [C, Bsp[x]], [Bpc[x]])
                if bi > 0:
                    cbp, Bcbp = crb[(bi - 1) % 2], Bcr[(bi - 1) % 2]
                    self.mm(psc[x][0:kr, 0:N], self.negones[32 * x:32 * x + 1, 0:kr], cbp[32 * x:32 * x + 1, 0:N], False, True,
                            [C, Bcbp], [Bpc[x]], tile_position=(32 * x, 0))
            if bi < nblk - 1:
                for x in range(2):
                    self.mm(pst[32 * x:32 * x + 1, 0:N], self.ones1[0:kr, 0:1], spb[x][0:kr, 0:N], bi == 0, bi == nblk - 2,
                            [C, Bsp[x]], [Bpt], tile_position=(0, 32 * x))
                cb_, Bcb = crb[bi % 2], Bcr[bi % 2]
                fw.op("dve", lambda h, cb_=cb_: h.tensor_copy(out=cb_[0:33, 0:N], in_=pst[0:33, 0:N]), reads=[Bpt], writes=[Bcb])
                qk(bi + 1)
            for x in range(2):
                fw.op("act", lambda h, x=x: h.activation(out=psc[x][0:kr, 0:N], in_=psc[x][0:kr, 0:N], func=AF.Exp), reads=[Bpc[x]], writes=[Bpc[x]])
                fw.op("dve", lambda h, x=x: h.tensor_tensor(out=wb[x][0:kr, 0:N], in0=psc[x][0:kr, 0:N], in1=e_[x][0:kr, 0:N], op=ALU.mult),
                      reads=[Bpc[x], Be[x]], writes=[Bw[x]])
            for x in range(2):
                hd = 2 * hp + x
                self.mm(po[64 * x:64 * x + 64, 0:N], self.VT[0:kr, vblk, hd * 64:(hd + 1) * 64], wb[x][0:kr, 0:N], bi == 0, bi == nblk - 1,
                        [Bvt, Bw[x]], [Bpo], tile_position=(0, 64 * x))
        fw.op("dve", lambda h, c=c: h.tensor_copy(out=attno[:, c, q0:q1], in_=po[:, 0:N]), reads=[Bpo], writes=[self.Battno])


def attn_sample_loop(self, qcols, blocks):
    fw = self.fw
    C = self.Bconst
    PS, BP = self.PS, self.BP
    ma, mab = self.ma, self.mab
    QT = mab[:, 0:2048].rearrange("p (c n) -> p c n", c=4)
    attno = mab[:, 4096:6144].rearrange("p (c n) -> p c n", c=4)
    e_ = [ma[:, 3072 + 512 * i:3072 + 512 * (i + 1)] for i in range(2)]
    spb = [mab[:, 8192 + 512 * i:8192 + 512 * (i + 1)] for i in range(2)]
    wb = [mab[:, 9216 + 512 * i:9216 + 512 * (i + 1)] for i in range(2)]
    Be, Bsp, Bw = self.Be, self.Bsp, self.Bw
    crb = [self.lnt[0:1, 2 + i, :].bitcast(BF16)[:, 0:512] for i in range(2)]
    Bcr = self.Bcrowb2
    q0, q1 = qcols
    NQ = q1 - q0
    W = 8 * NQ
    nblk = len(blocks)
    po, Bpo = PS[7], BP[7]
    pst, Bpt = PS[6], BP[6]

    def qk(bi):
        k0, kr, vblk, jj = blocks[bi]
        Bkt = self.BKT[min(k0 // 512, 7)]
        for hd in range(8):
            c, xx = hd // 2, hd % 2
            hb_ = 64 * xx
            pss, Bps = PS[2 * (bi % 2) + xx], BP[2 * (bi % 2) + xx]
            self.mm(pss[0:kr, c * NQ:(c + 1) * NQ], self.KT[hb_:hb_ + 64, c, k0:k0 + kr], QT[hb_:hb_ + 64, c, q0:q1], True, True,
                    [Bkt, self.BQT], [Bps], tile_position=(hb_, 0))

    qk(0)
    for bi, (k0, kr, vblk, jj) in enumerate(blocks):
        x = bi % 2
        Bvt = self.BVT[min(vblk // 4, 7)]
        psc, Bpc = PS[4 + x], BP[4 + x]
        H4 = 4 * NQ
        for xx in range(2):
            pss, Bps = PS[2 * x + xx], BP[2 * x + xx]
            fw.op("act", lambda h, xx=xx, pss=pss: h.activation(out=e_[x][0:kr, xx * H4:(xx + 1) * H4], in_=pss[0:kr, 0:H4], func=AF.Exp),
                  reads=[Bps], writes=[Be[x]])
        if jj is not None:
            fw.op("pool", lambda h: h.affine_select(out=e_[x][0:kr, 0:W], in_=e_[x][0:kr, 0:W], pattern=[[0, 8], [1, NQ]], compare_op=ALU.is_gt,
                                                    fill=0.0, base=-128 * jj, channel_multiplier=-1), reads=[Be[x]], writes=[Be[x]])
        fw.op("act", lambda h: h.activation(out=spb[x][0:kr, 0:W], in_=e_[x][0:kr, 0:W], func=AF.Ln, bias=1.0), reads=[Be[x]], writes=[Bsp[x]])
        self.mm(psc[0:kr, 0:W], self.nuincl[0:kr, 0:kr], spb[x][0:kr, 0:W], True, bi == 0, [C, Bsp[x]], [Bpc])
        if bi > 0:
            self.mm(psc[0:kr, 0:W], self.negones[0:1, 0:kr], crb[(bi - 1) % 2][0:1, 0:W], False, True, [C, Bcr[(bi - 1) % 2]], [Bpc])
        if bi < nblk - 1:
            self.mm(pst[0:1, 0:W], self.ones1[0:kr, 0:1], spb[x][0:kr, 0:W], bi == 0, bi == nblk - 2, [C, Bsp[x]], [Bpt])
            fw.op("dve", lambda h: h.tensor_copy(out=crb[x][0:1, 0:W], in_=pst[0:1, 0:W]), reads=[Bpt], writes=[Bcr[x]])
            qk(bi + 1)
        fw.op("act", lambda h: h.activation(out=psc[0:kr, 0:W], in_=psc[0:kr, 0:W], func=AF.Exp), reads=[Bpc], writes=[Bpc])
        fw.op("dve", lambda h: h.tensor_tensor(out=wb[x][0:kr, 0:W], in0=psc[0:kr, 0:W], in1=e_[x][0:kr, 0:W], op=ALU.mult),
              reads=[Bpc, Be[x]], writes=[Bw[x]])
        for hd in range(8):
            hp, xx = hd // 2, hd % 2
            slot = xx * 4 + hp
            self.mm(po[64 * xx:64 * xx + 64, hp * NQ:(hp + 1) * NQ], self.VT[0:kr, vblk, hd * 64:(hd + 1) * 64], wb[x][0:kr, slot * NQ:(slot + 1) * NQ],
                    bi == 0 and hp == 0, bi == nblk - 1, [Bvt, Bw[x]], [Bpo], tile_position=(0, 64 * xx), skip_group_check=True)
    fw.op("dve", lambda h: h.tensor_copy(out=attno[:, :, q0:q1], in_=po[:, 0:4 * NQ].rearrange("p (c q) -> p c q", c=4)), reads=[Bpo], writes=[self.Battno])


def mixer0(self, tile, N):
    fw = self.fw
    kind, s, ti = tile
    C = self.Bconst
    PS, BP = self.PS, self.BP
    ma, mab, hbf = self.ma, self.mab, self.hbf
    xf, xb, Bxf, Bxb = self.xf, self.xb, self.Bxf, self.Bxb
    if not hasattr(self, "Buf_uf"):
        self.Buf_uf, self.Buf_yf, self.Bub, self.Bkvst = Buf("uf"), Buf("yf"), Buf("ub"), Buf("kvst")
        self.BQT, self.Bs5out, self.Battno = Buf("QT"), Buf("s5out"), Buf("attno")
        self.BT = [Buf("T%d" % i) for i in range(4)]
        self.Bhh = [Buf("hh%d" % i) for i in range(2)]
        self.Be = [Buf("e%d" % i) for i in range(2)]
        self.Bsp = [Buf("sp%d" % i) for i in range(2)]
        self.Bw = [Buf("w%d" % i) for i in range(2)]
        self.Bknew = Buf("knew")
    fw.barrier()
    uf = hbf[:, 0:2048].rearrange("p (c n) -> p c n", c=4)
    ub = self.hb[:, 16:20, :]
    kvst = hbf[:, 5120:5632]
    QT = mab[:, 0:2048].rearrange("p (c n) -> p c n", c=4)
    s5out = mab[:, 2048:4096].rearrange("p (c n) -> p c n", c=4)
    attno = mab[:, 4096:6144].rearrange("p (c n) -> p c n", c=4)
    win = self.S["mix0_w_in"][0].rearrange("(k p) f -> p k f", p=128)
    pos = ti * 512

    def slab_in(ci):
        t, B = self.get_slab([(lambda t: t[:, :].rearrange("p (k f) -> p k f", k=8), win[:, :, 512 * ci:512 * ci + 512], "mix0_w_in")])
        return t[:, :].rearrange("p (k f) -> p k f", k=8), B

    tw, B = slab_in(0)
    for oc in range(4):
        pp, Bp = PS[oc % 2], BP[oc % 2]
        for k in range(NCH):
            self.mm(pp[:, 0:N], tw[:, k, oc * 128:(oc + 1) * 128], xb[:, k, 0:N], k == 0, k == NCH - 1, [B, Bxb[k]], [Bp])
        fw.op("act", lambda h, pp=pp, oc=oc: h.activation(out=uf[:, oc, 0:N], in_=pp[:, 0:N], func=AF.Copy), reads=[Bp], writes=[self.Buf_uf])
        fw.op("dve", lambda h, pp=pp, oc=oc: h.tensor_copy(out=ub[:, oc, 0:N], in_=pp[:, 0:N]), reads=[Bp], writes=[self.Bub])
    tw, B = slab_in(1)
    for oc in range(4):
        pp, Bp = PS[2 + oc % 2], BP[2 + oc % 2]
        for k in range(NCH):
            self.mm(pp[:, 0:N], tw[:, k, oc * 128:(oc + 1) * 128], xb[:, k, 0:N], k == 0, k == NCH - 1, [B, Bxb[k]], [Bp])
        fw.op("act", lambda h, pp=pp, oc=oc: h.activation(out=QT[:, oc, 0:N], in_=pp[:, 0:N], func=AF.Copy, scale=0.125), reads=[Bp], writes=[self.BQT])
    tw, B = slab_in(2)
    for oc in range(4):
        pp, Bp = PS[oc % 2], BP[oc % 2]
        for k in range(NCH):
            self.mm(pp[:, 0:N], tw[:, k, oc * 128:(oc + 1) * 128], xb[:, k, 0:N], k == 0, k == NCH - 1, [B, Bxb[k]], [Bp])
        if kind == "p":
            fw.op("act", lambda h, pp=pp, oc=oc: h.activation(out=self.KT[:, oc, pos:pos + N], in_=pp[:, 0:N], func=AF.Copy), reads=[Bp], writes=[self.BKT[ti]])
        else:
            fw.op("act", lambda h, pp=pp, oc=oc: h.activation(out=self.knew[:, oc, 0:N], in_=pp[:, 0:N], func=AF.Copy), reads=[Bp], writes=[self.Bknew])
    twv, Bv = slab_in(3)
    if kind == "p":
        segs = [(b * 128, 128, s, pos + b * 128) for b in range(4)]
    else:
        segs = [(sq * DEC_SEQ, DEC_SEQ, sq, 0) for sq in range(self.NS)]
    sfx = "prompt" if kind == "p" else "sample"
    for (c0, rows, seq, sp_) in segs:
        for which, (tws, Bs_) in enumerate(((tw, B), (twv, Bv))):
            pp, Bp = PS[2 + which], BP[2 + which]
            for k in range(NCH):
                self.mm(pp[0:rows, :], xb[:, k, c0:c0 + rows], tws[:, k, :], k == 0, k == NCH - 1, [Bs_, Bxb[k]], [Bp])
            fw.op("act", lambda h, pp=pp, rows=rows: h.activation(out=kvst[0:rows, :], in_=pp[0:rows, :], func=AF.Copy), reads=[Bp], writes=[self.Bkvst])
            if which == 1:
                if kind == "p":
                    blk = sp_ // 128
                    fw.op("dve", lambda h, pp=pp, blk=blk: h.tensor_copy(out=self.VT[:, blk, :], in_=pp[:, :]), reads=[Bp], writes=[self.BVT[blk // 4]])
                else:
                    fw.op("dve", lambda h, pp=pp, seq=seq, rows=rows: h.tensor_copy(out=self.vnew[0:rows, seq, :], in_=pp[0:rows, :]), reads=[Bp], writes=[self.Bknew])
            nm = ("sb_k_" if which == 0 else "sb_v_") + sfx
            dst = self.O[nm][0, seq, sp_:sp_ + rows].rearrange("t h d -> t (h d)")
            fw.dma("sp", dst, kvst[0:rows, :], self.Bkvst, reads=[self.Bkvst])
    if self.stop < 3:
        return
    if kind == "s":
        self.s5_sample_init()
    fw.mark("s5")
    s5_phase(self, tile, N)
    fw.barrier()
    fw.mark("attn")
    if self.stop < 4:
        return
    if kind == "p":
        nkb = 4 * (ti + 1)
        blocks = []
        for kb in reversed(range(nkb)):
            jj = kb - 4 * ti if kb >= 4 * ti else None
            blocks.append((kb * 128, 128, kb, jj))
        attn_head_loop(self, N, (0, N), blocks, 0)
    else:
        for sq in range(self.NS):
            self.load_cache(sq)
            blocks = [(PAST, DEC_SEQ, 16, 0)] + [(kb * 128, 128, kb, None) for kb in reversed(range(PAST // 128))]
            attn_sample_loop(self, (sq * DEC_SEQ, (sq + 1) * DEC_SEQ), blocks)
    fw.mark("out0")
    wout = self.S["mix0_w_out"][0].rearrange("(k p) f -> p k f", p=128)
    for half in range(2):
        t, B = self.get_slab([(lambda t: t[:, :].rearrange("p (k f) -> p k f", k=8), wout[:, :, 512 * half:512 * half + 512], "mix0_w_out")])
        tw = t[:, :].rearrange("p (k f) -> p k f", k=8)
        for oc in range(4):
            o = 4 * half + oc
            pp, Bp = PS[o % 2], BP[o % 2]
            for k in range(NCH):
                rhs = s5out[:, k, 0:N] if k < 4 else attno[:, k - 4, 0:N]
                Br = self.Bs5out if k < 4 else self.Battno
                self.mm(pp[:, 0:N], tw[:, k, oc * 128:(oc + 1) * 128], rhs, k == 0, k == NCH - 1, [B, Br], [Bp])
            fw.op("dve", lambda h, o=o, pp=pp: h.scalar_tensor_tensor(out=xf[:, o, 0:N], in0=xf[:, o, 0:N], scalar=ALPHA, in1=pp[:, 0:N],
                                                                     op0=ALU.mult, op1=ALU.add), reads=[Bxf[o], Bp], writes=[Bxf[o]])
    fw.barrier([self.Bkvst])
    self.layer_norm(1, N)


def s5_sample_init(self):
    fw, I = self.fw, self.I
    C = self.Bconst
    NS = self.NS
    if not hasattr(self, "Gs"):
        self.Gs = fw.sbuf([128, NS, 16, 2], F32, "Gs")
        self.BGs = Buf("Gs", strict=True)
        self.HLs = fw.sbuf([128, NS, 2, 16], F32, "HLs")
        self.h0 = fw.sbuf([128, 2, NS, 16], F32, "h0")
        self.h0t = fw.sbuf([128, NS, 16], F32, "h0t")
    for reim, nm in ((0, "state_s5_re"), (1, "state_s5_im")):
        self.load_T(I[nm][0].rearrange("s (j g) n -> (s j) (g n)", g=2), NS * 16, self.h0[:, reim].rearrange("p s j -> p (s j)"), self.BGs)
    c1 = bc(self.Ct[:, :, 1:2].rearrange("p j o -> p o j"), [128, NS, 16])
    s1 = bc(self.St[:, :, 1:2].rearrange("p j o -> p o j"), [128, NS, 16])
    hre, him = self.h0[:, 0], self.h0[:, 1]
    B = self.BGs
    dve = lambda fn: fw.op("dve", fn, reads=[B, C], writes=[B])
    dve(lambda h: h.tensor_tensor(out=self.h0t[:], in0=him, in1=s1, op=ALU.mult))
    dve(lambda h: h.tensor_tensor(out=self.Gs[:, :, :, 0], in0=hre, in1=c1, op=ALU.mult))
    dve(lambda h: h.tensor_tensor(out=self.Gs[:, :, :, 0], in0=self.Gs[:, :, :, 0], in1=self.h0t[:], op=ALU.subtract))
    dve(lambda h: h.tensor_tensor(out=self.h0t[:], in0=hre, in1=s1, op=ALU.mult))
    dve(lambda h: h.tensor_tensor(out=self.Gs[:, :, :, 1], in0=him, in1=c1, op=ALU.mult))
    dve(lambda h: h.tensor_tensor(out=self.Gs[:, :, :, 1], in0=self.Gs[:, :, :, 1], in1=self.h0t[:], op=ALU.add))


def load_cache(self, sq):
    fw, I = self.fw, self.I
    C = self.Bconst
    PS, BP = self.PS, self.BP
    stg = self.hbf[:, 0:4096].rearrange("p (b f) -> p b f", b=8)
    Bst = [self.Buf_uf, self.Buf_yf]
    kc = I["cache_sb_k"][0, sq].rearrange("(b p) h d -> p b (h d)", p=128)
    vc = I["cache_sb_v"][0, sq].rearrange("(b p) h d -> p b (h d)", p=128)
    fw.dma("pool", self.VT[:, 0:16, :], vc, self.BVT[0], writes=self.BVT[0:4])
    for half in range(2):
        for q4 in range(2):
            b0 = half * 8 + q4 * 4
            fw.dma("sp", stg[:, q4 * 4:q4 * 4 + 4, :], kc[:, b0:b0 + 4, :], Bst[q4], writes=[Bst[q4]])
        for c in range(4):
            for g in range(2):
                pp, Bp = PS[(2 * c + g) % 4], BP[(2 * c + g) % 4]
                for bb in range(4):
                    b = g * 4 + bb
                    fw.op("pe", lambda h, pp=pp, b=b, bb=bb, c=c: h.transpose(pp[:, bb * 128:(bb + 1) * 128], stg[:, b, c * 128:(c + 1) * 128], self.ident[:]),
                          reads=[Bst[g], C], writes=[Bp])
                col0 = (half * 8 + g * 4) * 128
                eng = "act" if (c + g) % 2 == 0 else "dve"
                if eng == "act":
                    fw.op("act", lambda h, pp=pp, c=c, col0=col0: h.activation(out=self.KT[:, c, col0:col0 + 512], in_=pp[:, :], func=AF.Copy),
                          reads=[Bp], writes=[self.BKT[col0 // 512]])
                else:
                    fw.op("dve", lambda h, pp=pp, c=c, col0=col0: h.tensor_copy(out=self.KT[:, c, col0:col0 + 512], in_=pp[:, :]),
                          reads=[Bp], writes=[self.BKT[col0 // 512]])
    fw.op("dve", lambda h: h.tensor_copy(out=self.KT[:, :, PAST:PAST + DEC_SEQ], in_=self.knew[:, :, sq * DEC_SEQ:(sq + 1) * DEC_SEQ]),
          reads=[self.Bknew], writes=[self.BKT[4]])
    fw.op("dve", lambda h: h.tensor_copy(out=self.VT[0:DEC_SEQ, 16, :], in_=self.vnew[0:DEC_SEQ, sq, :]), reads=[self.Bknew], writes=[self.BVT[4]])

def setup_ssd(self):
    fw, I = self.fw, self.I
    C = self.Bconst
    self.cw = fw.sbuf([128, 96], F32, "cw")
    self.cb = fw.sbuf([128, 24], F32, "cb")
    self.ng = fw.sbuf([128, 16], F32, "ng")
    self.load_T(I["ssd_conv_w"].rearrange("o w (c p) -> (o w c) p", p=128), 96, self.cw[:], C)
    self.load_T(I["ssd_conv_b"].rearrange("o (c p) -> (o c) p", p=128), 24, self.cb[:], C)
    self.load_T(I["ssd_norm_g"].rearrange("o (c p) -> (o c) p", p=128), 16, self.ng[:], C)
    self.dtb = fw.sbuf([128, 32], F32, "dtb")
    self.Arow = fw.sbuf([128, 32], F32, "Arow")
    fw.dma("sp", self.dtb[:], I["ssd_dt_bias"][0:1, :].broadcast_to([128, 32]), C, writes=[C])
    fw.dma("sp", self.Arow[:], I["ssd_a_log"][0:1, :].broadcast_to([128, 32]), C, writes=[C])
    fw.op("act", lambda h: h.activation(out=self.Arow[:], in_=self.Arow[:], func=AF.Exp), reads=[C], writes=[C])
    fw.op("dve", lambda h: h.tensor_scalar(out=self.Arow[:], in0=self.Arow[:], scalar1=-1.0, scalar2=None, op0=ALU.mult), reads=[C], writes=[C])
    self.dvec = fw.sbuf([128, 16], F32, "dvec")
    dd = I["ssd_d"].rearrange("o (c t) -> o c t", t=2)
    fw.dma("sp", self.dvec[0:64, :], dd[0:1, :, 0].broadcast_to([64, 16]), C, writes=[C], slow=True)
    fw.dma("sp", self.dvec[64:128, :], dd[0:1, :, 1].broadcast_to([64, 16]), C, writes=[C], slow=True)
    self.tri = fw.sbuf([64, 64], F32, "tri")
    fw.op("pool", lambda h: h.memset(self.tri[:], 1.0), writes=[C])
    fw.op("pool", lambda h: h.affine_select(out=self.tri[:], in_=self.tri[:], pattern=[[1, 64]], compare_op=ALU.is_ge, fill=0.0,
                                            base=0, channel_multiplier=-1), reads=[C], writes=[C])
    self.ones512 = fw.sbuf([128, 128], BF16, "ones512")
    fw.op("pool", lambda h: h.memset(self.ones512[:], 1.0 / 512.0), writes=[C])
    self.hst = fw.sbuf([128, 2048], F32, "hst")
    self.hstb = fw.sbuf([128, 2048], BF16, "hstb")
    self.Bhst = [Buf("hst%d" % g) for g in range(4)]
    self.Bhstb = [Buf("hstb%d" % g) for g in range(4)]
    self.ctail = fw.sbuf([128, 24, 3], F32, "ctail")
    self.Bctail = Buf("ctail")
    self.dtt = fw.sbuf([64, 8, 4, 32], F32, "dtt")
    self.dend = fw.sbuf([128, 8, 32], F32, "dend")
    self.Bdtt = Buf("dtt")
    self.mcb = fw.sbuf([64, 64], F32, "mcb")
    self.Bmcb = Buf("mcb")


def ssd_core(self, c0, n, nseg, Lc):
    fw = self.fw
    C = self.Bconst
    PS, BP = self.PS, self.BP
    ma, mab, hbf = self.ma, self.mab, self.hbf
    xb, Bxb = self.xb, self.Bxb
    B1 = self.B1
    BT_ = mab[:, 0:2048].rearrange("p (c n) -> p c n", c=4)
    CT_ = mab[:, 2048:4096].rearrange("p (c n) -> p c n", c=4)
    zs = mab[:, 4096:6144].rearrange("p (c n) -> p c n", c=4)
    xsT = mab[:, 6144:8192].rearrange("p (c n) -> p c n", c=4)
    stg = ma[:, 4096:4611]
    DE = ma[:, 4612:5124]
    E2 = ma[:, 5124:5636]
    yg = ma[:, 4096:6144].rearrange("p (c n) -> p c n", c=4)
    yn = self.hb[:, 0:16, :]
    hbb = self.hb[:].rearrange("p m n -> p (m n)")
    sgb0 = self.sg[0][:].bitcast(BF16)
    sgb1 = self.sg[1][:].bitcast(BF16)
    Wt_ = [hbb[:, 8192:8704], sgb0[:, 0:512]]
    Ctl_ = [hbb[:, 8704:9216], sgb0[:, 512:1024]]
    xt_ = [hbb[:, 9216:9728], sgb1[:, 0:512]]
    xh_ = [hbb[:, 9728:10240], sgb1[:, 512:1024]]
    btm_ = [hbb[:, 10240:10368], hbb[:, 10368:10496]]
    self.ssd_it = 0
    win = self.S["ssd_w_in"][0].rearrange("(k p) f -> p k f", p=128)
    dve = lambda fn, r, w: fw.op("dve", fn, reads=list(r), writes=list(w))
    act = lambda fn, r, w: fw.op("act", fn, reads=list(r), writes=list(w))

    def slab_in(ci):
        w = 512 if ci < 10 else 32
        t, B = self.get_slab([(lambda t: t[:, 0:8 * w].rearrange("p (k f) -> p k f", k=8), win[:, :, 512 * ci:512 * ci + w], "ssd_w_in")])
        return t[:, 0:8 * w].rearrange("p (k f) -> p k f", k=8), B

    self.pp_i = 0

    def proj(tw, B, cc):
        i = 5 + self.pp_i % 2
        self.pp_i += 1
        pp, Bp = PS[i], BP[i]
        for k in range(NCH):
            self.mm(pp[:, 0:n], tw[:, k, cc * 128:(cc + 1) * 128], xb[:, k, c0:c0 + n], k == 0, k == NCH - 1, [B, Bxb[k]], [Bp])
        return pp, Bp

    def proj_conv(tw, B, cc, ci, out, Bout):
        pp, Bp = proj(tw, B, cc)
        dve(lambda h: h.tensor_copy(out=stg[:, 0:3], in_=self.ctail[:, ci, :]), [self.Bctail], [B1["stg"]])
        act(lambda h: h.activation(out=stg[:, 3:3 + n], in_=pp[:, 0:n], func=AF.Copy), [Bp], [B1["stg"]])
        dve(lambda h: h.tensor_copy(out=self.ctail[:, ci, :], in_=stg[:, n:n + 3]), [B1["stg"]], [self.Bctail])
        self.acc_i = getattr(self, "acc_i", 0) + 1
        acc, Bacc = (E2[:, 0:n], B1["E2"]) if self.acc_i % 2 == 0 else (DE[:, 0:n], B1["DE"])
        dve(lambda h: h.tensor_scalar(out=acc, in0=stg[:, 0:n], scalar1=self.cw[:, ci:ci + 1], scalar2=self.cb[:, ci:ci + 1], op0=ALU.mult, op1=ALU.add),
            [B1["stg"], C], [Bacc])
        for w in range(1, 4):
            dve(lambda h, w=w: h.scalar_tensor_tensor(out=acc, in0=stg[:, w:w + n], scalar=self.cw[:, w * 24 + ci:w * 24 + ci + 1], in1=acc,
                                                     op0=ALU.mult, op1=ALU.add), [B1["stg"], Bacc, C], [Bacc])
        act(lambda h: h.activation(out=out, in_=acc, func=AF.Silu), [Bacc], [Bout])

    fw.mark("m1_dt")
    tw, B = slab_in(10)
    dtt, Bd = self.dtt, self.Bdtt
    for sg in range(nseg):
        a = c0 + sg * Lc
        pp, Bp = PS[5], BP[5]
        for k in range(NCH):
            self.mm(pp[0:Lc, 0:32], xb[:, k, a:a + Lc], tw[:, k, 0:32], k == 0, k == NCH - 1, [B, Bxb[k]], [Bp])
        dt_, dtA, cs_, te = (dtt[0:Lc, sg, i, :] for i in range(4))
        dve(lambda h: h.tensor_tensor(out=dt_, in0=pp[0:Lc, 0:32], in1=self.dtb[0:Lc, :], op=ALU.add), [Bp, C], [Bd])
        act(lambda h: h.activation(out=dt_, in_=dt_, func=AF.Exp), [Bd], [Bd])
        act(lambda h: h.activation(out=dt_, in_=dt_, func=AF.Ln, bias=1.0), [Bd], [Bd])
        dve(lambda h: h.tensor_tensor(out=dtA, in0=dt_, in1=self.Arow[0:Lc, :], op=ALU.mult), [Bd, C], [Bd])
        pc, Bpc = PS[6], BP[6]
        self.mm(pc[0:Lc, 0:32], self.tri[0:Lc, 0:Lc], dtA, True, True, [C, Bd], [Bpc])
        pe_, Bpe = PS[7], BP[7]
        self.mm(pe_[:, 0:32], self.onesf[0:Lc, :], dtA, True, True, [C, Bd], [Bpe])
        act(lambda h: h.activation(out=cs_, in_=pc[0:Lc, 0:32], func=AF.Copy), [Bpc], [Bd])
        dve(lambda h: h.tensor_tensor(out=te, in0=pe_[0:Lc, 0:32], in1=cs_, op=ALU.subtract), [Bpe, Bd], [Bd])
        act(lambda h: h.activation(out=te, in_=te, func=AF.Exp), [Bd], [Bd])
        dve(lambda h: h.tensor_tensor(out=te, in0=te, in1=dt_, op=ALU.mult), [Bd], [Bd])
        act(lambda h, sg=sg: h.activation(out=self.dend[:, sg, :], in_=pe_[:, 0:32], func=AF.Exp), [Bpe], [Bd])
    fw.mark("m1_bc")
    tw, B = slab_in(8)
    for g in range(4):
        proj_conv(tw, B, g, 16 + g, BT_[:, g, 0:n], B1["BT"])
    tw, B = slab_in(9)
    for g in range(4):
        proj_conv(tw, B, g, 20 + g, CT_[:, g, 0:n], B1["CT"])
    pxt_b = PS[6][:].bitcast(BF16)
    for g in range(4):
        fw.mark("m1_inproj")
        tw, B = slab_in(g)
        for hc in range(4):
            pp, Bp = proj(tw, B, hc)
            act(lambda h, hc=hc, pp=pp: h.activation(out=zs[:, hc, 0:n], in_=pp[:, 0:n], func=AF.Silu), [Bp], [B1["zs"]])
        tw, B = slab_in(4 + g)
        for hc in range(4):
            proj_conv(tw, B, hc, 4 * g + hc, xsT[:, hc, 0:n], B1["xs"])
        fw.mark("m1_seg")
        for sg in range(nseg):
            a = sg * Lc
            par = self.ssd_it % 2
            self.ssd_it += 1
            Wt, Ctl, xt, xh, btm = Wt_[par], Ctl_[par], xt_[par], xh_[par], btm_[par]
            BW, BCt, Bxt, Bxh, Bbt = (B1[k + str(par)] for k in ("W", "Ct", "xt", "xh", "btm"))
            dt_, dtA, cs_, te = (dtt[0:Lc, sg, i, :] for i in range(4))
            pcb, Bpcb = PS[5], BP[5]
            self.mm(pcb[0:Lc, 0:Lc], BT_[:, g, a:a + Lc], CT_[:, g, a:a + Lc], True, True, [B1["BT"], B1["CT"]], [Bpcb])
            dve(lambda h: h.tensor_tensor(out=self.mcb[0:Lc, 0:Lc], in0=pcb[0:Lc, 0:Lc], in1=self.tri[0:Lc, 0:Lc], op=ALU.mult), [Bpcb, C], [self.Bmcb])
            pr, Bpr = PS[4], BP[4]
            for hh in range(8):
                hd = 8 * g + hh
                self.mm(pr[:, hh * Lc:(hh + 1) * Lc], dtA[:, hd:hd + 1].broadcast_to([Lc, 128]), self.tri[0:Lc, 0:Lc], True, True, [Bd, C], [Bpr])
            v3 = lambda ap, P: ap[0:P, 0:8 * Lc].rearrange("p (h t) -> p h t", h=8)
            dve(lambda h: h.tensor_tensor(out=v3(DE, Lc), in0=v3(pr, Lc), in1=bc(cs_[:, 8 * g:8 * g + 8].unsqueeze(2), [Lc, 8, Lc]), op=ALU.subtract),
                [Bpr, Bd], [B1["DE"]])
            act(lambda h: h.activation(out=DE[0:Lc, 0:8 * Lc], in_=DE[0:Lc, 0:8 * Lc], func=AF.Exp), [B1["DE"]], [B1["DE"]])
            dve(lambda h: h.scalar_tensor_tensor(out=v3(Wt, Lc), in0=v3(DE, Lc), scalar=1.0, in1=bc(self.mcb[0:Lc, 0:Lc].unsqueeze(1), [Lc, 8, Lc]),
                                                 op0=ALU.min, op1=ALU.mult), [B1["DE"], self.Bmcb], [BW])
            act(lambda h: h.activation(out=E2[:, 0:8 * Lc], in_=pr[:, 0:8 * Lc], func=AF.Exp), [Bpr], [B1["E2"]])
            dve(lambda h: h.tensor_tensor(out=v3(Ctl, 128), in0=v3(E2, 128), in1=bc(CT_[:, g, a:a + Lc].unsqueeze(1), [128, 8, Lc]), op=ALU.mult),
                [B1["E2"], B1["CT"]], [BCt])
            pxt, Bpxt = pxt_b, BP[6]
            for hc in range(4):
                fw.op("pe", lambda h, hc=hc: h.transpose(pxt[0:Lc, hc * 128:(hc + 1) * 128], xsT[:, hc, a:a + Lc], self.identb[:, :]),
                      reads=[B1["xs"], C], writes=[Bpxt])
            fw.op("pe", lambda h: h.transpose(pxt[0:Lc, 512:640], BT_[:, g, a:a + Lc], self.identb[:, :]), reads=[B1["BT"], C], writes=[Bpxt])
            x3 = lambda ap: ap[0:Lc, 0:512].rearrange("p (h q) -> p h q", h=8)
            dve(lambda h: h.tensor_tensor(out=x3(xt), in0=x3(pxt), in1=bc(dt_[:, 8 * g:8 * g + 8].unsqueeze(2), [Lc, 8, 64]), op=ALU.mult),
                [Bpxt, Bd], [Bxt])
            dve(lambda h: h.tensor_tensor(out=x3(xh), in0=x3(pxt), in1=bc(te[:, 8 * g:8 * g + 8].unsqueeze(2), [Lc, 8, 64]), op=ALU.mult),
                [Bpxt, Bd], [Bxh])
            act(lambda h: h.activation(out=btm[0:Lc, 0:128], in_=pxt[0:Lc, 512:640], func=AF.Copy), [Bpxt], [Bbt])
            for hc in range(4):
                py, Bpy = PS[hc], BP[hc]
                for hh in (2 * hc, 2 * hc + 1):
                    pb = 64 * (hh % 2)
                    self.mm(py[pb:pb + 64, a:a + Lc], xt[0:Lc, hh * 64:(hh + 1) * 64], Wt[0:Lc, hh * Lc:(hh + 1) * Lc], True, False,
                            [Bxt, BW], [Bpy], tile_position=(0, pb))
                for hh in (2 * hc, 2 * hc + 1):
                    pb = 64 * (hh % 2)
                    hd = 8 * g + hh
                    self.mm(py[pb:pb + 64, a:a + Lc], self.hstb[:, hd * 64:(hd + 1) * 64], Ctl[:, hh * Lc:(hh + 1) * Lc], False, True,
                            [self.Bhstb[g], BCt], [Bpy], tile_position=(0, pb))
            pS, BpS = PS[7], BP[7]
            self.mm(pS[:, 0:512], btm[0:Lc, 0:128], xh[0:Lc, 0:512], True, True, [Bbt, Bxh], [BpS])
            hg = self.hst[:, 512 * g:512 * (g + 1)]
            h3 = hg.rearrange("p (h q) -> p h q", h=8)
            dve(lambda h, sg=sg: h.tensor_tensor(out=h3, in0=h3, in1=bc(self.dend[:, sg, 8 * g:8 * g + 8].unsqueeze(2), [128, 8, 64]), op=ALU.mult),
                [self.Bhst[g], Bd], [self.Bhst[g]])
            dve(lambda h: h.tensor_tensor(out=hg, in0=hg, in1=pS[:, 0:512], op=ALU.add), [self.Bhst[g], BpS], [self.Bhst[g]])
            act(lambda h: h.activation(out=self.hstb[:, 512 * g:512 * (g + 1)], in_=hg, func=AF.Copy), [self.Bhst[g]], [self.Bhstb[g]])
        fw.mark("m1_epi")
        fw.barrier()
        for hc in range(4):
            py, Bpy = PS[hc], BP[hc]
            dve(lambda h, hc=hc, py=py: h.scalar_tensor_tensor(out=yg[:, hc, 0:n], in0=xsT[:, hc, 0:n], scalar=self.dvec[:, 4 * g + hc:4 * g + hc + 1],
                                                               in1=py[:, 0:n], op0=ALU.mult, op1=ALU.add), [B1["xs"], Bpy, C], [B1["yg"]])
            dve(lambda h, hc=hc: h.tensor_tensor(out=yg[:, hc, 0:n], in0=yg[:, hc, 0:n], in1=zs[:, hc, 0:n], op=ALU.mult), [B1["yg"], B1["zs"]], [B1["yg"]])
        act(lambda h: h.activation(out=xsT[:, :, 0:n], in_=yg[:, :, 0:n], func=AF.Square), [B1["yg"]], [B1["xs"]])
        pms, Bpms = PS[5], BP[5]
        for hc in range(4):
            self.mm(pms[:, 0:n], self.ones512[:], xsT[:, hc, 0:n], hc == 0, hc == 3, [C, B1["xs"]], [Bpms])
        rstd = self.lnt[:, 1, 0:n]
        Bl = self.Blnt[1]
        dve(lambda h: h.tensor_scalar(out=rstd, in0=pms[:, 0:n], scalar1=RMS_EPS, scalar2=None, op0=ALU.add), [Bpms], [Bl])
        act(lambda h: h.activation(out=rstd, in_=rstd, func=AF.Sqrt), [Bl], [Bl])
        dve(lambda h: h.reciprocal(out=rstd, in_=rstd), [Bl], [Bl])
        for hc in range(4):
            ch = 4 * g + hc
            dve(lambda h, hc=hc, ch=ch: h.scalar_tensor_tensor(out=yn[:, ch, c0:c0 + n], in0=yg[:, hc, 0:n], scalar=self.ng[:, ch:ch + 1], in1=rstd,
                                                               op0=ALU.mult, op1=ALU.mult), [B1["yg"], Bl, C], [B1["yn"]])
        fw.barrier()


def _st_stage(self, r):
    slot = 0 if r % 2 == 0 else 3
    return self.lnt[:, slot, :].rearrange("p (t n) -> p t n", t=4), self.Blnt[slot]


def ssd_state_in(self, sq):
    fw, I = self.fw, self.I
    C = self.Bconst
    src = I["state_ssd"][0, sq].rearrange("h p n -> (h p) n").rearrange("(r t p) n -> r p t n", t=4, p=128)
    for r in range(4):
        st, Bst = _st_stage(self, r)
        ps, Bp = self.PS[6 + r % 2], self.BP[6 + r % 2]
        fw.dma("sp", st, src[r], Bst, writes=[Bst])
        for t in range(4):
            fw.op("pe", lambda h, t=t: h.transpose(ps[:, t * 128:(t + 1) * 128], st[:, t, :], self.ident[:, :]), reads=[Bst, C], writes=[Bp])
        fw.op("dve", lambda h: h.tensor_copy(out=self.hst[:, 512 * r:512 * (r + 1)], in_=ps[:, 0:512]), reads=[Bp], writes=[self.Bhst[r]])
        fw.op("act", lambda h: h.activation(out=self.hstb[:, 512 * r:512 * (r + 1)], in_=ps[:, 0:512], func=AF.Copy), reads=[Bp], writes=[self.Bhstb[r]])
    srcc = I["state_conv"][0, sq].rearrange("w (c p) -> (w c) p", p=128)
    fw.dma("sp", self.stg[0:72, :], srcc, self.Bstg, writes=[self.Bstg])
    fw.op("pe", lambda h: h.transpose(self.PS[7][:, 0:72], self.stg[0:72, :], self.ident[0:72, 0:72]), reads=[self.Bstg, C], writes=[self.BP[7]])
    fw.op("dve", lambda h: h.tensor_copy(out=self.ctail[:].rearrange("p c w -> p w c"), in_=self.PS[7][:, 0:72].rearrange("p (w c) -> p w c", w=3)),
          reads=[self.BP[7]], writes=[self.Bctail])


def ssd_state_out(self, sfx, seq):
    fw = self.fw
    C = self.Bconst
    dst = self.O["ssd_" + sfx][0, seq].rearrange("h p n -> (h p) n").rearrange("(r t p) n -> r p t n", t=4, p=128)
    for r in range(4):
        st, Bst = _st_stage(self, r)
        ps, Bp = self.PS[6 + r % 2], self.BP[6 + r % 2]
        for t in range(4):
            i = 4 * r + t
            fw.op("pe", lambda h, i=i, t=t: h.transpose(ps[:, t * 128:(t + 1) * 128], self.hst[:, i * 128:(i + 1) * 128], self.ident[:, :]),
                  reads=[self.Bhst[r], C], writes=[Bp])
        fw.op("dve", lambda h: h.tensor_copy(out=st, in_=ps[:, 0:512].rearrange("p (t n) -> p t n", t=4)), reads=[Bp], writes=[Bst])
        fw.dma("sp", dst[r], st, Bst, reads=[Bst])
    tmp = self.lnt[:, 2, 0:72]
    fw.op("dve", lambda h: h.tensor_copy(out=tmp.rearrange("p (w c) -> p w c", w=3), in_=self.ctail[:].rearrange("p c w -> p w c")),
          reads=[self.Bctail], writes=[self.Blnt[2]])
    fw.op("pe", lambda h: h.transpose(self.PS[7][0:72, 0:128], tmp, self.ident[:, :]), reads=[self.Blnt[2], C], writes=[self.BP[7]])
    fw.op("dve", lambda h: h.tensor_copy(out=self.stg[0:72, :], in_=self.PS[7][0:72, 0:128]), reads=[self.BP[7]], writes=[self.Bstg])
    dstc = self.O["conv_" + sfx][0, seq].rearrange("w (c p) -> (w c) p", p=128)
    fw.dma("sp", dstc, self.stg[0:72, :], self.Bstg, reads=[self.Bstg])


def mixer1(self, tile, N):
    fw = self.fw
    kind, s, ti = tile
    C = self.Bconst
    PS, BP = self.PS, self.BP
    xf, Bxf = self.xf, self.Bxf
    if not hasattr(self, "B1"):
        self.B1 = {k: Buf(k) for k in ("stg", "E2", "DE", "BT", "CT", "zs", "xs", "yg", "yn")}
        for k in ("W", "Ct", "xt", "xh", "btm"):
            for par in range(2):
                self.B1[k + str(par)] = Buf(k + str(par))
    fw.barrier()
    if kind == "p":
        if ti == 0:
            fw.op("dve", lambda h: h.memset(self.hst[:], 0.0), writes=self.Bhst)
            fw.op("pool", lambda h: h.memset(self.hstb[:], 0.0), writes=self.Bhstb)
            fw.op("pool", lambda h: h.memset(self.ctail[:], 0.0), writes=[self.Bctail])
        ssd_core(self, 0, 512, 8, 64)
        if ti == self.SEQ // 512 - 1:
            ssd_state_out(self, "prompt", s)
    else:
        for sq in range(self.NS):
            ssd_state_in(self, sq)
            ssd_core(self, sq * DEC_SEQ, DEC_SEQ, 1, DEC_SEQ)
            ssd_state_out(self, "sample", sq)
    fw.mark("m1_out")
    yn = self.hb[:, 0:16, :]
    wout = self.S["ssd_w_out"][0].rearrange("(k p) f -> p k f", p=128)
    for o2 in range(4):
        t, B = self.get_slab([(lambda t: t[:, :].rearrange("p (k f) -> p k f", k=16), wout[:, :, 256 * o2:256 * o2 + 256], "ssd_w_out")])
        tw = t[:, :].rearrange("p (k f) -> p k f", k=16)
        for oi in range(2):
            o = 2 * o2 + oi
            pp, Bp = PS[o % 2], BP[o % 2]
            for k in range(16):
                self.mm(pp[:, 0:N], tw[:, k, oi * 128:(oi + 1) * 128], yn[:, k, 0:N], k == 0, k == 15, [B, self.B1["yn"]], [Bp])
            fw.op("dve", lambda h, o=o, pp=pp: h.scalar_tensor_tensor(out=xf[:, o, 0:N], in0=xf[:, o, 0:N], scalar=ALPHA, in1=pp[:, 0:N],
                                                                     op0=ALU.mult, op1=ALU.add), reads=[Bxf[o], Bp], writes=[Bxf[o]])
    fw.barrier([self.Bstg, self.Blnt[0], self.Blnt[3]])
    self.layer_norm(4, N)

Builder.setup_mix0 = setup_mix0
Builder.mixer0 = mixer0
Builder.s5_sample_init = s5_sample_init
Builder.load_cache = load_cache
Builder.setup_ssd = setup_ssd
Builder.mixer1 = mixer1

_OUT_ORDER = ["y_prompt", "y_sample", "s5_re_prompt", "s5_im_prompt", "sb_k_prompt", "sb_v_prompt", "ssd_prompt",
              "conv_prompt", "s5_re_sample", "s5_im_sample", "sb_k_sample", "sb_v_sample", "ssd_sample", "conv_sample"]
_BATCH_AXIS = {"x_prompt": 0, "x_sample": 0, "state_s5_re": 1, "state_s5_im": 1, "cache_sb_k": 1, "cache_sb_v": 1,
               "state_ssd": 1, "state_conv": 1}


def make_in_maps(inputs, n_cores):
    maps = []
    for c in range(n_cores):
        m = {}
        for k, v in inputs.items():
            v = np.asarray(v)
            if k in _BATCH_AXIS:
                ax = _BATCH_AXIS[k]
                n = v.shape[ax] // n_cores
                sl = [slice(None)] * v.ndim
                sl[ax] = slice(c * n, (c + 1) * n)
                m[k] = np.ascontiguousarray(v[tuple(sl)])
            else:
                m[k] = v
        maps.append(m)
    return maps


def gather(results):
    outs = []
    for nm in _OUT_ORDER:
        ax = 0 if nm in ("y_prompt", "y_sample") else 1
        outs.append(np.concatenate([r[nm] for r in results], axis=ax).astype(np.float32))
    return tuple(outs)


def kernel(**inputs):
    n = 8
    b = Builder(SEQ=4096, NP=2, NS=4)
    nc = b.build()
    res = run_bass_kernel_spmd(nc, make_in_maps(inputs, n), core_ids=list(range(n)))
    return gather(res.results)
```
